# Optimizing a Trainium2 kernel written in Bass

```python
import math
import jax, jax.numpy as jnp
from jax import lax
import numpy as np


D_MODEL = 2048
BATCH = 8
SEQ = 2048
DEPTH = 1

GRID_W = 64
CTX_LEN = 256
D_MIX = D_MODEL
D_LRU = D_MIX // 2
LRU_HEADS = 8
LRU_HEAD_DIM = D_LRU // LRU_HEADS
LRU_C = 8.0
D_GDN = D_MIX - D_LRU
GDN_HEADS = 8
GDN_HEAD_DIM = D_GDN // GDN_HEADS
N_DIR = 2
CONV_W = 4
CONV_PAD = (2, 1)
CHUNK = 64
D_FF = 4 * D_MODEL
NORM_EPS = 1e-6
LRU_X_END = D_LRU
LRU_Y_END = 2 * D_LRU
QKV_END = LRU_Y_END + 3 * D_GDN
Z_END = QKV_END + D_GDN
BETA_END = Z_END + N_DIR * GDN_HEADS
D_IN = BETA_END + N_DIR * GDN_HEADS

kernel_name = 'hymba_rglru_gdn_prefix_dit_block'


def rms_norm(t, w):
    tf = t.astype(jnp.float32)
    y = tf * lax.rsqrt(jnp.mean(tf * tf, axis=-1, keepdims=True) + NORM_EPS)
    return (y * w.astype(jnp.float32)).astype(t.dtype)


def l2_normalize(t):
    return t * lax.rsqrt(jnp.sum(t * t, axis=-1, keepdims=True) + NORM_EPS)


def to_column_major(t, rows):
    b, s, ch = t.shape
    return t.reshape(b, rows, GRID_W, ch).transpose(0, 2, 1, 3).reshape(b, s, ch)


def to_raster(t, rows):
    b, s, ch = t.shape
    return t.reshape(b, GRID_W, rows, ch).transpose(0, 2, 1, 3).reshape(b, s, ch)


def back_order(t, n_ctx):
    return jnp.concatenate([jnp.flip(t[:, :n_ctx], axis=1), jnp.flip(t[:, n_ctx:], axis=1)], axis=1)


def centred_depthwise_conv(t, w):
    return lax.conv_general_dilated(t, w[:, None, :].astype(t.dtype), window_strides=(1,),
                                    padding=[CONV_PAD], dimension_numbers=('NWC', 'WIO', 'NWC'),
                                    feature_group_count=t.shape[-1])


def _lru_combine(e1, e2):
    a1, u1 = e1
    a2, u2 = e2
    return a1 * a2, a2 * u1 + u2


def rglru_scan(seq, gate_w, gate_b, lam):
    b, n, _ = seq.shape
    xf = seq.astype(jnp.float32)
    gates = jnp.einsum('blhi,ghij->gblhj', xf.reshape(b, n, LRU_HEADS, LRU_HEAD_DIM),
                       gate_w.astype(jnp.float32)).reshape(2, b, n, D_LRU)
    r, i = jax.nn.sigmoid(gates + gate_b.astype(jnp.float32)[:, None, None, :])
    log_a = -LRU_C * r * jax.nn.softplus(-lam.astype(jnp.float32))
    a = jnp.exp(log_a)
    u = jnp.sqrt(-jnp.expm1(2.0 * log_a)) * (i * xf)
    _, h = lax.associative_scan(_lru_combine, (a, u), axis=1)
    return h


def gated_delta_chunked(q, k, v, g, beta):
    b, n_tok, h, dk = q.shape
    dv = v.shape[-1]
    nc = n_tok // CHUNK
    f32 = jnp.float32

    def chunks(t):
        return t.astype(f32).reshape(b, nc, CHUNK, h, -1).transpose(0, 3, 1, 2, 4)

    q_c, k_c, v_c = chunks(q), chunks(k), chunks(v)
    g_c = g.astype(f32).reshape(b, nc, CHUNK, h).transpose(0, 3, 1, 2)
    b_c = beta.astype(f32).reshape(b, nc, CHUNK, h).transpose(0, 3, 1, 2)
    g_cum = jnp.cumsum(g_c, axis=-1)
    tril = jnp.tril(jnp.ones((CHUNK, CHUNK), dtype=bool))
    decay = jnp.exp(jnp.where(tril, g_cum[..., :, None] - g_cum[..., None, :], -jnp.inf))
    k_beta = k_c * b_c[..., None]
    v_beta = v_c * b_c[..., None]
    l_mat = jnp.einsum('bhncd,bhnsd->bhncs', k_beta, k_c) * decay
    rhs = jnp.concatenate([v_beta, k_beta * jnp.exp(g_cum)[..., None]], axis=-1)
    sol = lax.linalg.triangular_solve(l_mat, rhs, left_side=True, lower=True, unit_diagonal=True)
    u_c, w_c = sol[..., :dv], sol[..., dv:]
    qk = jnp.einsum('bhncd,bhnsd->bhncs', q_c, k_c) * decay
    q_dec = q_c * jnp.exp(g_cum)[..., None]
    k_dec = k_c * jnp.exp(g_cum[..., -1:] - g_cum)[..., None]
    chunk_decay = jnp.exp(g_cum[..., -1])

    def step(state, xs):
        qk_i, q_dec_i, k_dec_i, u_i, w_i, cd_i = xs
        v_new = u_i - jnp.einsum('bhck,bhkv->bhcv', w_i, state)
        o_i = jnp.einsum('bhck,bhkv->bhcv', q_dec_i, state) + jnp.einsum('bhcs,bhsv->bhcv', qk_i, v_new)
        state = state * cd_i[..., None, None] + jnp.einsum('bhck,bhcv->bhkv', k_dec_i, v_new)
        return state, o_i

    xs = tuple(jnp.moveaxis(t, 2, 0) for t in (qk, q_dec, k_dec, u_c, w_c, chunk_decay))
    state0 = jnp.zeros((b, h, dk, dv), f32)
    _, o = lax.scan(step, state0, xs)
    return o.transpose(1, 0, 3, 2, 4).reshape(b, n_tok, h, dv)


def hybrid_mixer(h_ctx, h_lat, rows, need_ctx, w_in, lru_conv_w, lru_conv_b, lru_gate_w, lru_gate_b,
                 lru_lambda, gdn_conv_w, gdn_a_log, gdn_dt_bias, gdn_norm_w, w_out):
    n_ctx = h_ctx.shape[1]
    p = jnp.concatenate([h_ctx, h_lat], axis=1) @ w_in
    b, n_all, _ = p.shape
    p_ctx, p_lat = p[:, :n_ctx], p[:, n_ctx:]

    xr = jnp.concatenate([centred_depthwise_conv(p_ctx[..., :LRU_X_END], lru_conv_w),
                          centred_depthwise_conv(to_column_major(p_lat[..., :LRU_X_END], rows), lru_conv_w)],
                         axis=1) + lru_conv_b
    h_lru = rglru_scan(xr, lru_gate_w[0], lru_gate_b[0], lru_lambda[0]) + back_order(
        rglru_scan(back_order(xr, n_ctx), lru_gate_w[1], lru_gate_b[1], lru_lambda[1]), n_ctx)
    h_lru = jnp.concatenate([h_lru[:, :n_ctx], to_raster(h_lru[:, n_ctx:], rows)], axis=1).astype(p.dtype)
    y_gate = jax.nn.gelu(p[..., LRU_X_END:LRU_Y_END])

    qkv = jnp.concatenate([centred_depthwise_conv(p_ctx[..., LRU_Y_END:QKV_END], gdn_conv_w),
                           centred_depthwise_conv(p_lat[..., LRU_Y_END:QKV_END], gdn_conv_w)], axis=1)
    qkv = jax.nn.silu(qkv).astype(jnp.float32).reshape(b, n_all, 3, GDN_HEADS, GDN_HEAD_DIM)
    q = l2_normalize(qkv[:, :, 0]) * GDN_HEAD_DIM ** -0.5
    k = l2_normalize(qkv[:, :, 1])
    v = qkv[:, :, 2]
    gb = p[..., Z_END:].astype(jnp.float32).reshape(b, n_all, 2, N_DIR, GDN_HEADS)
    beta = jax.nn.sigmoid(gb[:, :, 0])
    g = -jnp.exp(gdn_a_log.astype(jnp.float32)) * jax.nn.softplus(gb[:, :, 1] + gdn_dt_bias.astype(jnp.float32))
    o = gated_delta_chunked(q, k, v, g[:, :, 0], beta[:, :, 0]) + back_order(
        gated_delta_chunked(back_order(q, n_ctx), back_order(k, n_ctx), back_order(v, n_ctx),
                            back_order(g[:, :, 1], n_ctx), back_order(beta[:, :, 1], n_ctx)), n_ctx)
    z = p[..., QKV_END:Z_END]

    def project(lo, hi):
        gdn = rms_norm(o[:, lo:hi], gdn_norm_w).reshape(b, hi - lo, D_GDN).astype(p.dtype) * jax.nn.silu(z[:, lo:hi])
        lru = h_lru[:, lo:hi] * y_gate[:, lo:hi]
        return jnp.concatenate([lru, gdn], axis=-1) @ w_out

    out_lat = project(n_ctx, n_all)
    out_ctx = project(0, n_ctx) if need_ctx else None
    return out_ctx, out_lat


def sq_relu_mlp(h, w1, w2):
    return jnp.square(jax.nn.relu(h @ w1)) @ w2


def setup_inputs(seed: int = 0) -> dict:
    key = jax.random.key(seed)
    ks = jax.random.split(key, 20)
    f32 = jnp.float32

    def nrm(k, shape, scale):
        return jax.random.normal(k, shape, f32) * scale

    x = nrm(ks[0], (BATCH, SEQ, D_MODEL), 1.0)
    c = nrm(ks[1], (BATCH, D_MODEL), 1.0)
    ctx = nrm(ks[2], (BATCH, CTX_LEN, D_MODEL), 1.0)
    c_ctx = nrm(ks[3], (D_MODEL,), 1.0)
    w_mod = nrm(ks[4], (DEPTH, D_MODEL, 6 * D_MODEL), 0.5 * D_MODEL ** -0.5)
    b_mod = nrm(ks[5], (DEPTH, 6 * D_MODEL), 0.01)
    norm_w = 1.0 + nrm(ks[6], (DEPTH, 4, D_MODEL), 0.01)
    w_in = nrm(ks[7], (DEPTH, D_MODEL, D_IN), D_MODEL ** -0.5)
    lru_conv_w = nrm(ks[8], (DEPTH, CONV_W, D_LRU), CONV_W ** -0.5)
    lru_conv_b = nrm(ks[9], (DEPTH, D_LRU), 0.01)
    lru_gate_w = nrm(ks[10], (DEPTH, N_DIR, 2, LRU_HEADS, LRU_HEAD_DIM, LRU_HEAD_DIM), LRU_HEAD_DIM ** -0.5)
    lru_gate_b = nrm(ks[11], (DEPTH, N_DIR, 2, D_LRU), 0.01)
    a_pow = jax.random.uniform(ks[12], (DEPTH, N_DIR, D_LRU), f32, 0.9, 0.999)
    a0 = a_pow ** (1.0 / LRU_C)
    lru_lambda = jnp.log(a0) - jnp.log1p(-a0)
    gdn_conv_w = nrm(ks[13], (DEPTH, CONV_W, 3 * D_GDN), CONV_W ** -0.5)
    gdn_a_log = jnp.log(jax.random.uniform(ks[14], (DEPTH, N_DIR, GDN_HEADS), f32, 1.0, 16.0))
    dt = jnp.exp(jax.random.uniform(ks[15], (DEPTH, N_DIR, GDN_HEADS), f32, math.log(0.001), math.log(0.1)))
    gdn_dt_bias = dt + jnp.log(-jnp.expm1(-dt))
    gdn_norm_w = 1.0 + nrm(ks[16], (DEPTH, GDN_HEAD_DIM), 0.01)
    w_out = nrm(ks[17], (DEPTH, D_MIX, D_MODEL), D_MIX ** -0.5)
    w_ff1 = nrm(ks[18], (DEPTH, D_MODEL, D_FF), D_MODEL ** -0.5)
    w_ff2 = nrm(ks[19], (DEPTH, D_FF, D_MODEL), D_FF ** -0.5)
    return {'x': x, 'c': c, 'ctx': ctx, 'c_ctx': c_ctx, 'w_mod': w_mod, 'b_mod': b_mod, 'norm_w': norm_w,
            'w_in': w_in, 'lru_conv_w': lru_conv_w, 'lru_conv_b': lru_conv_b, 'lru_gate_w': lru_gate_w,
            'lru_gate_b': lru_gate_b, 'lru_lambda': lru_lambda, 'gdn_conv_w': gdn_conv_w,
            'gdn_a_log': gdn_a_log, 'gdn_dt_bias': gdn_dt_bias, 'gdn_norm_w': gdn_norm_w,
            'w_out': w_out, 'w_ff1': w_ff1, 'w_ff2': w_ff2}


def reference(x, c, ctx, c_ctx, w_mod, b_mod, norm_w, w_in, lru_conv_w, lru_conv_b, lru_gate_w,
              lru_gate_b, lru_lambda, gdn_conv_w, gdn_a_log, gdn_dt_bias, gdn_norm_w, w_out, w_ff1, w_ff2):
    rows = x.shape[1] // GRID_W
    for layer in range(DEPTH):
        need_ctx = layer < DEPTH - 1
        mod = jax.nn.silu(c) @ w_mod[layer] + b_mod[layer]
        mod_c = jax.nn.silu(c_ctx) @ w_mod[layer] + b_mod[layer]
        sh_m, sc_m, g_m, sh_f, sc_f, g_f = jnp.split(mod[:, None, :], 6, axis=-1)
        csh_m, csc_m, cg_m, csh_f, csc_f, cg_f = jnp.split(mod_c, 6, axis=-1)
        nw = norm_w[layer]

        h_lat = rms_norm(x, nw[0]) * (1.0 + sc_m) + sh_m
        h_ctx = rms_norm(ctx, nw[0]) * (1.0 + csc_m) + csh_m
        m_ctx, m_lat = hybrid_mixer(h_ctx, h_lat, rows, need_ctx, w_in[layer], lru_conv_w[layer],
                                    lru_conv_b[layer], lru_gate_w[layer], lru_gate_b[layer], lru_lambda[layer],
                                    gdn_conv_w[layer], gdn_a_log[layer], gdn_dt_bias[layer], gdn_norm_w[layer],
                                    w_out[layer])
        x = x + g_m * rms_norm(m_lat, nw[1])
        h = rms_norm(x, nw[2]) * (1.0 + sc_f) + sh_f
        x = x + g_f * rms_norm(sq_relu_mlp(h, w_ff1[layer], w_ff2[layer]), nw[3])

        if need_ctx:
            ctx = ctx + cg_m * rms_norm(m_ctx, nw[1])
            hc = rms_norm(ctx, nw[2]) * (1.0 + csc_f) + csh_f
            ctx = ctx + cg_f * rms_norm(sq_relu_mlp(hc, w_ff1[layer], w_ff2[layer]), nw[3])
    return x
```

```python
from contextlib import ExitStack
import numpy as np
import concourse.bass as bass
import concourse.mybir as mybir
from concourse.alu_op_type import AluOpType as ALU
from concourse.bass_utils import run_bass_kernel_spmd

F32 = mybir.dt.float32
F32R = mybir.dt.float32r
BF16 = mybir.dt.bfloat16
AF = mybir.ActivationFunctionType

ENGS = ("pe", "dve", "act", "pool", "sp")
INF = 1 << 60
EPS = 1e-6


class _Rec:
    def __init__(self):
        self.call = None

    def __getattr__(self, name):
        def f(*a, **k):
            self.call = (name, a, k)
            return self
        return f


def _capture(fn):
    r = _Rec()
    fn(r)
    assert r.call is not None
    return r.call


class Prog:
    def __init__(self, nc, stack, n_dma_sems=30):
        self.nc = nc
        self.ops = {e: [] for e in ENGS}
        self.cnt = {e: 0 for e in ENGS}
        self.sem = {e: stack.enter_context(nc.semaphore("s_" + e)) for e in ENGS}
        self.dsem = [stack.enter_context(nc.semaphore("d%d" % i)) for i in range(n_dma_sems)]
        self.dcnt = [0] * n_dma_sems
        self.dnext = 0
        self.dnext_q = {}
        self.waited = {e: {} for e in ENGS}
        self.res = {}
        self.nops = 0
        self.psrd = {}

    def _deps(self, reads, writes):
        deps = []
        for (name, lo, hi) in reads:
            st = self.res.setdefault(name, {"w": [], "r": []})
            for (a, b, tok) in st["w"]:
                if a < hi and lo < b:
                    deps.append(tok)
        for (name, lo, hi) in writes:
            st = self.res.setdefault(name, {"w": [], "r": []})
            for (a, b, tok) in st["w"]:
                if a < hi and lo < b:
                    deps.append(tok)
            for (a, b, tok) in st["r"]:
                if a < hi and lo < b:
                    deps.append(tok)
        return deps

    def _record(self, reads, writes, tok):
        for (name, lo, hi) in writes:
            st = self.res[name]
            st["w"] = [(a, b, t) for (a, b, t) in st["w"] if not (lo <= a and b <= hi)]
            st["r"] = [(a, b, t) for (a, b, t) in st["r"] if not (lo <= a and b <= hi)]
            st["w"].append((lo, hi, tok))
        for (name, lo, hi) in reads:
            st = self.res[name]
            st["r"] = [(a, b, t) for (a, b, t) in st["r"]
                       if not (t[0] == tok[0] and lo <= a and b <= hi)]
            st["r"].append((lo, hi, tok))

    @staticmethod
    def _norm(rs):
        out = []
        for r in rs:
            if isinstance(r, str):
                out.append((r, 0, INF))
            elif len(r) == 2:
                out.append((r[0], r[1], r[1] + 1))
            else:
                out.append(tuple(r))
        return out

    def _waits(self, eng, deps):
        ws = {}
        for (key, val, deng) in deps:
            if deng == eng and eng == "pe":
                continue
            if self.waited[eng].get(key, 0) >= val:
                continue
            ws[key] = max(ws.get(key, 0), val)
        for k, v in ws.items():
            self.waited[eng][k] = v
        return list(ws.items())

    def op(self, eng, fn, reads=(), writes=(), inc=True):
        reads = self._norm(reads)
        writes = self._norm(writes)
        deps = self._deps(reads, writes)
        psread = eng in ("dve", "act") and any(r[0] == "ps" for r in reads)
        if psread:
            other = "act" if eng == "dve" else "dve"
            if other in self.psrd:
                deps.append(self.psrd[other])
        waits = self._waits(eng, deps)
        idx = self.cnt[eng] + 1
        if inc:
            self.cnt[eng] = idx
        tok = (("e", eng), idx, eng)
        if psread:
            self.psrd[eng] = tok
        self._record(reads, writes, tok)
        self.ops[eng].append((waits, _capture(fn), ("e", eng) if inc else None, 1))
        self.nops += 1
        return tok

    def dma(self, eng, fn, reads=(), writes=()):
        reads = self._norm(reads)
        writes = self._norm(writes)
        deps = self._deps(reads, writes)
        nd = len(self.dsem)
        lo, hi = (0, (nd * 3) // 5) if eng == "sp" else ((nd * 3) // 5, nd)
        cur = self.dnext_q.get(eng, lo)
        j = cur
        self.dnext_q[eng] = lo + (cur + 1 - lo) % (hi - lo)
        if self.dcnt[j] > 0:
            deps.append((("d", j), 16 * self.dcnt[j], "dma"))
        waits = self._waits(eng, deps)
        self.dcnt[j] += 1
        tok = (("d", j), 16 * self.dcnt[j], "dma")
        self._record(reads, writes, tok)
        self.ops[eng].append((waits, _capture(fn), ("d", j), 16))
        self.nops += 1
        return tok

    def barrier(self):
        deps = [(("e", e), self.cnt[e], "x") for e in ENGS if self.cnt[e] > 0]
        deps += [(("d", j), 16 * c, "dma") for j, c in enumerate(self.dcnt) if c > 0]
        for e in ENGS:
            waits = self._waits(e, [d for d in deps if d[0] != ("e", e)])
            if waits:
                self.ops[e].append((waits, None, None, 0))
        self.res = {}

    def final_wait(self, eng, toks):
        waits = self._waits(eng, list(toks))
        self.ops[eng].append((waits, None, None, 0))

    def _semobj(self, key):
        return self.sem[key[1]] if key[0] == "e" else self.dsem[key[1]]

    def emit(self, block):
        def mk(ename):
            def body(e):
                for (waits, fn, inckey, incv) in self.ops[ename]:
                    for (k, v) in waits:
                        e.wait_ge(self._semobj(k), v)
                    if fn is None:
                        continue
                    name, a, k = fn
                    ins = getattr(e, name)(*a, **k)
                    if inckey is not None:
                        ins.then_inc(self._semobj(inckey), incv)
            return body
        if self.ops["sp"]:
            block.sync(mk("sp"))
        if self.ops["pe"]:
            block.tensor(mk("pe"))
        if self.ops["dve"]:
            block.vector(mk("dve"))
        if self.ops["act"]:
            block.scalar(mk("act"))
        if self.ops["pool"]:
            block.gpsimd(mk("pool"))


D = 2048
S = 2048
NCTX = 256
NALL = NCTX + S
DIN = 6176
DFF = 8192
NCH = NALL // 128
XW = 2310
TW = 2307
LT = 259
NEGBIG = -30000.0


def ch_off(ch):
    return ch * 128 if ch < 2 else LT + (ch - 2) * 128


def build(dbg=False, phases=("A", "B1", "B2", "C", "D")):
    nc = bass.Bass("TRN2", target_bir_lowering=False)
    din = lambda n, s, d=F32: nc.dram_tensor(n, s, d, kind="ExternalInput").ap()
    x_d = din("x", [S, D]); ctx_d = din("ctx", [NCTX, D]); cc_d = din("cc", [128, 32])
    wmod_d = din("w_mod", [D, 6 * D]); bmodT_d = din("b_modT", [128, 96]); nwT_d = din("nwT", [128, 64])
    win_d = din("w_in", [D, DIN])
    lcw_d = din("lcw", [128, 32]); lcb_d = din("lcb", [128, 8]); lgw_d = din("lgw", [128, 4096])
    lgb_d = din("lgb", [128, 32]); llam_d = din("llam", [128, 16])
    gcw_d = din("gcw", [128, 96]); galog_d = din("galog", [128, 16]); gdtb_d = din("gdtb", [128, 16])
    gnw_d = din("gnw", [128, 1])
    wout_d = din("w_out", [D, D]); w1_d = din("w_ff1", [D, DFF]); w2_d = din("w_ff2", [DFF, D])
    out_d = nc.dram_tensor("out", [S, D], F32, kind="ExternalOutput").ap()
    skind = "ExternalOutput" if dbg else "Internal"
    mix_scr = nc.dram_tensor("mix_scr", [16, 128, S], BF16, kind=skind).ap()
    h2_scr = nc.dram_tensor("h2_scr", [16, 128, S], BF16, kind=skind).ap()
    x1_scr = nc.dram_tensor("x1_scr", [S, D], F32, kind=skind).ap()
    w1b = nc.dram_tensor("w1b", [64, 128, 16, 128], BF16, kind="Internal").ap()
    w2b = nc.dram_tensor("w2b", [16, 128, 64, 128], BF16, kind="Internal").ap()
    if dbg:
        hT_dbg = nc.dram_tensor("hT_dbg", [128, 16, NALL], BF16, kind="ExternalOutput").ap()
        modT_dbg = nc.dram_tensor("modT_dbg", [128, 192], F32, kind="ExternalOutput").ap()

    with ExitStack() as top:
        P = Prog(nc, top)
        psb = [top.enter_context(nc.psum_tensor("ps%d" % i, [128, 512], F32)) for i in range(8)]
        pst = {"i": 0}

        reserved = set()

        def bank():
            while True:
                i = pst["i"]; pst["i"] = (i + 1) % 8
                if i not in reserved:
                    return i

        sbt = lambda st, n, s, d=F32: st.enter_context(nc.sbuf_tensor("sb_" + n, s, d))
        ident = sbt(top, "ident", [128, 128]); ones = sbt(top, "ones", [128, 128])
        onesb = sbt(top, "onesb", [128, 128], BF16)
        par = sbt(top, "par", [128, 64 + 96 + 32 + 8 + 32 + 16 + 96 + 16 + 16 + 1])
        o_ = [0]
        def psl(n):
            a = o_[0]; o_[0] += n
            return (a, a + n)
        s_nw, s_bm, s_lcw, s_lcb, s_lgb, s_llam, s_gcw, s_gal, s_gdt, s_gnw = [psl(n) for n in (64, 96, 32, 8, 32, 16, 96, 16, 16, 1)]
        pv = lambda s: par[:, s[0]:s[1]]
        modT = sbt(top, "modT", [128, 192])
        vecs = sbt(top, "vecs", [128, 8, 16])
        cneg = sbt(top, "cneg", [128, 16])
        sccb = sbt(top, "sccb", [128, 32], BF16)
        C3 = [128, NCH, 16]
        beta = sbt(top, "beta", C3); nbeta = sbt(top, "nbeta", C3); gg = sbt(top, "gg", C3); gam = sbt(top, "gam", C3)
        ngam = sbt(top, "ngam", C3); glast = sbt(top, "glast", C3); cdl = sbt(top, "cdl", C3); neg_eg = sbt(top, "neg_eg", C3); kdec = sbt(top, "kdec", C3)
        nega = sbt(top, "nega", [128, 16])
        trif = sbt(top, "trif", [128, 128]); trib = sbt(top, "trib", [128, 128])
        negm = [sbt(top, "negm%d" % d, [128, 128]) for d in range(2)]
        offd = sbt(top, "offd", [128, 128]); bd32 = sbt(top, "bd32", [128, 128]); od64 = sbt(top, "od64", [128, 128]); od128 = sbt(top, "od128", [128, 128])
        s_h = ExitStack()
        hT = sbt(s_h, "hT", [128, 16, NALL], BF16)
        p_scr = nc.dram_tensor("p_scr", [8, 3, 128, NALL], F32, kind="Internal").ap()
        zs_scr = nc.dram_tensor("zs_scr", [8, 128, S], BF16, kind="Internal").ap()
        xs_scr = nc.dram_tensor("xs_scr", [8, 128, NALL], F32, kind="Internal").ap()
        gy_scr = nc.dram_tensor("gy_scr", [8, 128, S], BF16, kind="Internal").ap()
        oacc_scr = nc.dram_tensor("oacc_scr", [16, 128, 128], F32, kind="Internal").ap()

        def make_row(st, vi, dn):
            dst = sbt(st, dn, [128, D]); dg = sbt(st, "dg_" + dn, [128, 512])
            for g4 in range(4):
                for q in range(4):
                    k = g4 * 4 + q
                    P.op("dve", lambda e: e.tensor_scalar(out=dg[:, q * 128:(q + 1) * 128], in0=ident[:], scalar1=vecs[:, vi, k:k + 1], scalar2=None, op0=ALU.mult),
                         reads=["vecs", "ident"], writes=[("dg", q)])
                b = bank()
                P.op("pe", lambda e: e.matmul(psb[b][:], lhsT=ones[:], rhs=dg[:], start=True, stop=True), reads=["ones", "dg"], writes=[("ps", b)])
                P.op("act", lambda e: e.copy(out=dst[:, g4 * 512:(g4 + 1) * 512], in_=psb[b][:]), reads=[("ps", b)], writes=[(dn, g4)])
            return dst

        def load(dst, src, q="sp", name=None, reads=()):
            return P.dma(q, lambda e: e.dma_start(out=dst, in_=src), reads=reads, writes=[name] if name else [])

        P.op("pool", lambda e: e.memset(ones[:], 1.0), writes=["ones"])
        P.op("pool", lambda e: e.memset(onesb[:], 1.0), writes=["onesb"])
        P.op("pool", lambda e: e.memset(ident[:], 1.0), writes=["ident"])
        P.op("pool", lambda e: e.affine_select(out=ident[:], in_=ident[:], pattern=[[-1, 128]], compare_op=ALU.is_equal,
                                               fill=0.0, base=0, channel_multiplier=1), reads=["ident"], writes=["ident"])
        for (sl, src) in ((s_nw, nwT_d), (s_bm, bmodT_d), (s_lcw, lcw_d), (s_lcb, lcb_d), (s_lgb, lgb_d), (s_llam, llam_d),
                          (s_gcw, gcw_d), (s_gal, galog_d), (s_gdt, gdtb_d), (s_gnw, gnw_d)):
            load(pv(sl), src, name=("par", sl[0], sl[1]))

        with ExitStack() as sa:
            cc = sbt(sa, "cc", [128, 32]); scc = sbt(sa, "scc", [128, 32])
            wm = [sbt(sa, "wm%d" % i, [128, 4096], BF16) for i in range(3)]
            load(cc[:], cc_d, name="cc")
            P.op("act", lambda e: e.activation(out=scc[:], in_=cc[:], func=AF.Silu), reads=["cc"], writes=["scc"])
            P.op("dve", lambda e: e.tensor_copy(out=sccb[:], in_=scc[:]), reads=["scc"], writes=["sccb"])
            mb = bank()
            first = True
            for k in range(16):
                i = k % 3
                P.dma("pool", lambda e: e.dma_start(out=wm[i][:], in_=wmod_d[k * 128:(k + 1) * 128, 0:4096]), writes=["wm%d" % i])
                for j in range(32):
                    P.op("pe", lambda e: e.matmul(psb[mb][:, j * 2:j * 2 + 2], lhsT=wm[i][:, j * 128:(j + 1) * 128], rhs=sccb[:, k * 2:k * 2 + 2],
                                                  start=first, stop=(k == 15), skip_group_check=True),
                         reads=["wm%d" % i, "sccb"], writes=[("ps", mb)], inc=(j == 31))
                    first = False
            P.op("dve", lambda e: e.tensor_tensor(out=modT[:, 0:64].rearrange("p (j r) -> p j r", r=2),
                                                  in0=psb[mb][:, 0:64].rearrange("p (j r) -> p j r", r=2),
                                                  in1=pv(s_bm)[:, 0:32].unsqueeze(2).to_broadcast([128, 32, 2]), op=ALU.add),
                 reads=[("ps", mb), ("par", s_bm[0], s_bm[1])], writes=[("modT", 0, 64)])
            mv = modT[:].rearrange("p (j r) -> p j r", r=2)
            nw = pv(s_nw).rearrange("p (w k) -> p w k", w=4)
            rp = [("par", s_nw[0], s_nw[1]), "modT"]
            def scl(dst, sc, w):
                P.op("dve", lambda e: e.scalar_tensor_tensor(out=dst, in0=sc, scalar=1.0, in1=w, op0=ALU.add, op1=ALU.mult),
                     reads=rp, writes=["vecs"])
            scl(vecs[:, 0, :], mv[:, 16:32, 0], nw[:, 0, :])
            P.op("dve", lambda e: e.tensor_copy(out=vecs[:, 1, :], in_=mv[:, 0:16, 0]), reads=rp, writes=["vecs"])
            scl(vecs[:, 2, :], mv[:, 16:32, 1], nw[:, 0, :])
            P.op("dve", lambda e: e.tensor_copy(out=vecs[:, 3, :], in_=mv[:, 0:16, 1]), reads=rp, writes=["vecs"])
            P.op("act", lambda e: e.activation(out=cneg[:], in_=pv(s_llam), func=AF.Exp, scale=-1.0), reads=[("par", s_llam[0], s_llam[1])], writes=["cneg"])
            P.op("act", lambda e: e.activation(out=cneg[:], in_=cneg[:], func=AF.Ln, bias=1.0), reads=["cneg"], writes=["cneg"])
            P.op("dve", lambda e: e.tensor_scalar(out=cneg[:], in0=cneg[:], scalar1=-8.0, scalar2=None, op0=ALU.mult), reads=["cneg"], writes=["cneg"])
            if dbg:
                load(modT_dbg, modT[:], reads=["modT"])

            xt = [sbt(sa, "xt%d" % i, [128, D]) for i in range(2)]
            junk = sbt(sa, "junkA", [128, D])
            st1 = [sbt(sa, "st1_%d" % i, [128, 4]) for i in range(2)]

            def norm_T(src_ap, tname, ti, scl_i, sh_i, dstT, dname, col0, stt, stn):
                t = xt[ti]
                P.op("act", lambda e: e.activation(out=junk[:], in_=t[:], func=AF.Square, accum_out=stt[:, 0:1]),
                     reads=[tname], writes=["junkA", stn])
                P.op("act", lambda e: e.activation(out=stt[:, 1:2], in_=stt[:, 0:1], func=AF.Sqrt, scale=1.0 / D, bias=EPS), reads=[stn], writes=[stn])
                P.op("dve", lambda e: e.reciprocal(out=stt[:, 2:3], in_=stt[:, 1:2]), reads=[stn], writes=[stn])
                P.op("act", lambda e: e.activation(out=t[:], in_=t[:], func=AF.Copy, scale=stt[:, 2:3]), reads=[stn, tname], writes=[tname])
                for g4 in range(4):
                    b = bank()
                    for q in range(4):
                        k = g4 * 4 + q
                        P.op("pe", (lambda b, q, k: lambda e: e.transpose(psb[b][:, q * 128:(q + 1) * 128], t[:, k * 128:(k + 1) * 128], ident[:]))(b, q, k),
                             reads=[tname, "ident"], writes=[("ps", b)], inc=(q == 3))
                    for q in range(4):
                        k = g4 * 4 + q
                        eng = "dve" if q % 2 == 0 else "act"
                        if eng == "dve":
                            fn = (lambda b, q, k: lambda e: e.tensor_scalar(out=dstT[:, k, col0:col0 + 128], in0=psb[b][:, q * 128:(q + 1) * 128],
                                                                            scalar1=vecs[:, scl_i, k:k + 1], scalar2=vecs[:, sh_i, k:k + 1],
                                                                            op0=ALU.mult, op1=ALU.add))(b, q, k)
                        else:
                            fn = (lambda b, q, k: lambda e: e.activation(out=dstT[:, k, col0:col0 + 128], in_=psb[b][:, q * 128:(q + 1) * 128],
                                                                         func=AF.Identity, scale=vecs[:, scl_i, k:k + 1], bias=vecs[:, sh_i, k:k + 1]))(b, q, k)
                        P.op(eng, fn, reads=[("ps", b), "vecs"], writes=[(dname, k * 100000 + col0, k * 100000 + col0 + 128)])

            import os as _os
            for ti in range(int(_os.environ.get("K_NTI", NCH))):
                src = ctx_d[ti * 128:(ti + 1) * 128, :] if ti < 2 else x_d[(ti - 2) * 128:(ti - 1) * 128, :]
                i = ti % 2
                load(xt[i][:], src, name="xt%d" % i)
                norm_T(src, "xt%d" % i, i, 2 if ti < 2 else 0, 3 if ti < 2 else 1, hT, "hT", ti * 128, st1[i], "st1_%d" % i)
            if dbg:
                load(hT_dbg, hT[:], reads=["hT"])
        P.barrier()

        hTr = lambda k, c0, c1: ("hT", k * 100000 + c0, k * 100000 + c1)
        hT_all = [("hT", 0, INF)]

        def precast():
            import os as _os
            if _os.environ.get("K_NOPRECAST"):
                return
            w1v = w1_d.rearrange("(k p) (o c) -> o p k c", p=128, c=128)
            for o in range(0, 64, 4):
                for oo in range(4):
                    P.dma("pool", (lambda o: lambda e: e.dma_start(out=w1b[o], in_=w1v[o]))(o + oo), writes=[("w1b", o + oo)])
            w2v = w2_d.rearrange("(k p) (o c) -> o p k c", p=128, c=128)
            for o in range(16):
                for kh in range(4):
                    P.dma("pool", (lambda o, kh: lambda e: e.dma_start(out=w2b[o][:, kh * 16:(kh + 1) * 16, :], in_=w2v[o][:, kh * 16:(kh + 1) * 16, :]))(o, kh),
                          writes=[("w2b", o * 4 + kh)])

        winv = win_d.rearrange("(k p) c -> p k c", p=128)

        def inproj(wt, wname, woff, tok_groups, consume):
            for gi, (c0, n) in enumerate(tok_groups):
                b = bank()
                for k in range(16):
                    P.op("pe", (lambda b, k, c0, n: lambda e: e.matmul(psb[b][:, 0:n], lhsT=wt[:, k, woff:woff + 128], rhs=hT[:, k, c0:c0 + n],
                                                                          start=(k == 0), stop=(k == 15)))(b, k, c0, n),
                         reads=[wname, hTr(k, c0, c0 + n)], writes=[("ps", b)], inc=(k == 15))
                consume(b, gi)

        TG_ALL = [(0, 256)] + [(256 + g * 512, 512) for g in range(4)]
        TG_LAT = [(256 + g * 512, 512) for g in range(4)]

        if "B2" in phases:
          with ExitStack() as sb2a:
            wgb = sbt(sb2a, "wgb", [128, 16, 32], BF16)
            def msk(t, tn, base_val, fill, pattern, cmp, base, cm, src=None):
                if src is None:
                    P.op("pool", lambda e: e.memset(t[:], base_val), writes=[tn])
                P.op("pool", lambda e: e.affine_select(out=t[:], in_=t[:], pattern=pattern, compare_op=cmp, fill=fill, base=base, channel_multiplier=cm),
                     reads=[tn], writes=[tn])
            msk(trif, "trif", 1.0, 0.0, [[1, 128]], ALU.is_ge, 0, -1)
            msk(trib, "trib", 1.0, 0.0, [[-1, 128]], ALU.is_ge, 0, 1)
            msk(negm[0], "negm0", 0.0, NEGBIG, [[1, 128]], ALU.is_ge, 0, -1)
            msk(negm[1], "negm1", 0.0, NEGBIG, [[-1, 128]], ALU.is_ge, 0, 1)
            msk(offd, "offd", 1.0, 0.0, [[-1, 128]], ALU.not_equal, 0, 1)
            for (t, tn, fn_) in ((bd32, "bd32", lambda pb, cb: 1.0 if pb == cb else 0.0),
                                 (od64, "od64", lambda pb, cb: 1.0 if (pb != cb and pb // 2 == cb // 2) else 0.0),
                                 (od128, "od128", lambda pb, cb: 1.0 if pb // 2 != cb // 2 else 0.0)):
                for pb in range(4):
                    for cb in range(4):
                        P.op("pool", (lambda t, pb, cb, v: lambda e: e.memset(t[pb * 32:(pb + 1) * 32, cb * 32:(cb + 1) * 32], v))(t, pb, cb, fn_(pb, cb)), writes=[tn])
            P.dma("pool", lambda e: e.dma_start(out=wgb[:], in_=winv[:, :, 6144:6176]), writes=["wgb"])
            gbank = [bank(), bank()]
            for ch in range(NCH):
                b = gbank[ch // 9]; co = (ch % 9) * 32
                for k in range(16):
                    P.op("pe", (lambda b, co, ch, k: lambda e: e.matmul(psb[b][:, co:co + 32], lhsT=hT[:, k, ch * 128:(ch + 1) * 128], rhs=wgb[:, k, :],
                                                                          start=(k == 0), stop=(k == 15), skip_group_check=True))(b, co, ch, k),
                         reads=["wgb"] + hT_all, writes=[("ps", b)], inc=(k == 15))
            P.op("act", lambda e: e.activation(out=nega[:], in_=pv(s_gal), func=AF.Exp), reads=[("par", 0, INF)], writes=["nega"])
            P.op("dve", lambda e: e.tensor_scalar(out=nega[:], in0=nega[:], scalar1=-1.0, scalar2=None, op0=ALU.mult), reads=["nega"], writes=["nega"])
            for half in range(2):
                b = gbank[half]
                pvw = psb[b][:, 0:288].rearrange("p (c f) -> p c f", f=32)
                cs = slice(half * 9, half * 9 + 9)
                P.op("act", (lambda pvw, cs: lambda e: e.activation(out=beta[:, cs, :], in_=pvw[:, :, 0:16], func=AF.Sigmoid))(pvw, cs), reads=[("ps", b)], writes=["beta"])
                P.op("dve", (lambda pvw, cs: lambda e: e.tensor_tensor(out=gg[:, cs, :], in0=pvw[:, :, 16:32], in1=pv(s_gdt).unsqueeze(1).to_broadcast([128, 9, 16]), op=ALU.add))(pvw, cs),
                     reads=[("ps", b), ("par", 0, INF)], writes=["gg"])
            P.op("act", lambda e: e.activation(out=gg[:], in_=gg[:], func=AF.Exp), reads=["gg"], writes=["gg"])
            P.op("act", lambda e: e.activation(out=gg[:], in_=gg[:], func=AF.Ln, bias=1.0), reads=["gg"], writes=["gg"])
            P.op("dve", lambda e: e.tensor_tensor(out=gg[:], in0=gg[:], in1=nega[:].unsqueeze(1).to_broadcast([128, NCH, 16]), op=ALU.mult), reads=["gg", "nega"], writes=["gg"])
            P.op("dve", lambda e: e.tensor_scalar(out=nbeta[:], in0=beta[:], scalar1=-1.0, scalar2=None, op0=ALU.mult), reads=["beta"], writes=["nbeta"])
            for d in range(2):
                b = bank()
                tri = trif if d == 0 else trib
                P.op("pe", (lambda b, tri, d: lambda e: e.matmul(psb[b][:, 0:NCH * 8].rearrange("p (c f) -> p c f", f=8), lhsT=tri[:], rhs=gg[:, :, d * 8:(d + 1) * 8], start=True, stop=True))(b, tri, d),
                     reads=["trif", "trib", "gg"], writes=[("ps", b)])
                P.op("act", (lambda b, d: lambda e: e.copy(out=gam[:, :, d * 8:(d + 1) * 8], in_=psb[b][:, 0:NCH * 8].rearrange("p (c f) -> p c f", f=8)))(b, d),
                     reads=[("ps", b)], writes=["gam"])
            b = bank()
            P.op("pe", (lambda b: lambda e: e.matmul(psb[b][:, 0:NCH * 16], lhsT=ones[:], rhs=gg[:].rearrange("p c f -> p (c f)"), start=True, stop=True))(b),
                 reads=["ones", "gg"], writes=[("ps", b)])
            P.op("act", (lambda b: lambda e: e.copy(out=glast[:].rearrange("p c f -> p (c f)"), in_=psb[b][:, 0:NCH * 16]))(b), reads=[("ps", b)], writes=["glast"])
            P.op("dve", lambda e: e.tensor_scalar(out=ngam[:], in0=gam[:], scalar1=-1.0, scalar2=None, op0=ALU.mult), reads=["gam"], writes=["ngam"])
            P.op("act", lambda e: e.activation(out=cdl[:], in_=glast[:], func=AF.Exp), reads=["glast"], writes=["cdl"])
            P.op("dve", lambda e: e.tensor_tensor(out=kdec[:], in0=glast[:], in1=gam[:], op=ALU.subtract), reads=["glast", "gam"], writes=["kdec"])
            P.op("act", lambda e: e.activation(out=kdec[:], in_=kdec[:], func=AF.Exp), reads=["kdec"], writes=["kdec"])
            P.op("act", lambda e: e.activation(out=neg_eg[:], in_=gam[:], func=AF.Exp), reads=["gam"], writes=["neg_eg"])
            P.op("dve", lambda e: e.tensor_scalar(out=neg_eg[:], in0=neg_eg[:], scalar1=-1.0, scalar2=None, op0=ALU.mult), reads=["neg_eg"], writes=["neg_eg"])

            wm2 = [sbt(sb2a, "wm2_%d" % i, [128, 4096], BF16) for i in range(3)]
            mb2 = bank(); reserved.add(mb2)
            mpieces = [(k, pc) for k in range(16) for pc in (1, 2)]
            mpi = [0]

            def mod_pieces(n):
                for _ in range(n):
                    if mpi[0] >= len(mpieces):
                        return
                    k, pc = mpieces[mpi[0]]
                    i = mpi[0] % 3
                    first = (mpi[0] == 0)
                    mpi[0] += 1
                    P.dma("pool", lambda e: e.dma_start(out=wm2[i][:], in_=wmod_d[k * 128:(k + 1) * 128, pc * 4096:(pc + 1) * 4096]), writes=["wm2_%d" % i])
                    for j in range(32):
                        jj = (pc - 1) * 32 + j
                        P.op("pe", lambda e: e.matmul(psb[mb2][:, jj * 2:jj * 2 + 2], lhsT=wm2[i][:, j * 128:(j + 1) * 128], rhs=sccb[:, k * 2:k * 2 + 2],
                                                      start=(first and j == 0), stop=(k == 15), skip_group_check=True),
                             reads=["wm2_%d" % i, "sccb"], writes=[("ps", mb2)], inc=(j == 31))

            wg = [sbt(sb2a, "wg%d" % i, [128, 16, 512], BF16) for i in range(2)]
            stgs = [sbt(sb2a, "stgs%d" % i, [128, 512]) for i in range(4)]
            zst = [sbt(sb2a, "zst%d" % i, [128, 512], BF16) for i in range(2)]
            sti = [0]
            for h in range(8):
                w = wg[h % 2]; wn = "wg%d" % (h % 2)
                for t4 in range(4):
                    c0 = 2048 + t4 * 1024 + h * 128
                    P.dma("pool", lambda e: e.dma_start(out=w[:, :, t4 * 128:(t4 + 1) * 128], in_=winv[:, :, c0:c0 + 128]), writes=[(wn, t4)])
                for t3 in range(3):
                    def cons_p(b, gi):
                        c0, n = TG_ALL[gi]
                        i = sti[0] % 4; sti[0] += 1
                        P.op("act", lambda e: e.copy(out=stgs[i][:, 0:n], in_=psb[b][:, 0:n]), reads=[("ps", b)], writes=["stgs%d" % i])
                        load(p_scr[h, t3][:, c0:c0 + n], stgs[i][:, 0:n], reads=["stgs%d" % i], name=("p_scr", h * 3 + t3))
                    inproj(w, (wn, t3), t3 * 128, TG_ALL, cons_p)

                def cons_zs(b, gi):
                    i = gi % 2
                    P.op("act", lambda e: e.activation(out=zst[i][:], in_=psb[b][:], func=AF.Silu), reads=[("ps", b)], writes=["zst%d" % i])
                    load(zs_scr[h][:, gi * 512:(gi + 1) * 512], zst[i][:], reads=["zst%d" % i], name=("zs_scr", h))
                inproj(w, (wn, 3), 384, TG_LAT, cons_zs)
                mod_pieces(2)
            xst = [sbt(sb2a, "xst%d" % i, [128, NALL]) for i in range(2)]
            gyst = [sbt(sb2a, "gyst%d" % i, [128, S], BF16) for i in range(2)]
            for h in range(8):
                w = wg[h % 2]; wn = "wg%d" % (h % 2)
                P.dma("pool", lambda e: e.dma_start(out=w[:, :, 0:128], in_=winv[:, :, h * 128:(h + 1) * 128]), writes=[(wn, 0)])
                P.dma("pool", lambda e: e.dma_start(out=w[:, :, 128:256], in_=winv[:, :, 1024 + h * 128:1024 + (h + 1) * 128]), writes=[(wn, 1)])
                xs = xst[h % 2]; xsn = "xst%d" % (h % 2)
                gs_ = gyst[h % 2]; gsn = "gyst%d" % (h % 2)

                def cons_x(b, gi):
                    if gi == 0:
                        P.op("act", lambda e: e.copy(out=xs[:, 0:256], in_=psb[b][:, 0:256]), reads=[("ps", b)], writes=[(xsn, 0)])
                    else:
                        r0 = (gi - 1) * 8
                        ov = xs[:, 256:NALL].rearrange("p (c r) -> p r c", r=32)[:, r0:r0 + 8, :]
                        iv = psb[b][:].rearrange("p (r c) -> p r c", c=64)
                        P.op("act", lambda e: e.copy(out=ov, in_=iv), reads=[("ps", b)], writes=[(xsn, gi)])
                inproj(w, (wn, 0), 0, TG_ALL, cons_x)
                load(xs_scr[h], xs[:], reads=[xsn], name=("xs_scr", h))

                def cons_y(b, gi):
                    P.op("act", lambda e: e.activation(out=gs_[:, gi * 512:(gi + 1) * 512], in_=psb[b][:], func=AF.Gelu_apprx_tanh), reads=[("ps", b)], writes=[(gsn, gi)])
                inproj(w, (wn, 1), 128, TG_LAT, cons_y)
                load(gy_scr[h], gs_[:], reads=[gsn], name=("gy_scr", h))
                mod_pieces(2)
            mod_pieces(100)
            P.op("dve", lambda e: e.tensor_tensor(out=modT[:, 64:192].rearrange("p (j r) -> p j r", r=2),
                                                  in0=psb[mb2][:, 0:128].rearrange("p (j r) -> p j r", r=2),
                                                  in1=pv(s_bm)[:, 32:96].unsqueeze(2).to_broadcast([128, 64, 2]), op=ALU.add),
                 reads=[("ps", mb2), ("par", 0, INF)], writes=[("modT", 64, 192)])
            reserved.discard(mb2)
            mv = modT[:].rearrange("p (j r) -> p j r", r=2)
            nw = pv(s_nw).rearrange("p (w k) -> p w k", w=4)
            rp = [("par", 0, INF), "modT"]
            P.op("dve", lambda e: e.tensor_tensor(out=vecs[:, 4, :], in0=mv[:, 32:48, 0], in1=nw[:, 1, :], op=ALU.mult), reads=rp, writes=[("vecs", 4)])
            P.op("dve", lambda e: e.scalar_tensor_tensor(out=vecs[:, 5, :], in0=mv[:, 64:80, 0], scalar=1.0, in1=nw[:, 2, :], op0=ALU.add, op1=ALU.mult), reads=rp, writes=[("vecs", 5)])
            P.op("dve", lambda e: e.tensor_copy(out=vecs[:, 6, :], in_=mv[:, 48:64, 0]), reads=rp, writes=[("vecs", 6)])
            P.op("dve", lambda e: e.tensor_tensor(out=vecs[:, 7, :], in0=mv[:, 80:96, 0], in1=nw[:, 3, :], op=ALU.mult), reads=rp, writes=[("vecs", 7)])
          P.barrier()
        s_h.close()

        freeb = set(range(8))

        def take(n):
            while len(freeb) < n:
                yield
            return [freeb.pop() for _ in range(n)]

        def rel(*bs):
            for b_ in bs:
                assert b_ not in freeb
                freeb.add(b_)

        def run_tasks(gens):
            active = list(gens)
            while active:
                for g in list(active):
                    try:
                        next(g)
                    except StopIteration:
                        active.remove(g)

        if "B1" in phases:
          with ExitStack() as sb1:
            lgw = sbt(sb1, "lgw", [128, 4096], BF16)
            P.dma("pool", lambda e: e.dma_start(out=lgw[:], in_=lgw_d), writes=["lgw"])
            xpad2 = [sbt(sb1, "xpad0", [128, XW])] * 2
            gyb2 = [sbt(sb1, "gyb%d" % i, [128, S], BF16) for i in range(2)]
            acc2 = [sbt(sb1, "acc0", [128, TW])] * 2; xrb2 = [sbt(sb1, "xrb0", [128, TW], BF16)] * 2
            mixs2 = [sbt(sb1, "mixs%d" % i, [128, S], BF16) for i in range(2)]
            RtA = [[sbt(sb1, "Rt%d_%d" % (d, i), [128, TW]) for d in range(2)] for i in range(2)]
            ItA = [[sbt(sb1, "It%d_%d" % (d, i), [128, TW]) for d in range(2)] for i in range(2)]
            StA = [[sbt(sb1, "St%d_%d" % (d, i), [128, TW]) for d in range(2)] for i in range(2)]
            P.op("pool", lambda e: e.memset(xpad2[0][:], 0.0), writes=["xpad0"])
            lcw = pv(s_lcw).rearrange("p (h j) -> p h j", j=4); lcb = pv(s_lcb)
            lgb = pv(s_lgb).rearrange("p (d g h) -> p d g h", d=2, g=2)
            rpar = [("par", 0, INF)]
            BLK = [(0, 256)] + [(LT + i * 512, 512) for i in range(4)]
            rg_ = lambda nm, c0, n: (nm, c0, c0 + n)

            xp = xpad2[0]; xpn = "xpad0"; acc = acc2[0]; accn = "acc0"; xrb = xrb2[0]; xrn = "xrb0"

            def lpre(h):
                hb = h % 2
                load(xp[:, 2:258], xs_scr[h][:, 0:256], reads=[("xs_scr", h)], name=(xpn, 2, 258))
                load(xp[:, 261:2309], xs_scr[h][:, 256:NALL], reads=[("xs_scr", h)], name=(xpn, 261, 2309))
                load(gyb2[hb][:], gy_scr[h], reads=[("gy_scr", h)], name="gyb%d" % hb)
                P.op("act", lambda e: e.activation(out=acc[:], in_=xp[:, 0:TW], func=AF.Identity, scale=lcw[:, h, 0:1], bias=lcb[:, h:h + 1]), reads=[xpn] + rpar, writes=[accn])
                for j in (1, 2, 3):
                    P.op("dve", lambda e: e.scalar_tensor_tensor(out=acc[:], in0=xp[:, j:j + TW], scalar=lcw[:, h, j:j + 1], in1=acc[:], op0=ALU.mult, op1=ALU.add),
                         reads=[xpn, accn] + rpar, writes=[accn])
                P.op("pool", lambda e: e.tensor_copy(out=xrb[:], in_=acc[:]), reads=[accn], writes=[xrn])

            def lst12(h):
                hb = h % 2
                Rt, It, St = RtA[hb], ItA[hb], StA[hb]
                sfx = "_%d" % hb
                for (c0, n) in BLK:
                    for d in range(2):
                        for g, dst, dn in ((0, Rt[d], "Rt%d" % d + sfx), (1, It[d], "It%d" % d + sfx)):
                            woff = ((d * 2 + g) * 8 + h) * 128
                            bb = bank()
                            P.op("pe", lambda e: e.matmul(psb[bb][:, 0:n], lhsT=lgw[:, woff:woff + 128], rhs=xrb[:, c0:c0 + n], start=True, stop=True),
                                 reads=["lgw", rg_(xrn, c0, n)], writes=[("ps", bb)])
                            P.op("act", lambda e: e.activation(out=dst[:, c0:c0 + n], in_=psb[bb][:, 0:n], func=AF.Sigmoid, bias=lgb[:, d, g, h:h + 1]),
                                 reads=[("ps", bb)] + rpar, writes=[rg_(dn, c0, n)])
                for (c0, n) in BLK:
                    for d in range(2):
                        ci = d * 8 + h
                        P.op("act", lambda e: e.activation(out=Rt[d][:, c0:c0 + n], in_=Rt[d][:, c0:c0 + n], func=AF.Exp, scale=cneg[:, ci:ci + 1]),
                             reads=[rg_("Rt%d" % d + sfx, c0, n), "cneg"], writes=[rg_("Rt%d" % d + sfx, c0, n)])
                        P.op("pool", lambda e: e.tensor_tensor(out=It[d][:, c0:c0 + n], in0=It[d][:, c0:c0 + n], in1=acc[:, c0:c0 + n], op=ALU.mult),
                             reads=[rg_("It%d" % d + sfx, c0, n), rg_(accn, c0, n)], writes=[rg_("It%d" % d + sfx, c0, n)])
                        P.op("pool", lambda e: e.tensor_tensor(out=St[d][:, c0:c0 + n], in0=Rt[d][:, c0:c0 + n], in1=Rt[d][:, c0:c0 + n], op=ALU.mult),
                             reads=[rg_("Rt%d" % d + sfx, c0, n)], writes=[rg_("St%d" % d + sfx, c0, n)])

            def lst3(h):
                hb = h % 2
                Rt, It, St = RtA[hb], ItA[hb], StA[hb]
                Hx = St
                sfx = "_%d" % hb
                for d in range(2):
                    for (c0, n) in BLK:
                        P.op("act", lambda e: e.activation(out=St[d][:, c0:c0 + n], in_=St[d][:, c0:c0 + n], func=AF.Sqrt, scale=-1.0, bias=1.0),
                             reads=[rg_("St%d" % d + sfx, c0, n)], writes=[rg_("St%d" % d + sfx, c0, n)])
                for d in range(2):
                    order = BLK if d == 0 else [BLK[0]] + BLK[:0:-1]
                    hx = Hx[d]; hxn = "St%d" % d + sfx
                    prev = None
                    for (c0, n) in order:
                        P.op("dve", lambda e: e.tensor_tensor(out=St[d][:, c0:c0 + n], in0=St[d][:, c0:c0 + n], in1=It[d][:, c0:c0 + n], op=ALU.mult),
                             reads=[rg_("St%d" % d + sfx, c0, n), rg_("It%d" % d + sfx, c0, n)], writes=[rg_("St%d" % d + sfx, c0, n)])
                        if prev is None:
                            init = 0.0; rd = []
                        else:
                            pc0, pn = prev
                            init = hx[:, pc0 + pn - 1:pc0 + pn] if d == 0 else hx[:, pc0:pc0 + 1]
                            rd = [rg_(hxn, pc0, pn)]
                        a_v = Rt[d][:, c0:c0 + n]; u_v = St[d][:, c0:c0 + n]; o_v = hx[:, c0:c0 + n]
                        if d == 1:
                            a_v, u_v, o_v = a_v[:, ::-1], u_v[:, ::-1], o_v[:, ::-1]
                        P.op("dve", lambda e: e.tensor_tensor_scan(out=o_v, data0=a_v, data1=u_v, initial=init, op0=ALU.mult, op1=ALU.add),
                             reads=[rg_("Rt%d" % d + sfx, c0, n), rg_("St%d" % d + sfx, c0, n)] + rd, writes=[rg_(hxn, c0, n)])
                        prev = (c0, n)
                ms = mixs2[hb]; msn = "mixs%d" % hb
                P.op("dve", lambda e: e.tensor_tensor(out=Hx[0][:, LT:TW], in0=Hx[0][:, LT:TW], in1=Hx[1][:, LT:TW], op=ALU.add), reads=["St0" + sfx, "St1" + sfx], writes=["St0" + sfx])
                P.op("pool", lambda e: e.tensor_tensor(out=ms[:].rearrange("p (r c) -> p r c", c=64), in0=gyb2[hb][:].rearrange("p (r c) -> p r c", c=64),
                                                       in1=Hx[0][:, LT:TW].rearrange("p (c r) -> p r c", r=32), op=ALU.mult), reads=["gyb%d" % hb, "St0" + sfx], writes=[msn])
                load(mix_scr[h], ms[:], reads=[msn], name=("mix_scr", h))

            lpre(0)
            for h in range(8):
                lst12(h)
                if h + 1 < 8:
                    lpre(h + 1)
                lst3(h)
            assert len(freeb) == 8
          P.barrier()


        if "B2" in phases:
          with ExitStack() as sb2:
            NSETS = 3
            graw = sbt(sb2, "graw0", [128, XW]); grn = "graw0"
            cacc = sbt(sb2, "cacc", [128, TW]); sqb = sbt(sb2, "sqb", [128, TW], BF16)
            rng_ = [sbt(sb2, "rng%d" % i, [128, 512]) for i in range(2)]
            qn2 = [sbt(sb2, "qn%d" % i, [128, TW], F32R) for i in range(2)]; kn2 = [sbt(sb2, "kn%d" % i, [128, TW], F32R) for i in range(2)]
            vf = cacc
            zs2 = [sbt(sb2, "zs%d" % i, [128, S], BF16) for i in range(2)]
            Ktok2 = [sbt(sb2, "Ktok%d" % i, [128, NCH, 128]) for i in range(2)]; Vtok2 = [sbt(sb2, "Vtok%d" % i, [128, NCH, 128]) for i in range(2)]
            oacc = sbt(sb2, "oacc", [128, 16, 128])
            mixg = sbt(sb2, "mixg0", [128, S], BF16); mgn = "mixg0"
            Sst = [sbt(sb2, "Sst%d" % d, [128, 128], F32R) for d in range(2)]
            G4 = [128, 4, 128]
            SETS = []
            for si in range(NSETS):
                t = {}
                for nm in ("Gs", "DT", "Erow", "QdT", "QKT", "Kd", "Nm", "NmT"):
                    t[nm] = sbt(sb2, "%s_%d" % (nm, si), G4, F32)
                    t[nm + "_n"] = "%s_%d" % (nm, si)
                for nm, al in (("Wm", "Gs"), ("Qa", "DT"), ("Qb", "Erow")):
                    t[nm], t[nm + "_n"] = t[al], t[al + "_n"]
                t["tmpWT"], t["tmpWT_n"] = t["Nm"], t["Nm_n"]
                t["tmpZ"], t["tmpZ_n"] = t["Qa"], t["Qa_n"]
                SETS.append(t)
            Rp = [sbt(sb2, "Rp%d" % i, [128, 128], F32R) for i in range(2)]; Vn = [sbt(sb2, "Vn%d" % i, [128, 128], F32R) for i in range(2)]
            ot = [sbt(sb2, "ot%d" % i, [128, 128]) for i in range(2)]; junkg = [sbt(sb2, "junkg%d" % i, [128, 128]) for i in range(2)]
            stg = [sbt(sb2, "stg%d" % i, [128, 4]) for i in range(2)]
            pcs = [sbt(sb2, "pcs%d" % i, [128, 16, 128], BF16) for i in range(2)]
            P.op("pool", lambda e: e.memset(graw[:], 0.0), writes=[grn])
            gcw = pv(s_gcw).rearrange("p (t h j) -> p t h j", t=3, j=4)
            f32v = lambda ap: ap.bitcast(F32)
            bcast4 = lambda m: m[:].unsqueeze(1).to_broadcast([128, 4, 128])
            fl = lambda t: t[:].rearrange("p u c -> p (u c)")
            fwd_order = list(range(NCH)); bwd_order = [1, 0] + list(range(17, 1, -1))
            rpar = [("par", 0, INF)]
            w1v_ = w1_d.rearrange("(k p) (o c) -> o p k c", p=128, c=128)
            w2v_ = w2_d.rearrange("(k p) (o c) -> o p k c", p=128, c=128)
            pjobs = [(w1v_[o], w1b[o], ("w1b", o)) for o in range(64)]
            pjobs += [(w2v_[o][:, kh * 16:(kh + 1) * 16, :], w2b[o][:, kh * 16:(kh + 1) * 16, :], ("w2b", o * 4 + kh)) for o in range(16) for kh in range(4)]
            pji = [0]

            def precast_some(n):
                for _ in range(n):
                    if pji[0] >= len(pjobs):
                        return
                    src, dst, rn = pjobs[pji[0]]
                    i = pji[0] % 2
                    pji[0] += 1
                    P.dma("pool", lambda e: e.dma_start(out=pcs[i][:], in_=src), writes=["pcs%d" % i])
                    P.dma("sp", lambda e: e.dma_start(out=dst, in_=pcs[i][:]), reads=["pcs%d" % i], writes=[rn])

            import os as _os
            r_ = lambda ap: ap.bitcast(F32R)
            NH = int(_os.environ.get("K_NGDN", 8))
            NIT = NCH // 2

            def mm4(b, lhs, ln, rhs, rn):
                for u in range(4):
                    P.op("pe", lambda e: e.matmul(psb[b][:, u * 128:(u + 1) * 128], lhsT=r_(lhs[:, u, :]), rhs=r_(rhs[:, u, :]), start=True, stop=True, skip_group_check=True),
                         reads=[ln, rn], writes=[("ps", b)], inc=(u == 3))

            def transpose4(b, src, sname):
                for u in range(4):
                    P.op("pe", lambda e: e.transpose(psb[b][:, u * 128:(u + 1) * 128], src[:, u, :], ident[:]),
                         reads=[sname, "ident"], writes=[("ps", b)], inc=(u == 3))

            def headpre(h):
                hb = h % 2
                qn, kn, Ktok, Vtok, zs = qn2[hb], kn2[hb], Ktok2[hb], Vtok2[hb], zs2[hb]
                qnn, knn, Ktn, Vtn, zsn = "qn%d" % hb, "kn%d" % hb, "Ktok%d" % hb, "Vtok%d" % hb, "zs%d" % hb
                load(zs[:], zs_scr[h], reads=[("zs_scr", h)], name=zsn)
                for t3 in range(3):
                    load(graw[:, 2:258], p_scr[h, t3][:, 0:256], reads=[("p_scr", h * 3 + t3)], name=(grn, 2, 258))
                    load(graw[:, 261:2309], p_scr[h, t3][:, 256:NALL], reads=[("p_scr", h * 3 + t3)], name=(grn, 261, 2309))
                    P.op("act", lambda e: e.activation(out=cacc[:], in_=graw[:, 0:TW], func=AF.Copy, scale=gcw[:, t3, h, 0:1]), reads=[grn] + rpar, writes=["cacc"])
                    yield
                    for j in (1, 2, 3):
                        P.op("dve", lambda e: e.scalar_tensor_tensor(out=cacc[:], in0=graw[:, j:j + TW], scalar=gcw[:, t3, h, j:j + 1], in1=cacc[:], op0=ALU.mult, op1=ALU.add),
                             reads=[grn, "cacc"] + rpar, writes=["cacc"])
                        yield
                    P.op("act", lambda e: e.activation(out=cacc[:], in_=cacc[:], func=AF.Silu), reads=["cacc"], writes=["cacc"])
                    yield
                    if t3 < 2:
                        P.op("pool", lambda e: e.tensor_tensor(out=sqb[:], in0=cacc[:], in1=cacc[:], op=ALU.mult), reads=["cacc"], writes=["sqb"])
                        yield
                        sc_ = 128.0 if t3 == 0 else 1.0
                        dq = qn if t3 == 0 else kn; dqn = qnn if t3 == 0 else knn
                        for gi, (c0, n) in enumerate([(g * 512, 512) for g in range(4)] + [(2048, TW - 2048)]):
                            b, = yield from take(1)
                            rg = rng_[gi % 2]; rgn = "rng%d" % (gi % 2)
                            P.op("pe", lambda e: e.matmul(psb[b][:, 0:n], lhsT=onesb[:], rhs=sqb[:, c0:c0 + n], start=True, stop=True), reads=["onesb", "sqb"], writes=[("ps", b)])
                            yield
                            P.op("dve", lambda e: e.tensor_copy(out=rg[:, 0:n], in_=psb[b][:, 0:n]), reads=[("ps", b)], writes=[rgn])
                            rel(b)
                            yield
                            P.op("act", lambda e: e.activation(out=rg[:, 0:n], in_=rg[:, 0:n], func=AF.Ln, scale=sc_, bias=sc_ * EPS), reads=[rgn], writes=[rgn])
                            P.op("act", lambda e: e.activation(out=rg[:, 0:n], in_=rg[:, 0:n], func=AF.Exp, scale=-0.5), reads=[rgn], writes=[rgn])
                            yield
                            P.op("pool", lambda e: e.tensor_tensor(out=dq[:, c0:c0 + n], in0=cacc[:, c0:c0 + n], in1=rg[:, 0:n], op=ALU.mult), reads=["cacc", rgn], writes=[(dqn, c0, c0 + n)])
                            yield
                    if t3 >= 1:
                        srcT, sname, dstK, dname = (kn, knn, Ktok, Ktn) if t3 == 1 else (vf, "cacc", Vtok, Vtn)
                        for c4 in range(0, NCH, 4):
                            n4 = min(4, NCH - c4)
                            b, = yield from take(1)
                            for q in range(n4):
                                co = ch_off(c4 + q)
                                P.op("pe", lambda e: e.transpose(psb[b][:, q * 128:(q + 1) * 128], f32v(srcT[:, co:co + 128]), ident[:]),
                                     reads=[sname, "ident"], writes=[("ps", b)], inc=(q == n4 - 1))
                            yield
                            P.op("dve", lambda e: e.tensor_copy(out=dstK[:, c4:c4 + n4, :], in_=psb[b][:, 0:n4 * 128].rearrange("p (q c) -> p q c", c=128)),
                                 reads=[("ps", b)], writes=[dname])
                            rel(b)
                            yield

            def head_tasks(h):
                hb = h % 2
                qn, kn, Ktok, Vtok, zs = qn2[hb], kn2[hb], Ktok2[hb], Vtok2[hb], zs2[hb]
                qnn, knn, Ktn, Vtn, zsn = "qn%d" % hb, "kn%d" % hb, "Ktok%d" % hb, "Vtok%d" % hb, "zs%d" % hb
                hd = lambda d: d * 8 + h
                for d in range(2):
                    P.op("pool", lambda e: e.tensor_scalar(out=Sst[d][:], in0=ident[:], scalar1=0.0, scalar2=None, op0=ALU.mult), reads=["ident"], writes=["Sst%d" % d])
                odone = [False] * 16
                prep_done = [False] * NIT
                rec_done = [[False] * NIT for _ in range(2)]

                def units_of(it):
                    return [(0, fwd_order[2 * it]), (0, fwd_order[2 * it + 1]), (1, bwd_order[2 * it]), (1, bwd_order[2 * it + 1])]

                def prep(it):
                    T_ = SETS[it % NSETS]
                    while it >= NSETS and not (rec_done[0][it - NSETS] and rec_done[1][it - NSETS]):
                        yield
                    units = units_of(it)
                    Gs, DT, Erow, QdT, QKT, Kd, Nm, NmT, Qa, Qb, Wm = [T_[k] for k in ("Gs", "DT", "Erow", "QdT", "QKT", "Kd", "Nm", "NmT", "Qa", "Qb", "Wm")]
                    n_ = lambda k: T_[k + "_n"]
                    precast_some(2)
                    for u, (d, ch) in enumerate(units):
                        P.op("act", lambda e: e.activation(out=NmT[:, u, :], in_=ident[:], func=AF.Copy, scale=gam[:, ch, hd(d):hd(d) + 1]),
                             reads=["ident", "gam"], writes=[(n_("NmT"), u)])
                        P.op("act", lambda e: e.activation(out=r_(Kd[:, u, :]), in_=Ktok[:, ch, :], func=AF.Copy, scale=kdec[:, ch, hd(d):hd(d) + 1]),
                             reads=[Ktn, "kdec"], writes=[(n_("Kd"), u)])
                    yield
                    bG, bK, bQ = yield from take(3)
                    P.op("pe", lambda e: e.matmul(psb[bG][:], lhsT=ones[:], rhs=fl(NmT), start=True, stop=True), reads=["ones", n_("NmT")], writes=[("ps", bG)])
                    for u, (d, ch) in enumerate(units):
                        co = ch_off(ch)
                        P.op("pe", lambda e: e.matmul(psb[bK][:, u * 128:(u + 1) * 128], lhsT=kn[:, co:co + 128], rhs=kn[:, co:co + 128], start=True, stop=True, skip_group_check=True),
                             reads=[knn], writes=[("ps", bK)], inc=False)
                        P.op("pe", lambda e: e.matmul(psb[bQ][:, u * 128:(u + 1) * 128], lhsT=kn[:, co:co + 128], rhs=qn[:, co:co + 128], start=True, stop=True, skip_group_check=True),
                             reads=[knn, qnn], writes=[("ps", bQ)], inc=(u == 3))
                    yield
                    P.op("dve", lambda e: e.tensor_copy(out=r_(fl(Gs)), in_=psb[bG][:]), reads=[("ps", bG)], writes=[n_("Gs")])
                    for u, (d, ch) in enumerate(units):
                        P.op("dve", lambda e: e.scalar_tensor_tensor(out=r_(DT[:, u, :]), in0=psb[bG][:, u * 128:(u + 1) * 128], scalar=gam[:, ch, hd(d):hd(d) + 1], in1=negm[d][:], op0=ALU.subtract, op1=ALU.add),
                             reads=[("ps", bG), "gam", "negm%d" % d], writes=[(n_("DT"), u)])
                    rel(bG)
                    yield
                    P.op("act", lambda e: e.activation(out=r_(fl(Erow)), in_=fl(Gs), func=AF.Exp), reads=[n_("Gs")], writes=[n_("Erow")])
                    P.op("act", lambda e: e.activation(out=r_(fl(DT)), in_=fl(DT), func=AF.Exp), reads=[n_("DT")], writes=[n_("DT")])
                    for u, (d, ch) in enumerate(units):
                        co = ch_off(ch)
                        if ch >= 2:
                            P.op("pool", lambda e: e.tensor_tensor(out=r_(QdT[:, u, :]), in0=f32v(qn[:, co:co + 128]), in1=Erow[:, u, :], op=ALU.mult),
                                 reads=[qnn, n_("Erow")], writes=[(n_("QdT"), u)])
                    yield
                    P.op("dve", lambda e: e.tensor_tensor(out=r_(fl(QKT)), in0=psb[bQ][:], in1=fl(DT), op=ALU.mult), reads=[("ps", bQ), n_("DT")], writes=[n_("QKT")])
                    for u, (d, ch) in enumerate(units):
                        P.op("dve", lambda e: e.scalar_tensor_tensor(out=r_(Nm[:, u, :]), in0=psb[bK][:, u * 128:(u + 1) * 128], scalar=nbeta[:, ch, hd(d):hd(d) + 1], in1=DT[:, u, :], op0=ALU.mult, op1=ALU.mult),
                             reads=[("ps", bK), "nbeta", n_("DT")], writes=[(n_("Nm"), u)])
                    rel(bK, bQ)
                    yield
                    P.op("pool", lambda e: e.tensor_tensor(out=r_(Nm[:]), in0=Nm[:], in1=bcast4(offd), op=ALU.mult), reads=[n_("Nm"), "offd"], writes=[n_("Nm")])
                    yield
                    bT, = yield from take(1)
                    transpose4(bT, Nm, n_("Nm"))
                    P.op("pool", lambda e: e.tensor_tensor(out=r_(Qa[:]), in0=Nm[:], in1=bcast4(bd32), op=ALU.mult), reads=[n_("Nm"), "bd32"], writes=[n_("Qa")])
                    yield
                    P.op("dve", lambda e: e.tensor_copy(out=fl(NmT), in_=psb[bT][:]), reads=[("ps", bT)], writes=[n_("NmT")])
                    rel(bT)
                    P.op("pool", lambda e: e.tensor_tensor(out=r_(Wm[:]), in0=Qa[:], in1=bcast4(ident), op=ALU.add), reads=[n_("Qa"), "ident"], writes=[n_("Wm")])
                    yield
                    P.op("pool", lambda e: e.tensor_tensor(out=r_(Qb[:]), in0=NmT[:], in1=bcast4(bd32), op=ALU.mult), reads=[n_("NmT"), "bd32"], writes=[n_("Qb")])
                    yield
                    for lvl in range(4):
                        last = (lvl == 3)
                        bTq, bNq = yield from take(2)
                        mm4(bTq, Qa, n_("Qa"), Qb, n_("Qb"))
                        if not last:
                            mm4(bNq, Qb, n_("Qb"), Qa, n_("Qa"))
                        yield
                        P.op("dve", lambda e: e.tensor_copy(out=r_(fl(Qb)), in_=psb[bTq][:]), reads=[("ps", bTq)], writes=[n_("Qb")])
                        if not last:
                            P.op("dve", lambda e: e.tensor_copy(out=r_(fl(Qa)), in_=psb[bNq][:]), reads=[("ps", bNq)], writes=[n_("Qa")])
                        rel(bTq, bNq)
                        yield
                        bW, = yield from take(1)
                        mm4(bW, Qb, n_("Qb"), Wm, n_("Wm"))
                        yield
                        P.op("dve", lambda e: e.tensor_tensor(out=r_(fl(Wm)), in0=psb[bW][:], in1=fl(Wm), op=ALU.add), reads=[("ps", bW), n_("Wm")], writes=[n_("Wm")])
                        rel(bW)
                        yield
                    for om, omn in ((od64, "od64"), (od128, "od128")):
                        bt, bZ = yield from take(2)
                        transpose4(bt, Wm, n_("Wm"))
                        P.op("pool", lambda e: e.tensor_tensor(out=r_(Qb[:]), in0=NmT[:], in1=bcast4(om), op=ALU.mult), reads=[n_("NmT"), omn], writes=[n_("Qb")])
                        yield
                        mm4(bZ, Qb, n_("Qb"), Wm, n_("Wm"))
                        P.op("dve", lambda e: e.tensor_copy(out=r_(fl(T_["tmpWT"])), in_=psb[bt][:]), reads=[("ps", bt)], writes=[T_["tmpWT_n"]])
                        yield
                        P.op("dve", lambda e: e.tensor_copy(out=r_(fl(T_["tmpZ"])), in_=psb[bZ][:]), reads=[("ps", bZ)], writes=[T_["tmpZ_n"]])
                        rel(bt, bZ)
                        yield
                        bW, = yield from take(1)
                        mm4(bW, T_["tmpWT"], T_["tmpWT_n"], T_["tmpZ"], T_["tmpZ_n"])
                        yield
                        P.op("dve", lambda e: e.tensor_tensor(out=r_(fl(Wm)), in0=psb[bW][:], in1=fl(Wm), op=ALU.add), reads=[("ps", bW), n_("Wm")], writes=[n_("Wm")])
                        rel(bW)
                        yield
                    prep_done[it] = True

                def recur(d, it):
                    T_ = SETS[it % NSETS]
                    while not prep_done[it]:
                        yield
                    QdT, QKT, Kd, Wf = T_["QdT"], T_["QKT"], T_["Kd"], T_["Wm"]
                    n_ = lambda k: T_[k + "_n"]
                    units = units_of(it)
                    for u in (2 * d, 2 * d + 1):
                        ch = units[u][1]
                        co = ch_off(ch); Sd = Sst[d]; Sn = "Sst%d" % d; hdd = hd(d)
                        ri = d
                        b1, = yield from take(1)
                        P.op("pe", lambda e: e.matmul(psb[b1][:, 0:128], lhsT=kn[:, co:co + 128], rhs=Sd[:], start=True, stop=True), reads=[knn, Sn], writes=[("ps", b1)])
                        yield
                        P.op("dve", lambda e: e.scalar_tensor_tensor(out=Rp[ri][:], in0=psb[b1][:, 0:128], scalar=neg_eg[:, ch, hdd:hdd + 1], in1=Vtok[:, ch, :], op0=ALU.mult, op1=ALU.add),
                             reads=[("ps", b1), "neg_eg", Vtn], writes=["Rp%d" % ri])
                        rel(b1)
                        yield
                        b2, = yield from take(1)
                        P.op("pe", lambda e: e.matmul(psb[b2][:, 0:128], lhsT=r_(Wf[:, u, :]), rhs=Rp[ri][:], start=True, stop=True), reads=[n_("Wm"), "Rp%d" % ri], writes=[("ps", b2)])
                        yield
                        P.op("dve", lambda e: e.tensor_scalar(out=Vn[ri][:], in0=psb[b2][:, 0:128], scalar1=beta[:, ch, hdd:hdd + 1], scalar2=None, op0=ALU.mult),
                             reads=[("ps", b2), "beta"], writes=["Vn%d" % ri])
                        rel(b2)
                        yield
                        b5, b3 = yield from take(2)
                        P.op("pe", lambda e: e.matmul(psb[b5][:, 0:128], lhsT=r_(Kd[:, u, :]), rhs=Vn[ri][:], start=True, stop=True), reads=[(n_("Kd"), u), "Vn%d" % ri], writes=[("ps", b5)])
                        if ch >= 2:
                            lc = ch - 2
                            P.op("pe", lambda e: e.matmul(psb[b3][:, 0:128], lhsT=r_(QdT[:, u, :]), rhs=Sd[:], start=True, stop=False), reads=[(n_("QdT"), u), Sn], writes=[("ps", b3)], inc=False)
                            P.op("pe", lambda e: e.matmul(psb[b3][:, 0:128], lhsT=r_(QKT[:, u, :]), rhs=Vn[ri][:], start=False, stop=True), reads=[n_("QKT"), "Vn%d" % ri], writes=[("ps", b3)])
                        yield
                        P.op("dve", lambda e: e.scalar_tensor_tensor(out=Sd[:], in0=f32v(Sd[:]), scalar=cdl[:, ch, hdd:hdd + 1], in1=psb[b5][:, 0:128], op0=ALU.mult, op1=ALU.add),
                             reads=[("ps", b5), "cdl", Sn], writes=[Sn])
                        rel(b5)
                        if ch < 2:
                            rel(b3)
                        if ch >= 2:
                            if not odone[lc]:
                                odone[lc] = True
                                P.op("dve", lambda e: e.tensor_copy(out=oacc[:, lc, :], in_=psb[b3][:, 0:128]), reads=[("ps", b3)], writes=[("oacc", lc)])
                                rel(b3)
                            else:
                                o_ = ot[d]; on_ = "ot%d" % d; sg = stg[d]; sgn = "stg%d" % d
                                P.op("dve", lambda e: e.tensor_tensor(out=o_[:], in0=psb[b3][:, 0:128], in1=oacc[:, lc, :], op=ALU.add), reads=[("ps", b3), ("oacc", lc)], writes=[on_])
                                rel(b3)
                                yield
                                P.op("act", lambda e: e.activation(out=junkg[d][:], in_=o_[:], func=AF.Square, accum_out=sg[:, 0:1]), reads=[on_], writes=["junkg%d" % d, sgn])
                                P.op("act", lambda e: e.activation(out=sg[:, 1:2], in_=sg[:, 0:1], func=AF.Sqrt, scale=1.0 / 128, bias=EPS), reads=[sgn], writes=[sgn])
                                yield
                                P.op("dve", lambda e: e.reciprocal(out=sg[:, 2:3], in_=sg[:, 1:2]), reads=[sgn], writes=[sgn])
                                yield
                                P.op("act", lambda e: e.activation(out=o_[:], in_=o_[:], func=AF.Copy, scale=sg[:, 2:3]), reads=[on_, sgn], writes=[on_])
                                yield
                                b4, = yield from take(1)
                                P.op("pe", lambda e: e.transpose(psb[b4][:, 0:128], o_[:], ident[:]), reads=[on_, "ident"], writes=[("ps", b4)])
                                yield
                                P.op("dve", lambda e: e.scalar_tensor_tensor(out=mixg[:, lc * 128:(lc + 1) * 128], in0=psb[b4][:, 0:128], scalar=pv(s_gnw), in1=zs[:, lc * 128:(lc + 1) * 128], op0=ALU.mult, op1=ALU.mult),
                                     reads=[("ps", b4), zsn] + rpar, writes=[(mgn, lc)])
                                rel(b4)
                        yield
                    rec_done[d][it] = True

                def chain(fn, *a):
                    for it in range(NIT):
                        yield from fn(*a, it)

                def prep_lane(l):
                    for it in range(l, NIT, NSETS):
                        yield from prep(it)

                return [prep_lane(l) for l in range(NSETS)] + [chain(recur, 0), chain(recur, 1)]

            run_tasks([headpre(0)])
            for h in range(NH):
                tasks = head_tasks(h)
                if h + 1 < NH:
                    tasks.append(headpre(h + 1))
                run_tasks(tasks)
                assert len(freeb) == 8
                load(mix_scr[8 + h], mixg[:], reads=[mgn], name=("mix_scr", 8 + h))
            precast_some(1000)
          P.barrier()

        if "C" in phases:
          with ExitStack() as sc:
            wo = sbt(sc, "wo", [128, 16, D], BF16)
            GM_row = make_row(sc, 4, "GM_row")
            wov = wout_d.rearrange("(k p) c -> p k c", p=128)
            for k4 in range(0, 16, 4):
                P.dma("pool", (lambda k4: lambda e: e.dma_start(out=wo[:, k4:k4 + 4, :], in_=wov[:, k4:k4 + 4, :]))(k4), writes=[("wo", k4, k4 + 4)])
            mt = [sbt(sc, "mt%d" % i, [128, 16, 512], BF16) for i in range(2)]
            xc = [sbt(sc, "xc%d" % i, [128, D]) for i in range(2)]
            x1t = [sbt(sc, "x1t%d" % i, [128, D]) for i in range(2)]
            h2t = [sbt(sc, "h2t%d" % i, [128, 16, 128], BF16) for i in range(2)]
            junkc = sbt(sc, "junkc", [128, D]); stc2 = [sbt(sc, "stc%d" % i, [128, 16]) for i in range(2)]
            mixv = mix_scr.rearrange("k p t -> p k t")
            h2v = h2_scr.rearrange("k p t -> p k t")
            def stageA(tt):
                g = tt // 4
                m = mt[g % 2]; mn = "mt%d" % (g % 2)
                if tt % 4 == 0:
                    load(m[:], mixv[:, :, g * 512:(g + 1) * 512], reads=["mix_scr"], name=mn)
                i = tt % 2
                stc = stc2[i]; stn = "stc%d" % i
                load(xc[i][:], x_d[tt * 128:(tt + 1) * 128, :], name="xc%d" % i)
                bs = []
                for cg in range(4):
                    b = bank(); bs.append(b)
                    for k in range(16):
                        P.op("pe", lambda e: e.matmul(psb[b][:], lhsT=m[:, k, (tt % 4) * 128:(tt % 4 + 1) * 128], rhs=wo[:, k, cg * 512:(cg + 1) * 512], start=(k == 0), stop=(k == 15)),
                             reads=[mn, ("wo", k)], writes=[("ps", b)], inc=(k == 15))
                for cg in range(4):
                    b = bs[cg]
                    P.op("act", lambda e: e.activation(out=junkc[:, cg * 512:(cg + 1) * 512], in_=psb[b][:], func=AF.Square, accum_out=stc[:, cg:cg + 1]),
                         reads=[("ps", b)], writes=[("junkc", cg), (stn, cg)])
                P.op("dve", lambda e: e.tensor_reduce(out=stc[:, 4:5], in_=stc[:, 0:4], axis=mybir.AxisListType.X, op=ALU.add), reads=[stn], writes=[stn])
                P.op("act", lambda e: e.activation(out=stc[:, 5:6], in_=stc[:, 4:5], func=AF.Sqrt, scale=1.0 / D, bias=EPS), reads=[stn], writes=[stn])
                P.op("dve", lambda e: e.reciprocal(out=stc[:, 6:7], in_=stc[:, 5:6]), reads=[stn], writes=[stn])
                x1 = x1t[i]; x1n = "x1t%d" % i
                for cg in range(4):
                    b = bs[cg]
                    cs = slice(cg * 512, (cg + 1) * 512)
                    P.op("dve", lambda e: e.scalar_tensor_tensor(out=x1[:, cs], in0=psb[b][:], scalar=stc[:, 6:7], in1=GM_row[:, cs], op0=ALU.mult, op1=ALU.mult),
                         reads=[("ps", b), stn, "GM_row"], writes=[(x1n, cg)])
                    P.op("pool", lambda e: e.tensor_tensor(out=x1[:, cs], in0=x1[:, cs], in1=xc[i][:, cs], op=ALU.add), reads=[(x1n, cg), "xc%d" % i], writes=[(x1n, cg)])
                load(x1_scr[tt * 128:(tt + 1) * 128, :], x1[:], reads=[x1n], name=("x1_scr", tt))
                P.op("act", lambda e: e.activation(out=junkc[:], in_=x1[:], func=AF.Square, accum_out=stc[:, 8:9]), reads=[x1n], writes=["junkc", stn])
                P.op("act", lambda e: e.activation(out=stc[:, 9:10], in_=stc[:, 8:9], func=AF.Sqrt, scale=1.0 / D, bias=EPS), reads=[stn], writes=[stn])
                P.op("dve", lambda e: e.reciprocal(out=stc[:, 10:11], in_=stc[:, 9:10]), reads=[stn], writes=[stn])
                P.op("act", lambda e: e.activation(out=xc[i][:], in_=x1[:], func=AF.Copy, scale=stc[:, 10:11]), reads=[x1n, stn], writes=["xc%d" % i])

            def stageB(tt):
                i = tt % 2
                xn = xc[i]; xnn = "xc%d" % i
                ht = h2t[i]; htn = "h2t%d" % i
                for g4 in range(4):
                    b = bank()
                    for q in range(4):
                        k = g4 * 4 + q
                        P.op("pe", lambda e: e.transpose(psb[b][:, q * 128:(q + 1) * 128], xn[:, k * 128:(k + 1) * 128], ident[:]), reads=[xnn, "ident"], writes=[("ps", b)], inc=(q == 3))
                    for q in range(4):
                        k = g4 * 4 + q
                        P.op("dve", lambda e: e.tensor_scalar(out=ht[:, k, :], in0=psb[b][:, q * 128:(q + 1) * 128], scalar1=vecs[:, 5, k:k + 1], scalar2=vecs[:, 6, k:k + 1], op0=ALU.mult, op1=ALU.add),
                             reads=[("ps", b), "vecs"], writes=[(htn, k)])
                load(h2v[:, :, tt * 128:(tt + 1) * 128], ht[:], reads=[htn], name=("h2_scr", tt))

            stageA(0)
            for tt in range(1, 16):
                stageA(tt)
                stageB(tt - 1)
            stageB(15)
          P.barrier()

        out_toks = []
        if "D" in phases:
          with ExitStack() as sd:
            h2 = sbt(sd, "h2", [128, 16, 512], BF16)
            GF_row = make_row(sd, 7, "GF_row")
            f1 = sbt(sd, "f1", [128, 64, 512], BF16)
            w1t = [sbt(sd, "w1t%d" % i, [128, 16, 128], BF16) for i in range(3)]
            w2t = [sbt(sd, "w2t%d" % i, [128, 64, 128], BF16) for i in range(2)]
            rl = [sbt(sd, "rl%d" % i, [128, 512]) for i in range(2)]
            y2b = [sbt(sd, "y2b%d" % i, [128, 512]) for i in range(2)]
            y2 = sbt(sd, "y2", [128, 4, D])
            x1d = sbt(sd, "x1d", [128, D]); junkd = sbt(sd, "junkd", [128, D], BF16); std = sbt(sd, "std", [128, 8])
            h2v = h2_scr.rearrange("k p t -> p k t")
            w1v = w1_d.rearrange("(k p) (o c) -> o p k c", p=128, c=128)
            w2v = w2_d.rearrange("(k p) (o c) -> o p k c", p=128, c=128)
            def epi(T, q):
                tt = T * 4 + q
                load(x1d[:], x1_scr[tt * 128:(tt + 1) * 128, :], q="pool", reads=[("x1_scr", tt)], name="x1d")
                P.op("act", lambda e: e.activation(out=junkd[:], in_=y2[:, q, :], func=AF.Square, accum_out=std[:, 0:1]), reads=["y2"], writes=["junkd", "std"])
                P.op("act", lambda e: e.activation(out=std[:, 1:2], in_=std[:, 0:1], func=AF.Sqrt, scale=1.0 / D, bias=EPS), reads=["std"], writes=["std"])
                P.op("dve", lambda e: e.reciprocal(out=std[:, 2:3], in_=std[:, 1:2]), reads=["std"], writes=["std"])
                P.op("dve", lambda e: e.scalar_tensor_tensor(out=y2[:, q, :], in0=y2[:, q, :], scalar=std[:, 2:3], in1=GF_row[:], op0=ALU.mult, op1=ALU.mult),
                     reads=["y2", "std", "GF_row"], writes=["y2"])
                P.op("dve", lambda e: e.tensor_tensor(out=x1d[:], in0=x1d[:], in1=y2[:, q, :], op=ALU.add), reads=["x1d", "y2"], writes=["x1d"])
                out_toks.append(load(out_d[tt * 128:(tt + 1) * 128, :], x1d[:], q="pool", reads=["x1d"], name=("out", tt)))

            def ff1(T):
                for o in range(64):
                    if T > 0 and o in (6, 12, 18, 24):
                        epi(T - 1, (o // 6) - 1)
                    wt = w1t[o % 3]; wtn = "w1t%d" % (o % 3)
                    load(wt[:], w1b[o], reads=[("w1b", o)], name=wtn)
                    b = bank()
                    for k in range(16):
                        P.op("pe", lambda e: e.matmul(psb[b][:], lhsT=wt[:, k, :], rhs=h2[:, k, :], start=(k == 0), stop=(k == 15)), reads=[wtn, "h2"], writes=[("ps", b)], inc=(k == 15))
                    r = rl[o % 2]; rn = "rl%d" % (o % 2)
                    P.op("act", lambda e: e.activation(out=r[:], in_=psb[b][:], func=AF.Relu), reads=[("ps", b)], writes=[rn])
                    P.op("dve", lambda e: e.tensor_tensor(out=f1[:, o, :], in0=r[:], in1=r[:], op=ALU.mult), reads=[rn], writes=[("f1", o)])

            def ff2(T):
                for o in range(16):
                    wt = w2t[o % 2]; wtn = "w2t%d" % (o % 2)
                    for kh in range(4):
                        load(wt[:, kh * 16:(kh + 1) * 16, :], w2b[o][:, kh * 16:(kh + 1) * 16, :], reads=[("w2b", o * 4 + kh)], name=(wtn, kh))
                    b = bank()
                    for k in range(64):
                        P.op("pe", lambda e: e.matmul(psb[b][:], lhsT=wt[:, k, :], rhs=f1[:, k, :], start=(k == 0), stop=(k == 63)), reads=[(wtn, k // 16), ("f1", k)], writes=[("ps", b)], inc=(k == 63))
                    yb = y2b[o % 2]; ybn = "y2b%d" % (o % 2)
                    P.op("act", lambda e: e.copy(out=yb[:], in_=psb[b][:]), reads=[("ps", b)], writes=[ybn])
                    b2 = bank()
                    for q in range(4):
                        P.op("pe", lambda e: e.transpose(psb[b2][:, q * 128:(q + 1) * 128], yb[:, q * 128:(q + 1) * 128], ident[:]), reads=[ybn, "ident"], writes=[("ps", b2)], inc=(q == 3))
                    P.op("dve", lambda e: e.tensor_copy(out=y2[:, :, o * 128:(o + 1) * 128], in_=psb[b2][:].rearrange("p (q c) -> p q c", c=128)), reads=[("ps", b2)], writes=[("y2", o)])

            load(h2[:], h2v[:, :, 0:512], reads=["h2_scr"], name="h2")
            for T in range(4):
                ff1(T)
                if T < 3:
                    load(h2[:], h2v[:, :, (T + 1) * 512:(T + 2) * 512], reads=["h2_scr"], name="h2")
                ff2(T)
            for q in range(4):
                epi(3, q)
        if not out_toks:
            zt = sbt(top, "zt", [128, D])
            P.op("pool", lambda e: e.memset(zt[:], 0.0), writes=["zt"])
            for tt in range(16):
                out_toks.append(load(out_d[tt * 128:(tt + 1) * 128, :], zt[:], reads=["zt"]))
        P.barrier()
        P.final_wait("sp", out_toks)
        global LAST_PROG
        LAST_PROG = P
        with nc.Block() as block:
            P.emit(block)
    return nc


def host_layout(inp, b):
    f = lambda a: np.ascontiguousarray(a, dtype=np.float32)
    pk = lambda v: f(v.reshape(-1, 128).T)
    cc = np.stack([pk(inp["c"][b]), pk(inp["c_ctx"])], axis=2).reshape(128, 32)
    nw = inp["norm_w"][0]
    nwT = np.stack([pk(nw[i]) for i in range(4)], axis=1).reshape(128, 64)
    lcw = inp["lru_conv_w"][0].reshape(4, 8, 128).transpose(2, 1, 0).reshape(128, 32)
    lcb = inp["lru_conv_b"][0].reshape(8, 128).T
    lgw = inp["lru_gate_w"][0].transpose(3, 0, 1, 2, 4).reshape(128, 4096)
    lgb = inp["lru_gate_b"][0].reshape(2, 2, 8, 128).transpose(3, 0, 1, 2).reshape(128, 32)
    llam = inp["lru_lambda"][0].reshape(2, 8, 128).transpose(2, 0, 1).reshape(128, 16)
    gcw = inp["gdn_conv_w"][0].reshape(4, 3, 8, 128).transpose(3, 1, 2, 0).reshape(128, 96)
    galog = np.broadcast_to(inp["gdn_a_log"][0].reshape(1, 16), (128, 16))
    gdtb = np.broadcast_to(inp["gdn_dt_bias"][0].reshape(1, 16), (128, 16))
    return {
        "x": f(inp["x"][b]), "ctx": f(inp["ctx"][b]), "cc": f(cc),
        "w_mod": f(inp["w_mod"][0]), "b_modT": pk(inp["b_mod"][0]), "nwT": f(nwT),
        "w_in": f(inp["w_in"][0]), "lcw": f(lcw), "lcb": f(lcb), "lgw": f(lgw), "lgb": f(lgb), "llam": f(llam),
        "gcw": f(gcw), "galog": f(galog), "gdtb": f(gdtb), "gnw": f(inp["gdn_norm_w"][0].reshape(128, 1)),
        "w_out": f(inp["w_out"][0]), "w_ff1": f(inp["w_ff1"][0]), "w_ff2": f(inp["w_ff2"][0]),
    }


def kernel(**inputs):
    inp = {k: np.asarray(v) for k, v in inputs.items()}
    nc = build()
    in_maps = [host_layout(inp, b) for b in range(8)]
    res = run_bass_kernel_spmd(nc, in_maps, core_ids=list(range(8)))
    return np.stack([np.asarray(r["out"], dtype=np.float32) for r in res.results], axis=0)
```

```python
from contextlib import ExitStack
import numpy as np
import concourse.bass as bass
import concourse.mybir as mybir
from concourse.alu_op_type import AluOpType as ALU
from concourse.bass_utils import run_bass_kernel_spmd

F32 = mybir.dt.float32
F32R = mybir.dt.float32r
BF16 = mybir.dt.bfloat16
AF = mybir.ActivationFunctionType

ENGS = ("pe", "dve", "act", "pool", "sp")
INF = 1 << 60
EPS = 1e-6


class _Rec:
    def __init__(self):
        self.call = None

    def __getattr__(self, name):
        def f(*a, **k):
            self.call = (name, a, k)
            return self
        return f


def _capture(fn):
    r = _Rec()
    fn(r)
    assert r.call is not None
    return r.call


class Prog:
    def __init__(self, nc, stack, n_dma_sems=30):
        self.nc = nc
        self.ops = {e: [] for e in ENGS}
        self.cnt = {e: 0 for e in ENGS}
        self.sem = {e: stack.enter_context(nc.semaphore("s_" + e)) for e in ENGS}
        self.dsem = [stack.enter_context(nc.semaphore("d%d" % i)) for i in range(n_dma_sems)]
        self.dcnt = [0] * n_dma_sems
        self.dnext = 0
        self.dnext_q = {}
        self.waited = {e: {} for e in ENGS}
        self.res = {}
        self.nops = 0
        self.psrd = {}

    def _deps(self, reads, writes):
        deps = []
        for (name, lo, hi) in reads:
            st = self.res.setdefault(name, {"w": [], "r": []})
            for (a, b, tok) in st["w"]:
                if a < hi and lo < b:
                    deps.append(tok)
        for (name, lo, hi) in writes:
            st = self.res.setdefault(name, {"w": [], "r": []})
            for (a, b, tok) in st["w"]:
                if a < hi and lo < b:
                    deps.append(tok)
            for (a, b, tok) in st["r"]:
                if a < hi and lo < b:
                    deps.append(tok)
        return deps

    def _record(self, reads, writes, tok):
        for (name, lo, hi) in writes:
            st = self.res[name]
            st["w"] = [(a, b, t) for (a, b, t) in st["w"] if not (lo <= a and b <= hi)]
            st["r"] = [(a, b, t) for (a, b, t) in st["r"] if not (lo <= a and b <= hi)]
            st["w"].append((lo, hi, tok))
        for (name, lo, hi) in reads:
            st = self.res[name]
            st["r"] = [(a, b, t) for (a, b, t) in st["r"]
                       if not (t[0] == tok[0] and lo <= a and b <= hi)]
            st["r"].append((lo, hi, tok))

    @staticmethod
    def _norm(rs):
        out = []
        for r in rs:
            if isinstance(r, str):
                out.append((r, 0, INF))
            elif len(r) == 2:
                out.append((r[0], r[1], r[1] + 1))
            else:
                out.append(tuple(r))
        return out

    def _waits(self, eng, deps):
        ws = {}
        for (key, val, deng) in deps:
            if deng == eng and eng == "pe":
                continue
            if self.waited[eng].get(key, 0) >= val:
                continue
            ws[key] = max(ws.get(key, 0), val)
        for k, v in ws.items():
            self.waited[eng][k] = v
        return list(ws.items())

    def op(self, eng, fn, reads=(), writes=(), inc=True):
        reads = self._norm(reads)
        writes = self._norm(writes)
        deps = self._deps(reads, writes)
        psread = eng in ("dve", "act") and any(r[0] == "ps" for r in reads)
        if psread:
            other = "act" if eng == "dve" else "dve"
            if other in self.psrd:
                deps.append(self.psrd[other])
        waits = self._waits(eng, deps)
        idx = self.cnt[eng] + 1
        if inc:
            self.cnt[eng] = idx
        tok = (("e", eng), idx, eng)
        if psread:
            self.psrd[eng] = tok
        self._record(reads, writes, tok)
        self.ops[eng].append((waits, _capture(fn), ("e", eng) if inc else None, 1))
        self.nops += 1
        return tok

    def dma(self, eng, fn, reads=(), writes=()):
        reads = self._norm(reads)
        writes = self._norm(writes)
        deps = self._deps(reads, writes)
        nd = len(self.dsem)
        lo, hi = (0, (nd * 3) // 5) if eng == "sp" else ((nd * 3) // 5, nd)
        cur = self.dnext_q.get(eng, lo)
        j = cur
        self.dnext_q[eng] = lo + (cur + 1 - lo) % (hi - lo)
        if self.dcnt[j] > 0:
            deps.append((("d", j), 16 * self.dcnt[j], "dma"))
        waits = self._waits(eng, deps)
        self.dcnt[j] += 1
        tok = (("d", j), 16 * self.dcnt[j], "dma")
        self._record(reads, writes, tok)
        self.ops[eng].append((waits, _capture(fn), ("d", j), 16))
        self.nops += 1
        return tok

    def barrier(self):
        deps = [(("e", e), self.cnt[e], "x") for e in ENGS if self.cnt[e] > 0]
        deps += [(("d", j), 16 * c, "dma") for j, c in enumerate(self.dcnt) if c > 0]
        for e in ENGS:
            waits = self._waits(e, [d for d in deps if d[0] != ("e", e)])
            if waits:
                self.ops[e].append((waits, None, None, 0))
        self.res = {}

    def final_wait(self, eng, toks):
        waits = self._waits(eng, list(toks))
        self.ops[eng].append((waits, None, None, 0))

    def _semobj(self, key):
        return self.sem[key[1]] if key[0] == "e" else self.dsem[key[1]]

    def emit(self, block):
        def mk(ename):
            def body(e):
                for (waits, fn, inckey, incv) in self.ops[ename]:
                    for (k, v) in waits:
                        e.wait_ge(self._semobj(k), v)
                    if fn is None:
                        continue
                    name, a, k = fn
                    ins = getattr(e, name)(*a, **k)
                    if inckey is not None:
                        ins.then_inc(self._semobj(inckey), incv)
            return body
        if self.ops["sp"]:
            block.sync(mk("sp"))
        if self.ops["pe"]:
            block.tensor(mk("pe"))
        if self.ops["dve"]:
            block.vector(mk("dve"))
        if self.ops["act"]:
            block.scalar(mk("act"))
        if self.ops["pool"]:
            block.gpsimd(mk("pool"))


D = 2048
S = 2048
NCTX = 256
NALL = NCTX + S
DIN = 6176
DFF = 8192
NCH = NALL // 128
XW = 2310
TW = 2307
LT = 259
NEGBIG = -30000.0


def ch_off(ch):
    return ch * 128 if ch < 2 else LT + (ch - 2) * 128


def build(dbg=False, phases=("A", "B1", "B2", "C", "D")):
    nc = bass.Bass("TRN2", target_bir_lowering=False)
    din = lambda n, s, d=F32: nc.dram_tensor(n, s, d, kind="ExternalInput").ap()
    x_d = din("x", [S, D]); ctx_d = din("ctx", [NCTX, D]); cc_d = din("cc", [128, 32])
    wmod_d = din("w_mod", [D, 6 * D]); bmodT_d = din("b_modT", [128, 96]); nwT_d = din("nwT", [128, 64])
    win_d = din("w_in", [D, DIN])
    lcw_d = din("lcw", [128, 32]); lcb_d = din("lcb", [128, 8]); lgw_d = din("lgw", [128, 4096])
    lgb_d = din("lgb", [128, 32]); llam_d = din("llam", [128, 16])
    gcw_d = din("gcw", [128, 96]); galog_d = din("galog", [128, 16]); gdtb_d = din("gdtb", [128, 16])
    gnw_d = din("gnw", [128, 1])
    wout_d = din("w_out", [D, D]); w1_d = din("w_ff1", [D, DFF]); w2_d = din("w_ff2", [DFF, D])
    out_d = nc.dram_tensor("out", [S, D], F32, kind="ExternalOutput").ap()
    skind = "ExternalOutput" if dbg else "Internal"
    mix_scr = nc.dram_tensor("mix_scr", [16, 128, S], BF16, kind=skind).ap()
    h2_scr = nc.dram_tensor("h2_scr", [16, 128, S], BF16, kind=skind).ap()
    x1_scr = nc.dram_tensor("x1_scr", [S, D], F32, kind=skind).ap()
    w1b = nc.dram_tensor("w1b", [64, 128, 16, 128], BF16, kind="Internal").ap()
    w2b = nc.dram_tensor("w2b", [16, 128, 64, 128], BF16, kind="Internal").ap()
    if dbg:
        hT_dbg = nc.dram_tensor("hT_dbg", [128, 16, NALL], BF16, kind="ExternalOutput").ap()
        modT_dbg = nc.dram_tensor("modT_dbg", [128, 192], F32, kind="ExternalOutput").ap()

    with ExitStack() as top:
        P = Prog(nc, top)
        psb = [top.enter_context(nc.psum_tensor("ps%d" % i, [128, 512], F32)) for i in range(8)]
        pst = {"i": 0}

        reserved = set()

        def bank():
            while True:
                i = pst["i"]; pst["i"] = (i + 1) % 8
                if i not in reserved:
                    return i

        sbt = lambda st, n, s, d=F32: st.enter_context(nc.sbuf_tensor("sb_" + n, s, d))
        ident = sbt(top, "ident", [128, 128]); ones = sbt(top, "ones", [128, 128])
        onesb = sbt(top, "onesb", [128, 128], BF16)
        par = sbt(top, "par", [128, 64 + 96 + 32 + 8 + 32 + 16 + 96 + 16 + 16 + 1])
        o_ = [0]
        def psl(n):
            a = o_[0]; o_[0] += n
            return (a, a + n)
        s_nw, s_bm, s_lcw, s_lcb, s_lgb, s_llam, s_gcw, s_gal, s_gdt, s_gnw = [psl(n) for n in (64, 96, 32, 8, 32, 16, 96, 16, 16, 1)]
        pv = lambda s: par[:, s[0]:s[1]]
        modT = sbt(top, "modT", [128, 192])
        vecs = sbt(top, "vecs", [128, 8, 16])
        cneg = sbt(top, "cneg", [128, 16])
        sccb = sbt(top, "sccb", [128, 32], BF16)
        C3 = [128, NCH, 16]
        beta = sbt(top, "beta", C3); nbeta = sbt(top, "nbeta", C3); gg = sbt(top, "gg", C3); gam = sbt(top, "gam", C3)
        ngam = sbt(top, "ngam", C3); glast = sbt(top, "glast", C3); cdl = sbt(top, "cdl", C3); neg_eg = sbt(top, "neg_eg", C3); kdec = sbt(top, "kdec", C3)
        nega = sbt(top, "nega", [128, 16])
        trif = sbt(top, "trif", [128, 128]); trib = sbt(top, "trib", [128, 128])
        negm = [sbt(top, "negm%d" % d, [128, 128]) for d in range(2)]
        offd = sbt(top, "offd", [128, 128]); bd32 = sbt(top, "bd32", [128, 128]); od64 = sbt(top, "od64", [128, 128]); od128 = sbt(top, "od128", [128, 128])
        s_h = ExitStack()
        hT = sbt(s_h, "hT", [128, 16, NALL], BF16)
        p_scr = nc.dram_tensor("p_scr", [8, 3, 128, NALL], F32, kind="Internal").ap()
        zs_scr = nc.dram_tensor("zs_scr", [8, 128, S], BF16, kind="Internal").ap()
        xs_scr = nc.dram_tensor("xs_scr", [8, 128, NALL], F32, kind="Internal").ap()
        gy_scr = nc.dram_tensor("gy_scr", [8, 128, S], BF16, kind="Internal").ap()
        oacc_scr = nc.dram_tensor("oacc_scr", [16, 128, 128], F32, kind="Internal").ap()

        def make_row(st, vi, dn):
            dst = sbt(st, dn, [128, D]); dg = sbt(st, "dg_" + dn, [128, 512])
            for g4 in range(4):
                for q in range(4):
                    k = g4 * 4 + q
                    P.op("dve", lambda e: e.tensor_scalar(out=dg[:, q * 128:(q + 1) * 128], in0=ident[:], scalar1=vecs[:, vi, k:k + 1], scalar2=None, op0=ALU.mult),
                         reads=["vecs", "ident"], writes=[("dg", q)])
                b = bank()
                P.op("pe", lambda e: e.matmul(psb[b][:], lhsT=ones[:], rhs=dg[:], start=True, stop=True), reads=["ones", "dg"], writes=[("ps", b)])
                P.op("act", lambda e: e.copy(out=dst[:, g4 * 512:(g4 + 1) * 512], in_=psb[b][:]), reads=[("ps", b)], writes=[(dn, g4)])
            return dst

        def load(dst, src, q="sp", name=None, reads=()):
            return P.dma(q, lambda e: e.dma_start(out=dst, in_=src), reads=reads, writes=[name] if name else [])

        P.op("pool", lambda e: e.memset(ones[:], 1.0), writes=["ones"])
        P.op("pool", lambda e: e.memset(onesb[:], 1.0), writes=["onesb"])
        P.op("pool", lambda e: e.memset(ident[:], 1.0), writes=["ident"])
        P.op("pool", lambda e: e.affine_select(out=ident[:], in_=ident[:], pattern=[[-1, 128]], compare_op=ALU.is_equal,
                                               fill=0.0, base=0, channel_multiplier=1), reads=["ident"], writes=["ident"])
        for (sl, src) in ((s_nw, nwT_d), (s_bm, bmodT_d), (s_lcw, lcw_d), (s_lcb, lcb_d), (s_lgb, lgb_d), (s_llam, llam_d),
                          (s_gcw, gcw_d), (s_gal, galog_d), (s_gdt, gdtb_d), (s_gnw, gnw_d)):
            load(pv(sl), src, name=("par", sl[0], sl[1]))

        with ExitStack() as sa:
            cc = sbt(sa, "cc", [128, 32]); scc = sbt(sa, "scc", [128, 32])
            wm = [sbt(sa, "wm%d" % i, [128, 4096], BF16) for i in range(3)]
            load(cc[:], cc_d, name="cc")
            P.op("act", lambda e: e.activation(out=scc[:], in_=cc[:], func=AF.Silu), reads=["cc"], writes=["scc"])
            P.op("dve", lambda e: e.tensor_copy(out=sccb[:], in_=scc[:]), reads=["scc"], writes=["sccb"])
            mb = bank()
            first = True
            for k in range(16):
                i = k % 3
                P.dma("pool", lambda e: e.dma_start(out=wm[i][:], in_=wmod_d[k * 128:(k + 1) * 128, 0:4096]), writes=["wm%d" % i])
                for j in range(32):
                    P.op("pe", lambda e: e.matmul(psb[mb][:, j * 2:j * 2 + 2], lhsT=wm[i][:, j * 128:(j + 1) * 128], rhs=sccb[:, k * 2:k * 2 + 2],
                                                  start=first, stop=(k == 15), skip_group_check=True),
                         reads=["wm%d" % i, "sccb"], writes=[("ps", mb)], inc=(j == 31))
                    first = False
            P.op("dve", lambda e: e.tensor_tensor(out=modT[:, 0:64].rearrange("p (j r) -> p j r", r=2),
                                                  in0=psb[mb][:, 0:64].rearrange("p (j r) -> p j r", r=2),
                                                  in1=pv(s_bm)[:, 0:32].unsqueeze(2).to_broadcast([128, 32, 2]), op=ALU.add),
                 reads=[("ps", mb), ("par", s_bm[0], s_bm[1])], writes=[("modT", 0, 64)])
            mv = modT[:].rearrange("p (j r) -> p j r", r=2)
            nw = pv(s_nw).rearrange("p (w k) -> p w k", w=4)
            rp = [("par", s_nw[0], s_nw[1]), "modT"]
            def scl(dst, sc, w):
                P.op("dve", lambda e: e.scalar_tensor_tensor(out=dst, in0=sc, scalar=1.0, in1=w, op0=ALU.add, op1=ALU.mult),
                     reads=rp, writes=["vecs"])
            scl(vecs[:, 0, :], mv[:, 16:32, 0], nw[:, 0, :])
            P.op("dve", lambda e: e.tensor_copy(out=vecs[:, 1, :], in_=mv[:, 0:16, 0]), reads=rp, writes=["vecs"])
            scl(vecs[:, 2, :], mv[:, 16:32, 1], nw[:, 0, :])
            P.op("dve", lambda e: e.tensor_copy(out=vecs[:, 3, :], in_=mv[:, 0:16, 1]), reads=rp, writes=["vecs"])
            P.op("act", lambda e: e.activation(out=cneg[:], in_=pv(s_llam), func=AF.Exp, scale=-1.0), reads=[("par", s_llam[0], s_llam[1])], writes=["cneg"])
            P.op("act", lambda e: e.activation(out=cneg[:], in_=cneg[:], func=AF.Ln, bias=1.0), reads=["cneg"], writes=["cneg"])
            P.op("dve", lambda e: e.tensor_scalar(out=cneg[:], in0=cneg[:], scalar1=-8.0, scalar2=None, op0=ALU.mult), reads=["cneg"], writes=["cneg"])
            if dbg:
                load(modT_dbg, modT[:], reads=["modT"])

            xt = [sbt(sa, "xt%d" % i, [128, D]) for i in range(2)]
            junk = sbt(sa, "junkA", [128, D])
            st1 = [sbt(sa, "st1_%d" % i, [128, 4]) for i in range(2)]

            def norm_T(src_ap, tname, ti, scl_i, sh_i, dstT, dname, col0, stt, stn):
                t = xt[ti]
                P.op("act", lambda e: e.activation(out=junk[:], in_=t[:], func=AF.Square, accum_out=stt[:, 0:1]),
                     reads=[tname], writes=["junkA", stn])
                P.op("act", lambda e: e.activation(out=stt[:, 1:2], in_=stt[:, 0:1], func=AF.Sqrt, scale=1.0 / D, bias=EPS), reads=[stn], writes=[stn])
                P.op("dve", lambda e: e.reciprocal(out=stt[:, 2:3], in_=stt[:, 1:2]), reads=[stn], writes=[stn])
                P.op("act", lambda e: e.activation(out=t[:], in_=t[:], func=AF.Copy, scale=stt[:, 2:3]), reads=[stn, tname], writes=[tname])
                for g4 in range(4):
                    b = bank()
                    for q in range(4):
                        k = g4 * 4 + q
                        P.op("pe", (lambda b, q, k: lambda e: e.transpose(psb[b][:, q * 128:(q + 1) * 128], t[:, k * 128:(k + 1) * 128], ident[:]))(b, q, k),
                             reads=[tname, "ident"], writes=[("ps", b)], inc=(q == 3))
                    for q in range(4):
                        k = g4 * 4 + q
                        eng = "dve" if q % 2 == 0 else "act"
                        if eng == "dve":
                            fn = (lambda b, q, k: lambda e: e.tensor_scalar(out=dstT[:, k, col0:col0 + 128], in0=psb[b][:, q * 128:(q + 1) * 128],
                                                                            scalar1=vecs[:, scl_i, k:k + 1], scalar2=vecs[:, sh_i, k:k + 1],
                                                                            op0=ALU.mult, op1=ALU.add))(b, q, k)
                        else:
                            fn = (lambda b, q, k: lambda e: e.activation(out=dstT[:, k, col0:col0 + 128], in_=psb[b][:, q * 128:(q + 1) * 128],
                                                                         func=AF.Identity, scale=vecs[:, scl_i, k:k + 1], bias=vecs[:, sh_i, k:k + 1]))(b, q, k)
                        P.op(eng, fn, reads=[("ps", b), "vecs"], writes=[(dname, k * 100000 + col0, k * 100000 + col0 + 128)])

            import os as _os
            for ti in range(int(_os.environ.get("K_NTI", NCH))):
                src = ctx_d[ti * 128:(ti + 1) * 128, :] if ti < 2 else x_d[(ti - 2) * 128:(ti - 1) * 128, :]
                i = ti % 2
                load(xt[i][:], src, name="xt%d" % i)
                norm_T(src, "xt%d" % i, i, 2 if ti < 2 else 0, 3 if ti < 2 else 1, hT, "hT", ti * 128, st1[i], "st1_%d" % i)
            if dbg:
                load(hT_dbg, hT[:], reads=["hT"])
        P.barrier()

        hTr = lambda k, c0, c1: ("hT", k * 100000 + c0, k * 100000 + c1)
        hT_all = [("hT", 0, INF)]

        def precast():
            import os as _os
            if _os.environ.get("K_NOPRECAST"):
                return
            w1v = w1_d.rearrange("(k p) (o c) -> o p k c", p=128, c=128)
            for o in range(0, 64, 4):
                for oo in range(4):
                    P.dma("pool", (lambda o: lambda e: e.dma_start(out=w1b[o], in_=w1v[o]))(o + oo), writes=[("w1b", o + oo)])
            w2v = w2_d.rearrange("(k p) (o c) -> o p k c", p=128, c=128)
            for o in range(16):
                for kh in range(4):
                    P.dma("pool", (lambda o, kh: lambda e: e.dma_start(out=w2b[o][:, kh * 16:(kh + 1) * 16, :], in_=w2v[o][:, kh * 16:(kh + 1) * 16, :]))(o, kh),
                          writes=[("w2b", o * 4 + kh)])

        winv = win_d.rearrange("(k p) c -> p k c", p=128)

        def inproj(wt, wname, woff, tok_groups, consume):
            for gi, (c0, n) in enumerate(tok_groups):
                b = bank()
                for k in range(16):
                    P.op("pe", (lambda b, k, c0, n: lambda e: e.matmul(psb[b][:, 0:n], lhsT=wt[:, k, woff:woff + 128], rhs=hT[:, k, c0:c0 + n],
                                                                          start=(k == 0), stop=(k == 15)))(b, k, c0, n),
                         reads=[wname, hTr(k, c0, c0 + n)], writes=[("ps", b)], inc=(k == 15))
                consume(b, gi)

        TG_ALL = [(0, 256)] + [(256 + g * 512, 512) for g in range(4)]
        TG_LAT = [(256 + g * 512, 512) for g in range(4)]

        if "B2" in phases:
          with ExitStack() as sb2a:
            wgb = sbt(sb2a, "wgb", [128, 16, 32], BF16)
            def msk(t, tn, base_val, fill, pattern, cmp, base, cm, src=None):
                if src is None:
                    P.op("pool", lambda e: e.memset(t[:], base_val), writes=[tn])
                P.op("pool", lambda e: e.affine_select(out=t[:], in_=t[:], pattern=pattern, compare_op=cmp, fill=fill, base=base, channel_multiplier=cm),
                     reads=[tn], writes=[tn])
            msk(trif, "trif", 1.0, 0.0, [[1, 128]], ALU.is_ge, 0, -1)
            msk(trib, "trib", 1.0, 0.0, [[-1, 128]], ALU.is_ge, 0, 1)
            msk(negm[0], "negm0", 0.0, NEGBIG, [[1, 128]], ALU.is_ge, 0, -1)
            msk(negm[1], "negm1", 0.0, NEGBIG, [[-1, 128]], ALU.is_ge, 0, 1)
            msk(offd, "offd", 1.0, 0.0, [[-1, 128]], ALU.not_equal, 0, 1)
            for (t, tn, fn_) in ((bd32, "bd32", lambda pb, cb: 1.0 if pb == cb else 0.0),
                                 (od64, "od64", lambda pb, cb: 1.0 if (pb != cb and pb // 2 == cb // 2) else 0.0),
                                 (od128, "od128", lambda pb, cb: 1.0 if pb // 2 != cb // 2 else 0.0)):
                for pb in range(4):
                    for cb in range(4):
                        P.op("pool", (lambda t, pb, cb, v: lambda e: e.memset(t[pb * 32:(pb + 1) * 32, cb * 32:(cb + 1) * 32], v))(t, pb, cb, fn_(pb, cb)), writes=[tn])
            P.dma("pool", lambda e: e.dma_start(out=wgb[:], in_=winv[:, :, 6144:6176]), writes=["wgb"])
            gbank = [bank(), bank()]
            for ch in range(NCH):
                b = gbank[ch // 9]; co = (ch % 9) * 32
                for k in range(16):
                    P.op("pe", (lambda b, co, ch, k: lambda e: e.matmul(psb[b][:, co:co + 32], lhsT=hT[:, k, ch * 128:(ch + 1) * 128], rhs=wgb[:, k, :],
                                                                          start=(k == 0), stop=(k == 15), skip_group_check=True))(b, co, ch, k),
                         reads=["wgb"] + hT_all, writes=[("ps", b)], inc=(k == 15))
            P.op("act", lambda e: e.activation(out=nega[:], in_=pv(s_gal), func=AF.Exp), reads=[("par", 0, INF)], writes=["nega"])
            P.op("dve", lambda e: e.tensor_scalar(out=nega[:], in0=nega[:], scalar1=-1.0, scalar2=None, op0=ALU.mult), reads=["nega"], writes=["nega"])
            for half in range(2):
                b = gbank[half]
                pvw = psb[b][:, 0:288].rearrange("p (c f) -> p c f", f=32)
                cs = slice(half * 9, half * 9 + 9)
                P.op("act", (lambda pvw, cs: lambda e: e.activation(out=beta[:, cs, :], in_=pvw[:, :, 0:16], func=AF.Sigmoid))(pvw, cs), reads=[("ps", b)], writes=["beta"])
                P.op("dve", (lambda pvw, cs: lambda e: e.tensor_tensor(out=gg[:, cs, :], in0=pvw[:, :, 16:32], in1=pv(s_gdt).unsqueeze(1).to_broadcast([128, 9, 16]), op=ALU.add))(pvw, cs),
                     reads=[("ps", b), ("par", 0, INF)], writes=["gg"])
            P.op("act", lambda e: e.activation(out=gg[:], in_=gg[:], func=AF.Exp), reads=["gg"], writes=["gg"])
            P.op("act", lambda e: e.activation(out=gg[:], in_=gg[:], func=AF.Ln, bias=1.0), reads=["gg"], writes=["gg"])
            P.op("dve", lambda e: e.tensor_tensor(out=gg[:], in0=gg[:], in1=nega[:].unsqueeze(1).to_broadcast([128, NCH, 16]), op=ALU.mult), reads=["gg", "nega"], writes=["gg"])
            P.op("dve", lambda e: e.tensor_scalar(out=nbeta[:], in0=beta[:], scalar1=-1.0, scalar2=None, op0=ALU.mult), reads=["beta"], writes=["nbeta"])
            for d in range(2):
                b = bank()
                tri = trif if d == 0 else trib
                P.op("pe", (lambda b, tri, d: lambda e: e.matmul(psb[b][:, 0:NCH * 8].rearrange("p (c f) -> p c f", f=8), lhsT=tri[:], rhs=gg[:, :, d * 8:(d + 1) * 8], start=True, stop=True))(b, tri, d),
                     reads=["trif", "trib", "gg"], writes=[("ps", b)])
                P.op("act", (lambda b, d: lambda e: e.copy(out=gam[:, :, d * 8:(d + 1) * 8], in_=psb[b][:, 0:NCH * 8].rearrange("p (c f) -> p c f", f=8)))(b, d),
                     reads=[("ps", b)], writes=["gam"])
            b = bank()
            P.op("pe", (lambda b: lambda e: e.matmul(psb[b][:, 0:NCH * 16], lhsT=ones[:], rhs=gg[:].rearrange("p c f -> p (c f)"), start=True, stop=True))(b),
                 reads=["ones", "gg"], writes=[("ps", b)])
            P.op("act", (lambda b: lambda e: e.copy(out=glast[:].rearrange("p c f -> p (c f)"), in_=psb[b][:, 0:NCH * 16]))(b), reads=[("ps", b)], writes=["glast"])
            P.op("dve", lambda e: e.tensor_scalar(out=ngam[:], in0=gam[:], scalar1=-1.0, scalar2=None, op0=ALU.mult), reads=["gam"], writes=["ngam"])
            P.op("act", lambda e: e.activation(out=cdl[:], in_=glast[:], func=AF.Exp), reads=["glast"], writes=["cdl"])
            P.op("dve", lambda e: e.tensor_tensor(out=kdec[:], in0=glast[:], in1=gam[:], op=ALU.subtract), reads=["glast", "gam"], writes=["kdec"])
            P.op("act", lambda e: e.activation(out=kdec[:], in_=kdec[:], func=AF.Exp), reads=["kdec"], writes=["kdec"])
            P.op("act", lambda e: e.activation(out=neg_eg[:], in_=gam[:], func=AF.Exp), reads=["gam"], writes=["neg_eg"])
            P.op("dve", lambda e: e.tensor_scalar(out=neg_eg[:], in0=neg_eg[:], scalar1=-1.0, scalar2=None, op0=ALU.mult), reads=["neg_eg"], writes=["neg_eg"])

            wm2 = [sbt(sb2a, "wm2_%d" % i, [128, 4096], BF16) for i in range(3)]
            mb2 = bank(); reserved.add(mb2)
            mpieces = [(k, pc) for k in range(16) for pc in (1, 2)]
            mpi = [0]

            def mod_pieces(n):
                for _ in range(n):
                    if mpi[0] >= len(mpieces):
                        return
                    k, pc = mpieces[mpi[0]]
                    i = mpi[0] % 3
                    first = (mpi[0] == 0)
                    mpi[0] += 1
                    P.dma("pool", lambda e: e.dma_start(out=wm2[i][:], in_=wmod_d[k * 128:(k + 1) * 128, pc * 4096:(pc + 1) * 4096]), writes=["wm2_%d" % i])
                    for j in range(32):
                        jj = (pc - 1) * 32 + j
                        P.op("pe", lambda e: e.matmul(psb[mb2][:, jj * 2:jj * 2 + 2], lhsT=wm2[i][:, j * 128:(j + 1) * 128], rhs=sccb[:, k * 2:k * 2 + 2],
                                                      start=(first and j == 0), stop=(k == 15), skip_group_check=True),
                             reads=["wm2_%d" % i, "sccb"], writes=[("ps", mb2)], inc=(j == 31))

            wg = [sbt(sb2a, "wg%d" % i, [128, 16, 512], BF16) for i in range(2)]
            stgs = [sbt(sb2a, "stgs%d" % i, [128, 512]) for i in range(4)]
            zst = [sbt(sb2a, "zst%d" % i, [128, 512], BF16) for i in range(2)]
            sti = [0]
            for h in range(8):
                w = wg[h % 2]; wn = "wg%d" % (h % 2)
                for t4 in range(4):
                    c0 = 2048 + t4 * 1024 + h * 128
                    P.dma("pool", lambda e: e.dma_start(out=w[:, :, t4 * 128:(t4 + 1) * 128], in_=winv[:, :, c0:c0 + 128]), writes=[(wn, t4)])
                for t3 in range(3):
                    def cons_p(b, gi):
                        c0, n = TG_ALL[gi]
                        i = sti[0] % 4; sti[0] += 1
                        P.op("act", lambda e: e.copy(out=stgs[i][:, 0:n], in_=psb[b][:, 0:n]), reads=[("ps", b)], writes=["stgs%d" % i])
                        load(p_scr[h, t3][:, c0:c0 + n], stgs[i][:, 0:n], reads=["stgs%d" % i], name=("p_scr", h * 3 + t3))
                    inproj(w, (wn, t3), t3 * 128, TG_ALL, cons_p)

                def cons_zs(b, gi):
                    i = gi % 2
                    P.op("act", lambda e: e.activation(out=zst[i][:], in_=psb[b][:], func=AF.Silu), reads=[("ps", b)], writes=["zst%d" % i])
                    load(zs_scr[h][:, gi * 512:(gi + 1) * 512], zst[i][:], reads=["zst%d" % i], name=("zs_scr", h))
                inproj(w, (wn, 3), 384, TG_LAT, cons_zs)
                mod_pieces(2)
            xst = [sbt(sb2a, "xst%d" % i, [128, NALL]) for i in range(2)]
            gyst = [sbt(sb2a, "gyst%d" % i, [128, S], BF16) for i in range(2)]
            for h in range(8):
                w = wg[h % 2]; wn = "wg%d" % (h % 2)
                P.dma("pool", lambda e: e.dma_start(out=w[:, :, 0:128], in_=winv[:, :, h * 128:(h + 1) * 128]), writes=[(wn, 0)])
                P.dma("pool", lambda e: e.dma_start(out=w[:, :, 128:256], in_=winv[:, :, 1024 + h * 128:1024 + (h + 1) * 128]), writes=[(wn, 1)])
                xs = xst[h % 2]; xsn = "xst%d" % (h % 2)
                gs_ = gyst[h % 2]; gsn = "gyst%d" % (h % 2)

                def cons_x(b, gi):
                    if gi == 0:
                        P.op("act", lambda e: e.copy(out=xs[:, 0:256], in_=psb[b][:, 0:256]), reads=[("ps", b)], writes=[(xsn, 0)])
                    else:
                        r0 = (gi - 1) * 8
                        ov = xs[:, 256:NALL].rearrange("p (c r) -> p r c", r=32)[:, r0:r0 + 8, :]
                        iv = psb[b][:].rearrange("p (r c) -> p r c", c=64)
                        P.op("act", lambda e: e.copy(out=ov, in_=iv), reads=[("ps", b)], writes=[(xsn, gi)])
                inproj(w, (wn, 0), 0, TG_ALL, cons_x)
                load(xs_scr[h], xs[:], reads=[xsn], name=("xs_scr", h))

                def cons_y(b, gi):
                    P.op("act", lambda e: e.activation(out=gs_[:, gi * 512:(gi + 1) * 512], in_=psb[b][:], func=AF.Gelu_apprx_tanh), reads=[("ps", b)], writes=[(gsn, gi)])
                inproj(w, (wn, 1), 128, TG_LAT, cons_y)
                load(gy_scr[h], gs_[:], reads=[gsn], name=("gy_scr", h))
                mod_pieces(2)
            mod_pieces(100)
            P.op("dve", lambda e: e.tensor_tensor(out=modT[:, 64:192].rearrange("p (j r) -> p j r", r=2),
                                                  in0=psb[mb2][:, 0:128].rearrange("p (j r) -> p j r", r=2),
                                                  in1=pv(s_bm)[:, 32:96].unsqueeze(2).to_broadcast([128, 64, 2]), op=ALU.add),
                 reads=[("ps", mb2), ("par", 0, INF)], writes=[("modT", 64, 192)])
            reserved.discard(mb2)
            mv = modT[:].rearrange("p (j r) -> p j r", r=2)
            nw = pv(s_nw).rearrange("p (w k) -> p w k", w=4)
            rp = [("par", 0, INF), "modT"]
            P.op("dve", lambda e: e.tensor_tensor(out=vecs[:, 4, :], in0=mv[:, 32:48, 0], in1=nw[:, 1, :], op=ALU.mult), reads=rp, writes=[("vecs", 4)])
            P.op("dve", lambda e: e.scalar_tensor_tensor(out=vecs[:, 5, :], in0=mv[:, 64:80, 0], scalar=1.0, in1=nw[:, 2, :], op0=ALU.add, op1=ALU.mult), reads=rp, writes=[("vecs", 5)])
            P.op("dve", lambda e: e.tensor_copy(out=vecs[:, 6, :], in_=mv[:, 48:64, 0]), reads=rp, writes=[("vecs", 6)])
            P.op("dve", lambda e: e.tensor_tensor(out=vecs[:, 7, :], in0=mv[:, 80:96, 0], in1=nw[:, 3, :], op=ALU.mult), reads=rp, writes=[("vecs", 7)])
          P.barrier()
        s_h.close()

        freeb = set(range(8))

        def take(n):
            while len(freeb) < n:
                yield
            return [freeb.pop() for _ in range(n)]

        def rel(*bs):
            for b_ in bs:
                assert b_ not in freeb
                freeb.add(b_)

        def run_tasks(gens):
            active = list(gens)
            while active:
                for g in list(active):
                    try:
                        next(g)
                    except StopIteration:
                        active.remove(g)

        if "B1" in phases:
          with ExitStack() as sb1:
            lgw = sbt(sb1, "lgw", [128, 4096], BF16)
            P.dma("pool", lambda e: e.dma_start(out=lgw[:], in_=lgw_d), writes=["lgw"])
            xpad2 = [sbt(sb1, "xpad0", [128, XW])] * 2
            gyb2 = [sbt(sb1, "gyb%d" % i, [128, S], BF16) for i in range(2)]
            acc2 = [sbt(sb1, "acc0", [128, TW])] * 2; xrb2 = [sbt(sb1, "xrb0", [128, TW], BF16)] * 2
            mixs2 = [sbt(sb1, "mixs%d" % i, [128, S], BF16) for i in range(2)]
            RtA = [[sbt(sb1, "Rt%d_%d" % (d, i), [128, TW]) for d in range(2)] for i in range(2)]
            ItA = [[sbt(sb1, "It%d_%d" % (d, i), [128, TW]) for d in range(2)] for i in range(2)]
            StA = [[sbt(sb1, "St%d_%d" % (d, i), [128, TW]) for d in range(2)] for i in range(2)]
            P.op("pool", lambda e: e.memset(xpad2[0][:], 0.0), writes=["xpad0"])
            lcw = pv(s_lcw).rearrange("p (h j) -> p h j", j=4); lcb = pv(s_lcb)
            lgb = pv(s_lgb).rearrange("p (d g h) -> p d g h", d=2, g=2)
            rpar = [("par", 0, INF)]
            BLK = [(0, 256)] + [(LT + i * 512, 512) for i in range(4)]
            rg_ = lambda nm, c0, n: (nm, c0, c0 + n)

            xp = xpad2[0]; xpn = "xpad0"; acc = acc2[0]; accn = "acc0"; xrb = xrb2[0]; xrn = "xrb0"

            def lpre(h):
                hb = h % 2
                load(xp[:, 2:258], xs_scr[h][:, 0:256], reads=[("xs_scr", h)], name=(xpn, 2, 258))
                load(xp[:, 261:2309], xs_scr[h][:, 256:NALL], reads=[("xs_scr", h)], name=(xpn, 261, 2309))
                load(gyb2[hb][:], gy_scr[h], reads=[("gy_scr", h)], name="gyb%d" % hb)
                P.op("act", lambda e: e.activation(out=acc[:], in_=xp[:, 0:TW], func=AF.Identity, scale=lcw[:, h, 0:1], bias=lcb[:, h:h + 1]), reads=[xpn] + rpar, writes=[accn])
                for j in (1, 2, 3):
                    P.op("dve", lambda e: e.scalar_tensor_tensor(out=acc[:], in0=xp[:, j:j + TW], scalar=lcw[:, h, j:j + 1], in1=acc[:], op0=ALU.mult, op1=ALU.add),
                         reads=[xpn, accn] + rpar, writes=[accn])
                P.op("pool", lambda e: e.tensor_copy(out=xrb[:], in_=acc[:]), reads=[accn], writes=[xrn])

            def lst12(h):
                hb = h % 2
                Rt, It, St = RtA[hb], ItA[hb], StA[hb]
                sfx = "_%d" % hb
                for (c0, n) in BLK:
                    for d in range(2):
                        for g, dst, dn in ((0, Rt[d], "Rt%d" % d + sfx), (1, It[d], "It%d" % d + sfx)):
                            woff = ((d * 2 + g) * 8 + h) * 128
                            bb = bank()
                            P.op("pe", lambda e: e.matmul(psb[bb][:, 0:n], lhsT=lgw[:, woff:woff + 128], rhs=xrb[:, c0:c0 + n], start=True, stop=True),
                                 reads=["lgw", rg_(xrn, c0, n)], writes=[("ps", bb)])
                            P.op("act", lambda e: e.activation(out=dst[:, c0:c0 + n], in_=psb[bb][:, 0:n], func=AF.Sigmoid, bias=lgb[:, d, g, h:h + 1]),
                                 reads=[("ps", bb)] + rpar, writes=[rg_(dn, c0, n)])
                for (c0, n) in BLK:
                    for d in range(2):
                        ci = d * 8 + h
                        P.op("act", lambda e: e.activation(out=Rt[d][:, c0:c0 + n], in_=Rt[d][:, c0:c0 + n], func=AF.Exp, scale=cneg[:, ci:ci + 1]),
                             reads=[rg_("Rt%d" % d + sfx, c0, n), "cneg"], writes=[rg_("Rt%d" % d + sfx, c0, n)])
                        P.op("pool", lambda e: e.tensor_tensor(out=It[d][:, c0:c0 + n], in0=It[d][:, c0:c0 + n], in1=acc[:, c0:c0 + n], op=ALU.mult),
                             reads=[rg_("It%d" % d + sfx, c0, n), rg_(accn, c0, n)], writes=[rg_("It%d" % d + sfx, c0, n)])
                        P.op("pool", lambda e: e.tensor_tensor(out=St[d][:, c0:c0 + n], in0=Rt[d][:, c0:c0 + n], in1=Rt[d][:, c0:c0 + n], op=ALU.mult),
                             reads=[rg_("Rt%d" % d + sfx, c0, n)], writes=[rg_("St%d" % d + sfx, c0, n)])

            def lst3(h):
                hb = h % 2
                Rt, It, St = RtA[hb], ItA[hb], StA[hb]
                Hx = St
                sfx = "_%d" % hb
                for d in range(2):
                    for (c0, n) in BLK:
                        P.op("act", lambda e: e.activation(out=St[d][:, c0:c0 + n], in_=St[d][:, c0:c0 + n], func=AF.Sqrt, scale=-1.0, bias=1.0),
                             reads=[rg_("St%d" % d + sfx, c0, n)], writes=[rg_("St%d" % d + sfx, c0, n)])
                for d in range(2):
                    order = BLK if d == 0 else [BLK[0]] + BLK[:0:-1]
                    hx = Hx[d]; hxn = "St%d" % d + sfx
                    prev = None
                    for (c0, n) in order:
                        P.op("dve", lambda e: e.tensor_tensor(out=St[d][:, c0:c0 + n], in0=St[d][:, c0:c0 + n], in1=It[d][:, c0:c0 + n], op=ALU.mult),
                             reads=[rg_("St%d" % d + sfx, c0, n), rg_("It%d" % d + sfx, c0, n)], writes=[rg_("St%d" % d + sfx, c0, n)])
                        if prev is None:
                            init = 0.0; rd = []
                        else:
                            pc0, pn = prev
                            init = hx[:, pc0 + pn - 1:pc0 + pn] if d == 0 else hx[:, pc0:pc0 + 1]
                            rd = [rg_(hxn, pc0, pn)]
                        a_v = Rt[d][:, c0:c0 + n]; u_v = St[d][:, c0:c0 + n]; o_v = hx[:, c0:c0 + n]
                        if d == 1:
                            a_v, u_v, o_v = a_v[:, ::-1], u_v[:, ::-1], o_v[:, ::-1]
                        P.op("dve", lambda e: e.tensor_tensor_scan(out=o_v, data0=a_v, data1=u_v, initial=init, op0=ALU.mult, op1=ALU.add),
                             reads=[rg_("Rt%d" % d + sfx, c0, n), rg_("St%d" % d + sfx, c0, n)] + rd, writes=[rg_(hxn, c0, n)])
                        prev = (c0, n)
                ms = mixs2[hb]; msn = "mixs%d" % hb
                P.op("dve", lambda e: e.tensor_tensor(out=Hx[0][:, LT:TW], in0=Hx[0][:, LT:TW], in1=Hx[1][:, LT:TW], op=ALU.add), reads=["St0" + sfx, "St1" + sfx], writes=["St0" + sfx])
                P.op("pool", lambda e: e.tensor_tensor(out=ms[:].rearrange("p (r c) -> p r c", c=64), in0=gyb2[hb][:].rearrange("p (r c) -> p r c", c=64),
                                                       in1=Hx[0][:, LT:TW].rearrange("p (c r) -> p r c", r=32), op=ALU.mult), reads=["gyb%d" % hb, "St0" + sfx], writes=[msn])
                load(mix_scr[h], ms[:], reads=[msn], name=("mix_scr", h))

            lpre(0)
            for h in range(8):
                lst12(h)
                if h + 1 < 8:
                    lpre(h + 1)
                lst3(h)
            assert len(freeb) == 8
          P.barrier()


        if "B2" in phases:
          with ExitStack() as sb2:
            NSETS = 3
            graw = sbt(sb2, "graw0", [128, XW]); grn = "graw0"
            cacc = sbt(sb2, "cacc", [128, TW]); sqb = sbt(sb2, "sqb", [128, TW], BF16)
            rng_ = [sbt(sb2, "rng%d" % i, [128, 512]) for i in range(2)]
            qn2 = [sbt(sb2, "qn%d" % i, [128, TW], F32R) for i in range(2)]; kn2 = [sbt(sb2, "kn%d" % i, [128, TW], F32R) for i in range(2)]
            vf = cacc
            zs2 = [sbt(sb2, "zs%d" % i, [128, S], BF16) for i in range(2)]
            Ktok2 = [sbt(sb2, "Ktok%d" % i, [128, NCH, 128]) for i in range(2)]; Vtok2 = [sbt(sb2, "Vtok%d" % i, [128, NCH, 128]) for i in range(2)]
            oacc = sbt(sb2, "oacc", [128, 16, 128])
            mixg = sbt(sb2, "mixg0", [128, S], BF16); mgn = "mixg0"
            Sst = [sbt(sb2, "Sst%d" % d, [128, 128], F32R) for d in range(2)]
            G4 = [128, 4, 128]
            SETS = []
            for si in range(NSETS):
                t = {}
                for nm in ("Gs", "DT", "Erow", "QdT", "QKT", "Kd", "Nm", "NmT"):
                    t[nm] = sbt(sb2, "%s_%d" % (nm, si), G4, F32)
                    t[nm + "_n"] = "%s_%d" % (nm, si)
                for nm, al in (("Wm", "Gs"), ("Qa", "DT"), ("Qb", "Erow")):
                    t[nm], t[nm + "_n"] = t[al], t[al + "_n"]
                t["tmpWT"], t["tmpWT_n"] = t["Nm"], t["Nm_n"]
                t["tmpZ"], t["tmpZ_n"] = t["Qa"], t["Qa_n"]
                SETS.append(t)
            Rp = [sbt(sb2, "Rp%d" % i, [128, 128], F32R) for i in range(2)]; Vn = [sbt(sb2, "Vn%d" % i, [128, 128], F32R) for i in range(2)]
            ot = [sbt(sb2, "ot%d" % i, [128, 128]) for i in range(2)]; junkg = [sbt(sb2, "junkg%d" % i, [128, 128]) for i in range(2)]
            stg = [sbt(sb2, "stg%d" % i, [128, 4]) for i in range(2)]
            pcs = [sbt(sb2, "pcs%d" % i, [128, 16, 128], BF16) for i in range(2)]
            P.op("pool", lambda e: e.memset(graw[:], 0.0), writes=[grn])
            gcw = pv(s_gcw).rearrange("p (t h j) -> p t h j", t=3, j=4)
            f32v = lambda ap: ap.bitcast(F32)
            bcast4 = lambda m: m[:].unsqueeze(1).to_broadcast([128, 4, 128])
            fl = lambda t: t[:].rearrange("p u c -> p (u c)")
            fwd_order = list(range(NCH)); bwd_order = [1, 0] + list(range(17, 1, -1))
            rpar = [("par", 0, INF)]
            w1v_ = w1_d.rearrange("(k p) (o c) -> o p k c", p=128, c=128)
            w2v_ = w2_d.rearrange("(k p) (o c) -> o p k c", p=128, c=128)
            pjobs = [(w1v_[o], w1b[o], ("w1b", o)) for o in range(64)]
            pjobs += [(w2v_[o][:, kh * 16:(kh + 1) * 16, :], w2b[o][:, kh * 16:(kh + 1) * 16, :], ("w2b", o * 4 + kh)) for o in range(16) for kh in range(4)]
            pji = [0]

            def precast_some(n):
                for _ in range(n):
                    if pji[0] >= len(pjobs):
                        return
                    src, dst, rn = pjobs[pji[0]]
                    i = pji[0] % 2
                    pji[0] += 1
                    P.dma("pool", lambda e: e.dma_start(out=pcs[i][:], in_=src), writes=["pcs%d" % i])
                    P.dma("sp", lambda e: e.dma_start(out=dst, in_=pcs[i][:]), reads=["pcs%d" % i], writes=[rn])

            import os as _os
            r_ = lambda ap: ap.bitcast(F32R)
            NH = int(_os.environ.get("K_NGDN", 8))
            NIT = NCH // 2

            def mm4(b, lhs, ln, rhs, rn):
                for u in range(4):
                    P.op("pe", lambda e: e.matmul(psb[b][:, u * 128:(u + 1) * 128], lhsT=r_(lhs[:, u, :]), rhs=r_(rhs[:, u, :]), start=True, stop=True, skip_group_check=True),
                         reads=[ln, rn], writes=[("ps", b)], inc=(u == 3))

            def transpose4(b, src, sname):
                for u in range(4):
                    P.op("pe", lambda e: e.transpose(psb[b][:, u * 128:(u + 1) * 128], src[:, u, :], ident[:]),
                         reads=[sname, "ident"], writes=[("ps", b)], inc=(u == 3))

            def headpre(h):
                hb = h % 2
                qn, kn, Ktok, Vtok, zs = qn2[hb], kn2[hb], Ktok2[hb], Vtok2[hb], zs2[hb]
                qnn, knn, Ktn, Vtn, zsn = "qn%d" % hb, "kn%d" % hb, "Ktok%d" % hb, "Vtok%d" % hb, "zs%d" % hb
                load(zs[:], zs_scr[h], reads=[("zs_scr", h)], name=zsn)
                for t3 in range(3):
                    load(graw[:, 2:258], p_scr[h, t3][:, 0:256], reads=[("p_scr", h * 3 + t3)], name=(grn, 2, 258))
                    load(graw[:, 261:2309], p_scr[h, t3][:, 256:NALL], reads=[("p_scr", h * 3 + t3)], name=(grn, 261, 2309))
                    P.op("act", lambda e: e.activation(out=cacc[:], in_=graw[:, 0:TW], func=AF.Copy, scale=gcw[:, t3, h, 0:1]), reads=[grn] + rpar, writes=["cacc"])
                    yield
                    for j in (1, 2, 3):
                        P.op("dve", lambda e: e.scalar_tensor_tensor(out=cacc[:], in0=graw[:, j:j + TW], scalar=gcw[:, t3, h, j:j + 1], in1=cacc[:], op0=ALU.mult, op1=ALU.add),
                             reads=[grn, "cacc"] + rpar, writes=["cacc"])
                        yield
                    P.op("act", lambda e: e.activation(out=cacc[:], in_=cacc[:], func=AF.Silu), reads=["cacc"], writes=["cacc"])
                    yield
                    if t3 < 2:
                        P.op("pool", lambda e: e.tensor_tensor(out=sqb[:], in0=cacc[:], in1=cacc[:], op=ALU.mult), reads=["cacc"], writes=["sqb"])
                        yield
                        sc_ = 128.0 if t3 == 0 else 1.0
                        dq = qn if t3 == 0 else kn; dqn = qnn if t3 == 0 else knn
                        for gi, (c0, n) in enumerate([(g * 512, 512) for g in range(4)] + [(2048, TW - 2048)]):
                            b, = yield from take(1)
                            rg = rng_[gi % 2]; rgn = "rng%d" % (gi % 2)
                            P.op("pe", lambda e: e.matmul(psb[b][:, 0:n], lhsT=onesb[:], rhs=sqb[:, c0:c0 + n], start=True, stop=True), reads=["onesb", "sqb"], writes=[("ps", b)])
                            yield
                            P.op("dve", lambda e: e.tensor_copy(out=rg[:, 0:n], in_=psb[b][:, 0:n]), reads=[("ps", b)], writes=[rgn])
                            rel(b)
                            yield
                            P.op("act", lambda e: e.activation(out=rg[:, 0:n], in_=rg[:, 0:n], func=AF.Ln, scale=sc_, bias=sc_ * EPS), reads=[rgn], writes=[rgn])
                            P.op("act", lambda e: e.activation(out=rg[:, 0:n], in_=rg[:, 0:n], func=AF.Exp, scale=-0.5), reads=[rgn], writes=[rgn])
                            yield
                            P.op("pool", lambda e: e.tensor_tensor(out=dq[:, c0:c0 + n], in0=cacc[:, c0:c0 + n], in1=rg[:, 0:n], op=ALU.mult), reads=["cacc", rgn], writes=[(dqn, c0, c0 + n)])
                            yield
                    if t3 >= 1:
                        srcT, sname, dstK, dname = (kn, knn, Ktok, Ktn) if t3 == 1 else (vf, "cacc", Vtok, Vtn)
                        for c4 in range(0, NCH, 4):
                            n4 = min(4, NCH - c4)
                            b, = yield from take(1)
                            for q in range(n4):
                                co = ch_off(c4 + q)
                                P.op("pe", lambda e: e.transpose(psb[b][:, q * 128:(q + 1) * 128], f32v(srcT[:, co:co + 128]), ident[:]),
                                     reads=[sname, "ident"], writes=[("ps", b)], inc=(q == n4 - 1))
                            yield
                            P.op("dve", lambda e: e.tensor_copy(out=dstK[:, c4:c4 + n4, :], in_=psb[b][:, 0:n4 * 128].rearrange("p (q c) -> p q c", c=128)),
                                 reads=[("ps", b)], writes=[dname])
                            rel(b)
                            yield

            def head_tasks(h):
                hb = h % 2
                qn, kn, Ktok, Vtok, zs = qn2[hb], kn2[hb], Ktok2[hb], Vtok2[hb], zs2[hb]
                qnn, knn, Ktn, Vtn, zsn = "qn%d" % hb, "kn%d" % hb, "Ktok%d" % hb, "Vtok%d" % hb, "zs%d" % hb
                hd = lambda d: d * 8 + h
                for d in range(2):
                    P.op("pool", lambda e: e.tensor_scalar(out=Sst[d][:], in0=ident[:], scalar1=0.0, scalar2=None, op0=ALU.mult), reads=["ident"], writes=["Sst%d" % d])
                odone = [False] * 16
                prep_done = [False] * NIT
                rec_done = [[False] * NIT for _ in range(2)]

                def units_of(it):
                    return [(0, fwd_order[2 * it]), (0, fwd_order[2 * it + 1]), (1, bwd_order[2 * it]), (1, bwd_order[2 * it + 1])]

                def prep(it):
                    T_ = SETS[it % NSETS]
                    while it >= NSETS and not (rec_done[0][it - NSETS] and rec_done[1][it - NSETS]):
                        yield
                    units = units_of(it)
                    Gs, DT, Erow, QdT, QKT, Kd, Nm, NmT, Qa, Qb, Wm = [T_[k] for k in ("Gs", "DT", "Erow", "QdT", "QKT", "Kd", "Nm", "NmT", "Qa", "Qb", "Wm")]
                    n_ = lambda k: T_[k + "_n"]
                    precast_some(2)
                    for u, (d, ch) in enumerate(units):
                        P.op("act", lambda e: e.activation(out=NmT[:, u, :], in_=ident[:], func=AF.Copy, scale=gam[:, ch, hd(d):hd(d) + 1]),
                             reads=["ident", "gam"], writes=[(n_("NmT"), u)])
                        P.op("act", lambda e: e.activation(out=r_(Kd[:, u, :]), in_=Ktok[:, ch, :], func=AF.Copy, scale=kdec[:, ch, hd(d):hd(d) + 1]),
                             reads=[Ktn, "kdec"], writes=[(n_("Kd"), u)])
                    yield
                    bG, bK, bQ = yield from take(3)
                    P.op("pe", lambda e: e.matmul(psb[bG][:], lhsT=ones[:], rhs=fl(NmT), start=True, stop=True), reads=["ones", n_("NmT")], writes=[("ps", bG)])
                    for u, (d, ch) in enumerate(units):
                        co = ch_off(ch)
                        P.op("pe", lambda e: e.matmul(psb[bK][:, u * 128:(u + 1) * 128], lhsT=kn[:, co:co + 128], rhs=kn[:, co:co + 128], start=True, stop=True, skip_group_check=True),
                             reads=[knn], writes=[("ps", bK)], inc=False)
                        P.op("pe", lambda e: e.matmul(psb[bQ][:, u * 128:(u + 1) * 128], lhsT=kn[:, co:co + 128], rhs=qn[:, co:co + 128], start=True, stop=True, skip_group_check=True),
                             reads=[knn, qnn], writes=[("ps", bQ)], inc=(u == 3))
                    yield
                    P.op("dve", lambda e: e.tensor_copy(out=r_(fl(Gs)), in_=psb[bG][:]), reads=[("ps", bG)], writes=[n_("Gs")])
                    for u, (d, ch) in enumerate(units):
                        P.op("dve", lambda e: e.scalar_tensor_tensor(out=r_(DT[:, u, :]), in0=psb[bG][:, u * 128:(u + 1) * 128], scalar=gam[:, ch, hd(d):hd(d) + 1], in1=negm[d][:], op0=ALU.subtract, op1=ALU.add),
                             reads=[("ps", bG), "gam", "negm%d" % d], writes=[(n_("DT"), u)])
                    rel(bG)
                    yield
                    P.op("act", lambda e: e.activation(out=r_(fl(Erow)), in_=fl(Gs), func=AF.Exp), reads=[n_("Gs")], writes=[n_("Erow")])
                    P.op("act", lambda e: e.activation(out=r_(fl(DT)), in_=fl(DT), func=AF.Exp), reads=[n_("DT")], writes=[n_("DT")])
                    for u, (d, ch) in enumerate(units):
                        co = ch_off(ch)
                        if ch >= 2:
                            P.op("pool", lambda e: e.tensor_tensor(out=r_(QdT[:, u, :]), in0=f32v(qn[:, co:co + 128]), in1=Erow[:, u, :], op=ALU.mult),
                                 reads=[qnn, n_("Erow")], writes=[(n_("QdT"), u)])
                    yield
                    P.op("dve", lambda e: e.tensor_tensor(out=r_(fl(QKT)), in0=psb[bQ][:], in1=fl(DT), op=ALU.mult), reads=[("ps", bQ), n_("DT")], writes=[n_("QKT")])
                    for u, (d, ch) in enumerate(units):
                        P.op("dve", lambda e: e.scalar_tensor_tensor(out=r_(Nm[:, u, :]), in0=psb[bK][:, u * 128:(u + 1) * 128], scalar=nbeta[:, ch, hd(d):hd(d) + 1], in1=DT[:, u, :], op0=ALU.mult, op1=ALU.mult),
                             reads=[("ps", bK), "nbeta", n_("DT")], writes=[(n_("Nm"), u)])
                    rel(bK, bQ)
                    yield
                    P.op("pool", lambda e: e.tensor_tensor(out=r_(Nm[:]), in0=Nm[:], in1=bcast4(offd), op=ALU.mult), reads=[n_("Nm"), "offd"], writes=[n_("Nm")])
                    yield
                    bT, = yield from take(1)
                    transpose4(bT, Nm, n_("Nm"))
                    P.op("pool", lambda e: e.tensor_tensor(out=r_(Qa[:]), in0=Nm[:], in1=bcast4(bd32), op=ALU.mult), reads=[n_("Nm"), "bd32"], writes=[n_("Qa")])
                    yield
                    P.op("dve", lambda e: e.tensor_copy(out=fl(NmT), in_=psb[bT][:]), reads=[("ps", bT)], writes=[n_("NmT")])
                    rel(bT)
                    P.op("pool", lambda e: e.tensor_tensor(out=r_(Wm[:]), in0=Qa[:], in1=bcast4(ident), op=ALU.add), reads=[n_("Qa"), "ident"], writes=[n_("Wm")])
                    yield
                    P.op("pool", lambda e: e.tensor_tensor(out=r_(Qb[:]), in0=NmT[:], in1=bcast4(bd32), op=ALU.mult), reads=[n_("NmT"), "bd32"], writes=[n_("Qb")])
                    yield
                    for lvl in range(4):
                        last = (lvl == 3)
                        bTq, bNq = yield from take(2)
                        mm4(bTq, Qa, n_("Qa"), Qb, n_("Qb"))
                        if not last:
                            mm4(bNq, Qb, n_("Qb"), Qa, n_("Qa"))
                        yield
                        P.op("dve", lambda e: e.tensor_copy(out=r_(fl(Qb)), in_=psb[bTq][:]), reads=[("ps", bTq)], writes=[n_("Qb")])
                        if not last:
                            P.op("dve", lambda e: e.tensor_copy(out=r_(fl(Qa)), in_=psb[bNq][:]), reads=[("ps", bNq)], writes=[n_("Qa")])
                        rel(bTq, bNq)
                        yield
                        bW, = yield from take(1)
                        mm4(bW, Qb, n_("Qb"), Wm, n_("Wm"))
                        yield
                        P.op("dve", lambda e: e.tensor_tensor(out=r_(fl(Wm)), in0=psb[bW][:], in1=fl(Wm), op=ALU.add), reads=[("ps", bW), n_("Wm")], writes=[n_("Wm")])
                        rel(bW)
                        yield
                    for om, omn in ((od64, "od64"), (od128, "od128")):
                        bt, bZ = yield from take(2)
                        transpose4(bt, Wm, n_("Wm"))
                        P.op("pool", lambda e: e.tensor_tensor(out=r_(Qb[:]), in0=NmT[:], in1=bcast4(om), op=ALU.mult), reads=[n_("NmT"), omn], writes=[n_("Qb")])
                        yield
                        mm4(bZ, Qb, n_("Qb"), Wm, n_("Wm"))
                        P.op("dve", lambda e: e.tensor_copy(out=r_(fl(T_["tmpWT"])), in_=psb[bt][:]), reads=[("ps", bt)], writes=[T_["tmpWT_n"]])
                        yield
                        P.op("dve", lambda e: e.tensor_copy(out=r_(fl(T_["tmpZ"])), in_=psb[bZ][:]), reads=[("ps", bZ)], writes=[T_["tmpZ_n"]])
                        rel(bt, bZ)
                        yield
                        bW, = yield from take(1)
                        mm4(bW, T_["tmpWT"], T_["tmpWT_n"], T_["tmpZ"], T_["tmpZ_n"])
                        yield
                        P.op("dve", lambda e: e.tensor_tensor(out=r_(fl(Wm)), in0=psb[bW][:], in1=fl(Wm), op=ALU.add), reads=[("ps", bW), n_("Wm")], writes=[n_("Wm")])
                        rel(bW)
                        yield
                    prep_done[it] = True

                def recur(d, it):
                    T_ = SETS[it % NSETS]
                    while not prep_done[it]:
                        yield
                    QdT, QKT, Kd, Wf = T_["QdT"], T_["QKT"], T_["Kd"], T_["Wm"]
                    n_ = lambda k: T_[k + "_n"]
                    units = units_of(it)
                    for u in (2 * d, 2 * d + 1):
                        ch = units[u][1]
                        co = ch_off(ch); Sd = Sst[d]; Sn = "Sst%d" % d; hdd = hd(d)
                        ri = d
                        b1, = yield from take(1)
                        P.op("pe", lambda e: e.matmul(psb[b1][:, 0:128], lhsT=kn[:, co:co + 128], rhs=Sd[:], start=True, stop=True), reads=[knn, Sn], writes=[("ps", b1)])
                        yield
                        P.op("dve", lambda e: e.scalar_tensor_tensor(out=Rp[ri][:], in0=psb[b1][:, 0:128], scalar=neg_eg[:, ch, hdd:hdd + 1], in1=Vtok[:, ch, :], op0=ALU.mult, op1=ALU.add),
                             reads=[("ps", b1), "neg_eg", Vtn], writes=["Rp%d" % ri])
                        rel(b1)
                        yield
                        b2, = yield from take(1)
                        P.op("pe", lambda e: e.matmul(psb[b2][:, 0:128], lhsT=r_(Wf[:, u, :]), rhs=Rp[ri][:], start=True, stop=True), reads=[n_("Wm"), "Rp%d" % ri], writes=[("ps", b2)])
                        yield
                        P.op("dve", lambda e: e.tensor_scalar(out=Vn[ri][:], in0=psb[b2][:, 0:128], scalar1=beta[:, ch, hdd:hdd + 1], scalar2=None, op0=ALU.mult),
                             reads=[("ps", b2), "beta"], writes=["Vn%d" % ri])
                        rel(b2)
                        yield
                        b5, b3 = yield from take(2)
                        P.op("pe", lambda e: e.matmul(psb[b5][:, 0:128], lhsT=r_(Kd[:, u, :]), rhs=Vn[ri][:], start=True, stop=True), reads=[(n_("Kd"), u), "Vn%d" % ri], writes=[("ps", b5)])
                        if ch >= 2:
                            lc = ch - 2
                            P.op("pe", lambda e: e.matmul(psb[b3][:, 0:128], lhsT=r_(QdT[:, u, :]), rhs=Sd[:], start=True, stop=False), reads=[(n_("QdT"), u), Sn], writes=[("ps", b3)], inc=False)
                            P.op("pe", lambda e: e.matmul(psb[b3][:, 0:128], lhsT=r_(QKT[:, u, :]), rhs=Vn[ri][:], start=False, stop=True), reads=[n_("QKT"), "Vn%d" % ri], writes=[("ps", b3)])
                        yield
                        P.op("dve", lambda e: e.scalar_tensor_tensor(out=Sd[:], in0=f32v(Sd[:]), scalar=cdl[:, ch, hdd:hdd + 1], in1=psb[b5][:, 0:128], op0=ALU.mult, op1=ALU.add),
                             reads=[("ps", b5), "cdl", Sn], writes=[Sn])
                        rel(b5)
                        if ch < 2:
                            rel(b3)
                        if ch >= 2:
                            if not odone[lc]:
                                odone[lc] = True
                                P.op("dve", lambda e: e.tensor_copy(out=oacc[:, lc, :], in_=psb[b3][:, 0:128]), reads=[("ps", b3)], writes=[("oacc", lc)])
                                rel(b3)
                            else:
                                o_ = ot[d]; on_ = "ot%d" % d; sg = stg[d]; sgn = "stg%d" % d
                                P.op("dve", lambda e: e.tensor_tensor(out=o_[:], in0=psb[b3][:, 0:128], in1=oacc[:, lc, :], op=ALU.add), reads=[("ps", b3), ("oacc", lc)], writes=[on_])
                                rel(b3)
                                yield
                                P.op("act", lambda e: e.activation(out=junkg[d][:], in_=o_[:], func=AF.Square, accum_out=sg[:, 0:1]), reads=[on_], writes=["junkg%d" % d, sgn])
                                P.op("act", lambda e: e.activation(out=sg[:, 1:2], in_=sg[:, 0:1], func=AF.Sqrt, scale=1.0 / 128, bias=EPS), reads=[sgn], writes=[sgn])
                                yield
                                P.op("dve", lambda e: e.reciprocal(out=sg[:, 2:3], in_=sg[:, 1:2]), reads=[sgn], writes=[sgn])
                                yield
                                P.op("act", lambda e: e.activation(out=o_[:], in_=o_[:], func=AF.Copy, scale=sg[:, 2:3]), reads=[on_, sgn], writes=[on_])
                                yield
                                b4, = yield from take(1)
                                P.op("pe", lambda e: e.transpose(psb[b4][:, 0:128], o_[:], ident[:]), reads=[on_, "ident"], writes=[("ps", b4)])
                                yield
                                P.op("dve", lambda e: e.scalar_tensor_tensor(out=mixg[:, lc * 128:(lc + 1) * 128], in0=psb[b4][:, 0:128], scalar=pv(s_gnw), in1=zs[:, lc * 128:(lc + 1) * 128], op0=ALU.mult, op1=ALU.mult),
                                     reads=[("ps", b4), zsn] + rpar, writes=[(mgn, lc)])
                                rel(b4)
                        yield
                    rec_done[d][it] = True

                def chain(fn, *a):
                    for it in range(NIT):
                        yield from fn(*a, it)

                def prep_lane(l):
                    for it in range(l, NIT, NSETS):
                        yield from prep(it)

                return [prep_lane(l) for l in range(NSETS)] + [chain(recur, 0), chain(recur, 1)]

            run_tasks([headpre(0)])
            for h in range(NH):
                tasks = head_tasks(h)
                if h + 1 < NH:
                    tasks.append(headpre(h + 1))
                run_tasks(tasks)
                assert len(freeb) == 8
                load(mix_scr[8 + h], mixg[:], reads=[mgn], name=("mix_scr", 8 + h))
            precast_some(1000)
          P.barrier()

        if "C" in phases:
          with ExitStack() as sc:
            wo = sbt(sc, "wo", [128, 16, D], BF16)
            GM_row = make_row(sc, 4, "GM_row")
            wov = wout_d.rearrange("(k p) c -> p k c", p=128)
            for k4 in range(0, 16, 4):
                P.dma("pool", (lambda k4: lambda e: e.dma_start(out=wo[:, k4:k4 + 4, :], in_=wov[:, k4:k4 + 4, :]))(k4), writes=[("wo", k4, k4 + 4)])
            mt = [sbt(sc, "mt%d" % i, [128, 16, 512], BF16) for i in range(2)]
            xc = [sbt(sc, "xc%d" % i, [128, D]) for i in range(2)]
            x1t = [sbt(sc, "x1t%d" % i, [128, D]) for i in range(2)]
            h2t = [sbt(sc, "h2t%d" % i, [128, 16, 128], BF16) for i in range(2)]
            junkc = sbt(sc, "junkc", [128, D]); stc2 = [sbt(sc, "stc%d" % i, [128, 16]) for i in range(2)]
            mixv = mix_scr.rearrange("k p t -> p k t")
            h2v = h2_scr.rearrange("k p t -> p k t")
            def stageA(tt):
                g = tt // 4
                m = mt[g % 2]; mn = "mt%d" % (g % 2)
                if tt % 4 == 0:
                    load(m[:], mixv[:, :, g * 512:(g + 1) * 512], reads=["mix_scr"], name=mn)
                i = tt % 2
                stc = stc2[i]; stn = "stc%d" % i
                load(xc[i][:], x_d[tt * 128:(tt + 1) * 128, :], name="xc%d" % i)
                bs = []
                for cg in range(4):
                    b = bank(); bs.append(b)
                    for k in range(16):
                        P.op("pe", lambda e: e.matmul(psb[b][:], lhsT=m[:, k, (tt % 4) * 128:(tt % 4 + 1) * 128], rhs=wo[:, k, cg * 512:(cg + 1) * 512], start=(k == 0), stop=(k == 15)),
                             reads=[mn, ("wo", k)], writes=[("ps", b)], inc=(k == 15))
                for cg in range(4):
                    b = bs[cg]
                    P.op("act", lambda e: e.activation(out=junkc[:, cg * 512:(cg + 1) * 512], in_=psb[b][:], func=AF.Square, accum_out=stc[:, cg:cg + 1]),
                         reads=[("ps", b)], writes=[("junkc", cg), (stn, cg)])
                P.op("dve", lambda e: e.tensor_reduce(out=stc[:, 4:5], in_=stc[:, 0:4], axis=mybir.AxisListType.X, op=ALU.add), reads=[stn], writes=[stn])
                P.op("act", lambda e: e.activation(out=stc[:, 5:6], in_=stc[:, 4:5], func=AF.Sqrt, scale=1.0 / D, bias=EPS), reads=[stn], writes=[stn])
                P.op("dve", lambda e: e.reciprocal(out=stc[:, 6:7], in_=stc[:, 5:6]), reads=[stn], writes=[stn])
                x1 = x1t[i]; x1n = "x1t%d" % i
                for cg in range(4):
                    b = bs[cg]
                    cs = slice(cg * 512, (cg + 1) * 512)
                    P.op("dve", lambda e: e.scalar_tensor_tensor(out=x1[:, cs], in0=psb[b][:], scalar=stc[:, 6:7], in1=GM_row[:, cs], op0=ALU.mult, op1=ALU.mult),
                         reads=[("ps", b), stn, "GM_row"], writes=[(x1n, cg)])
                    P.op("pool", lambda e: e.tensor_tensor(out=x1[:, cs], in0=x1[:, cs], in1=xc[i][:, cs], op=ALU.add), reads=[(x1n, cg), "xc%d" % i], writes=[(x1n, cg)])
                load(x1_scr[tt * 128:(tt + 1) * 128, :], x1[:], q="pool", reads=[x1n], name=("x1_scr", tt))
                P.op("act", lambda e: e.activation(out=junkc[:], in_=x1[:], func=AF.Square, accum_out=stc[:, 8:9]), reads=[x1n], writes=["junkc", stn])
                P.op("act", lambda e: e.activation(out=stc[:, 9:10], in_=stc[:, 8:9], func=AF.Sqrt, scale=1.0 / D, bias=EPS), reads=[stn], writes=[stn])
                P.op("dve", lambda e: e.reciprocal(out=stc[:, 10:11], in_=stc[:, 9:10]), reads=[stn], writes=[stn])
                P.op("act", lambda e: e.activation(out=xc[i][:], in_=x1[:], func=AF.Copy, scale=stc[:, 10:11]), reads=[x1n, stn], writes=["xc%d" % i])

            def stageB(tt):
                i = tt % 2
                xn = xc[i]; xnn = "xc%d" % i
                ht = h2t[i]; htn = "h2t%d" % i
                for g4 in range(4):
                    b = bank()
                    for q in range(4):
                        k = g4 * 4 + q
                        P.op("pe", lambda e: e.transpose(psb[b][:, q * 128:(q + 1) * 128], xn[:, k * 128:(k + 1) * 128], ident[:]), reads=[xnn, "ident"], writes=[("ps", b)], inc=(q == 3))
                    for q in range(4):
                        k = g4 * 4 + q
                        P.op("dve", lambda e: e.tensor_scalar(out=ht[:, k, :], in0=psb[b][:, q * 128:(q + 1) * 128], scalar1=vecs[:, 5, k:k + 1], scalar2=vecs[:, 6, k:k + 1], op0=ALU.mult, op1=ALU.add),
                             reads=[("ps", b), "vecs"], writes=[(htn, k)])
                load(h2v[:, :, tt * 128:(tt + 1) * 128], ht[:], q="pool", reads=[htn], name=("h2_scr", tt))

            stageA(0)
            for tt in range(1, 16):
                stageA(tt)
                stageB(tt - 1)
            stageB(15)
          P.barrier()

        out_toks = []
        if "D" in phases:
          with ExitStack() as sd:
            h2 = sbt(sd, "h2", [128, 16, 512], BF16)
            GF_row = make_row(sd, 7, "GF_row")
            f1 = sbt(sd, "f1", [128, 64, 512], BF16)
            w1t = [sbt(sd, "w1t%d" % i, [128, 16, 128], BF16) for i in range(3)]
            w2t = [sbt(sd, "w2t%d" % i, [128, 64, 128], BF16) for i in range(2)]
            rl = [sbt(sd, "rl%d" % i, [128, 512]) for i in range(2)]
            y2b = [sbt(sd, "y2b%d" % i, [128, 512]) for i in range(2)]
            y2 = sbt(sd, "y2", [128, 4, D])
            x1d = sbt(sd, "x1d", [128, D]); junkd = sbt(sd, "junkd", [128, D], BF16); std = sbt(sd, "std", [128, 8])
            h2v = h2_scr.rearrange("k p t -> p k t")
            w1v = w1_d.rearrange("(k p) (o c) -> o p k c", p=128, c=128)
            w2v = w2_d.rearrange("(k p) (o c) -> o p k c", p=128, c=128)
            def epi(T, q):
                tt = T * 4 + q
                load(x1d[:], x1_scr[tt * 128:(tt + 1) * 128, :], q="pool", reads=[("x1_scr", tt)], name="x1d")
                P.op("act", lambda e: e.activation(out=junkd[:], in_=y2[:, q, :], func=AF.Square, accum_out=std[:, 0:1]), reads=["y2"], writes=["junkd", "std"])
                P.op("act", lambda e: e.activation(out=std[:, 1:2], in_=std[:, 0:1], func=AF.Sqrt, scale=1.0 / D, bias=EPS), reads=["std"], writes=["std"])
                P.op("dve", lambda e: e.reciprocal(out=std[:, 2:3], in_=std[:, 1:2]), reads=["std"], writes=["std"])
                P.op("dve", lambda e: e.scalar_tensor_tensor(out=y2[:, q, :], in0=y2[:, q, :], scalar=std[:, 2:3], in1=GF_row[:], op0=ALU.mult, op1=ALU.mult),
                     reads=["y2", "std", "GF_row"], writes=["y2"])
                P.op("dve", lambda e: e.tensor_tensor(out=x1d[:], in0=x1d[:], in1=y2[:, q, :], op=ALU.add), reads=["x1d", "y2"], writes=["x1d"])
                out_toks.append(load(out_d[tt * 128:(tt + 1) * 128, :], x1d[:], q="pool", reads=["x1d"], name=("out", tt)))

            def ff1(T):
                for o in range(64):
                    if T > 0 and o in (6, 12, 18, 24):
                        epi(T - 1, (o // 6) - 1)
                    wt = w1t[o % 3]; wtn = "w1t%d" % (o % 3)
                    load(wt[:], w1b[o], reads=[("w1b", o)], name=wtn)
                    b = bank()
                    for k in range(16):
                        P.op("pe", lambda e: e.matmul(psb[b][:], lhsT=wt[:, k, :], rhs=h2[:, k, :], start=(k == 0), stop=(k == 15)), reads=[wtn, "h2"], writes=[("ps", b)], inc=(k == 15))
                    r = rl[o % 2]; rn = "rl%d" % (o % 2)
                    P.op("act", lambda e: e.activation(out=r[:], in_=psb[b][:], func=AF.Relu), reads=[("ps", b)], writes=[rn])
                    P.op("dve", lambda e: e.tensor_tensor(out=f1[:, o, :], in0=r[:], in1=r[:], op=ALU.mult), reads=[rn], writes=[("f1", o)])

            def ff2(T):
                for o in range(16):
                    wt = w2t[o % 2]; wtn = "w2t%d" % (o % 2)
                    for kh in range(4):
                        load(wt[:, kh * 16:(kh + 1) * 16, :], w2b[o][:, kh * 16:(kh + 1) * 16, :], reads=[("w2b", o * 4 + kh)], name=(wtn, kh))
                    b = bank()
                    for k in range(64):
                        P.op("pe", lambda e: e.matmul(psb[b][:], lhsT=wt[:, k, :], rhs=f1[:, k, :], start=(k == 0), stop=(k == 63)), reads=[(wtn, k // 16), ("f1", k)], writes=[("ps", b)], inc=(k == 63))
                    yb = y2b[o % 2]; ybn = "y2b%d" % (o % 2)
                    P.op("act", lambda e: e.copy(out=yb[:], in_=psb[b][:]), reads=[("ps", b)], writes=[ybn])
                    b2 = bank()
                    for q in range(4):
                        P.op("pe", lambda e: e.transpose(psb[b2][:, q * 128:(q + 1) * 128], yb[:, q * 128:(q + 1) * 128], ident[:]), reads=[ybn, "ident"], writes=[("ps", b2)], inc=(q == 3))
                    P.op("dve", lambda e: e.tensor_copy(out=y2[:, :, o * 128:(o + 1) * 128], in_=psb[b2][:].rearrange("p (q c) -> p q c", c=128)), reads=[("ps", b2)], writes=[("y2", o)])

            load(h2[:], h2v[:, :, 0:512], reads=["h2_scr"], name="h2")
            for T in range(4):
                ff1(T)
                if T < 3:
                    load(h2[:], h2v[:, :, (T + 1) * 512:(T + 2) * 512], reads=["h2_scr"], name="h2")
                ff2(T)
            for q in range(4):
                epi(3, q)
        if not out_toks:
            zt = sbt(top, "zt", [128, D])
            P.op("pool", lambda e: e.memset(zt[:], 0.0), writes=["zt"])
            for tt in range(16):
                out_toks.append(load(out_d[tt * 128:(tt + 1) * 128, :], zt[:], reads=["zt"]))
        P.barrier()
        P.final_wait("sp", out_toks)
        global LAST_PROG
        LAST_PROG = P
        with nc.Block() as block:
            P.emit(block)
    return nc


def host_layout(inp, b):
    f = lambda a: np.ascontiguousarray(a, dtype=np.float32)
    pk = lambda v: f(v.reshape(-1, 128).T)
    cc = np.stack([pk(inp["c"][b]), pk(inp["c_ctx"])], axis=2).reshape(128, 32)
    nw = inp["norm_w"][0]
    nwT = np.stack([pk(nw[i]) for i in range(4)], axis=1).reshape(128, 64)
    lcw = inp["lru_conv_w"][0].reshape(4, 8, 128).transpose(2, 1, 0).reshape(128, 32)
    lcb = inp["lru_conv_b"][0].reshape(8, 128).T
    lgw = inp["lru_gate_w"][0].transpose(3, 0, 1, 2, 4).reshape(128, 4096)
    lgb = inp["lru_gate_b"][0].reshape(2, 2, 8, 128).transpose(3, 0, 1, 2).reshape(128, 32)
    llam = inp["lru_lambda"][0].reshape(2, 8, 128).transpose(2, 0, 1).reshape(128, 16)
    gcw = inp["gdn_conv_w"][0].reshape(4, 3, 8, 128).transpose(3, 1, 2, 0).reshape(128, 96)
    galog = np.broadcast_to(inp["gdn_a_log"][0].reshape(1, 16), (128, 16))
    gdtb = np.broadcast_to(inp["gdn_dt_bias"][0].reshape(1, 16), (128, 16))
    return {
        "x": f(inp["x"][b]), "ctx": f(inp["ctx"][b]), "cc": f(cc),
        "w_mod": f(inp["w_mod"][0]), "b_modT": pk(inp["b_mod"][0]), "nwT": f(nwT),
        "w_in": f(inp["w_in"][0]), "lcw": f(lcw), "lcb": f(lcb), "lgw": f(lgw), "lgb": f(lgb), "llam": f(llam),
        "gcw": f(gcw), "galog": f(galog), "gdtb": f(gdtb), "gnw": f(inp["gdn_norm_w"][0].reshape(128, 1)),
        "w_out": f(inp["w_out"][0]), "w_ff1": f(inp["w_ff1"][0]), "w_ff2": f(inp["w_ff2"][0]),
    }


def kernel(**inputs):
    inp = {k: np.asarray(v) for k, v in inputs.items()}
    nc = build()
    in_maps = [host_layout(inp, b) for b in range(8)]
    res = run_bass_kernel_spmd(nc, in_maps, core_ids=list(range(8)))
    return np.stack([np.asarray(r["out"], dtype=np.float32) for r in res.results], axis=0)
```

```python
from contextlib import ExitStack
import numpy as np
import concourse.bass as bass
import concourse.mybir as mybir
from concourse.alu_op_type import AluOpType as ALU
from concourse.bass_utils import run_bass_kernel_spmd

F32 = mybir.dt.float32
F32R = mybir.dt.float32r
BF16 = mybir.dt.bfloat16
AF = mybir.ActivationFunctionType

ENGS = ("pe", "dve", "act", "pool", "sp")
INF = 1 << 60
EPS = 1e-6


class _Rec:
    def __init__(self):
        self.call = None

    def __getattr__(self, name):
        def f(*a, **k):
            self.call = (name, a, k)
            return self
        return f


def _capture(fn):
    r = _Rec()
    fn(r)
    assert r.call is not None
    return r.call


class Prog:
    def __init__(self, nc, stack, n_dma_sems=30):
        self.nc = nc
        self.ops = {e: [] for e in ENGS}
        self.cnt = {e: 0 for e in ENGS}
        self.sem = {e: stack.enter_context(nc.semaphore("s_" + e)) for e in ENGS}
        self.dsem = [stack.enter_context(nc.semaphore("d%d" % i)) for i in range(n_dma_sems)]
        self.dcnt = [0] * n_dma_sems
        self.dnext = 0
        self.dnext_q = {}
        self.waited = {e: {} for e in ENGS}
        self.res = {}
        self.nops = 0
        self.psrd = {}

    def _deps(self, reads, writes):
        deps = []
        for (name, lo, hi) in reads:
            st = self.res.setdefault(name, {"w": [], "r": []})
            for (a, b, tok) in st["w"]:
                if a < hi and lo < b:
                    deps.append(tok)
        for (name, lo, hi) in writes:
            st = self.res.setdefault(name, {"w": [], "r": []})
            for (a, b, tok) in st["w"]:
                if a < hi and lo < b:
                    deps.append(tok)
            for (a, b, tok) in st["r"]:
                if a < hi and lo < b:
                    deps.append(tok)
        return deps

    def _record(self, reads, writes, tok):
        for (name, lo, hi) in writes:
            st = self.res[name]
            st["w"] = [(a, b, t) for (a, b, t) in st["w"] if not (lo <= a and b <= hi)]
            st["r"] = [(a, b, t) for (a, b, t) in st["r"] if not (lo <= a and b <= hi)]
            st["w"].append((lo, hi, tok))
        for (name, lo, hi) in reads:
            st = self.res[name]
            st["r"] = [(a, b, t) for (a, b, t) in st["r"]
                       if not (t[0] == tok[0] and lo <= a and b <= hi)]
            st["r"].append((lo, hi, tok))

    @staticmethod
    def _norm(rs):
        out = []
        for r in rs:
            if isinstance(r, str):
                out.append((r, 0, INF))
            elif len(r) == 2:
                out.append((r[0], r[1], r[1] + 1))
            else:
                out.append(tuple(r))
        return out

    def _waits(self, eng, deps):
        ws = {}
        for (key, val, deng) in deps:
            if deng == eng and eng == "pe":
                continue
            if self.waited[eng].get(key, 0) >= val:
                continue
            ws[key] = max(ws.get(key, 0), val)
        for k, v in ws.items():
            self.waited[eng][k] = v
        return list(ws.items())

    def op(self, eng, fn, reads=(), writes=(), inc=True):
        reads = self._norm(reads)
        writes = self._norm(writes)
        deps = self._deps(reads, writes)
        psread = eng in ("dve", "act") and any(r[0] == "ps" for r in reads)
        if psread:
            other = "act" if eng == "dve" else "dve"
            if other in self.psrd:
                deps.append(self.psrd[other])
        waits = self._waits(eng, deps)
        idx = self.cnt[eng] + 1
        if inc:
            self.cnt[eng] = idx
        tok = (("e", eng), idx, eng)
        if psread:
            self.psrd[eng] = tok
        self._record(reads, writes, tok)
        self.ops[eng].append((waits, _capture(fn), ("e", eng) if inc else None, 1))
        self.nops += 1
        return tok

    def dma(self, eng, fn, reads=(), writes=()):
        reads = self._norm(reads)
        writes = self._norm(writes)
        deps = self._deps(reads, writes)
        nd = len(self.dsem)
        lo, hi = (0, (nd * 3) // 5) if eng == "sp" else ((nd * 3) // 5, nd)
        cur = self.dnext_q.get(eng, lo)
        j = cur
        self.dnext_q[eng] = lo + (cur + 1 - lo) % (hi - lo)
        if self.dcnt[j] > 0:
            deps.append((("d", j), 16 * self.dcnt[j], "dma"))
        waits = self._waits(eng, deps)
        self.dcnt[j] += 1
        tok = (("d", j), 16 * self.dcnt[j], "dma")
        self._record(reads, writes, tok)
        self.ops[eng].append((waits, _capture(fn), ("d", j), 16))
        self.nops += 1
        return tok

    def barrier(self):
        deps = [(("e", e), self.cnt[e], "x") for e in ENGS if self.cnt[e] > 0]
        deps += [(("d", j), 16 * c, "dma") for j, c in enumerate(self.dcnt) if c > 0]
        for e in ENGS:
            waits = self._waits(e, [d for d in deps if d[0] != ("e", e)])
            if waits:
                self.ops[e].append((waits, None, None, 0))
        self.res = {}

    def final_wait(self, eng, toks):
        waits = self._waits(eng, list(toks))
        self.ops[eng].append((waits, None, None, 0))

    def _semobj(self, key):
        return self.sem[key[1]] if key[0] == "e" else self.dsem[key[1]]

    def emit(self, block):
        def mk(ename):
            def body(e):
                for (waits, fn, inckey, incv) in self.ops[ename]:
                    for (k, v) in waits:
                        e.wait_ge(self._semobj(k), v)
                    if fn is None:
                        continue
                    name, a, k = fn
                    ins = getattr(e, name)(*a, **k)
                    if inckey is not None:
                        ins.then_inc(self._semobj(inckey), incv)
            return body
        if self.ops["sp"]:
            block.sync(mk("sp"))
        if self.ops["pe"]:
            block.tensor(mk("pe"))
        if self.ops["dve"]:
            block.vector(mk("dve"))
        if self.ops["act"]:
            block.scalar(mk("act"))
        if self.ops["pool"]:
            block.gpsimd(mk("pool"))


D = 2048
S = 2048
NCTX = 256
NALL = NCTX + S
DIN = 6176
DFF = 8192
NCH = NALL // 128
XW = 2310
TW = 2307
LT = 259
NEGBIG = -30000.0


def ch_off(ch):
    return ch * 128 if ch < 2 else LT + (ch - 2) * 128


def build(dbg=False, phases=("A", "B1", "B2", "C", "D")):
    nc = bass.Bass("TRN2", target_bir_lowering=False)
    din = lambda n, s, d=F32: nc.dram_tensor(n, s, d, kind="ExternalInput").ap()
    x_d = din("x", [S, D]); ctx_d = din("ctx", [NCTX, D]); cc_d = din("cc", [128, 32])
    wmod_d = din("w_mod", [D, 6 * D]); bmodT_d = din("b_modT", [128, 96]); nwT_d = din("nwT", [128, 64])
    win_d = din("w_in", [D, DIN])
    lcw_d = din("lcw", [128, 32]); lcb_d = din("lcb", [128, 8]); lgw_d = din("lgw", [128, 4096])
    lgb_d = din("lgb", [128, 32]); llam_d = din("llam", [128, 16])
    gcw_d = din("gcw", [128, 96]); galog_d = din("galog", [128, 16]); gdtb_d = din("gdtb", [128, 16])
    gnw_d = din("gnw", [128, 1])
    wout_d = din("w_out", [D, D]); w1_d = din("w_ff1", [D, DFF]); w2_d = din("w_ff2", [DFF, D])
    out_d = nc.dram_tensor("out", [S, D], F32, kind="ExternalOutput").ap()
    skind = "ExternalOutput" if dbg else "Internal"
    mix_scr = nc.dram_tensor("mix_scr", [16, 128, S], BF16, kind=skind).ap()
    h2_scr = nc.dram_tensor("h2_scr", [16, 128, S], BF16, kind=skind).ap()
    x1_scr = nc.dram_tensor("x1_scr", [S, D], F32, kind=skind).ap()
    w1b = nc.dram_tensor("w1b", [64, 128, 16, 128], BF16, kind="Internal").ap()
    w2b = nc.dram_tensor("w2b", [16, 128, 64, 128], BF16, kind="Internal").ap()
    if dbg:
        hT_dbg = nc.dram_tensor("hT_dbg", [128, 16, NALL], BF16, kind="ExternalOutput").ap()
        modT_dbg = nc.dram_tensor("modT_dbg", [128, 192], F32, kind="ExternalOutput").ap()

    with ExitStack() as top:
        P = Prog(nc, top)
        psb = [top.enter_context(nc.psum_tensor("ps%d" % i, [128, 512], F32)) for i in range(8)]
        pst = {"i": 0}

        reserved = set()

        def bank():
            while True:
                i = pst["i"]; pst["i"] = (i + 1) % 8
                if i not in reserved:
                    return i

        sbt = lambda st, n, s, d=F32: st.enter_context(nc.sbuf_tensor("sb_" + n, s, d))
        ident = sbt(top, "ident", [128, 128]); ones = sbt(top, "ones", [128, 128])
        onesb = sbt(top, "onesb", [128, 128], BF16)
        par = sbt(top, "par", [128, 64 + 96 + 32 + 8 + 32 + 16 + 96 + 16 + 16 + 1])
        o_ = [0]
        def psl(n):
            a = o_[0]; o_[0] += n
            return (a, a + n)
        s_nw, s_bm, s_lcw, s_lcb, s_lgb, s_llam, s_gcw, s_gal, s_gdt, s_gnw = [psl(n) for n in (64, 96, 32, 8, 32, 16, 96, 16, 16, 1)]
        pv = lambda s: par[:, s[0]:s[1]]
        modT = sbt(top, "modT", [128, 192])
        vecs = sbt(top, "vecs", [128, 8, 16])
        cneg = sbt(top, "cneg", [128, 16])
        sccb = sbt(top, "sccb", [128, 32], BF16)
        C3 = [128, NCH, 16]
        beta = sbt(top, "beta", C3); nbeta = sbt(top, "nbeta", C3); gg = sbt(top, "gg", C3); gam = sbt(top, "gam", C3)
        ngam = sbt(top, "ngam", C3); glast = sbt(top, "glast", C3); cdl = sbt(top, "cdl", C3); neg_eg = sbt(top, "neg_eg", C3); kdec = sbt(top, "kdec", C3)
        nega = sbt(top, "nega", [128, 16])
        trif = sbt(top, "trif", [128, 128]); trib = sbt(top, "trib", [128, 128])
        negm = [sbt(top, "negm%d" % d, [128, 128]) for d in range(2)]
        offd = sbt(top, "offd", [128, 128]); bd32 = sbt(top, "bd32", [128, 128]); od64 = sbt(top, "od64", [128, 128]); od128 = sbt(top, "od128", [128, 128])
        s_h = ExitStack()
        hT = sbt(s_h, "hT", [128, 16, NALL], BF16)
        p_scr = nc.dram_tensor("p_scr", [8, 3, 128, NALL], F32, kind="Internal").ap()
        zs_scr = nc.dram_tensor("zs_scr", [8, 128, S], BF16, kind="Internal").ap()
        xs_scr = nc.dram_tensor("xs_scr", [8, 128, NALL], F32, kind="Internal").ap()
        gy_scr = nc.dram_tensor("gy_scr", [8, 128, S], BF16, kind="Internal").ap()
        oacc_scr = nc.dram_tensor("oacc_scr", [16, 128, 128], F32, kind="Internal").ap()

        def make_row(st, vi, dn):
            dst = sbt(st, dn, [128, D]); dg = sbt(st, "dg_" + dn, [128, 512])
            for g4 in range(4):
                for q in range(4):
                    k = g4 * 4 + q
                    P.op("dve", lambda e: e.tensor_scalar(out=dg[:, q * 128:(q + 1) * 128], in0=ident[:], scalar1=vecs[:, vi, k:k + 1], scalar2=None, op0=ALU.mult),
                         reads=["vecs", "ident"], writes=[("dg", q)])
                b = bank()
                P.op("pe", lambda e: e.matmul(psb[b][:], lhsT=ones[:], rhs=dg[:], start=True, stop=True), reads=["ones", "dg"], writes=[("ps", b)])
                P.op("act", lambda e: e.copy(out=dst[:, g4 * 512:(g4 + 1) * 512], in_=psb[b][:]), reads=[("ps", b)], writes=[(dn, g4)])
            return dst

        def load(dst, src, q="sp", name=None, reads=()):
            return P.dma(q, lambda e: e.dma_start(out=dst, in_=src), reads=reads, writes=[name] if name else [])

        P.op("pool", lambda e: e.memset(ones[:], 1.0), writes=["ones"])
        P.op("pool", lambda e: e.memset(onesb[:], 1.0), writes=["onesb"])
        P.op("pool", lambda e: e.memset(ident[:], 1.0), writes=["ident"])
        P.op("pool", lambda e: e.affine_select(out=ident[:], in_=ident[:], pattern=[[-1, 128]], compare_op=ALU.is_equal,
                                               fill=0.0, base=0, channel_multiplier=1), reads=["ident"], writes=["ident"])
        for (sl, src) in ((s_nw, nwT_d), (s_bm, bmodT_d), (s_lcw, lcw_d), (s_lcb, lcb_d), (s_lgb, lgb_d), (s_llam, llam_d),
                          (s_gcw, gcw_d), (s_gal, galog_d), (s_gdt, gdtb_d), (s_gnw, gnw_d)):
            load(pv(sl), src, name=("par", sl[0], sl[1]))

        with ExitStack() as sa:
            cc = sbt(sa, "cc", [128, 32]); scc = sbt(sa, "scc", [128, 32])
            wm = [sbt(sa, "wm%d" % i, [128, 4096], BF16) for i in range(3)]
            load(cc[:], cc_d, name="cc")
            P.op("act", lambda e: e.activation(out=scc[:], in_=cc[:], func=AF.Silu), reads=["cc"], writes=["scc"])
            P.op("dve", lambda e: e.tensor_copy(out=sccb[:], in_=scc[:]), reads=["scc"], writes=["sccb"])
            mb = bank()
            first = True
            for k in range(16):
                i = k % 3
                P.dma("pool", lambda e: e.dma_start(out=wm[i][:], in_=wmod_d[k * 128:(k + 1) * 128, 0:4096]), writes=["wm%d" % i])
                for j in range(32):
                    P.op("pe", lambda e: e.matmul(psb[mb][:, j * 2:j * 2 + 2], lhsT=wm[i][:, j * 128:(j + 1) * 128], rhs=sccb[:, k * 2:k * 2 + 2],
                                                  start=first, stop=(k == 15), skip_group_check=True),
                         reads=["wm%d" % i, "sccb"], writes=[("ps", mb)], inc=(j == 31))
                    first = False
            P.op("dve", lambda e: e.tensor_tensor(out=modT[:, 0:64].rearrange("p (j r) -> p j r", r=2),
                                                  in0=psb[mb][:, 0:64].rearrange("p (j r) -> p j r", r=2),
                                                  in1=pv(s_bm)[:, 0:32].unsqueeze(2).to_broadcast([128, 32, 2]), op=ALU.add),
                 reads=[("ps", mb), ("par", s_bm[0], s_bm[1])], writes=[("modT", 0, 64)])
            mv = modT[:].rearrange("p (j r) -> p j r", r=2)
            nw = pv(s_nw).rearrange("p (w k) -> p w k", w=4)
            rp = [("par", s_nw[0], s_nw[1]), "modT"]
            def scl(dst, sc, w):
                P.op("dve", lambda e: e.scalar_tensor_tensor(out=dst, in0=sc, scalar=1.0, in1=w, op0=ALU.add, op1=ALU.mult),
                     reads=rp, writes=["vecs"])
            scl(vecs[:, 0, :], mv[:, 16:32, 0], nw[:, 0, :])
            P.op("dve", lambda e: e.tensor_copy(out=vecs[:, 1, :], in_=mv[:, 0:16, 0]), reads=rp, writes=["vecs"])
            scl(vecs[:, 2, :], mv[:, 16:32, 1], nw[:, 0, :])
            P.op("dve", lambda e: e.tensor_copy(out=vecs[:, 3, :], in_=mv[:, 0:16, 1]), reads=rp, writes=["vecs"])
            P.op("act", lambda e: e.activation(out=cneg[:], in_=pv(s_llam), func=AF.Exp, scale=-1.0), reads=[("par", s_llam[0], s_llam[1])], writes=["cneg"])
            P.op("act", lambda e: e.activation(out=cneg[:], in_=cneg[:], func=AF.Ln, bias=1.0), reads=["cneg"], writes=["cneg"])
            P.op("dve", lambda e: e.tensor_scalar(out=cneg[:], in0=cneg[:], scalar1=-8.0, scalar2=None, op0=ALU.mult), reads=["cneg"], writes=["cneg"])
            if dbg:
                load(modT_dbg, modT[:], reads=["modT"])

            xt = [sbt(sa, "xt%d" % i, [128, D]) for i in range(2)]
            junk = sbt(sa, "junkA", [128, D])
            st1 = [sbt(sa, "st1_%d" % i, [128, 4]) for i in range(2)]

            def norm_T(src_ap, tname, ti, scl_i, sh_i, dstT, dname, col0, stt, stn):
                t = xt[ti]
                P.op("act", lambda e: e.activation(out=junk[:], in_=t[:], func=AF.Square, accum_out=stt[:, 0:1]),
                     reads=[tname], writes=["junkA", stn])
                P.op("act", lambda e: e.activation(out=stt[:, 1:2], in_=stt[:, 0:1], func=AF.Sqrt, scale=1.0 / D, bias=EPS), reads=[stn], writes=[stn])
                P.op("dve", lambda e: e.reciprocal(out=stt[:, 2:3], in_=stt[:, 1:2]), reads=[stn], writes=[stn])
                P.op("act", lambda e: e.activation(out=t[:], in_=t[:], func=AF.Copy, scale=stt[:, 2:3]), reads=[stn, tname], writes=[tname])
                for g4 in range(4):
                    b = bank()
                    for q in range(4):
                        k = g4 * 4 + q
                        P.op("pe", (lambda b, q, k: lambda e: e.transpose(psb[b][:, q * 128:(q + 1) * 128], t[:, k * 128:(k + 1) * 128], ident[:]))(b, q, k),
                             reads=[tname, "ident"], writes=[("ps", b)], inc=(q == 3))
                    for q in range(4):
                        k = g4 * 4 + q
                        eng = "dve" if g4 % 2 == 0 else "act"
                        if eng == "dve":
                            fn = (lambda b, q, k: lambda e: e.tensor_scalar(out=dstT[:, k, col0:col0 + 128], in0=psb[b][:, q * 128:(q + 1) * 128],
                                                                            scalar1=vecs[:, scl_i, k:k + 1], scalar2=vecs[:, sh_i, k:k + 1],
                                                                            op0=ALU.mult, op1=ALU.add))(b, q, k)
                        else:
                            fn = (lambda b, q, k: lambda e: e.activation(out=dstT[:, k, col0:col0 + 128], in_=psb[b][:, q * 128:(q + 1) * 128],
                                                                         func=AF.Identity, scale=vecs[:, scl_i, k:k + 1], bias=vecs[:, sh_i, k:k + 1]))(b, q, k)
                        P.op(eng, fn, reads=[("ps", b), "vecs"], writes=[(dname, k * 100000 + col0, k * 100000 + col0 + 128)])

            import os as _os
            for ti in range(int(_os.environ.get("K_NTI", NCH))):
                src = ctx_d[ti * 128:(ti + 1) * 128, :] if ti < 2 else x_d[(ti - 2) * 128:(ti - 1) * 128, :]
                i = ti % 2
                load(xt[i][:], src, name="xt%d" % i)
                norm_T(src, "xt%d" % i, i, 2 if ti < 2 else 0, 3 if ti < 2 else 1, hT, "hT", ti * 128, st1[i], "st1_%d" % i)
            if dbg:
                load(hT_dbg, hT[:], reads=["hT"])
        P.barrier()

        hTr = lambda k, c0, c1: ("hT", k * 100000 + c0, k * 100000 + c1)
        hT_all = [("hT", 0, INF)]

        def precast():
            import os as _os
            if _os.environ.get("K_NOPRECAST"):
                return
            w1v = w1_d.rearrange("(k p) (o c) -> o p k c", p=128, c=128)
            for o in range(0, 64, 4):
                for oo in range(4):
                    P.dma("pool", (lambda o: lambda e: e.dma_start(out=w1b[o], in_=w1v[o]))(o + oo), writes=[("w1b", o + oo)])
            w2v = w2_d.rearrange("(k p) (o c) -> o p k c", p=128, c=128)
            for o in range(16):
                for kh in range(4):
                    P.dma("pool", (lambda o, kh: lambda e: e.dma_start(out=w2b[o][:, kh * 16:(kh + 1) * 16, :], in_=w2v[o][:, kh * 16:(kh + 1) * 16, :]))(o, kh),
                          writes=[("w2b", o * 4 + kh)])

        winv = win_d.rearrange("(k p) c -> p k c", p=128)

        def inproj(wt, wname, woff, tok_groups, consume):
            for gi, (c0, n) in enumerate(tok_groups):
                b = bank()
                for k in range(16):
                    P.op("pe", (lambda b, k, c0, n: lambda e: e.matmul(psb[b][:, 0:n], lhsT=wt[:, k, woff:woff + 128], rhs=hT[:, k, c0:c0 + n],
                                                                          start=(k == 0), stop=(k == 15)))(b, k, c0, n),
                         reads=[wname, hTr(k, c0, c0 + n)], writes=[("ps", b)], inc=(k == 15))
                consume(b, gi)

        TG_ALL = [(0, 256)] + [(256 + g * 512, 512) for g in range(4)]
        TG_LAT = [(256 + g * 512, 512) for g in range(4)]

        if "B2" in phases:
          with ExitStack() as sb2a:
            wgb = sbt(sb2a, "wgb", [128, 16, 32], BF16)
            def msk(t, tn, base_val, fill, pattern, cmp, base, cm, src=None):
                if src is None:
                    P.op("pool", lambda e: e.memset(t[:], base_val), writes=[tn])
                P.op("pool", lambda e: e.affine_select(out=t[:], in_=t[:], pattern=pattern, compare_op=cmp, fill=fill, base=base, channel_multiplier=cm),
                     reads=[tn], writes=[tn])
            msk(trif, "trif", 1.0, 0.0, [[1, 128]], ALU.is_ge, 0, -1)
            msk(trib, "trib", 1.0, 0.0, [[-1, 128]], ALU.is_ge, 0, 1)
            msk(negm[0], "negm0", 0.0, NEGBIG, [[1, 128]], ALU.is_ge, 0, -1)
            msk(negm[1], "negm1", 0.0, NEGBIG, [[-1, 128]], ALU.is_ge, 0, 1)
            msk(offd, "offd", 1.0, 0.0, [[-1, 128]], ALU.not_equal, 0, 1)
            for (t, tn, fn_) in ((bd32, "bd32", lambda pb, cb: 1.0 if pb == cb else 0.0),
                                 (od64, "od64", lambda pb, cb: 1.0 if (pb != cb and pb // 2 == cb // 2) else 0.0),
                                 (od128, "od128", lambda pb, cb: 1.0 if pb // 2 != cb // 2 else 0.0)):
                for pb in range(4):
                    for cb in range(4):
                        P.op("pool", (lambda t, pb, cb, v: lambda e: e.memset(t[pb * 32:(pb + 1) * 32, cb * 32:(cb + 1) * 32], v))(t, pb, cb, fn_(pb, cb)), writes=[tn])
            P.dma("pool", lambda e: e.dma_start(out=wgb[:], in_=winv[:, :, 6144:6176]), writes=["wgb"])
            gbank = [bank(), bank()]
            for ch in range(NCH):
                b = gbank[ch // 9]; co = (ch % 9) * 32
                for k in range(16):
                    P.op("pe", (lambda b, co, ch, k: lambda e: e.matmul(psb[b][:, co:co + 32], lhsT=hT[:, k, ch * 128:(ch + 1) * 128], rhs=wgb[:, k, :],
                                                                          start=(k == 0), stop=(k == 15), skip_group_check=True))(b, co, ch, k),
                         reads=["wgb"] + hT_all, writes=[("ps", b)], inc=(k == 15))
            P.op("act", lambda e: e.activation(out=nega[:], in_=pv(s_gal), func=AF.Exp), reads=[("par", 0, INF)], writes=["nega"])
            P.op("dve", lambda e: e.tensor_scalar(out=nega[:], in0=nega[:], scalar1=-1.0, scalar2=None, op0=ALU.mult), reads=["nega"], writes=["nega"])
            for half in range(2):
                b = gbank[half]
                pvw = psb[b][:, 0:288].rearrange("p (c f) -> p c f", f=32)
                cs = slice(half * 9, half * 9 + 9)
                P.op("act", (lambda pvw, cs: lambda e: e.activation(out=beta[:, cs, :], in_=pvw[:, :, 0:16], func=AF.Sigmoid))(pvw, cs), reads=[("ps", b)], writes=["beta"])
                P.op("dve", (lambda pvw, cs: lambda e: e.tensor_tensor(out=gg[:, cs, :], in0=pvw[:, :, 16:32], in1=pv(s_gdt).unsqueeze(1).to_broadcast([128, 9, 16]), op=ALU.add))(pvw, cs),
                     reads=[("ps", b), ("par", 0, INF)], writes=["gg"])
            P.op("act", lambda e: e.activation(out=gg[:], in_=gg[:], func=AF.Exp), reads=["gg"], writes=["gg"])
            P.op("act", lambda e: e.activation(out=gg[:], in_=gg[:], func=AF.Ln, bias=1.0), reads=["gg"], writes=["gg"])
            P.op("dve", lambda e: e.tensor_tensor(out=gg[:], in0=gg[:], in1=nega[:].unsqueeze(1).to_broadcast([128, NCH, 16]), op=ALU.mult), reads=["gg", "nega"], writes=["gg"])
            P.op("dve", lambda e: e.tensor_scalar(out=nbeta[:], in0=beta[:], scalar1=-1.0, scalar2=None, op0=ALU.mult), reads=["beta"], writes=["nbeta"])
            for d in range(2):
                b = bank()
                tri = trif if d == 0 else trib
                P.op("pe", (lambda b, tri, d: lambda e: e.matmul(psb[b][:, 0:NCH * 8].rearrange("p (c f) -> p c f", f=8), lhsT=tri[:], rhs=gg[:, :, d * 8:(d + 1) * 8], start=True, stop=True))(b, tri, d),
                     reads=["trif", "trib", "gg"], writes=[("ps", b)])
                P.op("act", (lambda b, d: lambda e: e.copy(out=gam[:, :, d * 8:(d + 1) * 8], in_=psb[b][:, 0:NCH * 8].rearrange("p (c f) -> p c f", f=8)))(b, d),
                     reads=[("ps", b)], writes=["gam"])
            b = bank()
            P.op("pe", (lambda b: lambda e: e.matmul(psb[b][:, 0:NCH * 16], lhsT=ones[:], rhs=gg[:].rearrange("p c f -> p (c f)"), start=True, stop=True))(b),
                 reads=["ones", "gg"], writes=[("ps", b)])
            P.op("act", (lambda b: lambda e: e.copy(out=glast[:].rearrange("p c f -> p (c f)"), in_=psb[b][:, 0:NCH * 16]))(b), reads=[("ps", b)], writes=["glast"])
            P.op("dve", lambda e: e.tensor_scalar(out=ngam[:], in0=gam[:], scalar1=-1.0, scalar2=None, op0=ALU.mult), reads=["gam"], writes=["ngam"])
            P.op("act", lambda e: e.activation(out=cdl[:], in_=glast[:], func=AF.Exp), reads=["glast"], writes=["cdl"])
            P.op("dve", lambda e: e.tensor_tensor(out=kdec[:], in0=glast[:], in1=gam[:], op=ALU.subtract), reads=["glast", "gam"], writes=["kdec"])
            P.op("act", lambda e: e.activation(out=kdec[:], in_=kdec[:], func=AF.Exp), reads=["kdec"], writes=["kdec"])
            P.op("act", lambda e: e.activation(out=neg_eg[:], in_=gam[:], func=AF.Exp), reads=["gam"], writes=["neg_eg"])
            P.op("dve", lambda e: e.tensor_scalar(out=neg_eg[:], in0=neg_eg[:], scalar1=-1.0, scalar2=None, op0=ALU.mult), reads=["neg_eg"], writes=["neg_eg"])

            wm2 = [sbt(sb2a, "wm2_%d" % i, [128, 4096], BF16) for i in range(3)]
            mb2 = bank(); reserved.add(mb2)
            mpieces = [(k, pc) for k in range(16) for pc in (1, 2)]
            mpi = [0]

            def mod_pieces(n):
                for _ in range(n):
                    if mpi[0] >= len(mpieces):
                        return
                    k, pc = mpieces[mpi[0]]
                    i = mpi[0] % 3
                    first = (mpi[0] == 0)
                    mpi[0] += 1
                    P.dma("pool", lambda e: e.dma_start(out=wm2[i][:], in_=wmod_d[k * 128:(k + 1) * 128, pc * 4096:(pc + 1) * 4096]), writes=["wm2_%d" % i])
                    for j in range(32):
                        jj = (pc - 1) * 32 + j
                        P.op("pe", lambda e: e.matmul(psb[mb2][:, jj * 2:jj * 2 + 2], lhsT=wm2[i][:, j * 128:(j + 1) * 128], rhs=sccb[:, k * 2:k * 2 + 2],
                                                      start=(first and j == 0), stop=(k == 15), skip_group_check=True),
                             reads=["wm2_%d" % i, "sccb"], writes=[("ps", mb2)], inc=(j == 31))

            wg = [sbt(sb2a, "wg%d" % i, [128, 16, 512], BF16) for i in range(2)]
            stgs = [sbt(sb2a, "stgs%d" % i, [128, 512]) for i in range(4)]
            zst = [sbt(sb2a, "zst%d" % i, [128, 512], BF16) for i in range(2)]
            sti = [0]
            for h in range(8):
                w = wg[h % 2]; wn = "wg%d" % (h % 2)
                for t4 in range(4):
                    c0 = 2048 + t4 * 1024 + h * 128
                    P.dma("pool", lambda e: e.dma_start(out=w[:, :, t4 * 128:(t4 + 1) * 128], in_=winv[:, :, c0:c0 + 128]), writes=[(wn, t4)])
                for t3 in range(3):
                    def cons_p(b, gi):
                        c0, n = TG_ALL[gi]
                        i = sti[0] % 4; sti[0] += 1
                        P.op("act", lambda e: e.copy(out=stgs[i][:, 0:n], in_=psb[b][:, 0:n]), reads=[("ps", b)], writes=["stgs%d" % i])
                        load(p_scr[h, t3][:, c0:c0 + n], stgs[i][:, 0:n], reads=["stgs%d" % i], name=("p_scr", h * 3 + t3))
                    inproj(w, (wn, t3), t3 * 128, TG_ALL, cons_p)

                def cons_zs(b, gi):
                    i = gi % 2
                    P.op("act", lambda e: e.activation(out=zst[i][:], in_=psb[b][:], func=AF.Silu), reads=[("ps", b)], writes=["zst%d" % i])
                    load(zs_scr[h][:, gi * 512:(gi + 1) * 512], zst[i][:], reads=["zst%d" % i], name=("zs_scr", h))
                inproj(w, (wn, 3), 384, TG_LAT, cons_zs)
                mod_pieces(2)
            xst = [sbt(sb2a, "xst%d" % i, [128, NALL]) for i in range(2)]
            gyst = [sbt(sb2a, "gyst%d" % i, [128, S], BF16) for i in range(2)]
            for h in range(8):
                w = wg[h % 2]; wn = "wg%d" % (h % 2)
                P.dma("pool", lambda e: e.dma_start(out=w[:, :, 0:128], in_=winv[:, :, h * 128:(h + 1) * 128]), writes=[(wn, 0)])
                P.dma("pool", lambda e: e.dma_start(out=w[:, :, 128:256], in_=winv[:, :, 1024 + h * 128:1024 + (h + 1) * 128]), writes=[(wn, 1)])
                xs = xst[h % 2]; xsn = "xst%d" % (h % 2)
                gs_ = gyst[h % 2]; gsn = "gyst%d" % (h % 2)

                def cons_x(b, gi):
                    if gi == 0:
                        P.op("act", lambda e: e.copy(out=xs[:, 0:256], in_=psb[b][:, 0:256]), reads=[("ps", b)], writes=[(xsn, 0)])
                    else:
                        r0 = (gi - 1) * 8
                        ov = xs[:, 256:NALL].rearrange("p (c r) -> p r c", r=32)[:, r0:r0 + 8, :]
                        iv = psb[b][:].rearrange("p (r c) -> p r c", c=64)
                        P.op("act", lambda e: e.copy(out=ov, in_=iv), reads=[("ps", b)], writes=[(xsn, gi)])
                inproj(w, (wn, 0), 0, TG_ALL, cons_x)
                load(xs_scr[h], xs[:], reads=[xsn], name=("xs_scr", h))

                def cons_y(b, gi):
                    P.op("act", lambda e: e.activation(out=gs_[:, gi * 512:(gi + 1) * 512], in_=psb[b][:], func=AF.Gelu_apprx_tanh), reads=[("ps", b)], writes=[(gsn, gi)])
                inproj(w, (wn, 1), 128, TG_LAT, cons_y)
                load(gy_scr[h], gs_[:], reads=[gsn], name=("gy_scr", h))
                mod_pieces(2)
            mod_pieces(100)
            P.op("dve", lambda e: e.tensor_tensor(out=modT[:, 64:192].rearrange("p (j r) -> p j r", r=2),
                                                  in0=psb[mb2][:, 0:128].rearrange("p (j r) -> p j r", r=2),
                                                  in1=pv(s_bm)[:, 32:96].unsqueeze(2).to_broadcast([128, 64, 2]), op=ALU.add),
                 reads=[("ps", mb2), ("par", 0, INF)], writes=[("modT", 64, 192)])
            reserved.discard(mb2)
            mv = modT[:].rearrange("p (j r) -> p j r", r=2)
            nw = pv(s_nw).rearrange("p (w k) -> p w k", w=4)
            rp = [("par", 0, INF), "modT"]
            P.op("dve", lambda e: e.tensor_tensor(out=vecs[:, 4, :], in0=mv[:, 32:48, 0], in1=nw[:, 1, :], op=ALU.mult), reads=rp, writes=[("vecs", 4)])
            P.op("dve", lambda e: e.scalar_tensor_tensor(out=vecs[:, 5, :], in0=mv[:, 64:80, 0], scalar=1.0, in1=nw[:, 2, :], op0=ALU.add, op1=ALU.mult), reads=rp, writes=[("vecs", 5)])
            P.op("dve", lambda e: e.tensor_copy(out=vecs[:, 6, :], in_=mv[:, 48:64, 0]), reads=rp, writes=[("vecs", 6)])
            P.op("dve", lambda e: e.tensor_tensor(out=vecs[:, 7, :], in0=mv[:, 80:96, 0], in1=nw[:, 3, :], op=ALU.mult), reads=rp, writes=[("vecs", 7)])
          P.barrier()
        s_h.close()

        freeb = set(range(8))

        def take(n):
            while len(freeb) < n:
                yield
            return [freeb.pop() for _ in range(n)]

        def rel(*bs):
            for b_ in bs:
                assert b_ not in freeb
                freeb.add(b_)

        def run_tasks(gens):
            active = list(gens)
            while active:
                for g in list(active):
                    try:
                        next(g)
                    except StopIteration:
                        active.remove(g)

        if "B1" in phases:
          with ExitStack() as sb1:
            lgw = sbt(sb1, "lgw", [128, 4096], BF16)
            P.dma("pool", lambda e: e.dma_start(out=lgw[:], in_=lgw_d), writes=["lgw"])
            xpad2 = [sbt(sb1, "xpad0", [128, XW])] * 2
            gyb2 = [sbt(sb1, "gyb%d" % i, [128, S], BF16) for i in range(2)]
            acc2 = [sbt(sb1, "acc0", [128, TW])] * 2; xrb2 = [sbt(sb1, "xrb0", [128, TW], BF16)] * 2
            mixs2 = [sbt(sb1, "mixs%d" % i, [128, S], BF16) for i in range(2)]
            RtA = [[sbt(sb1, "Rt%d_%d" % (d, i), [128, TW]) for d in range(2)] for i in range(2)]
            ItA = [[sbt(sb1, "It%d_%d" % (d, i), [128, TW]) for d in range(2)] for i in range(2)]
            StA = [[sbt(sb1, "St%d_%d" % (d, i), [128, TW]) for d in range(2)] for i in range(2)]
            P.op("pool", lambda e: e.memset(xpad2[0][:], 0.0), writes=["xpad0"])
            lcw = pv(s_lcw).rearrange("p (h j) -> p h j", j=4); lcb = pv(s_lcb)
            lgb = pv(s_lgb).rearrange("p (d g h) -> p d g h", d=2, g=2)
            rpar = [("par", 0, INF)]
            BLK = [(0, 256)] + [(LT + i * 512, 512) for i in range(4)]
            rg_ = lambda nm, c0, n: (nm, c0, c0 + n)

            xp = xpad2[0]; xpn = "xpad0"; acc = acc2[0]; accn = "acc0"; xrb = xrb2[0]; xrn = "xrb0"

            def lpre(h):
                hb = h % 2
                load(xp[:, 2:258], xs_scr[h][:, 0:256], reads=[("xs_scr", h)], name=(xpn, 2, 258))
                load(xp[:, 261:2309], xs_scr[h][:, 256:NALL], reads=[("xs_scr", h)], name=(xpn, 261, 2309))
                load(gyb2[hb][:], gy_scr[h], reads=[("gy_scr", h)], name="gyb%d" % hb)
                P.op("act", lambda e: e.activation(out=acc[:], in_=xp[:, 0:TW], func=AF.Identity, scale=lcw[:, h, 0:1], bias=lcb[:, h:h + 1]), reads=[xpn] + rpar, writes=[accn])
                for j in (1, 2, 3):
                    P.op("dve", lambda e: e.scalar_tensor_tensor(out=acc[:], in0=xp[:, j:j + TW], scalar=lcw[:, h, j:j + 1], in1=acc[:], op0=ALU.mult, op1=ALU.add),
                         reads=[xpn, accn] + rpar, writes=[accn])
                P.op("pool", lambda e: e.tensor_copy(out=xrb[:], in_=acc[:]), reads=[accn], writes=[xrn])

            def lst12(h):
                hb = h % 2
                Rt, It, St = RtA[hb], ItA[hb], StA[hb]
                sfx = "_%d" % hb
                for (c0, n) in BLK:
                    for d in range(2):
                        for g, dst, dn in ((0, Rt[d], "Rt%d" % d + sfx), (1, It[d], "It%d" % d + sfx)):
                            woff = ((d * 2 + g) * 8 + h) * 128
                            bb = bank()
                            P.op("pe", lambda e: e.matmul(psb[bb][:, 0:n], lhsT=lgw[:, woff:woff + 128], rhs=xrb[:, c0:c0 + n], start=True, stop=True),
                                 reads=["lgw", rg_(xrn, c0, n)], writes=[("ps", bb)])
                            P.op("act", lambda e: e.activation(out=dst[:, c0:c0 + n], in_=psb[bb][:, 0:n], func=AF.Sigmoid, bias=lgb[:, d, g, h:h + 1]),
                                 reads=[("ps", bb)] + rpar, writes=[rg_(dn, c0, n)])
                for (c0, n) in BLK:
                    for d in range(2):
                        ci = d * 8 + h
                        P.op("act", lambda e: e.activation(out=Rt[d][:, c0:c0 + n], in_=Rt[d][:, c0:c0 + n], func=AF.Exp, scale=cneg[:, ci:ci + 1]),
                             reads=[rg_("Rt%d" % d + sfx, c0, n), "cneg"], writes=[rg_("Rt%d" % d + sfx, c0, n)])
                        P.op("pool", lambda e: e.tensor_tensor(out=It[d][:, c0:c0 + n], in0=It[d][:, c0:c0 + n], in1=acc[:, c0:c0 + n], op=ALU.mult),
                             reads=[rg_("It%d" % d + sfx, c0, n), rg_(accn, c0, n)], writes=[rg_("It%d" % d + sfx, c0, n)])
                        P.op("pool", lambda e: e.tensor_tensor(out=St[d][:, c0:c0 + n], in0=Rt[d][:, c0:c0 + n], in1=Rt[d][:, c0:c0 + n], op=ALU.mult),
                             reads=[rg_("Rt%d" % d + sfx, c0, n)], writes=[rg_("St%d" % d + sfx, c0, n)])

            def lst3(h):
                hb = h % 2
                Rt, It, St = RtA[hb], ItA[hb], StA[hb]
                Hx = St
                sfx = "_%d" % hb
                for d in range(2):
                    for (c0, n) in BLK:
                        P.op("act", lambda e: e.activation(out=St[d][:, c0:c0 + n], in_=St[d][:, c0:c0 + n], func=AF.Sqrt, scale=-1.0, bias=1.0),
                             reads=[rg_("St%d" % d + sfx, c0, n)], writes=[rg_("St%d" % d + sfx, c0, n)])
                for d in range(2):
                    order = BLK if d == 0 else [BLK[0]] + BLK[:0:-1]
                    hx = Hx[d]; hxn = "St%d" % d + sfx
                    prev = None
                    for (c0, n) in order:
                        P.op("dve", lambda e: e.tensor_tensor(out=St[d][:, c0:c0 + n], in0=St[d][:, c0:c0 + n], in1=It[d][:, c0:c0 + n], op=ALU.mult),
                             reads=[rg_("St%d" % d + sfx, c0, n), rg_("It%d" % d + sfx, c0, n)], writes=[rg_("St%d" % d + sfx, c0, n)])
                        if prev is None:
                            init = 0.0; rd = []
                        else:
                            pc0, pn = prev
                            init = hx[:, pc0 + pn - 1:pc0 + pn] if d == 0 else hx[:, pc0:pc0 + 1]
                            rd = [rg_(hxn, pc0, pn)]
                        a_v = Rt[d][:, c0:c0 + n]; u_v = St[d][:, c0:c0 + n]; o_v = hx[:, c0:c0 + n]
                        if d == 1:
                            a_v, u_v, o_v = a_v[:, ::-1], u_v[:, ::-1], o_v[:, ::-1]
                        P.op("dve", lambda e: e.tensor_tensor_scan(out=o_v, data0=a_v, data1=u_v, initial=init, op0=ALU.mult, op1=ALU.add),
                             reads=[rg_("Rt%d" % d + sfx, c0, n), rg_("St%d" % d + sfx, c0, n)] + rd, writes=[rg_(hxn, c0, n)])
                        prev = (c0, n)
                ms = mixs2[hb]; msn = "mixs%d" % hb
                P.op("dve", lambda e: e.tensor_tensor(out=Hx[0][:, LT:TW], in0=Hx[0][:, LT:TW], in1=Hx[1][:, LT:TW], op=ALU.add), reads=["St0" + sfx, "St1" + sfx], writes=["St0" + sfx])
                P.op("pool", lambda e: e.tensor_tensor(out=ms[:].rearrange("p (r c) -> p r c", c=64), in0=gyb2[hb][:].rearrange("p (r c) -> p r c", c=64),
                                                       in1=Hx[0][:, LT:TW].rearrange("p (c r) -> p r c", r=32), op=ALU.mult), reads=["gyb%d" % hb, "St0" + sfx], writes=[msn])
                load(mix_scr[h], ms[:], reads=[msn], name=("mix_scr", h))

            lpre(0)
            for h in range(8):
                lst12(h)
                if h + 1 < 8:
                    lpre(h + 1)
                lst3(h)
            assert len(freeb) == 8
          P.barrier()


        if "B2" in phases:
          with ExitStack() as sb2:
            NSETS = 3
            graw = sbt(sb2, "graw0", [128, XW]); grn = "graw0"
            cacc = sbt(sb2, "cacc", [128, TW]); sqb = sbt(sb2, "sqb", [128, TW], BF16)
            rng_ = [sbt(sb2, "rng%d" % i, [128, 512]) for i in range(2)]
            qn2 = [sbt(sb2, "qn%d" % i, [128, TW], F32R) for i in range(2)]; kn2 = [sbt(sb2, "kn%d" % i, [128, TW], F32R) for i in range(2)]
            vf = cacc
            zs2 = [sbt(sb2, "zs%d" % i, [128, S], BF16) for i in range(2)]
            Ktok2 = [sbt(sb2, "Ktok%d" % i, [128, NCH, 128]) for i in range(2)]; Vtok2 = [sbt(sb2, "Vtok%d" % i, [128, NCH, 128]) for i in range(2)]
            oacc = sbt(sb2, "oacc", [128, 16, 128])
            mixg = sbt(sb2, "mixg0", [128, S], BF16); mgn = "mixg0"
            Sst = [sbt(sb2, "Sst%d" % d, [128, 128], F32R) for d in range(2)]
            G4 = [128, 4, 128]
            SETS = []
            for si in range(NSETS):
                t = {}
                for nm in ("Gs", "DT", "Erow", "QdT", "QKT", "Kd", "Nm", "NmT"):
                    t[nm] = sbt(sb2, "%s_%d" % (nm, si), G4, F32)
                    t[nm + "_n"] = "%s_%d" % (nm, si)
                for nm, al in (("Wm", "Gs"), ("Qa", "DT"), ("Qb", "Erow")):
                    t[nm], t[nm + "_n"] = t[al], t[al + "_n"]
                t["tmpWT"], t["tmpWT_n"] = t["Nm"], t["Nm_n"]
                t["tmpZ"], t["tmpZ_n"] = t["Qa"], t["Qa_n"]
                SETS.append(t)
            Rp = [sbt(sb2, "Rp%d" % i, [128, 128], F32R) for i in range(2)]; Vn = [sbt(sb2, "Vn%d" % i, [128, 128], F32R) for i in range(2)]
            ot = [sbt(sb2, "ot%d" % i, [128, 128]) for i in range(2)]; junkg = [sbt(sb2, "junkg%d" % i, [128, 128]) for i in range(2)]
            stg = [sbt(sb2, "stg%d" % i, [128, 4]) for i in range(2)]
            pcs = [sbt(sb2, "pcs%d" % i, [128, 16, 128], BF16) for i in range(2)]
            P.op("pool", lambda e: e.memset(graw[:], 0.0), writes=[grn])
            gcw = pv(s_gcw).rearrange("p (t h j) -> p t h j", t=3, j=4)
            f32v = lambda ap: ap.bitcast(F32)
            bcast4 = lambda m: m[:].unsqueeze(1).to_broadcast([128, 4, 128])
            fl = lambda t: t[:].rearrange("p u c -> p (u c)")
            fwd_order = list(range(NCH)); bwd_order = [1, 0] + list(range(17, 1, -1))
            rpar = [("par", 0, INF)]
            w1v_ = w1_d.rearrange("(k p) (o c) -> o p k c", p=128, c=128)
            w2v_ = w2_d.rearrange("(k p) (o c) -> o p k c", p=128, c=128)
            pjobs = [(w1v_[o], w1b[o], ("w1b", o)) for o in range(64)]
            pjobs += [(w2v_[o][:, kh * 16:(kh + 1) * 16, :], w2b[o][:, kh * 16:(kh + 1) * 16, :], ("w2b", o * 4 + kh)) for o in range(16) for kh in range(4)]
            pji = [0]

            def precast_some(n):
                for _ in range(n):
                    if pji[0] >= len(pjobs):
                        return
                    src, dst, rn = pjobs[pji[0]]
                    i = pji[0] % 2
                    pji[0] += 1
                    P.dma("pool", lambda e: e.dma_start(out=pcs[i][:], in_=src), writes=["pcs%d" % i])
                    P.dma("sp", lambda e: e.dma_start(out=dst, in_=pcs[i][:]), reads=["pcs%d" % i], writes=[rn])

            import os as _os
            r_ = lambda ap: ap.bitcast(F32R)
            NH = int(_os.environ.get("K_NGDN", 8))
            NIT = NCH // 2

            def mm4(b, lhs, ln, rhs, rn):
                for u in range(4):
                    P.op("pe", lambda e: e.matmul(psb[b][:, u * 128:(u + 1) * 128], lhsT=r_(lhs[:, u, :]), rhs=r_(rhs[:, u, :]), start=True, stop=True, skip_group_check=True),
                         reads=[ln, rn], writes=[("ps", b)], inc=(u == 3))

            def transpose4(b, src, sname):
                for u in range(4):
                    P.op("pe", lambda e: e.transpose(psb[b][:, u * 128:(u + 1) * 128], src[:, u, :], ident[:]),
                         reads=[sname, "ident"], writes=[("ps", b)], inc=(u == 3))

            def headpre(h):
                hb = h % 2
                qn, kn, Ktok, Vtok, zs = qn2[hb], kn2[hb], Ktok2[hb], Vtok2[hb], zs2[hb]
                qnn, knn, Ktn, Vtn, zsn = "qn%d" % hb, "kn%d" % hb, "Ktok%d" % hb, "Vtok%d" % hb, "zs%d" % hb
                load(zs[:], zs_scr[h], reads=[("zs_scr", h)], name=zsn)
                for t3 in range(3):
                    load(graw[:, 2:258], p_scr[h, t3][:, 0:256], reads=[("p_scr", h * 3 + t3)], name=(grn, 2, 258))
                    load(graw[:, 261:2309], p_scr[h, t3][:, 256:NALL], reads=[("p_scr", h * 3 + t3)], name=(grn, 261, 2309))
                    P.op("act", lambda e: e.activation(out=cacc[:], in_=graw[:, 0:TW], func=AF.Copy, scale=gcw[:, t3, h, 0:1]), reads=[grn] + rpar, writes=["cacc"])
                    yield
                    for j in (1, 2, 3):
                        P.op("dve", lambda e: e.scalar_tensor_tensor(out=cacc[:], in0=graw[:, j:j + TW], scalar=gcw[:, t3, h, j:j + 1], in1=cacc[:], op0=ALU.mult, op1=ALU.add),
                             reads=[grn, "cacc"] + rpar, writes=["cacc"])
                        yield
                    P.op("act", lambda e: e.activation(out=cacc[:], in_=cacc[:], func=AF.Silu), reads=["cacc"], writes=["cacc"])
                    yield
                    if t3 < 2:
                        P.op("pool", lambda e: e.tensor_tensor(out=sqb[:], in0=cacc[:], in1=cacc[:], op=ALU.mult), reads=["cacc"], writes=["sqb"])
                        yield
                        sc_ = 128.0 if t3 == 0 else 1.0
                        dq = qn if t3 == 0 else kn; dqn = qnn if t3 == 0 else knn
                        for gi, (c0, n) in enumerate([(g * 512, 512) for g in range(4)] + [(2048, TW - 2048)]):
                            b, = yield from take(1)
                            rg = rng_[gi % 2]; rgn = "rng%d" % (gi % 2)
                            P.op("pe", lambda e: e.matmul(psb[b][:, 0:n], lhsT=onesb[:], rhs=sqb[:, c0:c0 + n], start=True, stop=True), reads=["onesb", "sqb"], writes=[("ps", b)])
                            yield
                            P.op("dve", lambda e: e.tensor_copy(out=rg[:, 0:n], in_=psb[b][:, 0:n]), reads=[("ps", b)], writes=[rgn])
                            rel(b)
                            yield
                            P.op("act", lambda e: e.activation(out=rg[:, 0:n], in_=rg[:, 0:n], func=AF.Ln, scale=sc_, bias=sc_ * EPS), reads=[rgn], writes=[rgn])
                            P.op("act", lambda e: e.activation(out=rg[:, 0:n], in_=rg[:, 0:n], func=AF.Exp, scale=-0.5), reads=[rgn], writes=[rgn])
                            yield
                            P.op("pool", lambda e: e.tensor_tensor(out=dq[:, c0:c0 + n], in0=cacc[:, c0:c0 + n], in1=rg[:, 0:n], op=ALU.mult), reads=["cacc", rgn], writes=[(dqn, c0, c0 + n)])
                            yield
                    if t3 >= 1:
                        srcT, sname, dstK, dname = (kn, knn, Ktok, Ktn) if t3 == 1 else (vf, "cacc", Vtok, Vtn)
                        for c4 in range(0, NCH, 4):
                            n4 = min(4, NCH - c4)
                            b, = yield from take(1)
                            for q in range(n4):
                                co = ch_off(c4 + q)
                                P.op("pe", lambda e: e.transpose(psb[b][:, q * 128:(q + 1) * 128], f32v(srcT[:, co:co + 128]), ident[:]),
                                     reads=[sname, "ident"], writes=[("ps", b)], inc=(q == n4 - 1))
                            yield
                            P.op("dve", lambda e: e.tensor_copy(out=dstK[:, c4:c4 + n4, :], in_=psb[b][:, 0:n4 * 128].rearrange("p (q c) -> p q c", c=128)),
                                 reads=[("ps", b)], writes=[dname])
                            rel(b)
                            yield

            def head_tasks(h):
                hb = h % 2
                qn, kn, Ktok, Vtok, zs = qn2[hb], kn2[hb], Ktok2[hb], Vtok2[hb], zs2[hb]
                qnn, knn, Ktn, Vtn, zsn = "qn%d" % hb, "kn%d" % hb, "Ktok%d" % hb, "Vtok%d" % hb, "zs%d" % hb
                hd = lambda d: d * 8 + h
                for d in range(2):
                    P.op("pool", lambda e: e.tensor_scalar(out=Sst[d][:], in0=ident[:], scalar1=0.0, scalar2=None, op0=ALU.mult), reads=["ident"], writes=["Sst%d" % d])
                odone = [False] * 16
                prep_done = [False] * NIT
                rec_done = [[False] * NIT for _ in range(2)]

                def units_of(it):
                    return [(0, fwd_order[2 * it]), (0, fwd_order[2 * it + 1]), (1, bwd_order[2 * it]), (1, bwd_order[2 * it + 1])]

                def prep(it):
                    T_ = SETS[it % NSETS]
                    while it >= NSETS and not (rec_done[0][it - NSETS] and rec_done[1][it - NSETS]):
                        yield
                    units = units_of(it)
                    Gs, DT, Erow, QdT, QKT, Kd, Nm, NmT, Qa, Qb, Wm = [T_[k] for k in ("Gs", "DT", "Erow", "QdT", "QKT", "Kd", "Nm", "NmT", "Qa", "Qb", "Wm")]
                    n_ = lambda k: T_[k + "_n"]
                    precast_some(2)
                    for u, (d, ch) in enumerate(units):
                        P.op("act", lambda e: e.activation(out=NmT[:, u, :], in_=ident[:], func=AF.Copy, scale=gam[:, ch, hd(d):hd(d) + 1]),
                             reads=["ident", "gam"], writes=[(n_("NmT"), u)])
                        P.op("act", lambda e: e.activation(out=r_(Kd[:, u, :]), in_=Ktok[:, ch, :], func=AF.Copy, scale=kdec[:, ch, hd(d):hd(d) + 1]),
                             reads=[Ktn, "kdec"], writes=[(n_("Kd"), u)])
                    yield
                    bG, bK, bQ = yield from take(3)
                    P.op("pe", lambda e: e.matmul(psb[bG][:], lhsT=ones[:], rhs=fl(NmT), start=True, stop=True), reads=["ones", n_("NmT")], writes=[("ps", bG)])
                    for u, (d, ch) in enumerate(units):
                        co = ch_off(ch)
                        P.op("pe", lambda e: e.matmul(psb[bK][:, u * 128:(u + 1) * 128], lhsT=kn[:, co:co + 128], rhs=kn[:, co:co + 128], start=True, stop=True, skip_group_check=True),
                             reads=[knn], writes=[("ps", bK)], inc=False)
                        P.op("pe", lambda e: e.matmul(psb[bQ][:, u * 128:(u + 1) * 128], lhsT=kn[:, co:co + 128], rhs=qn[:, co:co + 128], start=True, stop=True, skip_group_check=True),
                             reads=[knn, qnn], writes=[("ps", bQ)], inc=(u == 3))
                    yield
                    P.op("dve", lambda e: e.tensor_copy(out=r_(fl(Gs)), in_=psb[bG][:]), reads=[("ps", bG)], writes=[n_("Gs")])
                    for u, (d, ch) in enumerate(units):
                        P.op("dve", lambda e: e.scalar_tensor_tensor(out=r_(DT[:, u, :]), in0=psb[bG][:, u * 128:(u + 1) * 128], scalar=gam[:, ch, hd(d):hd(d) + 1], in1=negm[d][:], op0=ALU.subtract, op1=ALU.add),
                             reads=[("ps", bG), "gam", "negm%d" % d], writes=[(n_("DT"), u)])
                    rel(bG)
                    yield
                    P.op("act", lambda e: e.activation(out=r_(fl(Erow)), in_=fl(Gs), func=AF.Exp), reads=[n_("Gs")], writes=[n_("Erow")])
                    P.op("act", lambda e: e.activation(out=r_(fl(DT)), in_=fl(DT), func=AF.Exp), reads=[n_("DT")], writes=[n_("DT")])
                    for u, (d, ch) in enumerate(units):
                        co = ch_off(ch)
                        if ch >= 2:
                            P.op("pool", lambda e: e.tensor_tensor(out=r_(QdT[:, u, :]), in0=f32v(qn[:, co:co + 128]), in1=Erow[:, u, :], op=ALU.mult),
                                 reads=[qnn, n_("Erow")], writes=[(n_("QdT"), u)])
                    yield
                    P.op("dve", lambda e: e.tensor_tensor(out=r_(fl(QKT)), in0=psb[bQ][:], in1=fl(DT), op=ALU.mult), reads=[("ps", bQ), n_("DT")], writes=[n_("QKT")])
                    for u, (d, ch) in enumerate(units):
                        P.op("dve", lambda e: e.scalar_tensor_tensor(out=r_(Nm[:, u, :]), in0=psb[bK][:, u * 128:(u + 1) * 128], scalar=nbeta[:, ch, hd(d):hd(d) + 1], in1=DT[:, u, :], op0=ALU.mult, op1=ALU.mult),
                             reads=[("ps", bK), "nbeta", n_("DT")], writes=[(n_("Nm"), u)])
                    rel(bK, bQ)
                    yield
                    P.op("pool", lambda e: e.tensor_tensor(out=r_(Nm[:]), in0=Nm[:], in1=bcast4(offd), op=ALU.mult), reads=[n_("Nm"), "offd"], writes=[n_("Nm")])
                    yield
                    bT, = yield from take(1)
                    transpose4(bT, Nm, n_("Nm"))
                    P.op("pool", lambda e: e.tensor_tensor(out=r_(Qa[:]), in0=Nm[:], in1=bcast4(bd32), op=ALU.mult), reads=[n_("Nm"), "bd32"], writes=[n_("Qa")])
                    yield
                    P.op("dve", lambda e: e.tensor_copy(out=fl(NmT), in_=psb[bT][:]), reads=[("ps", bT)], writes=[n_("NmT")])
                    rel(bT)
                    P.op("pool", lambda e: e.tensor_tensor(out=r_(Wm[:]), in0=Qa[:], in1=bcast4(ident), op=ALU.add), reads=[n_("Qa"), "ident"], writes=[n_("Wm")])
                    yield
                    P.op("pool", lambda e: e.tensor_tensor(out=r_(Qb[:]), in0=NmT[:], in1=bcast4(bd32), op=ALU.mult), reads=[n_("NmT"), "bd32"], writes=[n_("Qb")])
                    yield
                    for lvl in range(4):
                        last = (lvl == 3)
                        bTq, bNq = yield from take(2)
                        mm4(bTq, Qa, n_("Qa"), Qb, n_("Qb"))
                        if not last:
                            mm4(bNq, Qb, n_("Qb"), Qa, n_("Qa"))
                        yield
                        P.op("dve", lambda e: e.tensor_copy(out=r_(fl(Qb)), in_=psb[bTq][:]), reads=[("ps", bTq)], writes=[n_("Qb")])
                        if not last:
                            P.op("dve", lambda e: e.tensor_copy(out=r_(fl(Qa)), in_=psb[bNq][:]), reads=[("ps", bNq)], writes=[n_("Qa")])
                        rel(bTq, bNq)
                        yield
                        bW, = yield from take(1)
                        mm4(bW, Qb, n_("Qb"), Wm, n_("Wm"))
                        yield
                        P.op("dve", lambda e: e.tensor_tensor(out=r_(fl(Wm)), in0=psb[bW][:], in1=fl(Wm), op=ALU.add), reads=[("ps", bW), n_("Wm")], writes=[n_("Wm")])
                        rel(bW)
                        yield
                    for om, omn in ((od64, "od64"), (od128, "od128")):
                        bt, bZ = yield from take(2)
                        transpose4(bt, Wm, n_("Wm"))
                        P.op("pool", lambda e: e.tensor_tensor(out=r_(Qb[:]), in0=NmT[:], in1=bcast4(om), op=ALU.mult), reads=[n_("NmT"), omn], writes=[n_("Qb")])
                        yield
                        mm4(bZ, Qb, n_("Qb"), Wm, n_("Wm"))
                        P.op("dve", lambda e: e.tensor_copy(out=r_(fl(T_["tmpWT"])), in_=psb[bt][:]), reads=[("ps", bt)], writes=[T_["tmpWT_n"]])
                        yield
                        P.op("dve", lambda e: e.tensor_copy(out=r_(fl(T_["tmpZ"])), in_=psb[bZ][:]), reads=[("ps", bZ)], writes=[T_["tmpZ_n"]])
                        rel(bt, bZ)
                        yield
                        bW, = yield from take(1)
                        mm4(bW, T_["tmpWT"], T_["tmpWT_n"], T_["tmpZ"], T_["tmpZ_n"])
                        yield
                        P.op("dve", lambda e: e.tensor_tensor(out=r_(fl(Wm)), in0=psb[bW][:], in1=fl(Wm), op=ALU.add), reads=[("ps", bW), n_("Wm")], writes=[n_("Wm")])
                        rel(bW)
                        yield
                    prep_done[it] = True

                def recur(d, it):
                    T_ = SETS[it % NSETS]
                    while not prep_done[it]:
                        yield
                    QdT, QKT, Kd, Wf = T_["QdT"], T_["QKT"], T_["Kd"], T_["Wm"]
                    n_ = lambda k: T_[k + "_n"]
                    units = units_of(it)
                    for u in (2 * d, 2 * d + 1):
                        ch = units[u][1]
                        co = ch_off(ch); Sd = Sst[d]; Sn = "Sst%d" % d; hdd = hd(d)
                        ri = d
                        b1, = yield from take(1)
                        P.op("pe", lambda e: e.matmul(psb[b1][:, 0:128], lhsT=kn[:, co:co + 128], rhs=Sd[:], start=True, stop=True), reads=[knn, Sn], writes=[("ps", b1)])
                        yield
                        P.op("dve", lambda e: e.scalar_tensor_tensor(out=Rp[ri][:], in0=psb[b1][:, 0:128], scalar=neg_eg[:, ch, hdd:hdd + 1], in1=Vtok[:, ch, :], op0=ALU.mult, op1=ALU.add),
                             reads=[("ps", b1), "neg_eg", Vtn], writes=["Rp%d" % ri])
                        rel(b1)
                        yield
                        b2, = yield from take(1)
                        P.op("pe", lambda e: e.matmul(psb[b2][:, 0:128], lhsT=r_(Wf[:, u, :]), rhs=Rp[ri][:], start=True, stop=True), reads=[n_("Wm"), "Rp%d" % ri], writes=[("ps", b2)])
                        yield
                        P.op("dve", lambda e: e.tensor_scalar(out=Vn[ri][:], in0=psb[b2][:, 0:128], scalar1=beta[:, ch, hdd:hdd + 1], scalar2=None, op0=ALU.mult),
                             reads=[("ps", b2), "beta"], writes=["Vn%d" % ri])
                        rel(b2)
                        yield
                        b5, b3 = yield from take(2)
                        P.op("pe", lambda e: e.matmul(psb[b5][:, 0:128], lhsT=r_(Kd[:, u, :]), rhs=Vn[ri][:], start=True, stop=True), reads=[(n_("Kd"), u), "Vn%d" % ri], writes=[("ps", b5)])
                        if ch >= 2:
                            lc = ch - 2
                            P.op("pe", lambda e: e.matmul(psb[b3][:, 0:128], lhsT=r_(QdT[:, u, :]), rhs=Sd[:], start=True, stop=False), reads=[(n_("QdT"), u), Sn], writes=[("ps", b3)], inc=False)
                            P.op("pe", lambda e: e.matmul(psb[b3][:, 0:128], lhsT=r_(QKT[:, u, :]), rhs=Vn[ri][:], start=False, stop=True), reads=[n_("QKT"), "Vn%d" % ri], writes=[("ps", b3)])
                        yield
                        P.op("dve", lambda e: e.scalar_tensor_tensor(out=Sd[:], in0=f32v(Sd[:]), scalar=cdl[:, ch, hdd:hdd + 1], in1=psb[b5][:, 0:128], op0=ALU.mult, op1=ALU.add),
                             reads=[("ps", b5), "cdl", Sn], writes=[Sn])
                        rel(b5)
                        if ch < 2:
                            rel(b3)
                        if ch >= 2:
                            if not odone[lc]:
                                odone[lc] = True
                                P.op("dve", lambda e: e.tensor_copy(out=oacc[:, lc, :], in_=psb[b3][:, 0:128]), reads=[("ps", b3)], writes=[("oacc", lc)])
                                rel(b3)
                            else:
                                o_ = ot[d]; on_ = "ot%d" % d; sg = stg[d]; sgn = "stg%d" % d
                                P.op("dve", lambda e: e.tensor_tensor(out=o_[:], in0=psb[b3][:, 0:128], in1=oacc[:, lc, :], op=ALU.add), reads=[("ps", b3), ("oacc", lc)], writes=[on_])
                                rel(b3)
                                yield
                                P.op("act", lambda e: e.activation(out=junkg[d][:], in_=o_[:], func=AF.Square, accum_out=sg[:, 0:1]), reads=[on_], writes=["junkg%d" % d, sgn])
                                P.op("act", lambda e: e.activation(out=sg[:, 1:2], in_=sg[:, 0:1], func=AF.Sqrt, scale=1.0 / 128, bias=EPS), reads=[sgn], writes=[sgn])
                                yield
                                P.op("dve", lambda e: e.reciprocal(out=sg[:, 2:3], in_=sg[:, 1:2]), reads=[sgn], writes=[sgn])
                                yield
                                P.op("act", lambda e: e.activation(out=o_[:], in_=o_[:], func=AF.Copy, scale=sg[:, 2:3]), reads=[on_, sgn], writes=[on_])
                                yield
                                b4, = yield from take(1)
                                P.op("pe", lambda e: e.transpose(psb[b4][:, 0:128], o_[:], ident[:]), reads=[on_, "ident"], writes=[("ps", b4)])
                                yield
                                P.op("dve", lambda e: e.scalar_tensor_tensor(out=mixg[:, lc * 128:(lc + 1) * 128], in0=psb[b4][:, 0:128], scalar=pv(s_gnw), in1=zs[:, lc * 128:(lc + 1) * 128], op0=ALU.mult, op1=ALU.mult),
                                     reads=[("ps", b4), zsn] + rpar, writes=[(mgn, lc)])
                                rel(b4)
                        yield
                    rec_done[d][it] = True

                def chain(fn, *a):
                    for it in range(NIT):
                        yield from fn(*a, it)

                def prep_lane(l):
                    for it in range(l, NIT, NSETS):
                        yield from prep(it)

                return [prep_lane(l) for l in range(NSETS)] + [chain(recur, 0), chain(recur, 1)]

            run_tasks([headpre(0)])
            for h in range(NH):
                tasks = head_tasks(h)
                if h + 1 < NH:
                    tasks.append(headpre(h + 1))
                run_tasks(tasks)
                assert len(freeb) == 8
                load(mix_scr[8 + h], mixg[:], reads=[mgn], name=("mix_scr", 8 + h))
            precast_some(1000)
          P.barrier()

        if "C" in phases:
          with ExitStack() as sc:
            wo = sbt(sc, "wo", [128, 16, D], BF16)
            GM_row = make_row(sc, 4, "GM_row")
            wov = wout_d.rearrange("(k p) c -> p k c", p=128)
            for k4 in range(0, 16, 4):
                P.dma("pool", (lambda k4: lambda e: e.dma_start(out=wo[:, k4:k4 + 4, :], in_=wov[:, k4:k4 + 4, :]))(k4), writes=[("wo", k4, k4 + 4)])
            mt = [sbt(sc, "mt%d" % i, [128, 16, 512], BF16) for i in range(2)]
            xc = [sbt(sc, "xc%d" % i, [128, D]) for i in range(2)]
            x1t = [sbt(sc, "x1t%d" % i, [128, D]) for i in range(2)]
            h2t = [sbt(sc, "h2t%d" % i, [128, 16, 128], BF16) for i in range(2)]
            junkc = sbt(sc, "junkc", [128, D]); stc2 = [sbt(sc, "stc%d" % i, [128, 16]) for i in range(2)]
            mixv = mix_scr.rearrange("k p t -> p k t")
            h2v = h2_scr.rearrange("k p t -> p k t")
            def stageA(tt):
                g = tt // 4
                m = mt[g % 2]; mn = "mt%d" % (g % 2)
                if tt % 4 == 0:
                    load(m[:], mixv[:, :, g * 512:(g + 1) * 512], reads=["mix_scr"], name=mn)
                i = tt % 2
                stc = stc2[i]; stn = "stc%d" % i
                load(xc[i][:], x_d[tt * 128:(tt + 1) * 128, :], name="xc%d" % i)
                bs = []
                for cg in range(4):
                    b = bank(); bs.append(b)
                    for k in range(16):
                        P.op("pe", lambda e: e.matmul(psb[b][:], lhsT=m[:, k, (tt % 4) * 128:(tt % 4 + 1) * 128], rhs=wo[:, k, cg * 512:(cg + 1) * 512], start=(k == 0), stop=(k == 15)),
                             reads=[mn, ("wo", k)], writes=[("ps", b)], inc=(k == 15))
                for cg in range(4):
                    b = bs[cg]
                    P.op("act", lambda e: e.activation(out=junkc[:, cg * 512:(cg + 1) * 512], in_=psb[b][:], func=AF.Square, accum_out=stc[:, cg:cg + 1]),
                         reads=[("ps", b)], writes=[("junkc", cg), (stn, cg)])
                P.op("dve", lambda e: e.tensor_reduce(out=stc[:, 4:5], in_=stc[:, 0:4], axis=mybir.AxisListType.X, op=ALU.add), reads=[stn], writes=[stn])
                P.op("act", lambda e: e.activation(out=stc[:, 5:6], in_=stc[:, 4:5], func=AF.Sqrt, scale=1.0 / D, bias=EPS), reads=[stn], writes=[stn])
                P.op("dve", lambda e: e.reciprocal(out=stc[:, 6:7], in_=stc[:, 5:6]), reads=[stn], writes=[stn])
                x1 = x1t[i]; x1n = "x1t%d" % i
                for cg in range(4):
                    b = bs[cg]
                    cs = slice(cg * 512, (cg + 1) * 512)
                    P.op("act", lambda e: e.activation(out=x1[:, cs], in_=psb[b][:], func=AF.Copy, scale=stc[:, 6:7]), reads=[("ps", b), stn], writes=[(x1n, cg)])
                    P.op("dve", lambda e: e.tensor_tensor(out=x1[:, cs], in0=x1[:, cs], in1=GM_row[:, cs], op=ALU.mult), reads=[(x1n, cg), "GM_row"], writes=[(x1n, cg)])
                    P.op("pool", lambda e: e.tensor_tensor(out=x1[:, cs], in0=x1[:, cs], in1=xc[i][:, cs], op=ALU.add), reads=[(x1n, cg), "xc%d" % i], writes=[(x1n, cg)])
                load(x1_scr[tt * 128:(tt + 1) * 128, :], x1[:], q="pool", reads=[x1n], name=("x1_scr", tt))
                P.op("act", lambda e: e.activation(out=junkc[:], in_=x1[:], func=AF.Square, accum_out=stc[:, 8:9]), reads=[x1n], writes=["junkc", stn])
                P.op("act", lambda e: e.activation(out=stc[:, 9:10], in_=stc[:, 8:9], func=AF.Sqrt, scale=1.0 / D, bias=EPS), reads=[stn], writes=[stn])
                P.op("dve", lambda e: e.reciprocal(out=stc[:, 10:11], in_=stc[:, 9:10]), reads=[stn], writes=[stn])
                P.op("act", lambda e: e.activation(out=xc[i][:], in_=x1[:], func=AF.Copy, scale=stc[:, 10:11]), reads=[x1n, stn], writes=["xc%d" % i])

            def stageB(tt):
                i = tt % 2
                xn = xc[i]; xnn = "xc%d" % i
                ht = h2t[i]; htn = "h2t%d" % i
                for g4 in range(4):
                    b = bank()
                    for q in range(4):
                        k = g4 * 4 + q
                        P.op("pe", lambda e: e.transpose(psb[b][:, q * 128:(q + 1) * 128], xn[:, k * 128:(k + 1) * 128], ident[:]), reads=[xnn, "ident"], writes=[("ps", b)], inc=(q == 3))
                    for q in range(4):
                        k = g4 * 4 + q
                        P.op("dve", lambda e: e.tensor_scalar(out=ht[:, k, :], in0=psb[b][:, q * 128:(q + 1) * 128], scalar1=vecs[:, 5, k:k + 1], scalar2=vecs[:, 6, k:k + 1], op0=ALU.mult, op1=ALU.add),
                             reads=[("ps", b), "vecs"], writes=[(htn, k)])
                load(h2v[:, :, tt * 128:(tt + 1) * 128], ht[:], q="pool", reads=[htn], name=("h2_scr", tt))

            stageA(0)
            for tt in range(1, 16):
                stageA(tt)
                stageB(tt - 1)
            stageB(15)
          P.barrier()

        out_toks = []
        if "D" in phases:
          with ExitStack() as sd:
            h2 = sbt(sd, "h2", [128, 16, 512], BF16)
            GF_row = make_row(sd, 7, "GF_row")
            f1 = sbt(sd, "f1", [128, 64, 512], BF16)
            w1t = [sbt(sd, "w1t%d" % i, [128, 16, 128], BF16) for i in range(3)]
            w2t = [sbt(sd, "w2t%d" % i, [128, 64, 128], BF16) for i in range(2)]
            rl = [sbt(sd, "rl%d" % i, [128, 512]) for i in range(2)]
            y2b = [sbt(sd, "y2b%d" % i, [128, 512]) for i in range(2)]
            y2 = sbt(sd, "y2", [128, 4, D])
            x1d = sbt(sd, "x1d", [128, D]); junkd = sbt(sd, "junkd", [128, D], BF16); std = sbt(sd, "std", [128, 8])
            h2v = h2_scr.rearrange("k p t -> p k t")
            w1v = w1_d.rearrange("(k p) (o c) -> o p k c", p=128, c=128)
            w2v = w2_d.rearrange("(k p) (o c) -> o p k c", p=128, c=128)
            def epi(T, q):
                tt = T * 4 + q
                load(x1d[:], x1_scr[tt * 128:(tt + 1) * 128, :], q="pool", reads=[("x1_scr", tt)], name="x1d")
                P.op("act", lambda e: e.activation(out=junkd[:], in_=y2[:, q, :], func=AF.Square, accum_out=std[:, 0:1]), reads=["y2"], writes=["junkd", "std"])
                P.op("act", lambda e: e.activation(out=std[:, 1:2], in_=std[:, 0:1], func=AF.Sqrt, scale=1.0 / D, bias=EPS), reads=["std"], writes=["std"])
                P.op("dve", lambda e: e.reciprocal(out=std[:, 2:3], in_=std[:, 1:2]), reads=["std"], writes=["std"])
                P.op("dve", lambda e: e.scalar_tensor_tensor(out=y2[:, q, :], in0=y2[:, q, :], scalar=std[:, 2:3], in1=GF_row[:], op0=ALU.mult, op1=ALU.mult),
                     reads=["y2", "std", "GF_row"], writes=["y2"])
                P.op("dve", lambda e: e.tensor_tensor(out=x1d[:], in0=x1d[:], in1=y2[:, q, :], op=ALU.add), reads=["x1d", "y2"], writes=["x1d"])
                out_toks.append(load(out_d[tt * 128:(tt + 1) * 128, :], x1d[:], q="pool", reads=["x1d"], name=("out", tt)))

            def ff1(T):
                for o in range(64):
                    if T > 0 and o in (6, 12, 18, 24):
                        epi(T - 1, (o // 6) - 1)
                    wt = w1t[o % 3]; wtn = "w1t%d" % (o % 3)
                    load(wt[:], w1b[o], reads=[("w1b", o)], name=wtn)
                    b = bank()
                    for k in range(16):
                        P.op("pe", lambda e: e.matmul(psb[b][:], lhsT=wt[:, k, :], rhs=h2[:, k, :], start=(k == 0), stop=(k == 15)), reads=[wtn, "h2"], writes=[("ps", b)], inc=(k == 15))
                    r = rl[o % 2]; rn = "rl%d" % (o % 2)
                    P.op("act", lambda e: e.activation(out=r[:], in_=psb[b][:], func=AF.Relu), reads=[("ps", b)], writes=[rn])
                    P.op("dve", lambda e: e.tensor_tensor(out=f1[:, o, :], in0=r[:], in1=r[:], op=ALU.mult), reads=[rn], writes=[("f1", o)])

            def ff2(T):
                for o in range(16):
                    wt = w2t[o % 2]; wtn = "w2t%d" % (o % 2)
                    for kh in range(4):
                        load(wt[:, kh * 16:(kh + 1) * 16, :], w2b[o][:, kh * 16:(kh + 1) * 16, :], reads=[("w2b", o * 4 + kh)], name=(wtn, kh))
                    b = bank()
                    for k in range(64):
                        P.op("pe", lambda e: e.matmul(psb[b][:], lhsT=wt[:, k, :], rhs=f1[:, k, :], start=(k == 0), stop=(k == 63)), reads=[(wtn, k // 16), ("f1", k)], writes=[("ps", b)], inc=(k == 63))
                    yb = y2b[o % 2]; ybn = "y2b%d" % (o % 2)
                    P.op("act", lambda e: e.copy(out=yb[:], in_=psb[b][:]), reads=[("ps", b)], writes=[ybn])
                    b2 = bank()
                    for q in range(4):
                        P.op("pe", lambda e: e.transpose(psb[b2][:, q * 128:(q + 1) * 128], yb[:, q * 128:(q + 1) * 128], ident[:]), reads=[ybn, "ident"], writes=[("ps", b2)], inc=(q == 3))
                    P.op("dve", lambda e: e.tensor_copy(out=y2[:, :, o * 128:(o + 1) * 128], in_=psb[b2][:].rearrange("p (q c) -> p q c", c=128)), reads=[("ps", b2)], writes=[("y2", o)])

            load(h2[:], h2v[:, :, 0:512], reads=["h2_scr"], name="h2")
            for T in range(4):
                ff1(T)
                if T < 3:
                    load(h2[:], h2v[:, :, (T + 1) * 512:(T + 2) * 512], reads=["h2_scr"], name="h2")
                ff2(T)
            for q in range(4):
                epi(3, q)
        if not out_toks:
            zt = sbt(top, "zt", [128, D])
            P.op("pool", lambda e: e.memset(zt[:], 0.0), writes=["zt"])
            for tt in range(16):
                out_toks.append(load(out_d[tt * 128:(tt + 1) * 128, :], zt[:], reads=["zt"]))
        P.barrier()
        P.final_wait("sp", out_toks)
        global LAST_PROG
        LAST_PROG = P
        with nc.Block() as block:
            P.emit(block)
    return nc


def host_layout(inp, b):
    f = lambda a: np.ascontiguousarray(a, dtype=np.float32)
    pk = lambda v: f(v.reshape(-1, 128).T)
    cc = np.stack([pk(inp["c"][b]), pk(inp["c_ctx"])], axis=2).reshape(128, 32)
    nw = inp["norm_w"][0]
    nwT = np.stack([pk(nw[i]) for i in range(4)], axis=1).reshape(128, 64)
    lcw = inp["lru_conv_w"][0].reshape(4, 8, 128).transpose(2, 1, 0).reshape(128, 32)
    lcb = inp["lru_conv_b"][0].reshape(8, 128).T
    lgw = inp["lru_gate_w"][0].transpose(3, 0, 1, 2, 4).reshape(128, 4096)
    lgb = inp["lru_gate_b"][0].reshape(2, 2, 8, 128).transpose(3, 0, 1, 2).reshape(128, 32)
    llam = inp["lru_lambda"][0].reshape(2, 8, 128).transpose(2, 0, 1).reshape(128, 16)
    gcw = inp["gdn_conv_w"][0].reshape(4, 3, 8, 128).transpose(3, 1, 2, 0).reshape(128, 96)
    galog = np.broadcast_to(inp["gdn_a_log"][0].reshape(1, 16), (128, 16))
    gdtb = np.broadcast_to(inp["gdn_dt_bias"][0].reshape(1, 16), (128, 16))
    return {
        "x": f(inp["x"][b]), "ctx": f(inp["ctx"][b]), "cc": f(cc),
        "w_mod": f(inp["w_mod"][0]), "b_modT": pk(inp["b_mod"][0]), "nwT": f(nwT),
        "w_in": f(inp["w_in"][0]), "lcw": f(lcw), "lcb": f(lcb), "lgw": f(lgw), "lgb": f(lgb), "llam": f(llam),
        "gcw": f(gcw), "galog": f(galog), "gdtb": f(gdtb), "gnw": f(inp["gdn_norm_w"][0].reshape(128, 1)),
        "w_out": f(inp["w_out"][0]), "w_ff1": f(inp["w_ff1"][0]), "w_ff2": f(inp["w_ff2"][0]),
    }


def kernel(**inputs):
    inp = {k: np.asarray(v) for k, v in inputs.items()}
    nc = build()
    in_maps = [host_layout(inp, b) for b in range(8)]
    res = run_bass_kernel_spmd(nc, in_maps, core_ids=list(range(8)))
    return np.stack([np.asarray(r["out"], dtype=np.float32) for r in res.results], axis=0)
```

```python
from contextlib import ExitStack
import numpy as np
import concourse.bass as bass
import concourse.mybir as mybir
from concourse.alu_op_type import AluOpType as ALU
from concourse.bass_utils import run_bass_kernel_spmd

F32 = mybir.dt.float32
F32R = mybir.dt.float32r
BF16 = mybir.dt.bfloat16
AF = mybir.ActivationFunctionType

ENGS = ("pe", "dve", "act", "pool", "sp")
INF = 1 << 60
EPS = 1e-6


class _Rec:
    def __init__(self):
        self.call = None

    def __getattr__(self, name):
        def f(*a, **k):
            self.call = (name, a, k)
            return self
        return f


def _capture(fn):
    r = _Rec()
    fn(r)
    assert r.call is not None
    return r.call


class Prog:
    def __init__(self, nc, stack, n_dma_sems=30):
        self.nc = nc
        self.ops = {e: [] for e in ENGS}
        self.cnt = {e: 0 for e in ENGS}
        self.sem = {e: stack.enter_context(nc.semaphore("s_" + e)) for e in ENGS}
        self.dsem = [stack.enter_context(nc.semaphore("d%d" % i)) for i in range(n_dma_sems)]
        self.dcnt = [0] * n_dma_sems
        self.dnext = 0
        self.dnext_q = {}
        self.waited = {e: {} for e in ENGS}
        self.res = {}
        self.nops = 0
        self.psrd = {}

    def _deps(self, reads, writes):
        deps = []
        for (name, lo, hi) in reads:
            st = self.res.setdefault(name, {"w": [], "r": []})
            for (a, b, tok) in st["w"]:
                if a < hi and lo < b:
                    deps.append(tok)
        for (name, lo, hi) in writes:
            st = self.res.setdefault(name, {"w": [], "r": []})
            for (a, b, tok) in st["w"]:
                if a < hi and lo < b:
                    deps.append(tok)
            for (a, b, tok) in st["r"]:
                if a < hi and lo < b:
                    deps.append(tok)
        return deps

    def _record(self, reads, writes, tok):
        for (name, lo, hi) in writes:
            st = self.res[name]
            st["w"] = [(a, b, t) for (a, b, t) in st["w"] if not (lo <= a and b <= hi)]
            st["r"] = [(a, b, t) for (a, b, t) in st["r"] if not (lo <= a and b <= hi)]
            st["w"].append((lo, hi, tok))
        for (name, lo, hi) in reads:
            st = self.res[name]
            st["r"] = [(a, b, t) for (a, b, t) in st["r"]
                       if not (t[0] == tok[0] and lo <= a and b <= hi)]
            st["r"].append((lo, hi, tok))

    @staticmethod
    def _norm(rs):
        out = []
        for r in rs:
            if isinstance(r, str):
                out.append((r, 0, INF))
            elif len(r) == 2:
                out.append((r[0], r[1], r[1] + 1))
            else:
                out.append(tuple(r))
        return out

    def _waits(self, eng, deps):
        ws = {}
        for (key, val, deng) in deps:
            if deng == eng and eng == "pe":
                continue
            if self.waited[eng].get(key, 0) >= val:
                continue
            ws[key] = max(ws.get(key, 0), val)
        for k, v in ws.items():
            self.waited[eng][k] = v
        return list(ws.items())

    def op(self, eng, fn, reads=(), writes=(), inc=True):
        reads = self._norm(reads)
        writes = self._norm(writes)
        deps = self._deps(reads, writes)
        psread = eng in ("dve", "act") and any(r[0] == "ps" for r in reads)
        if psread:
            other = "act" if eng == "dve" else "dve"
            if other in self.psrd:
                deps.append(self.psrd[other])
        waits = self._waits(eng, deps)
        idx = self.cnt[eng] + 1
        if inc:
            self.cnt[eng] = idx
        tok = (("e", eng), idx, eng)
        if psread:
            self.psrd[eng] = tok
        self._record(reads, writes, tok)
        self.ops[eng].append((waits, _capture(fn), ("e", eng) if inc else None, 1))
        self.nops += 1
        return tok

    def dma(self, eng, fn, reads=(), writes=()):
        reads = self._norm(reads)
        writes = self._norm(writes)
        deps = self._deps(reads, writes)
        nd = len(self.dsem)
        lo, hi = (0, (nd * 3) // 5) if eng == "sp" else ((nd * 3) // 5, nd)
        cur = self.dnext_q.get(eng, lo)
        j = cur
        self.dnext_q[eng] = lo + (cur + 1 - lo) % (hi - lo)
        if self.dcnt[j] > 0:
            deps.append((("d", j), 16 * self.dcnt[j], "dma"))
        waits = self._waits(eng, deps)
        self.dcnt[j] += 1
        tok = (("d", j), 16 * self.dcnt[j], "dma")
        self._record(reads, writes, tok)
        self.ops[eng].append((waits, _capture(fn), ("d", j), 16))
        self.nops += 1
        return tok

    def barrier(self):
        deps = [(("e", e), self.cnt[e], "x") for e in ENGS if self.cnt[e] > 0]
        deps += [(("d", j), 16 * c, "dma") for j, c in enumerate(self.dcnt) if c > 0]
        for e in ENGS:
            waits = self._waits(e, [d for d in deps if d[0] != ("e", e)])
            if waits:
                self.ops[e].append((waits, None, None, 0))
        self.res = {}

    def final_wait(self, eng, toks):
        waits = self._waits(eng, list(toks))
        self.ops[eng].append((waits, None, None, 0))

    def _semobj(self, key):
        return self.sem[key[1]] if key[0] == "e" else self.dsem[key[1]]

    def emit(self, block):
        def mk(ename):
            def body(e):
                for (waits, fn, inckey, incv) in self.ops[ename]:
                    for (k, v) in waits:
                        e.wait_ge(self._semobj(k), v)
                    if fn is None:
                        continue
                    name, a, k = fn
                    ins = getattr(e, name)(*a, **k)
                    if inckey is not None:
                        ins.then_inc(self._semobj(inckey), incv)
            return body
        if self.ops["sp"]:
            block.sync(mk("sp"))
        if self.ops["pe"]:
            block.tensor(mk("pe"))
        if self.ops["dve"]:
            block.vector(mk("dve"))
        if self.ops["act"]:
            block.scalar(mk("act"))
        if self.ops["pool"]:
            block.gpsimd(mk("pool"))


D = 2048
S = 2048
NCTX = 256
NALL = NCTX + S
DIN = 6176
DFF = 8192
NCH = NALL // 128
XW = 2310
TW = 2307
LT = 259
NEGBIG = -30000.0


def ch_off(ch):
    return ch * 128 if ch < 2 else LT + (ch - 2) * 128


def build(dbg=False, phases=("A", "B1", "B2", "C", "D")):
    nc = bass.Bass("TRN2", target_bir_lowering=False)
    din = lambda n, s, d=F32: nc.dram_tensor(n, s, d, kind="ExternalInput").ap()
    x_d = din("x", [S, D]); ctx_d = din("ctx", [NCTX, D]); cc_d = din("cc", [128, 32])
    wmod_d = din("w_mod", [D, 6 * D]); bmodT_d = din("b_modT", [128, 96]); nwT_d = din("nwT", [128, 64])
    win_d = din("w_in", [D, DIN])
    lcw_d = din("lcw", [128, 32]); lcb_d = din("lcb", [128, 8]); lgw_d = din("lgw", [128, 4096])
    lgb_d = din("lgb", [128, 32]); llam_d = din("llam", [128, 16])
    gcw_d = din("gcw", [128, 96]); galog_d = din("galog", [128, 16]); gdtb_d = din("gdtb", [128, 16])
    gnw_d = din("gnw", [128, 1])
    wout_d = din("w_out", [D, D]); w1_d = din("w_ff1", [D, DFF]); w2_d = din("w_ff2", [DFF, D])
    out_d = nc.dram_tensor("out", [S, D], F32, kind="ExternalOutput").ap()
    skind = "ExternalOutput" if dbg else "Internal"
    mix_scr = nc.dram_tensor("mix_scr", [16, 128, S], BF16, kind=skind).ap()
    h2_scr = nc.dram_tensor("h2_scr", [16, 128, S], BF16, kind=skind).ap()
    x1_scr = nc.dram_tensor("x1_scr", [S, D], F32, kind=skind).ap()
    w1b = nc.dram_tensor("w1b", [64, 128, 16, 128], BF16, kind="Internal").ap()
    w2b = nc.dram_tensor("w2b", [16, 128, 64, 128], BF16, kind="Internal").ap()
    if dbg:
        hT_dbg = nc.dram_tensor("hT_dbg", [128, 16, NALL], BF16, kind="ExternalOutput").ap()
        modT_dbg = nc.dram_tensor("modT_dbg", [128, 192], F32, kind="ExternalOutput").ap()

    with ExitStack() as top:
        P = Prog(nc, top)
        psb = [top.enter_context(nc.psum_tensor("ps%d" % i, [128, 512], F32)) for i in range(8)]
        pst = {"i": 0}

        reserved = set()

        def bank():
            while True:
                i = pst["i"]; pst["i"] = (i + 1) % 8
                if i not in reserved:
                    return i

        sbt = lambda st, n, s, d=F32: st.enter_context(nc.sbuf_tensor("sb_" + n, s, d))
        ident = sbt(top, "ident", [128, 128]); ones = sbt(top, "ones", [128, 128])
        onesb = sbt(top, "onesb", [128, 128], BF16)
        par = sbt(top, "par", [128, 64 + 96 + 32 + 8 + 32 + 16 + 96 + 16 + 16 + 1])
        o_ = [0]
        def psl(n):
            a = o_[0]; o_[0] += n
            return (a, a + n)
        s_nw, s_bm, s_lcw, s_lcb, s_lgb, s_llam, s_gcw, s_gal, s_gdt, s_gnw = [psl(n) for n in (64, 96, 32, 8, 32, 16, 96, 16, 16, 1)]
        pv = lambda s: par[:, s[0]:s[1]]
        modT = sbt(top, "modT", [128, 192])
        vecs = sbt(top, "vecs", [128, 8, 16])
        cneg = sbt(top, "cneg", [128, 16])
        sccb = sbt(top, "sccb", [128, 32], BF16)
        C3 = [128, NCH, 16]
        beta = sbt(top, "beta", C3); nbeta = sbt(top, "nbeta", C3); gg = sbt(top, "gg", C3); gam = sbt(top, "gam", C3)
        ngam = sbt(top, "ngam", C3); glast = sbt(top, "glast", C3); cdl = sbt(top, "cdl", C3); neg_eg = sbt(top, "neg_eg", C3); kdec = sbt(top, "kdec", C3)
        nega = sbt(top, "nega", [128, 16])
        trif = sbt(top, "trif", [128, 128]); trib = sbt(top, "trib", [128, 128])
        negm = [sbt(top, "negm%d" % d, [128, 128]) for d in range(2)]
        offd = sbt(top, "offd", [128, 128]); bd32 = sbt(top, "bd32", [128, 128]); od64 = sbt(top, "od64", [128, 128]); od128 = sbt(top, "od128", [128, 128])
        s_h = ExitStack()
        hT = sbt(s_h, "hT", [128, 16, NALL], BF16)
        p_scr = nc.dram_tensor("p_scr", [8, 3, 128, NALL], F32, kind="Internal").ap()
        zs_scr = nc.dram_tensor("zs_scr", [8, 128, S], BF16, kind="Internal").ap()
        xs_scr = nc.dram_tensor("xs_scr", [8, 128, NALL], F32, kind="Internal").ap()
        gy_scr = nc.dram_tensor("gy_scr", [8, 128, S], BF16, kind="Internal").ap()
        oacc_scr = nc.dram_tensor("oacc_scr", [16, 128, 128], F32, kind="Internal").ap()

        def make_row(st, vi, dn):
            dst = sbt(st, dn, [128, D]); dg = sbt(st, "dg_" + dn, [128, 512])
            for g4 in range(4):
                for q in range(4):
                    k = g4 * 4 + q
                    P.op("dve", lambda e: e.tensor_scalar(out=dg[:, q * 128:(q + 1) * 128], in0=ident[:], scalar1=vecs[:, vi, k:k + 1], scalar2=None, op0=ALU.mult),
                         reads=["vecs", "ident"], writes=[("dg", q)])
                b = bank()
                P.op("pe", lambda e: e.matmul(psb[b][:], lhsT=ones[:], rhs=dg[:], start=True, stop=True), reads=["ones", "dg"], writes=[("ps", b)])
                P.op("act", lambda e: e.copy(out=dst[:, g4 * 512:(g4 + 1) * 512], in_=psb[b][:]), reads=[("ps", b)], writes=[(dn, g4)])
            return dst

        def load(dst, src, q="sp", name=None, reads=()):
            return P.dma(q, lambda e: e.dma_start(out=dst, in_=src), reads=reads, writes=[name] if name else [])

        P.op("pool", lambda e: e.memset(ones[:], 1.0), writes=["ones"])
        P.op("pool", lambda e: e.memset(onesb[:], 1.0), writes=["onesb"])
        P.op("pool", lambda e: e.memset(ident[:], 1.0), writes=["ident"])
        P.op("pool", lambda e: e.affine_select(out=ident[:], in_=ident[:], pattern=[[-1, 128]], compare_op=ALU.is_equal,
                                               fill=0.0, base=0, channel_multiplier=1), reads=["ident"], writes=["ident"])
        for (sl, src) in ((s_nw, nwT_d), (s_bm, bmodT_d), (s_lcw, lcw_d), (s_lcb, lcb_d), (s_lgb, lgb_d), (s_llam, llam_d),
                          (s_gcw, gcw_d), (s_gal, galog_d), (s_gdt, gdtb_d), (s_gnw, gnw_d)):
            load(pv(sl), src, name=("par", sl[0], sl[1]))

        with ExitStack() as sa:
            cc = sbt(sa, "cc", [128, 32]); scc = sbt(sa, "scc", [128, 32])
            wm = [sbt(sa, "wm%d" % i, [128, 4096], BF16) for i in range(3)]
            load(cc[:], cc_d, name="cc")
            P.op("act", lambda e: e.activation(out=scc[:], in_=cc[:], func=AF.Silu), reads=["cc"], writes=["scc"])
            P.op("dve", lambda e: e.tensor_copy(out=sccb[:], in_=scc[:]), reads=["scc"], writes=["sccb"])
            mb = bank()
            first = True
            for k in range(16):
                i = k % 3
                P.dma("pool", lambda e: e.dma_start(out=wm[i][:], in_=wmod_d[k * 128:(k + 1) * 128, 0:4096]), writes=["wm%d" % i])
                for j in range(32):
                    P.op("pe", lambda e: e.matmul(psb[mb][:, j * 2:j * 2 + 2], lhsT=wm[i][:, j * 128:(j + 1) * 128], rhs=sccb[:, k * 2:k * 2 + 2],
                                                  start=first, stop=(k == 15), skip_group_check=True),
                         reads=["wm%d" % i, "sccb"], writes=[("ps", mb)], inc=(j == 31))
                    first = False
            P.op("dve", lambda e: e.tensor_tensor(out=modT[:, 0:64].rearrange("p (j r) -> p j r", r=2),
                                                  in0=psb[mb][:, 0:64].rearrange("p (j r) -> p j r", r=2),
                                                  in1=pv(s_bm)[:, 0:32].unsqueeze(2).to_broadcast([128, 32, 2]), op=ALU.add),
                 reads=[("ps", mb), ("par", s_bm[0], s_bm[1])], writes=[("modT", 0, 64)])
            mv = modT[:].rearrange("p (j r) -> p j r", r=2)
            nw = pv(s_nw).rearrange("p (w k) -> p w k", w=4)
            rp = [("par", s_nw[0], s_nw[1]), "modT"]
            def scl(dst, sc, w):
                P.op("dve", lambda e: e.scalar_tensor_tensor(out=dst, in0=sc, scalar=1.0, in1=w, op0=ALU.add, op1=ALU.mult),
                     reads=rp, writes=["vecs"])
            scl(vecs[:, 0, :], mv[:, 16:32, 0], nw[:, 0, :])
            P.op("dve", lambda e: e.tensor_copy(out=vecs[:, 1, :], in_=mv[:, 0:16, 0]), reads=rp, writes=["vecs"])
            scl(vecs[:, 2, :], mv[:, 16:32, 1], nw[:, 0, :])
            P.op("dve", lambda e: e.tensor_copy(out=vecs[:, 3, :], in_=mv[:, 0:16, 1]), reads=rp, writes=["vecs"])
            P.op("act", lambda e: e.activation(out=cneg[:], in_=pv(s_llam), func=AF.Exp, scale=-1.0), reads=[("par", s_llam[0], s_llam[1])], writes=["cneg"])
            P.op("act", lambda e: e.activation(out=cneg[:], in_=cneg[:], func=AF.Ln, bias=1.0), reads=["cneg"], writes=["cneg"])
            P.op("dve", lambda e: e.tensor_scalar(out=cneg[:], in0=cneg[:], scalar1=-8.0, scalar2=None, op0=ALU.mult), reads=["cneg"], writes=["cneg"])
            if dbg:
                load(modT_dbg, modT[:], reads=["modT"])

            xt = [sbt(sa, "xt%d" % i, [128, D]) for i in range(2)]
            junk = sbt(sa, "junkA", [128, D])
            st1 = [sbt(sa, "st1_%d" % i, [128, 4]) for i in range(2)]

            def norm_T(src_ap, tname, ti, scl_i, sh_i, dstT, dname, col0, stt, stn):
                t = xt[ti]
                P.op("act", lambda e: e.activation(out=junk[:], in_=t[:], func=AF.Square, accum_out=stt[:, 0:1]),
                     reads=[tname], writes=["junkA", stn])
                P.op("act", lambda e: e.activation(out=stt[:, 1:2], in_=stt[:, 0:1], func=AF.Sqrt, scale=1.0 / D, bias=EPS), reads=[stn], writes=[stn])
                P.op("dve", lambda e: e.reciprocal(out=stt[:, 2:3], in_=stt[:, 1:2]), reads=[stn], writes=[stn])
                P.op("act", lambda e: e.activation(out=t[:], in_=t[:], func=AF.Copy, scale=stt[:, 2:3]), reads=[stn, tname], writes=[tname])
                for g4 in range(4):
                    b = bank()
                    for q in range(4):
                        k = g4 * 4 + q
                        P.op("pe", (lambda b, q, k: lambda e: e.transpose(psb[b][:, q * 128:(q + 1) * 128], t[:, k * 128:(k + 1) * 128], ident[:]))(b, q, k),
                             reads=[tname, "ident"], writes=[("ps", b)], inc=(q == 3))
                    for q in range(4):
                        k = g4 * 4 + q
                        eng = "dve" if g4 % 2 == 0 else "act"
                        if eng == "dve":
                            fn = (lambda b, q, k: lambda e: e.tensor_scalar(out=dstT[:, k, col0:col0 + 128], in0=psb[b][:, q * 128:(q + 1) * 128],
                                                                            scalar1=vecs[:, scl_i, k:k + 1], scalar2=vecs[:, sh_i, k:k + 1],
                                                                            op0=ALU.mult, op1=ALU.add))(b, q, k)
                        else:
                            fn = (lambda b, q, k: lambda e: e.activation(out=dstT[:, k, col0:col0 + 128], in_=psb[b][:, q * 128:(q + 1) * 128],
                                                                         func=AF.Identity, scale=vecs[:, scl_i, k:k + 1], bias=vecs[:, sh_i, k:k + 1]))(b, q, k)
                        P.op(eng, fn, reads=[("ps", b), "vecs"], writes=[(dname, k * 100000 + col0, k * 100000 + col0 + 128)])

            import os as _os
            for ti in range(int(_os.environ.get("K_NTI", NCH))):
                src = ctx_d[ti * 128:(ti + 1) * 128, :] if ti < 2 else x_d[(ti - 2) * 128:(ti - 1) * 128, :]
                i = ti % 2
                load(xt[i][:], src, name="xt%d" % i)
                norm_T(src, "xt%d" % i, i, 2 if ti < 2 else 0, 3 if ti < 2 else 1, hT, "hT", ti * 128, st1[i], "st1_%d" % i)
            if dbg:
                load(hT_dbg, hT[:], reads=["hT"])
        P.barrier()

        hTr = lambda k, c0, c1: ("hT", k * 100000 + c0, k * 100000 + c1)
        hT_all = [("hT", 0, INF)]

        def precast():
            import os as _os
            if _os.environ.get("K_NOPRECAST"):
                return
            w1v = w1_d.rearrange("(k p) (o c) -> o p k c", p=128, c=128)
            for o in range(0, 64, 4):
                for oo in range(4):
                    P.dma("pool", (lambda o: lambda e: e.dma_start(out=w1b[o], in_=w1v[o]))(o + oo), writes=[("w1b", o + oo)])
            w2v = w2_d.rearrange("(k p) (o c) -> o p k c", p=128, c=128)
            for o in range(16):
                for kh in range(4):
                    P.dma("pool", (lambda o, kh: lambda e: e.dma_start(out=w2b[o][:, kh * 16:(kh + 1) * 16, :], in_=w2v[o][:, kh * 16:(kh + 1) * 16, :]))(o, kh),
                          writes=[("w2b", o * 4 + kh)])

        winv = win_d.rearrange("(k p) c -> p k c", p=128)

        def inproj(wt, wname, woff, tok_groups, consume):
            for gi, (c0, n) in enumerate(tok_groups):
                b = bank()
                for k in range(16):
                    P.op("pe", (lambda b, k, c0, n: lambda e: e.matmul(psb[b][:, 0:n], lhsT=wt[:, k, woff:woff + 128], rhs=hT[:, k, c0:c0 + n],
                                                                          start=(k == 0), stop=(k == 15)))(b, k, c0, n),
                         reads=[wname, hTr(k, c0, c0 + n)], writes=[("ps", b)], inc=(k == 15))
                consume(b, gi)

        TG_ALL = [(0, 256)] + [(256 + g * 512, 512) for g in range(4)]
        TG_LAT = [(256 + g * 512, 512) for g in range(4)]

        if "B2" in phases:
          with ExitStack() as sb2a:
            wgb = sbt(sb2a, "wgb", [128, 16, 32], BF16)
            def msk(t, tn, base_val, fill, pattern, cmp, base, cm, src=None):
                if src is None:
                    P.op("pool", lambda e: e.memset(t[:], base_val), writes=[tn])
                P.op("pool", lambda e: e.affine_select(out=t[:], in_=t[:], pattern=pattern, compare_op=cmp, fill=fill, base=base, channel_multiplier=cm),
                     reads=[tn], writes=[tn])
            msk(trif, "trif", 1.0, 0.0, [[1, 128]], ALU.is_ge, 0, -1)
            msk(trib, "trib", 1.0, 0.0, [[-1, 128]], ALU.is_ge, 0, 1)
            msk(negm[0], "negm0", 0.0, NEGBIG, [[1, 128]], ALU.is_ge, 0, -1)
            msk(negm[1], "negm1", 0.0, NEGBIG, [[-1, 128]], ALU.is_ge, 0, 1)
            msk(offd, "offd", 1.0, 0.0, [[-1, 128]], ALU.not_equal, 0, 1)
            for (t, tn, fn_) in ((bd32, "bd32", lambda pb, cb: 1.0 if pb == cb else 0.0),
                                 (od64, "od64", lambda pb, cb: 1.0 if (pb != cb and pb // 2 == cb // 2) else 0.0),
                                 (od128, "od128", lambda pb, cb: 1.0 if pb // 2 != cb // 2 else 0.0)):
                for pb in range(4):
                    for cb in range(4):
                        P.op("pool", (lambda t, pb, cb, v: lambda e: e.memset(t[pb * 32:(pb + 1) * 32, cb * 32:(cb + 1) * 32], v))(t, pb, cb, fn_(pb, cb)), writes=[tn])
            P.dma("pool", lambda e: e.dma_start(out=wgb[:], in_=winv[:, :, 6144:6176]), writes=["wgb"])
            gbank = [bank(), bank()]
            for ch in range(NCH):
                b = gbank[ch // 9]; co = (ch % 9) * 32
                for k in range(16):
                    P.op("pe", (lambda b, co, ch, k: lambda e: e.matmul(psb[b][:, co:co + 32], lhsT=hT[:, k, ch * 128:(ch + 1) * 128], rhs=wgb[:, k, :],
                                                                          start=(k == 0), stop=(k == 15), skip_group_check=True))(b, co, ch, k),
                         reads=["wgb"] + hT_all, writes=[("ps", b)], inc=(k == 15))
            P.op("act", lambda e: e.activation(out=nega[:], in_=pv(s_gal), func=AF.Exp), reads=[("par", 0, INF)], writes=["nega"])
            P.op("dve", lambda e: e.tensor_scalar(out=nega[:], in0=nega[:], scalar1=-1.0, scalar2=None, op0=ALU.mult), reads=["nega"], writes=["nega"])
            for half in range(2):
                b = gbank[half]
                pvw = psb[b][:, 0:288].rearrange("p (c f) -> p c f", f=32)
                cs = slice(half * 9, half * 9 + 9)
                P.op("act", (lambda pvw, cs: lambda e: e.activation(out=beta[:, cs, :], in_=pvw[:, :, 0:16], func=AF.Sigmoid))(pvw, cs), reads=[("ps", b)], writes=["beta"])
                P.op("dve", (lambda pvw, cs: lambda e: e.tensor_tensor(out=gg[:, cs, :], in0=pvw[:, :, 16:32], in1=pv(s_gdt).unsqueeze(1).to_broadcast([128, 9, 16]), op=ALU.add))(pvw, cs),
                     reads=[("ps", b), ("par", 0, INF)], writes=["gg"])
            P.op("act", lambda e: e.activation(out=gg[:], in_=gg[:], func=AF.Exp), reads=["gg"], writes=["gg"])
            P.op("act", lambda e: e.activation(out=gg[:], in_=gg[:], func=AF.Ln, bias=1.0), reads=["gg"], writes=["gg"])
            P.op("dve", lambda e: e.tensor_tensor(out=gg[:], in0=gg[:], in1=nega[:].unsqueeze(1).to_broadcast([128, NCH, 16]), op=ALU.mult), reads=["gg", "nega"], writes=["gg"])
            P.op("dve", lambda e: e.tensor_scalar(out=nbeta[:], in0=beta[:], scalar1=-1.0, scalar2=None, op0=ALU.mult), reads=["beta"], writes=["nbeta"])
            for d in range(2):
                b = bank()
                tri = trif if d == 0 else trib
                P.op("pe", (lambda b, tri, d: lambda e: e.matmul(psb[b][:, 0:NCH * 8].rearrange("p (c f) -> p c f", f=8), lhsT=tri[:], rhs=gg[:, :, d * 8:(d + 1) * 8], start=True, stop=True))(b, tri, d),
                     reads=["trif", "trib", "gg"], writes=[("ps", b)])
                P.op("act", (lambda b, d: lambda e: e.copy(out=gam[:, :, d * 8:(d + 1) * 8], in_=psb[b][:, 0:NCH * 8].rearrange("p (c f) -> p c f", f=8)))(b, d),
                     reads=[("ps", b)], writes=["gam"])
            b = bank()
            P.op("pe", (lambda b: lambda e: e.matmul(psb[b][:, 0:NCH * 16], lhsT=ones[:], rhs=gg[:].rearrange("p c f -> p (c f)"), start=True, stop=True))(b),
                 reads=["ones", "gg"], writes=[("ps", b)])
            P.op("act", (lambda b: lambda e: e.copy(out=glast[:].rearrange("p c f -> p (c f)"), in_=psb[b][:, 0:NCH * 16]))(b), reads=[("ps", b)], writes=["glast"])
            P.op("dve", lambda e: e.tensor_scalar(out=ngam[:], in0=gam[:], scalar1=-1.0, scalar2=None, op0=ALU.mult), reads=["gam"], writes=["ngam"])
            P.op("act", lambda e: e.activation(out=cdl[:], in_=glast[:], func=AF.Exp), reads=["glast"], writes=["cdl"])
            P.op("dve", lambda e: e.tensor_tensor(out=kdec[:], in0=glast[:], in1=gam[:], op=ALU.subtract), reads=["glast", "gam"], writes=["kdec"])
            P.op("act", lambda e: e.activation(out=kdec[:], in_=kdec[:], func=AF.Exp), reads=["kdec"], writes=["kdec"])
            P.op("act", lambda e: e.activation(out=neg_eg[:], in_=gam[:], func=AF.Exp), reads=["gam"], writes=["neg_eg"])
            P.op("dve", lambda e: e.tensor_scalar(out=neg_eg[:], in0=neg_eg[:], scalar1=-1.0, scalar2=None, op0=ALU.mult), reads=["neg_eg"], writes=["neg_eg"])

            wm2 = [sbt(sb2a, "wm2_%d" % i, [128, 4096], BF16) for i in range(3)]
            mb2 = bank(); reserved.add(mb2)
            mpieces = [(k, pc) for k in range(16) for pc in (1, 2)]
            mpi = [0]

            def mod_pieces(n):
                for _ in range(n):
                    if mpi[0] >= len(mpieces):
                        return
                    k, pc = mpieces[mpi[0]]
                    i = mpi[0] % 3
                    first = (mpi[0] == 0)
                    mpi[0] += 1
                    P.dma("pool", lambda e: e.dma_start(out=wm2[i][:], in_=wmod_d[k * 128:(k + 1) * 128, pc * 4096:(pc + 1) * 4096]), writes=["wm2_%d" % i])
                    for j in range(32):
                        jj = (pc - 1) * 32 + j
                        P.op("pe", lambda e: e.matmul(psb[mb2][:, jj * 2:jj * 2 + 2], lhsT=wm2[i][:, j * 128:(j + 1) * 128], rhs=sccb[:, k * 2:k * 2 + 2],
                                                      start=(first and j == 0), stop=(k == 15), skip_group_check=True),
                             reads=["wm2_%d" % i, "sccb"], writes=[("ps", mb2)], inc=(j == 31))

            wg = [sbt(sb2a, "wg%d" % i, [128, 16, 512], BF16) for i in range(2)]
            stgs = [sbt(sb2a, "stgs%d" % i, [128, 512]) for i in range(4)]
            zst = [sbt(sb2a, "zst%d" % i, [128, 512], BF16) for i in range(2)]
            sti = [0]
            for h in range(8):
                w = wg[h % 2]; wn = "wg%d" % (h % 2)
                for t4 in range(4):
                    c0 = 2048 + t4 * 1024 + h * 128
                    P.dma("pool", lambda e: e.dma_start(out=w[:, :, t4 * 128:(t4 + 1) * 128], in_=winv[:, :, c0:c0 + 128]), writes=[(wn, t4)])
                for t3 in range(3):
                    def cons_p(b, gi):
                        c0, n = TG_ALL[gi]
                        i = sti[0] % 4; sti[0] += 1
                        P.op("act", lambda e: e.copy(out=stgs[i][:, 0:n], in_=psb[b][:, 0:n]), reads=[("ps", b)], writes=["stgs%d" % i])
                        load(p_scr[h, t3][:, c0:c0 + n], stgs[i][:, 0:n], reads=["stgs%d" % i], name=("p_scr", h * 3 + t3))
                    inproj(w, (wn, t3), t3 * 128, TG_ALL, cons_p)

                def cons_zs(b, gi):
                    i = gi % 2
                    P.op("act", lambda e: e.activation(out=zst[i][:], in_=psb[b][:], func=AF.Silu), reads=[("ps", b)], writes=["zst%d" % i])
                    load(zs_scr[h][:, gi * 512:(gi + 1) * 512], zst[i][:], reads=["zst%d" % i], name=("zs_scr", h))
                inproj(w, (wn, 3), 384, TG_LAT, cons_zs)
                mod_pieces(2)
            xst = [sbt(sb2a, "xst%d" % i, [128, NALL]) for i in range(2)]
            gyst = [sbt(sb2a, "gyst%d" % i, [128, S], BF16) for i in range(2)]
            for h in range(8):
                w = wg[h % 2]; wn = "wg%d" % (h % 2)
                P.dma("pool", lambda e: e.dma_start(out=w[:, :, 0:128], in_=winv[:, :, h * 128:(h + 1) * 128]), writes=[(wn, 0)])
                P.dma("pool", lambda e: e.dma_start(out=w[:, :, 128:256], in_=winv[:, :, 1024 + h * 128:1024 + (h + 1) * 128]), writes=[(wn, 1)])
                xs = xst[h % 2]; xsn = "xst%d" % (h % 2)
                gs_ = gyst[h % 2]; gsn = "gyst%d" % (h % 2)

                def cons_x(b, gi):
                    if gi == 0:
                        P.op("act", lambda e: e.copy(out=xs[:, 0:256], in_=psb[b][:, 0:256]), reads=[("ps", b)], writes=[(xsn, 0)])
                    else:
                        r0 = (gi - 1) * 8
                        ov = xs[:, 256:NALL].rearrange("p (c r) -> p r c", r=32)[:, r0:r0 + 8, :]
                        iv = psb[b][:].rearrange("p (r c) -> p r c", c=64)
                        P.op("act", lambda e: e.copy(out=ov, in_=iv), reads=[("ps", b)], writes=[(xsn, gi)])
                inproj(w, (wn, 0), 0, TG_ALL, cons_x)
                load(xs_scr[h], xs[:], reads=[xsn], name=("xs_scr", h))

                def cons_y(b, gi):
                    P.op("act", lambda e: e.activation(out=gs_[:, gi * 512:(gi + 1) * 512], in_=psb[b][:], func=AF.Gelu_apprx_tanh), reads=[("ps", b)], writes=[(gsn, gi)])
                inproj(w, (wn, 1), 128, TG_LAT, cons_y)
                load(gy_scr[h], gs_[:], reads=[gsn], name=("gy_scr", h))
                mod_pieces(2)
            mod_pieces(100)
            P.op("dve", lambda e: e.tensor_tensor(out=modT[:, 64:192].rearrange("p (j r) -> p j r", r=2),
                                                  in0=psb[mb2][:, 0:128].rearrange("p (j r) -> p j r", r=2),
                                                  in1=pv(s_bm)[:, 32:96].unsqueeze(2).to_broadcast([128, 64, 2]), op=ALU.add),
                 reads=[("ps", mb2), ("par", 0, INF)], writes=[("modT", 64, 192)])
            reserved.discard(mb2)
            mv = modT[:].rearrange("p (j r) -> p j r", r=2)
            nw = pv(s_nw).rearrange("p (w k) -> p w k", w=4)
            rp = [("par", 0, INF), "modT"]
            P.op("dve", lambda e: e.tensor_tensor(out=vecs[:, 4, :], in0=mv[:, 32:48, 0], in1=nw[:, 1, :], op=ALU.mult), reads=rp, writes=[("vecs", 4)])
            P.op("dve", lambda e: e.scalar_tensor_tensor(out=vecs[:, 5, :], in0=mv[:, 64:80, 0], scalar=1.0, in1=nw[:, 2, :], op0=ALU.add, op1=ALU.mult), reads=rp, writes=[("vecs", 5)])
            P.op("dve", lambda e: e.tensor_copy(out=vecs[:, 6, :], in_=mv[:, 48:64, 0]), reads=rp, writes=[("vecs", 6)])
            P.op("dve", lambda e: e.tensor_tensor(out=vecs[:, 7, :], in0=mv[:, 80:96, 0], in1=nw[:, 3, :], op=ALU.mult), reads=rp, writes=[("vecs", 7)])
          P.barrier()
        s_h.close()

        freeb = set(range(8))

        def take(n):
            while len(freeb) < n:
                yield
            return [freeb.pop() for _ in range(n)]

        def rel(*bs):
            for b_ in bs:
                assert b_ not in freeb
                freeb.add(b_)

        def run_tasks(gens):
            active = list(gens)
            while active:
                for g in list(active):
                    try:
                        next(g)
                    except StopIteration:
                        active.remove(g)

        if "B1" in phases:
          with ExitStack() as sb1:
            lgw = sbt(sb1, "lgw", [128, 4096], BF16)
            P.dma("pool", lambda e: e.dma_start(out=lgw[:], in_=lgw_d), writes=["lgw"])
            xpad2 = [sbt(sb1, "xpad0", [128, XW])] * 2
            gyb2 = [sbt(sb1, "gyb%d" % i, [128, S], BF16) for i in range(2)]
            acc2 = [sbt(sb1, "acc0", [128, TW])] * 2; xrb2 = [sbt(sb1, "xrb0", [128, TW], BF16)] * 2
            mixs2 = [sbt(sb1, "mixs%d" % i, [128, S], BF16) for i in range(2)]
            RtA = [[sbt(sb1, "Rt%d_%d" % (d, i), [128, TW]) for d in range(2)] for i in range(2)]
            ItA = [[sbt(sb1, "It%d_%d" % (d, i), [128, TW]) for d in range(2)] for i in range(2)]
            StA = [[sbt(sb1, "St%d_%d" % (d, i), [128, TW]) for d in range(2)] for i in range(2)]
            P.op("pool", lambda e: e.memset(xpad2[0][:], 0.0), writes=["xpad0"])
            lcw = pv(s_lcw).rearrange("p (h j) -> p h j", j=4); lcb = pv(s_lcb)
            lgb = pv(s_lgb).rearrange("p (d g h) -> p d g h", d=2, g=2)
            rpar = [("par", 0, INF)]
            BLK = [(0, 256)] + [(LT + i * 512, 512) for i in range(4)]
            rg_ = lambda nm, c0, n: (nm, c0, c0 + n)

            xp = xpad2[0]; xpn = "xpad0"; acc = acc2[0]; accn = "acc0"; xrb = xrb2[0]; xrn = "xrb0"

            def lpre(h):
                hb = h % 2
                load(xp[:, 2:258], xs_scr[h][:, 0:256], reads=[("xs_scr", h)], name=(xpn, 2, 258))
                load(xp[:, 261:2309], xs_scr[h][:, 256:NALL], reads=[("xs_scr", h)], name=(xpn, 261, 2309))
                load(gyb2[hb][:], gy_scr[h], reads=[("gy_scr", h)], name="gyb%d" % hb)
                P.op("act", lambda e: e.activation(out=acc[:], in_=xp[:, 0:TW], func=AF.Identity, scale=lcw[:, h, 0:1], bias=lcb[:, h:h + 1]), reads=[xpn] + rpar, writes=[accn])
                for j in (1, 2, 3):
                    P.op("dve", lambda e: e.scalar_tensor_tensor(out=acc[:], in0=xp[:, j:j + TW], scalar=lcw[:, h, j:j + 1], in1=acc[:], op0=ALU.mult, op1=ALU.add),
                         reads=[xpn, accn] + rpar, writes=[accn])
                P.op("pool", lambda e: e.tensor_copy(out=xrb[:], in_=acc[:]), reads=[accn], writes=[xrn])

            def lst12(h):
                hb = h % 2
                Rt, It, St = RtA[hb], ItA[hb], StA[hb]
                sfx = "_%d" % hb
                for (c0, n) in BLK:
                    for d in range(2):
                        for g, dst, dn in ((0, Rt[d], "Rt%d" % d + sfx), (1, It[d], "It%d" % d + sfx)):
                            woff = ((d * 2 + g) * 8 + h) * 128
                            bb = bank()
                            P.op("pe", lambda e: e.matmul(psb[bb][:, 0:n], lhsT=lgw[:, woff:woff + 128], rhs=xrb[:, c0:c0 + n], start=True, stop=True),
                                 reads=["lgw", rg_(xrn, c0, n)], writes=[("ps", bb)])
                            P.op("act", lambda e: e.activation(out=dst[:, c0:c0 + n], in_=psb[bb][:, 0:n], func=AF.Sigmoid, bias=lgb[:, d, g, h:h + 1]),
                                 reads=[("ps", bb)] + rpar, writes=[rg_(dn, c0, n)])
                for (c0, n) in BLK:
                    for d in range(2):
                        ci = d * 8 + h
                        P.op("act", lambda e: e.activation(out=Rt[d][:, c0:c0 + n], in_=Rt[d][:, c0:c0 + n], func=AF.Exp, scale=cneg[:, ci:ci + 1]),
                             reads=[rg_("Rt%d" % d + sfx, c0, n), "cneg"], writes=[rg_("Rt%d" % d + sfx, c0, n)])
                        P.op("pool", lambda e: e.tensor_tensor(out=It[d][:, c0:c0 + n], in0=It[d][:, c0:c0 + n], in1=acc[:, c0:c0 + n], op=ALU.mult),
                             reads=[rg_("It%d" % d + sfx, c0, n), rg_(accn, c0, n)], writes=[rg_("It%d" % d + sfx, c0, n)])
                        P.op("pool", lambda e: e.tensor_tensor(out=St[d][:, c0:c0 + n], in0=Rt[d][:, c0:c0 + n], in1=Rt[d][:, c0:c0 + n], op=ALU.mult),
                             reads=[rg_("Rt%d" % d + sfx, c0, n)], writes=[rg_("St%d" % d + sfx, c0, n)])

            def lst3(h):
                hb = h % 2
                Rt, It, St = RtA[hb], ItA[hb], StA[hb]
                Hx = St
                sfx = "_%d" % hb
                for d in range(2):
                    for (c0, n) in BLK:
                        P.op("act", lambda e: e.activation(out=St[d][:, c0:c0 + n], in_=St[d][:, c0:c0 + n], func=AF.Sqrt, scale=-1.0, bias=1.0),
                             reads=[rg_("St%d" % d + sfx, c0, n)], writes=[rg_("St%d" % d + sfx, c0, n)])
                for d in range(2):
                    order = BLK if d == 0 else [BLK[0]] + BLK[:0:-1]
                    hx = Hx[d]; hxn = "St%d" % d + sfx
                    prev = None
                    for (c0, n) in order:
                        P.op("dve", lambda e: e.tensor_tensor(out=St[d][:, c0:c0 + n], in0=St[d][:, c0:c0 + n], in1=It[d][:, c0:c0 + n], op=ALU.mult),
                             reads=[rg_("St%d" % d + sfx, c0, n), rg_("It%d" % d + sfx, c0, n)], writes=[rg_("St%d" % d + sfx, c0, n)])
                        if prev is None:
                            init = 0.0; rd = []
                        else:
                            pc0, pn = prev
                            init = hx[:, pc0 + pn - 1:pc0 + pn] if d == 0 else hx[:, pc0:pc0 + 1]
                            rd = [rg_(hxn, pc0, pn)]
                        a_v = Rt[d][:, c0:c0 + n]; u_v = St[d][:, c0:c0 + n]; o_v = hx[:, c0:c0 + n]
                        if d == 1:
                            a_v, u_v, o_v = a_v[:, ::-1], u_v[:, ::-1], o_v[:, ::-1]
                        P.op("dve", lambda e: e.tensor_tensor_scan(out=o_v, data0=a_v, data1=u_v, initial=init, op0=ALU.mult, op1=ALU.add),
                             reads=[rg_("Rt%d" % d + sfx, c0, n), rg_("St%d" % d + sfx, c0, n)] + rd, writes=[rg_(hxn, c0, n)])
                        prev = (c0, n)
                ms = mixs2[hb]; msn = "mixs%d" % hb
                P.op("dve", lambda e: e.tensor_tensor(out=Hx[0][:, LT:TW], in0=Hx[0][:, LT:TW], in1=Hx[1][:, LT:TW], op=ALU.add), reads=["St0" + sfx, "St1" + sfx], writes=["St0" + sfx])
                P.op("pool", lambda e: e.tensor_tensor(out=ms[:].rearrange("p (r c) -> p r c", c=64), in0=gyb2[hb][:].rearrange("p (r c) -> p r c", c=64),
                                                       in1=Hx[0][:, LT:TW].rearrange("p (c r) -> p r c", r=32), op=ALU.mult), reads=["gyb%d" % hb, "St0" + sfx], writes=[msn])
                load(mix_scr[h], ms[:], reads=[msn], name=("mix_scr", h))

            lpre(0)
            for h in range(8):
                lst12(h)
                if h + 1 < 8:
                    lpre(h + 1)
                lst3(h)
            assert len(freeb) == 8
          P.barrier()


        if "B2" in phases:
          with ExitStack() as sb2:
            NSETS = 3
            graw = sbt(sb2, "graw0", [128, XW]); grn = "graw0"
            cacc = sbt(sb2, "cacc", [128, TW]); sqb = sbt(sb2, "sqb", [128, TW], BF16)
            rng_ = [sbt(sb2, "rng%d" % i, [128, 512]) for i in range(2)]
            qn2 = [sbt(sb2, "qn%d" % i, [128, TW], F32R) for i in range(2)]; kn2 = [sbt(sb2, "kn%d" % i, [128, TW], F32R) for i in range(2)]
            vf = cacc
            zs2 = [sbt(sb2, "zs%d" % i, [128, S], BF16) for i in range(2)]
            Ktok2 = [sbt(sb2, "Ktok%d" % i, [128, NCH, 128]) for i in range(2)]; Vtok2 = [sbt(sb2, "Vtok%d" % i, [128, NCH, 128]) for i in range(2)]
            oacc = sbt(sb2, "oacc", [128, 16, 128])
            mixg = sbt(sb2, "mixg0", [128, S], BF16); mgn = "mixg0"
            Sst = [sbt(sb2, "Sst%d" % d, [128, 128], F32R) for d in range(2)]
            G4 = [128, 4, 128]
            SETS = []
            for si in range(NSETS):
                t = {}
                for nm in ("Gs", "DT", "Erow", "QdT", "QKT", "Kd", "Nm", "NmT"):
                    t[nm] = sbt(sb2, "%s_%d" % (nm, si), G4, F32)
                    t[nm + "_n"] = "%s_%d" % (nm, si)
                for nm, al in (("Wm", "Gs"), ("Qa", "DT"), ("Qb", "Erow")):
                    t[nm], t[nm + "_n"] = t[al], t[al + "_n"]
                t["tmpWT"], t["tmpWT_n"] = t["Nm"], t["Nm_n"]
                t["tmpZ"], t["tmpZ_n"] = t["Qa"], t["Qa_n"]
                SETS.append(t)
            Rp = [sbt(sb2, "Rp%d" % i, [128, 128], F32R) for i in range(2)]; Vn = [sbt(sb2, "Vn%d" % i, [128, 128], F32R) for i in range(2)]
            ot = [sbt(sb2, "ot%d" % i, [128, 128]) for i in range(2)]; junkg = [sbt(sb2, "junkg%d" % i, [128, 128]) for i in range(2)]
            stg = [sbt(sb2, "stg%d" % i, [128, 4]) for i in range(2)]
            pcs = [sbt(sb2, "pcs%d" % i, [128, 16, 128], BF16) for i in range(2)]
            P.op("pool", lambda e: e.memset(graw[:], 0.0), writes=[grn])
            gcw = pv(s_gcw).rearrange("p (t h j) -> p t h j", t=3, j=4)
            f32v = lambda ap: ap.bitcast(F32)
            bcast4 = lambda m: m[:].unsqueeze(1).to_broadcast([128, 4, 128])
            fl = lambda t: t[:].rearrange("p u c -> p (u c)")
            fwd_order = list(range(NCH)); bwd_order = [1, 0] + list(range(17, 1, -1))
            rpar = [("par", 0, INF)]
            w1v_ = w1_d.rearrange("(k p) (o c) -> o p k c", p=128, c=128)
            w2v_ = w2_d.rearrange("(k p) (o c) -> o p k c", p=128, c=128)
            pjobs = [(w1v_[o], w1b[o], ("w1b", o)) for o in range(64)]
            pjobs += [(w2v_[o][:, kh * 16:(kh + 1) * 16, :], w2b[o][:, kh * 16:(kh + 1) * 16, :], ("w2b", o * 4 + kh)) for o in range(16) for kh in range(4)]
            pji = [0]

            def precast_some(n):
                for _ in range(n):
                    if pji[0] >= len(pjobs):
                        return
                    src, dst, rn = pjobs[pji[0]]
                    i = pji[0] % 2
                    pji[0] += 1
                    P.dma("pool", lambda e: e.dma_start(out=pcs[i][:], in_=src), writes=["pcs%d" % i])
                    P.dma("sp", lambda e: e.dma_start(out=dst, in_=pcs[i][:]), reads=["pcs%d" % i], writes=[rn])

            import os as _os
            r_ = lambda ap: ap.bitcast(F32R)
            NH = int(_os.environ.get("K_NGDN", 8))
            NIT = NCH // 2

            def mm4(b, lhs, ln, rhs, rn):
                for u in range(4):
                    P.op("pe", lambda e: e.matmul(psb[b][:, u * 128:(u + 1) * 128], lhsT=r_(lhs[:, u, :]), rhs=r_(rhs[:, u, :]), start=True, stop=True, skip_group_check=True),
                         reads=[ln, rn], writes=[("ps", b)], inc=(u == 3))

            def transpose4(b, src, sname):
                for u in range(4):
                    P.op("pe", lambda e: e.transpose(psb[b][:, u * 128:(u + 1) * 128], src[:, u, :], ident[:]),
                         reads=[sname, "ident"], writes=[("ps", b)], inc=(u == 3))

            def headpre(h):
                hb = h % 2
                qn, kn, Ktok, Vtok, zs = qn2[hb], kn2[hb], Ktok2[hb], Vtok2[hb], zs2[hb]
                qnn, knn, Ktn, Vtn, zsn = "qn%d" % hb, "kn%d" % hb, "Ktok%d" % hb, "Vtok%d" % hb, "zs%d" % hb
                load(zs[:], zs_scr[h], reads=[("zs_scr", h)], name=zsn)
                for t3 in range(3):
                    load(graw[:, 2:258], p_scr[h, t3][:, 0:256], reads=[("p_scr", h * 3 + t3)], name=(grn, 2, 258))
                    load(graw[:, 261:2309], p_scr[h, t3][:, 256:NALL], reads=[("p_scr", h * 3 + t3)], name=(grn, 261, 2309))
                    P.op("act", lambda e: e.activation(out=cacc[:], in_=graw[:, 0:TW], func=AF.Copy, scale=gcw[:, t3, h, 0:1]), reads=[grn] + rpar, writes=["cacc"])
                    yield
                    for j in (1, 2, 3):
                        P.op("dve", lambda e: e.scalar_tensor_tensor(out=cacc[:], in0=graw[:, j:j + TW], scalar=gcw[:, t3, h, j:j + 1], in1=cacc[:], op0=ALU.mult, op1=ALU.add),
                             reads=[grn, "cacc"] + rpar, writes=["cacc"])
                        yield
                    P.op("act", lambda e: e.activation(out=cacc[:], in_=cacc[:], func=AF.Silu), reads=["cacc"], writes=["cacc"])
                    yield
                    if t3 < 2:
                        P.op("pool", lambda e: e.tensor_tensor(out=sqb[:], in0=cacc[:], in1=cacc[:], op=ALU.mult), reads=["cacc"], writes=["sqb"])
                        yield
                        sc_ = 128.0 if t3 == 0 else 1.0
                        dq = qn if t3 == 0 else kn; dqn = qnn if t3 == 0 else knn
                        for gi, (c0, n) in enumerate([(g * 512, 512) for g in range(4)] + [(2048, TW - 2048)]):
                            b, = yield from take(1)
                            rg = rng_[gi % 2]; rgn = "rng%d" % (gi % 2)
                            P.op("pe", lambda e: e.matmul(psb[b][:, 0:n], lhsT=onesb[:], rhs=sqb[:, c0:c0 + n], start=True, stop=True), reads=["onesb", "sqb"], writes=[("ps", b)])
                            yield
                            P.op("dve", lambda e: e.tensor_copy(out=rg[:, 0:n], in_=psb[b][:, 0:n]), reads=[("ps", b)], writes=[rgn])
                            rel(b)
                            yield
                            P.op("act", lambda e: e.activation(out=rg[:, 0:n], in_=rg[:, 0:n], func=AF.Ln, scale=sc_, bias=sc_ * EPS), reads=[rgn], writes=[rgn])
                            P.op("act", lambda e: e.activation(out=rg[:, 0:n], in_=rg[:, 0:n], func=AF.Exp, scale=-0.5), reads=[rgn], writes=[rgn])
                            yield
                            P.op("pool", lambda e: e.tensor_tensor(out=dq[:, c0:c0 + n], in0=cacc[:, c0:c0 + n], in1=rg[:, 0:n], op=ALU.mult), reads=["cacc", rgn], writes=[(dqn, c0, c0 + n)])
                            yield
                    if t3 >= 1:
                        srcT, sname, dstK, dname = (kn, knn, Ktok, Ktn) if t3 == 1 else (vf, "cacc", Vtok, Vtn)
                        for c4 in range(0, NCH, 4):
                            n4 = min(4, NCH - c4)
                            b, = yield from take(1)
                            for q in range(n4):
                                co = ch_off(c4 + q)
                                P.op("pe", lambda e: e.transpose(psb[b][:, q * 128:(q + 1) * 128], f32v(srcT[:, co:co + 128]), ident[:]),
                                     reads=[sname, "ident"], writes=[("ps", b)], inc=(q == n4 - 1))
                            yield
                            P.op("dve", lambda e: e.tensor_copy(out=dstK[:, c4:c4 + n4, :], in_=psb[b][:, 0:n4 * 128].rearrange("p (q c) -> p q c", c=128)),
                                 reads=[("ps", b)], writes=[dname])
                            rel(b)
                            yield

            NG = NH * NIT
            prep_done = [False] * NG
            rec_done = [[False] * NG for _ in range(2)]
            pre_done = [False] * NH
            head_fin = [False] * NH
            dir_fin = [[False] * NH for _ in range(2)]
            odone_h = [[False] * 16 for _ in range(NH)]

            if True:
                def units_of(it):
                    return [(0, fwd_order[2 * it]), (0, fwd_order[2 * it + 1]), (1, bwd_order[2 * it]), (1, bwd_order[2 * it + 1])]

                def prep(g):
                    h, it = divmod(g, NIT)
                    hb = h % 2
                    qn, kn, Ktok, Vtok, zs = qn2[hb], kn2[hb], Ktok2[hb], Vtok2[hb], zs2[hb]
                    qnn, knn, Ktn, Vtn, zsn = "qn%d" % hb, "kn%d" % hb, "Ktok%d" % hb, "Vtok%d" % hb, "zs%d" % hb
                    hd = lambda d: d * 8 + h
                    T_ = SETS[g % NSETS]
                    while (not pre_done[h]) or (g >= NSETS and not (rec_done[0][g - NSETS] and rec_done[1][g - NSETS])):
                        yield
                    units = units_of(it)
                    Gs, DT, Erow, QdT, QKT, Kd, Nm, NmT, Qa, Qb, Wm = [T_[k] for k in ("Gs", "DT", "Erow", "QdT", "QKT", "Kd", "Nm", "NmT", "Qa", "Qb", "Wm")]
                    n_ = lambda k: T_[k + "_n"]
                    precast_some(2)
                    for u, (d, ch) in enumerate(units):
                        P.op("act", lambda e: e.activation(out=NmT[:, u, :], in_=ident[:], func=AF.Copy, scale=gam[:, ch, hd(d):hd(d) + 1]),
                             reads=["ident", "gam"], writes=[(n_("NmT"), u)])
                        P.op("act", lambda e: e.activation(out=r_(Kd[:, u, :]), in_=Ktok[:, ch, :], func=AF.Copy, scale=kdec[:, ch, hd(d):hd(d) + 1]),
                             reads=[Ktn, "kdec"], writes=[(n_("Kd"), u)])
                    yield
                    bG, bK, bQ = yield from take(3)
                    P.op("pe", lambda e: e.matmul(psb[bG][:], lhsT=ones[:], rhs=fl(NmT), start=True, stop=True), reads=["ones", n_("NmT")], writes=[("ps", bG)])
                    for u, (d, ch) in enumerate(units):
                        co = ch_off(ch)
                        P.op("pe", lambda e: e.matmul(psb[bK][:, u * 128:(u + 1) * 128], lhsT=kn[:, co:co + 128], rhs=kn[:, co:co + 128], start=True, stop=True, skip_group_check=True),
                             reads=[knn], writes=[("ps", bK)], inc=False)
                        P.op("pe", lambda e: e.matmul(psb[bQ][:, u * 128:(u + 1) * 128], lhsT=kn[:, co:co + 128], rhs=qn[:, co:co + 128], start=True, stop=True, skip_group_check=True),
                             reads=[knn, qnn], writes=[("ps", bQ)], inc=(u == 3))
                    yield
                    P.op("dve", lambda e: e.tensor_copy(out=r_(fl(Gs)), in_=psb[bG][:]), reads=[("ps", bG)], writes=[n_("Gs")])
                    for u, (d, ch) in enumerate(units):
                        P.op("dve", lambda e: e.scalar_tensor_tensor(out=r_(DT[:, u, :]), in0=psb[bG][:, u * 128:(u + 1) * 128], scalar=gam[:, ch, hd(d):hd(d) + 1], in1=negm[d][:], op0=ALU.subtract, op1=ALU.add),
                             reads=[("ps", bG), "gam", "negm%d" % d], writes=[(n_("DT"), u)])
                    rel(bG)
                    yield
                    P.op("act", lambda e: e.activation(out=r_(fl(Erow)), in_=fl(Gs), func=AF.Exp), reads=[n_("Gs")], writes=[n_("Erow")])
                    P.op("act", lambda e: e.activation(out=r_(fl(DT)), in_=fl(DT), func=AF.Exp), reads=[n_("DT")], writes=[n_("DT")])
                    for u, (d, ch) in enumerate(units):
                        co = ch_off(ch)
                        if ch >= 2:
                            P.op("pool", lambda e: e.tensor_tensor(out=r_(QdT[:, u, :]), in0=f32v(qn[:, co:co + 128]), in1=Erow[:, u, :], op=ALU.mult),
                                 reads=[qnn, n_("Erow")], writes=[(n_("QdT"), u)])
                    yield
                    P.op("dve", lambda e: e.tensor_tensor(out=r_(fl(QKT)), in0=psb[bQ][:], in1=fl(DT), op=ALU.mult), reads=[("ps", bQ), n_("DT")], writes=[n_("QKT")])
                    for u, (d, ch) in enumerate(units):
                        P.op("dve", lambda e: e.scalar_tensor_tensor(out=r_(Nm[:, u, :]), in0=psb[bK][:, u * 128:(u + 1) * 128], scalar=nbeta[:, ch, hd(d):hd(d) + 1], in1=DT[:, u, :], op0=ALU.mult, op1=ALU.mult),
                             reads=[("ps", bK), "nbeta", n_("DT")], writes=[(n_("Nm"), u)])
                    rel(bK, bQ)
                    yield
                    P.op("pool", lambda e: e.tensor_tensor(out=r_(Nm[:]), in0=Nm[:], in1=bcast4(offd), op=ALU.mult), reads=[n_("Nm"), "offd"], writes=[n_("Nm")])
                    yield
                    bT, = yield from take(1)
                    transpose4(bT, Nm, n_("Nm"))
                    P.op("pool", lambda e: e.tensor_tensor(out=r_(Qa[:]), in0=Nm[:], in1=bcast4(bd32), op=ALU.mult), reads=[n_("Nm"), "bd32"], writes=[n_("Qa")])
                    yield
                    P.op("dve", lambda e: e.tensor_copy(out=fl(NmT), in_=psb[bT][:]), reads=[("ps", bT)], writes=[n_("NmT")])
                    rel(bT)
                    P.op("pool", lambda e: e.tensor_tensor(out=r_(Wm[:]), in0=Qa[:], in1=bcast4(ident), op=ALU.add), reads=[n_("Qa"), "ident"], writes=[n_("Wm")])
                    yield
                    P.op("pool", lambda e: e.tensor_tensor(out=r_(Qb[:]), in0=NmT[:], in1=bcast4(bd32), op=ALU.mult), reads=[n_("NmT"), "bd32"], writes=[n_("Qb")])
                    yield
                    for lvl in range(4):
                        last = (lvl == 3)
                        bTq, bNq = yield from take(2)
                        mm4(bTq, Qa, n_("Qa"), Qb, n_("Qb"))
                        if not last:
                            mm4(bNq, Qb, n_("Qb"), Qa, n_("Qa"))
                        yield
                        P.op("dve", lambda e: e.tensor_copy(out=r_(fl(Qb)), in_=psb[bTq][:]), reads=[("ps", bTq)], writes=[n_("Qb")])
                        if not last:
                            P.op("dve", lambda e: e.tensor_copy(out=r_(fl(Qa)), in_=psb[bNq][:]), reads=[("ps", bNq)], writes=[n_("Qa")])
                        rel(bTq, bNq)
                        yield
                        bW, = yield from take(1)
                        mm4(bW, Qb, n_("Qb"), Wm, n_("Wm"))
                        yield
                        P.op("dve", lambda e: e.tensor_tensor(out=r_(fl(Wm)), in0=psb[bW][:], in1=fl(Wm), op=ALU.add), reads=[("ps", bW), n_("Wm")], writes=[n_("Wm")])
                        rel(bW)
                        yield
                    for om, omn in ((od64, "od64"), (od128, "od128")):
                        bt, bZ = yield from take(2)
                        transpose4(bt, Wm, n_("Wm"))
                        P.op("pool", lambda e: e.tensor_tensor(out=r_(Qb[:]), in0=NmT[:], in1=bcast4(om), op=ALU.mult), reads=[n_("NmT"), omn], writes=[n_("Qb")])
                        yield
                        mm4(bZ, Qb, n_("Qb"), Wm, n_("Wm"))
                        P.op("dve", lambda e: e.tensor_copy(out=r_(fl(T_["tmpWT"])), in_=psb[bt][:]), reads=[("ps", bt)], writes=[T_["tmpWT_n"]])
                        yield
                        P.op("dve", lambda e: e.tensor_copy(out=r_(fl(T_["tmpZ"])), in_=psb[bZ][:]), reads=[("ps", bZ)], writes=[T_["tmpZ_n"]])
                        rel(bt, bZ)
                        yield
                        bW, = yield from take(1)
                        mm4(bW, T_["tmpWT"], T_["tmpWT_n"], T_["tmpZ"], T_["tmpZ_n"])
                        yield
                        P.op("dve", lambda e: e.tensor_tensor(out=r_(fl(Wm)), in0=psb[bW][:], in1=fl(Wm), op=ALU.add), reads=[("ps", bW), n_("Wm")], writes=[n_("Wm")])
                        rel(bW)
                        yield
                    prep_done[g] = True

                def recur(d, g):
                    h, it = divmod(g, NIT)
                    hb = h % 2
                    qn, kn, Ktok, Vtok, zs = qn2[hb], kn2[hb], Ktok2[hb], Vtok2[hb], zs2[hb]
                    qnn, knn, Ktn, Vtn, zsn = "qn%d" % hb, "kn%d" % hb, "Ktok%d" % hb, "Vtok%d" % hb, "zs%d" % hb
                    hd = lambda d_: d_ * 8 + h
                    odone = odone_h[h]
                    T_ = SETS[g % NSETS]
                    while (not prep_done[g]) or (it == 0 and h > 0 and not head_fin[h - 1]):
                        yield
                    if it == 0:
                        P.op("pool", lambda e: e.tensor_scalar(out=Sst[d][:], in0=ident[:], scalar1=0.0, scalar2=None, op0=ALU.mult), reads=["ident"], writes=["Sst%d" % d])
                    QdT, QKT, Kd, Wf = T_["QdT"], T_["QKT"], T_["Kd"], T_["Wm"]
                    n_ = lambda k: T_[k + "_n"]
                    units = units_of(it)
                    for u in (2 * d, 2 * d + 1):
                        ch = units[u][1]
                        co = ch_off(ch); Sd = Sst[d]; Sn = "Sst%d" % d; hdd = hd(d)
                        ri = d
                        b1, = yield from take(1)
                        P.op("pe", lambda e: e.matmul(psb[b1][:, 0:128], lhsT=kn[:, co:co + 128], rhs=Sd[:], start=True, stop=True), reads=[knn, Sn], writes=[("ps", b1)])
                        yield
                        P.op("dve", lambda e: e.scalar_tensor_tensor(out=Rp[ri][:], in0=psb[b1][:, 0:128], scalar=neg_eg[:, ch, hdd:hdd + 1], in1=Vtok[:, ch, :], op0=ALU.mult, op1=ALU.add),
                             reads=[("ps", b1), "neg_eg", Vtn], writes=["Rp%d" % ri])
                        rel(b1)
                        yield
                        b2, = yield from take(1)
                        P.op("pe", lambda e: e.matmul(psb[b2][:, 0:128], lhsT=r_(Wf[:, u, :]), rhs=Rp[ri][:], start=True, stop=True), reads=[n_("Wm"), "Rp%d" % ri], writes=[("ps", b2)])
                        yield
                        P.op("dve", lambda e: e.tensor_scalar(out=Vn[ri][:], in0=psb[b2][:, 0:128], scalar1=beta[:, ch, hdd:hdd + 1], scalar2=None, op0=ALU.mult),
                             reads=[("ps", b2), "beta"], writes=["Vn%d" % ri])
                        rel(b2)
                        yield
                        b5, b3 = yield from take(2)
                        P.op("pe", lambda e: e.matmul(psb[b5][:, 0:128], lhsT=r_(Kd[:, u, :]), rhs=Vn[ri][:], start=True, stop=True), reads=[(n_("Kd"), u), "Vn%d" % ri], writes=[("ps", b5)])
                        if ch >= 2:
                            lc = ch - 2
                            P.op("pe", lambda e: e.matmul(psb[b3][:, 0:128], lhsT=r_(QdT[:, u, :]), rhs=Sd[:], start=True, stop=False), reads=[(n_("QdT"), u), Sn], writes=[("ps", b3)], inc=False)
                            P.op("pe", lambda e: e.matmul(psb[b3][:, 0:128], lhsT=r_(QKT[:, u, :]), rhs=Vn[ri][:], start=False, stop=True), reads=[n_("QKT"), "Vn%d" % ri], writes=[("ps", b3)])
                        yield
                        P.op("dve", lambda e: e.scalar_tensor_tensor(out=Sd[:], in0=f32v(Sd[:]), scalar=cdl[:, ch, hdd:hdd + 1], in1=psb[b5][:, 0:128], op0=ALU.mult, op1=ALU.add),
                             reads=[("ps", b5), "cdl", Sn], writes=[Sn])
                        rel(b5)
                        if ch < 2:
                            rel(b3)
                        if ch >= 2:
                            if not odone[lc]:
                                odone[lc] = True
                                P.op("dve", lambda e: e.tensor_copy(out=oacc[:, lc, :], in_=psb[b3][:, 0:128]), reads=[("ps", b3)], writes=[("oacc", lc)])
                                rel(b3)
                            else:
                                o_ = ot[d]; on_ = "ot%d" % d; sg = stg[d]; sgn = "stg%d" % d
                                P.op("dve", lambda e: e.tensor_tensor(out=o_[:], in0=psb[b3][:, 0:128], in1=oacc[:, lc, :], op=ALU.add), reads=[("ps", b3), ("oacc", lc)], writes=[on_])
                                rel(b3)
                                yield
                                P.op("act", lambda e: e.activation(out=junkg[d][:], in_=o_[:], func=AF.Square, accum_out=sg[:, 0:1]), reads=[on_], writes=["junkg%d" % d, sgn])
                                P.op("act", lambda e: e.activation(out=sg[:, 1:2], in_=sg[:, 0:1], func=AF.Sqrt, scale=1.0 / 128, bias=EPS), reads=[sgn], writes=[sgn])
                                yield
                                P.op("dve", lambda e: e.reciprocal(out=sg[:, 2:3], in_=sg[:, 1:2]), reads=[sgn], writes=[sgn])
                                yield
                                P.op("act", lambda e: e.activation(out=o_[:], in_=o_[:], func=AF.Copy, scale=sg[:, 2:3]), reads=[on_, sgn], writes=[on_])
                                yield
                                b4, = yield from take(1)
                                P.op("pe", lambda e: e.transpose(psb[b4][:, 0:128], o_[:], ident[:]), reads=[on_, "ident"], writes=[("ps", b4)])
                                yield
                                P.op("dve", lambda e: e.scalar_tensor_tensor(out=mixg[:, lc * 128:(lc + 1) * 128], in0=psb[b4][:, 0:128], scalar=pv(s_gnw), in1=zs[:, lc * 128:(lc + 1) * 128], op0=ALU.mult, op1=ALU.mult),
                                     reads=[("ps", b4), zsn] + rpar, writes=[(mgn, lc)])
                                rel(b4)
                        yield
                    rec_done[d][g] = True
                    if it == NIT - 1:
                        dir_fin[d][h] = True
                        if dir_fin[1 - d][h]:
                            load(mix_scr[8 + h], mixg[:], reads=[mgn], name=("mix_scr", 8 + h))
                            head_fin[h] = True

                def chain(fn, *a):
                    for g in range(NG):
                        yield from fn(*a, g)

                def prep_lane(l):
                    for g in range(l, NG, NSETS):
                        yield from prep(g)

                def pre_chain():
                    for h in range(NH):
                        while h >= 2 and not head_fin[h - 2]:
                            yield
                        yield from headpre(h)
                        pre_done[h] = True

            run_tasks([pre_chain()] + [prep_lane(l) for l in range(NSETS)] + [chain(recur, 0), chain(recur, 1)])
            assert len(freeb) == 8
            precast_some(1000)
          P.barrier()

        if "C" in phases:
          with ExitStack() as sc:
            wo = sbt(sc, "wo", [128, 16, D], BF16)
            GM_row = make_row(sc, 4, "GM_row")
            wov = wout_d.rearrange("(k p) c -> p k c", p=128)
            for k4 in range(0, 16, 4):
                P.dma("pool", (lambda k4: lambda e: e.dma_start(out=wo[:, k4:k4 + 4, :], in_=wov[:, k4:k4 + 4, :]))(k4), writes=[("wo", k4, k4 + 4)])
            mt = [sbt(sc, "mt%d" % i, [128, 16, 512], BF16) for i in range(2)]
            xc = [sbt(sc, "xc%d" % i, [128, D]) for i in range(2)]
            x1t = [sbt(sc, "x1t%d" % i, [128, D]) for i in range(2)]
            h2t = [sbt(sc, "h2t%d" % i, [128, 16, 128], BF16) for i in range(2)]
            junkc = sbt(sc, "junkc", [128, D]); stc2 = [sbt(sc, "stc%d" % i, [128, 16]) for i in range(2)]
            mixv = mix_scr.rearrange("k p t -> p k t")
            h2v = h2_scr.rearrange("k p t -> p k t")
            def stageA(tt):
                g = tt // 4
                m = mt[g % 2]; mn = "mt%d" % (g % 2)
                if tt % 4 == 0:
                    load(m[:], mixv[:, :, g * 512:(g + 1) * 512], reads=["mix_scr"], name=mn)
                i = tt % 2
                stc = stc2[i]; stn = "stc%d" % i
                load(xc[i][:], x_d[tt * 128:(tt + 1) * 128, :], name="xc%d" % i)
                bs = []
                for cg in range(4):
                    b = bank(); bs.append(b)
                    for k in range(16):
                        P.op("pe", lambda e: e.matmul(psb[b][:], lhsT=m[:, k, (tt % 4) * 128:(tt % 4 + 1) * 128], rhs=wo[:, k, cg * 512:(cg + 1) * 512], start=(k == 0), stop=(k == 15)),
                             reads=[mn, ("wo", k)], writes=[("ps", b)], inc=(k == 15))
                for cg in range(4):
                    b = bs[cg]
                    P.op("act", lambda e: e.activation(out=junkc[:, cg * 512:(cg + 1) * 512], in_=psb[b][:], func=AF.Square, accum_out=stc[:, cg:cg + 1]),
                         reads=[("ps", b)], writes=[("junkc", cg), (stn, cg)])
                P.op("dve", lambda e: e.tensor_reduce(out=stc[:, 4:5], in_=stc[:, 0:4], axis=mybir.AxisListType.X, op=ALU.add), reads=[stn], writes=[stn])
                P.op("act", lambda e: e.activation(out=stc[:, 5:6], in_=stc[:, 4:5], func=AF.Sqrt, scale=1.0 / D, bias=EPS), reads=[stn], writes=[stn])
                P.op("dve", lambda e: e.reciprocal(out=stc[:, 6:7], in_=stc[:, 5:6]), reads=[stn], writes=[stn])
                x1 = x1t[i]; x1n = "x1t%d" % i
                for cg in range(4):
                    b = bs[cg]
                    cs = slice(cg * 512, (cg + 1) * 512)
                    P.op("act", lambda e: e.activation(out=x1[:, cs], in_=psb[b][:], func=AF.Copy, scale=stc[:, 6:7]), reads=[("ps", b), stn], writes=[(x1n, cg)])
                    P.op("dve", lambda e: e.tensor_tensor(out=x1[:, cs], in0=x1[:, cs], in1=GM_row[:, cs], op=ALU.mult), reads=[(x1n, cg), "GM_row"], writes=[(x1n, cg)])
                    P.op("pool", lambda e: e.tensor_tensor(out=x1[:, cs], in0=x1[:, cs], in1=xc[i][:, cs], op=ALU.add), reads=[(x1n, cg), "xc%d" % i], writes=[(x1n, cg)])
                load(x1_scr[tt * 128:(tt + 1) * 128, :], x1[:], q="pool", reads=[x1n], name=("x1_scr", tt))
                P.op("act", lambda e: e.activation(out=junkc[:], in_=x1[:], func=AF.Square, accum_out=stc[:, 8:9]), reads=[x1n], writes=["junkc", stn])
                P.op("act", lambda e: e.activation(out=stc[:, 9:10], in_=stc[:, 8:9], func=AF.Sqrt, scale=1.0 / D, bias=EPS), reads=[stn], writes=[stn])
                P.op("dve", lambda e: e.reciprocal(out=stc[:, 10:11], in_=stc[:, 9:10]), reads=[stn], writes=[stn])
                P.op("act", lambda e: e.activation(out=xc[i][:], in_=x1[:], func=AF.Copy, scale=stc[:, 10:11]), reads=[x1n, stn], writes=["xc%d" % i])

            def stageB(tt):
                i = tt % 2
                xn = xc[i]; xnn = "xc%d" % i
                ht = h2t[i]; htn = "h2t%d" % i
                for g4 in range(4):
                    b = bank()
                    for q in range(4):
                        k = g4 * 4 + q
                        P.op("pe", lambda e: e.transpose(psb[b][:, q * 128:(q + 1) * 128], xn[:, k * 128:(k + 1) * 128], ident[:]), reads=[xnn, "ident"], writes=[("ps", b)], inc=(q == 3))
                    for q in range(4):
                        k = g4 * 4 + q
                        P.op("dve", lambda e: e.tensor_scalar(out=ht[:, k, :], in0=psb[b][:, q * 128:(q + 1) * 128], scalar1=vecs[:, 5, k:k + 1], scalar2=vecs[:, 6, k:k + 1], op0=ALU.mult, op1=ALU.add),
                             reads=[("ps", b), "vecs"], writes=[(htn, k)])
                load(h2v[:, :, tt * 128:(tt + 1) * 128], ht[:], q="pool", reads=[htn], name=("h2_scr", tt))

            stageA(0)
            for tt in range(1, 16):
                stageA(tt)
                stageB(tt - 1)
            stageB(15)
          P.barrier()

        out_toks = []
        if "D" in phases:
          with ExitStack() as sd:
            h2 = sbt(sd, "h2", [128, 16, 512], BF16)
            GF_row = make_row(sd, 7, "GF_row")
            f1 = sbt(sd, "f1", [128, 64, 512], BF16)
            w1t = [sbt(sd, "w1t%d" % i, [128, 16, 128], BF16) for i in range(3)]
            w2t = [sbt(sd, "w2t%d" % i, [128, 64, 128], BF16) for i in range(2)]
            rl = [sbt(sd, "rl%d" % i, [128, 512]) for i in range(2)]
            y2b = [sbt(sd, "y2b%d" % i, [128, 512]) for i in range(2)]
            y2 = sbt(sd, "y2", [128, 4, D])
            x1d = sbt(sd, "x1d", [128, D]); junkd = sbt(sd, "junkd", [128, D], BF16); std = sbt(sd, "std", [128, 8])
            h2v = h2_scr.rearrange("k p t -> p k t")
            w1v = w1_d.rearrange("(k p) (o c) -> o p k c", p=128, c=128)
            w2v = w2_d.rearrange("(k p) (o c) -> o p k c", p=128, c=128)
            def epi(T, q):
                tt = T * 4 + q
                load(x1d[:], x1_scr[tt * 128:(tt + 1) * 128, :], q="pool", reads=[("x1_scr", tt)], name="x1d")
                P.op("act", lambda e: e.activation(out=junkd[:], in_=y2[:, q, :], func=AF.Square, accum_out=std[:, 0:1]), reads=["y2"], writes=["junkd", "std"])
                P.op("act", lambda e: e.activation(out=std[:, 1:2], in_=std[:, 0:1], func=AF.Sqrt, scale=1.0 / D, bias=EPS), reads=["std"], writes=["std"])
                P.op("dve", lambda e: e.reciprocal(out=std[:, 2:3], in_=std[:, 1:2]), reads=["std"], writes=["std"])
                P.op("dve", lambda e: e.scalar_tensor_tensor(out=y2[:, q, :], in0=y2[:, q, :], scalar=std[:, 2:3], in1=GF_row[:], op0=ALU.mult, op1=ALU.mult),
                     reads=["y2", "std", "GF_row"], writes=["y2"])
                P.op("dve", lambda e: e.tensor_tensor(out=x1d[:], in0=x1d[:], in1=y2[:, q, :], op=ALU.add), reads=["x1d", "y2"], writes=["x1d"])
                out_toks.append(load(out_d[tt * 128:(tt + 1) * 128, :], x1d[:], q="pool", reads=["x1d"], name=("out", tt)))

            def ff1(T):
                for o in range(64):
                    if T > 0 and o in (6, 12, 18, 24):
                        epi(T - 1, (o // 6) - 1)
                    wt = w1t[o % 3]; wtn = "w1t%d" % (o % 3)
                    load(wt[:], w1b[o], reads=[("w1b", o)], name=wtn)
                    b = bank()
                    for k in range(16):
                        P.op("pe", lambda e: e.matmul(psb[b][:], lhsT=wt[:, k, :], rhs=h2[:, k, :], start=(k == 0), stop=(k == 15)), reads=[wtn, "h2"], writes=[("ps", b)], inc=(k == 15))
                    r = rl[o % 2]; rn = "rl%d" % (o % 2)
                    P.op("act", lambda e: e.activation(out=r[:], in_=psb[b][:], func=AF.Relu), reads=[("ps", b)], writes=[rn])
                    P.op("dve", lambda e: e.tensor_tensor(out=f1[:, o, :], in0=r[:], in1=r[:], op=ALU.mult), reads=[rn], writes=[("f1", o)])

            def ff2(T):
                for o in range(16):
                    wt = w2t[o % 2]; wtn = "w2t%d" % (o % 2)
                    for kh in range(4):
                        load(wt[:, kh * 16:(kh + 1) * 16, :], w2b[o][:, kh * 16:(kh + 1) * 16, :], reads=[("w2b", o * 4 + kh)], name=(wtn, kh))
                    b = bank()
                    for k in range(64):
                        P.op("pe", lambda e: e.matmul(psb[b][:], lhsT=wt[:, k, :], rhs=f1[:, k, :], start=(k == 0), stop=(k == 63)), reads=[(wtn, k // 16), ("f1", k)], writes=[("ps", b)], inc=(k == 63))
                    yb = y2b[o % 2]; ybn = "y2b%d" % (o % 2)
                    P.op("act", lambda e: e.copy(out=yb[:], in_=psb[b][:]), reads=[("ps", b)], writes=[ybn])
                    b2 = bank()
                    for q in range(4):
                        P.op("pe", lambda e: e.transpose(psb[b2][:, q * 128:(q + 1) * 128], yb[:, q * 128:(q + 1) * 128], ident[:]), reads=[ybn, "ident"], writes=[("ps", b2)], inc=(q == 3))
                    P.op("dve", lambda e: e.tensor_copy(out=y2[:, :, o * 128:(o + 1) * 128], in_=psb[b2][:].rearrange("p (q c) -> p q c", c=128)), reads=[("ps", b2)], writes=[("y2", o)])

            load(h2[:], h2v[:, :, 0:512], reads=["h2_scr"], name="h2")
            for T in range(4):
                ff1(T)
                if T < 3:
                    load(h2[:], h2v[:, :, (T + 1) * 512:(T + 2) * 512], reads=["h2_scr"], name="h2")
                ff2(T)
            for q in range(4):
                epi(3, q)
        if not out_toks:
            zt = sbt(top, "zt", [128, D])
            P.op("pool", lambda e: e.memset(zt[:], 0.0), writes=["zt"])
            for tt in range(16):
                out_toks.append(load(out_d[tt * 128:(tt + 1) * 128, :], zt[:], reads=["zt"]))
        P.barrier()
        P.final_wait("sp", out_toks)
        global LAST_PROG
        LAST_PROG = P
        with nc.Block() as block:
            P.emit(block)
    return nc


def host_layout(inp, b):
    f = lambda a: np.ascontiguousarray(a, dtype=np.float32)
    pk = lambda v: f(v.reshape(-1, 128).T)
    cc = np.stack([pk(inp["c"][b]), pk(inp["c_ctx"])], axis=2).reshape(128, 32)
    nw = inp["norm_w"][0]
    nwT = np.stack([pk(nw[i]) for i in range(4)], axis=1).reshape(128, 64)
    lcw = inp["lru_conv_w"][0].reshape(4, 8, 128).transpose(2, 1, 0).reshape(128, 32)
    lcb = inp["lru_conv_b"][0].reshape(8, 128).T
    lgw = inp["lru_gate_w"][0].transpose(3, 0, 1, 2, 4).reshape(128, 4096)
    lgb = inp["lru_gate_b"][0].reshape(2, 2, 8, 128).transpose(3, 0, 1, 2).reshape(128, 32)
    llam = inp["lru_lambda"][0].reshape(2, 8, 128).transpose(2, 0, 1).reshape(128, 16)
    gcw = inp["gdn_conv_w"][0].reshape(4, 3, 8, 128).transpose(3, 1, 2, 0).reshape(128, 96)
    galog = np.broadcast_to(inp["gdn_a_log"][0].reshape(1, 16), (128, 16))
    gdtb = np.broadcast_to(inp["gdn_dt_bias"][0].reshape(1, 16), (128, 16))
    return {
        "x": f(inp["x"][b]), "ctx": f(inp["ctx"][b]), "cc": f(cc),
        "w_mod": f(inp["w_mod"][0]), "b_modT": pk(inp["b_mod"][0]), "nwT": f(nwT),
        "w_in": f(inp["w_in"][0]), "lcw": f(lcw), "lcb": f(lcb), "lgw": f(lgw), "lgb": f(lgb), "llam": f(llam),
        "gcw": f(gcw), "galog": f(galog), "gdtb": f(gdtb), "gnw": f(inp["gdn_norm_w"][0].reshape(128, 1)),
        "w_out": f(inp["w_out"][0]), "w_ff1": f(inp["w_ff1"][0]), "w_ff2": f(inp["w_ff2"][0]),
    }


def kernel(**inputs):
    inp = {k: np.asarray(v) for k, v in inputs.items()}
    nc = build()
    in_maps = [host_layout(inp, b) for b in range(8)]
    res = run_bass_kernel_spmd(nc, in_maps, core_ids=list(range(8)))
    return np.stack([np.asarray(r["out"], dtype=np.float32) for r in res.results], axis=0)
```

```python
from contextlib import ExitStack
import numpy as np
import concourse.bass as bass
import concourse.mybir as mybir
from concourse.alu_op_type import AluOpType as ALU
from concourse.bass_utils import run_bass_kernel_spmd

F32 = mybir.dt.float32
F32R = mybir.dt.float32r
BF16 = mybir.dt.bfloat16
AF = mybir.ActivationFunctionType

ENGS = ("pe", "dve", "act", "pool", "sp")
INF = 1 << 60
EPS = 1e-6


class _Rec:
    def __init__(self):
        self.call = None

    def __getattr__(self, name):
        def f(*a, **k):
            self.call = (name, a, k)
            return self
        return f


def _capture(fn):
    r = _Rec()
    fn(r)
    assert r.call is not None
    return r.call


class Prog:
    def __init__(self, nc, stack, n_dma_sems=30):
        self.nc = nc
        self.ops = {e: [] for e in ENGS}
        self.cnt = {e: 0 for e in ENGS}
        self.sem = {e: stack.enter_context(nc.semaphore("s_" + e)) for e in ENGS}
        self.dsem = [stack.enter_context(nc.semaphore("d%d" % i)) for i in range(n_dma_sems)]
        self.dcnt = [0] * n_dma_sems
        self.dnext = 0
        self.dnext_q = {}
        self.waited = {e: {} for e in ENGS}
        self.res = {}
        self.nops = 0
        self.psrd = {}

    def _deps(self, reads, writes):
        deps = []
        for (name, lo, hi) in reads:
            st = self.res.setdefault(name, {"w": [], "r": []})
            for (a, b, tok) in st["w"]:
                if a < hi and lo < b:
                    deps.append(tok)
        for (name, lo, hi) in writes:
            st = self.res.setdefault(name, {"w": [], "r": []})
            for (a, b, tok) in st["w"]:
                if a < hi and lo < b:
                    deps.append(tok)
            for (a, b, tok) in st["r"]:
                if a < hi and lo < b:
                    deps.append(tok)
        return deps

    def _record(self, reads, writes, tok):
        for (name, lo, hi) in writes:
            st = self.res[name]
            st["w"] = [(a, b, t) for (a, b, t) in st["w"] if not (lo <= a and b <= hi)]
            st["r"] = [(a, b, t) for (a, b, t) in st["r"] if not (lo <= a and b <= hi)]
            st["w"].append((lo, hi, tok))
        for (name, lo, hi) in reads:
            st = self.res[name]
            st["r"] = [(a, b, t) for (a, b, t) in st["r"]
                       if not (t[0] == tok[0] and lo <= a and b <= hi)]
            st["r"].append((lo, hi, tok))

    @staticmethod
    def _norm(rs):
        out = []
        for r in rs:
            if isinstance(r, str):
                out.append((r, 0, INF))
            elif len(r) == 2:
                out.append((r[0], r[1], r[1] + 1))
            else:
                out.append(tuple(r))
        return out

    def _waits(self, eng, deps):
        ws = {}
        for (key, val, deng) in deps:
            if deng == eng and eng == "pe":
                continue
            if self.waited[eng].get(key, 0) >= val:
                continue
            ws[key] = max(ws.get(key, 0), val)
        for k, v in ws.items():
            self.waited[eng][k] = v
        return list(ws.items())

    def op(self, eng, fn, reads=(), writes=(), inc=True):
        reads = self._norm(reads)
        writes = self._norm(writes)
        deps = self._deps(reads, writes)
        psread = eng in ("dve", "act") and any(r[0] == "ps" for r in reads)
        if psread:
            other = "act" if eng == "dve" else "dve"
            if other in self.psrd:
                deps.append(self.psrd[other])
        waits = self._waits(eng, deps)
        idx = self.cnt[eng] + 1
        if inc:
            self.cnt[eng] = idx
        tok = (("e", eng), idx, eng)
        if psread:
            self.psrd[eng] = tok
        self._record(reads, writes, tok)
        self.ops[eng].append((waits, _capture(fn), ("e", eng) if inc else None, 1))
        self.nops += 1
        return tok

    def dma(self, eng, fn, reads=(), writes=()):
        reads = self._norm(reads)
        writes = self._norm(writes)
        deps = self._deps(reads, writes)
        nd = len(self.dsem)
        lo, hi = (0, (nd * 3) // 5) if eng == "sp" else ((nd * 3) // 5, nd)
        cur = self.dnext_q.get(eng, lo)
        j = cur
        self.dnext_q[eng] = lo + (cur + 1 - lo) % (hi - lo)
        if self.dcnt[j] > 0:
            deps.append((("d", j), 16 * self.dcnt[j], "dma"))
        waits = self._waits(eng, deps)
        self.dcnt[j] += 1
        tok = (("d", j), 16 * self.dcnt[j], "dma")
        self._record(reads, writes, tok)
        self.ops[eng].append((waits, _capture(fn), ("d", j), 16))
        self.nops += 1
        return tok

    def barrier(self):
        deps = [(("e", e), self.cnt[e], "x") for e in ENGS if self.cnt[e] > 0]
        deps += [(("d", j), 16 * c, "dma") for j, c in enumerate(self.dcnt) if c > 0]
        for e in ENGS:
            waits = self._waits(e, [d for d in deps if d[0] != ("e", e)])
            if waits:
                self.ops[e].append((waits, None, None, 0))
        self.res = {}

    def final_wait(self, eng, toks):
        waits = self._waits(eng, list(toks))
        self.ops[eng].append((waits, None, None, 0))

    def _semobj(self, key):
        return self.sem[key[1]] if key[0] == "e" else self.dsem[key[1]]

    def emit(self, block):
        def mk(ename):
            def body(e):
                for (waits, fn, inckey, incv) in self.ops[ename]:
                    for (k, v) in waits:
                        e.wait_ge(self._semobj(k), v)
                    if fn is None:
                        continue
                    name, a, k = fn
                    ins = getattr(e, name)(*a, **k)
                    if inckey is not None:
                        ins.then_inc(self._semobj(inckey), incv)
            return body
        if self.ops["sp"]:
            block.sync(mk("sp"))
        if self.ops["pe"]:
            block.tensor(mk("pe"))
        if self.ops["dve"]:
            block.vector(mk("dve"))
        if self.ops["act"]:
            block.scalar(mk("act"))
        if self.ops["pool"]:
            block.gpsimd(mk("pool"))


D = 2048
S = 2048
NCTX = 256
NALL = NCTX + S
DIN = 6176
DFF = 8192
NCH = NALL // 128
XW = 2310
TW = 2307
LT = 259
NEGBIG = -30000.0


def ch_off(ch):
    return ch * 128 if ch < 2 else LT + (ch - 2) * 128


def build(dbg=False, phases=("A", "B1", "B2", "C", "D")):
    nc = bass.Bass("TRN2", target_bir_lowering=False)
    din = lambda n, s, d=F32: nc.dram_tensor(n, s, d, kind="ExternalInput").ap()
    x_d = din("x", [S, D]); ctx_d = din("ctx", [NCTX, D]); cc_d = din("cc", [128, 32])
    wmod_d = din("w_mod", [D, 6 * D]); bmodT_d = din("b_modT", [128, 96]); nwT_d = din("nwT", [128, 64])
    win_d = din("w_in", [D, DIN])
    lcw_d = din("lcw", [128, 32]); lcb_d = din("lcb", [128, 8]); lgw_d = din("lgw", [128, 4096])
    lgb_d = din("lgb", [128, 32]); llam_d = din("llam", [128, 16])
    gcw_d = din("gcw", [128, 96]); galog_d = din("galog", [128, 16]); gdtb_d = din("gdtb", [128, 16])
    gnw_d = din("gnw", [128, 1])
    wout_d = din("w_out", [D, D]); w1_d = din("w_ff1", [D, DFF]); w2_d = din("w_ff2", [DFF, D])
    out_d = nc.dram_tensor("out", [S, D], F32, kind="ExternalOutput").ap()
    skind = "ExternalOutput" if dbg else "Internal"
    mix_scr = nc.dram_tensor("mix_scr", [16, 128, S], BF16, kind=skind).ap()
    h2_scr = nc.dram_tensor("h2_scr", [16, 128, S], BF16, kind=skind).ap()
    x1_scr = nc.dram_tensor("x1_scr", [S, D], F32, kind=skind).ap()
    w1b = nc.dram_tensor("w1b", [64, 128, 16, 128], BF16, kind="Internal").ap()
    w2b = nc.dram_tensor("w2b", [16, 128, 64, 128], BF16, kind="Internal").ap()
    if dbg:
        hT_dbg = nc.dram_tensor("hT_dbg", [128, 16, NALL], BF16, kind="ExternalOutput").ap()
        modT_dbg = nc.dram_tensor("modT_dbg", [128, 192], F32, kind="ExternalOutput").ap()

    with ExitStack() as top:
        P = Prog(nc, top)
        psb = [top.enter_context(nc.psum_tensor("ps%d" % i, [128, 512], F32)) for i in range(8)]
        pst = {"i": 0}

        reserved = set()

        def bank():
            while True:
                i = pst["i"]; pst["i"] = (i + 1) % 8
                if i not in reserved:
                    return i

        sbt = lambda st, n, s, d=F32: st.enter_context(nc.sbuf_tensor("sb_" + n, s, d))
        ident = sbt(top, "ident", [128, 128]); ones = sbt(top, "ones", [128, 128])
        onesb = sbt(top, "onesb", [128, 128], BF16)
        par = sbt(top, "par", [128, 64 + 96 + 32 + 8 + 32 + 16 + 96 + 16 + 16 + 1])
        o_ = [0]
        def psl(n):
            a = o_[0]; o_[0] += n
            return (a, a + n)
        s_nw, s_bm, s_lcw, s_lcb, s_lgb, s_llam, s_gcw, s_gal, s_gdt, s_gnw = [psl(n) for n in (64, 96, 32, 8, 32, 16, 96, 16, 16, 1)]
        pv = lambda s: par[:, s[0]:s[1]]
        modT = sbt(top, "modT", [128, 192])
        vecs = sbt(top, "vecs", [128, 8, 16])
        cneg = sbt(top, "cneg", [128, 16])
        sccb = sbt(top, "sccb", [128, 32], BF16)
        C3 = [128, NCH, 16]
        beta = sbt(top, "beta", C3); nbeta = sbt(top, "nbeta", C3); gg = sbt(top, "gg", C3); gam = sbt(top, "gam", C3)
        ngam = sbt(top, "ngam", C3); glast = sbt(top, "glast", C3); cdl = sbt(top, "cdl", C3); neg_eg = sbt(top, "neg_eg", C3); kdec = sbt(top, "kdec", C3)
        nega = sbt(top, "nega", [128, 16])
        trif = sbt(top, "trif", [128, 128]); trib = sbt(top, "trib", [128, 128])
        negm = [sbt(top, "negm%d" % d, [128, 128]) for d in range(2)]
        offd = sbt(top, "offd", [128, 128]); bd32 = sbt(top, "bd32", [128, 128]); od64 = sbt(top, "od64", [128, 128]); od128 = sbt(top, "od128", [128, 128])
        s_h = ExitStack()
        hT = sbt(s_h, "hT", [128, 16, NALL], BF16)
        p_scr = nc.dram_tensor("p_scr", [8, 3, 128, NALL], F32, kind="Internal").ap()
        zs_scr = nc.dram_tensor("zs_scr", [8, 128, S], BF16, kind="Internal").ap()
        xs_scr = nc.dram_tensor("xs_scr", [8, 128, NALL], F32, kind="Internal").ap()
        gy_scr = nc.dram_tensor("gy_scr", [8, 128, S], BF16, kind="Internal").ap()
        oacc_scr = nc.dram_tensor("oacc_scr", [16, 128, 128], F32, kind="Internal").ap()

        def make_row(st, vi, dn):
            dst = sbt(st, dn, [128, D]); dg = sbt(st, "dg_" + dn, [128, 512])
            for g4 in range(4):
                for q in range(4):
                    k = g4 * 4 + q
                    P.op("dve", lambda e: e.tensor_scalar(out=dg[:, q * 128:(q + 1) * 128], in0=ident[:], scalar1=vecs[:, vi, k:k + 1], scalar2=None, op0=ALU.mult),
                         reads=["vecs", "ident"], writes=[("dg", q)])
                b = bank()
                P.op("pe", lambda e: e.matmul(psb[b][:], lhsT=ones[:], rhs=dg[:], start=True, stop=True), reads=["ones", "dg"], writes=[("ps", b)])
                P.op("act", lambda e: e.copy(out=dst[:, g4 * 512:(g4 + 1) * 512], in_=psb[b][:]), reads=[("ps", b)], writes=[(dn, g4)])
            return dst

        def load(dst, src, q="sp", name=None, reads=()):
            return P.dma(q, lambda e: e.dma_start(out=dst, in_=src), reads=reads, writes=[name] if name else [])

        P.op("pool", lambda e: e.memset(ones[:], 1.0), writes=["ones"])
        P.op("pool", lambda e: e.memset(onesb[:], 1.0), writes=["onesb"])
        P.op("pool", lambda e: e.memset(ident[:], 1.0), writes=["ident"])
        P.op("pool", lambda e: e.affine_select(out=ident[:], in_=ident[:], pattern=[[-1, 128]], compare_op=ALU.is_equal,
                                               fill=0.0, base=0, channel_multiplier=1), reads=["ident"], writes=["ident"])
        for (sl, src) in ((s_nw, nwT_d), (s_bm, bmodT_d), (s_lcw, lcw_d), (s_lcb, lcb_d), (s_lgb, lgb_d), (s_llam, llam_d),
                          (s_gcw, gcw_d), (s_gal, galog_d), (s_gdt, gdtb_d), (s_gnw, gnw_d)):
            load(pv(sl), src, name=("par", sl[0], sl[1]))

        with ExitStack() as sa:
            cc = sbt(sa, "cc", [128, 32]); scc = sbt(sa, "scc", [128, 32])
            wm = [sbt(sa, "wm%d" % i, [128, 4096], BF16) for i in range(3)]
            load(cc[:], cc_d, name="cc")
            P.op("act", lambda e: e.activation(out=scc[:], in_=cc[:], func=AF.Silu), reads=["cc"], writes=["scc"])
            P.op("dve", lambda e: e.tensor_copy(out=sccb[:], in_=scc[:]), reads=["scc"], writes=["sccb"])
            mb = bank()
            first = True
            for k in range(16):
                i = k % 3
                P.dma("pool", lambda e: e.dma_start(out=wm[i][:], in_=wmod_d[k * 128:(k + 1) * 128, 0:4096]), writes=["wm%d" % i])
                for j in range(32):
                    P.op("pe", lambda e: e.matmul(psb[mb][:, j * 2:j * 2 + 2], lhsT=wm[i][:, j * 128:(j + 1) * 128], rhs=sccb[:, k * 2:k * 2 + 2],
                                                  start=first, stop=(k == 15), skip_group_check=True),
                         reads=["wm%d" % i, "sccb"], writes=[("ps", mb)], inc=(j == 31))
                    first = False
            P.op("dve", lambda e: e.tensor_tensor(out=modT[:, 0:64].rearrange("p (j r) -> p j r", r=2),
                                                  in0=psb[mb][:, 0:64].rearrange("p (j r) -> p j r", r=2),
                                                  in1=pv(s_bm)[:, 0:32].unsqueeze(2).to_broadcast([128, 32, 2]), op=ALU.add),
                 reads=[("ps", mb), ("par", s_bm[0], s_bm[1])], writes=[("modT", 0, 64)])
            mv = modT[:].rearrange("p (j r) -> p j r", r=2)
            nw = pv(s_nw).rearrange("p (w k) -> p w k", w=4)
            rp = [("par", s_nw[0], s_nw[1]), "modT"]
            def scl(dst, sc, w):
                P.op("dve", lambda e: e.scalar_tensor_tensor(out=dst, in0=sc, scalar=1.0, in1=w, op0=ALU.add, op1=ALU.mult),
                     reads=rp, writes=["vecs"])
            scl(vecs[:, 0, :], mv[:, 16:32, 0], nw[:, 0, :])
            P.op("dve", lambda e: e.tensor_copy(out=vecs[:, 1, :], in_=mv[:, 0:16, 0]), reads=rp, writes=["vecs"])
            scl(vecs[:, 2, :], mv[:, 16:32, 1], nw[:, 0, :])
            P.op("dve", lambda e: e.tensor_copy(out=vecs[:, 3, :], in_=mv[:, 0:16, 1]), reads=rp, writes=["vecs"])
            P.op("act", lambda e: e.activation(out=cneg[:], in_=pv(s_llam), func=AF.Exp, scale=-1.0), reads=[("par", s_llam[0], s_llam[1])], writes=["cneg"])
            P.op("act", lambda e: e.activation(out=cneg[:], in_=cneg[:], func=AF.Ln, bias=1.0), reads=["cneg"], writes=["cneg"])
            P.op("dve", lambda e: e.tensor_scalar(out=cneg[:], in0=cneg[:], scalar1=-8.0, scalar2=None, op0=ALU.mult), reads=["cneg"], writes=["cneg"])
            if dbg:
                load(modT_dbg, modT[:], reads=["modT"])

            xt = [sbt(sa, "xt%d" % i, [128, D]) for i in range(3)]
            junk = sbt(sa, "junkA", [128, D])
            st1 = [sbt(sa, "st1_%d" % i, [128, 4]) for i in range(3)]

            def norm_T(src_ap, tname, ti, scl_i, sh_i, dstT, dname, col0, stt, stn):
                t = xt[ti]
                P.op("act", lambda e: e.activation(out=junk[:], in_=t[:], func=AF.Square, accum_out=stt[:, 0:1]),
                     reads=[tname], writes=["junkA", stn])
                P.op("act", lambda e: e.activation(out=stt[:, 1:2], in_=stt[:, 0:1], func=AF.Sqrt, scale=1.0 / D, bias=EPS), reads=[stn], writes=[stn])
                P.op("dve", lambda e: e.reciprocal(out=stt[:, 2:3], in_=stt[:, 1:2]), reads=[stn], writes=[stn])
                P.op("act", lambda e: e.activation(out=t[:], in_=t[:], func=AF.Copy, scale=stt[:, 2:3]), reads=[stn, tname], writes=[tname])
                for g4 in range(4):
                    b = bank()
                    for q in range(4):
                        k = g4 * 4 + q
                        P.op("pe", (lambda b, q, k: lambda e: e.transpose(psb[b][:, q * 128:(q + 1) * 128], t[:, k * 128:(k + 1) * 128], ident[:]))(b, q, k),
                             reads=[tname, "ident"], writes=[("ps", b)], inc=(q == 3))
                    for q in range(4):
                        k = g4 * 4 + q
                        eng = "dve" if g4 % 2 == 0 else "act"
                        if eng == "dve":
                            fn = (lambda b, q, k: lambda e: e.tensor_scalar(out=dstT[:, k, col0:col0 + 128], in0=psb[b][:, q * 128:(q + 1) * 128],
                                                                            scalar1=vecs[:, scl_i, k:k + 1], scalar2=vecs[:, sh_i, k:k + 1],
                                                                            op0=ALU.mult, op1=ALU.add))(b, q, k)
                        else:
                            fn = (lambda b, q, k: lambda e: e.activation(out=dstT[:, k, col0:col0 + 128], in_=psb[b][:, q * 128:(q + 1) * 128],
                                                                         func=AF.Identity, scale=vecs[:, scl_i, k:k + 1], bias=vecs[:, sh_i, k:k + 1]))(b, q, k)
                        P.op(eng, fn, reads=[("ps", b), "vecs"], writes=[(dname, k * 100000 + col0, k * 100000 + col0 + 128)])

            import os as _os
            for ti in range(int(_os.environ.get("K_NTI", NCH))):
                src = ctx_d[ti * 128:(ti + 1) * 128, :] if ti < 2 else x_d[(ti - 2) * 128:(ti - 1) * 128, :]
                i = ti % 3
                load(xt[i][:], src, name="xt%d" % i)
                norm_T(src, "xt%d" % i, i, 2 if ti < 2 else 0, 3 if ti < 2 else 1, hT, "hT", ti * 128, st1[i], "st1_%d" % i)
            if dbg:
                load(hT_dbg, hT[:], reads=["hT"])
        P.barrier()

        hTr = lambda k, c0, c1: ("hT", k * 100000 + c0, k * 100000 + c1)
        hT_all = [("hT", 0, INF)]

        def precast():
            import os as _os
            if _os.environ.get("K_NOPRECAST"):
                return
            w1v = w1_d.rearrange("(k p) (o c) -> o p k c", p=128, c=128)
            for o in range(0, 64, 4):
                for oo in range(4):
                    P.dma("pool", (lambda o: lambda e: e.dma_start(out=w1b[o], in_=w1v[o]))(o + oo), writes=[("w1b", o + oo)])
            w2v = w2_d.rearrange("(k p) (o c) -> o p k c", p=128, c=128)
            for o in range(16):
                for kh in range(4):
                    P.dma("pool", (lambda o, kh: lambda e: e.dma_start(out=w2b[o][:, kh * 16:(kh + 1) * 16, :], in_=w2v[o][:, kh * 16:(kh + 1) * 16, :]))(o, kh),
                          writes=[("w2b", o * 4 + kh)])

        winv = win_d.rearrange("(k p) c -> p k c", p=128)

        def inproj(wt, wname, woff, tok_groups, consume):
            for gi, (c0, n) in enumerate(tok_groups):
                b = bank()
                for k in range(16):
                    P.op("pe", (lambda b, k, c0, n: lambda e: e.matmul(psb[b][:, 0:n], lhsT=wt[:, k, woff:woff + 128], rhs=hT[:, k, c0:c0 + n],
                                                                          start=(k == 0), stop=(k == 15)))(b, k, c0, n),
                         reads=[wname, hTr(k, c0, c0 + n)], writes=[("ps", b)], inc=(k == 15))
                consume(b, gi)

        TG_ALL = [(0, 256)] + [(256 + g * 512, 512) for g in range(4)]
        TG_LAT = [(256 + g * 512, 512) for g in range(4)]

        if "B2" in phases:
          with ExitStack() as sb2a:
            wgb = sbt(sb2a, "wgb", [128, 16, 32], BF16)
            def msk(t, tn, base_val, fill, pattern, cmp, base, cm, src=None):
                if src is None:
                    P.op("pool", lambda e: e.memset(t[:], base_val), writes=[tn])
                P.op("pool", lambda e: e.affine_select(out=t[:], in_=t[:], pattern=pattern, compare_op=cmp, fill=fill, base=base, channel_multiplier=cm),
                     reads=[tn], writes=[tn])
            msk(trif, "trif", 1.0, 0.0, [[1, 128]], ALU.is_ge, 0, -1)
            msk(trib, "trib", 1.0, 0.0, [[-1, 128]], ALU.is_ge, 0, 1)
            msk(negm[0], "negm0", 0.0, NEGBIG, [[1, 128]], ALU.is_ge, 0, -1)
            msk(negm[1], "negm1", 0.0, NEGBIG, [[-1, 128]], ALU.is_ge, 0, 1)
            msk(offd, "offd", 1.0, 0.0, [[-1, 128]], ALU.not_equal, 0, 1)
            for (t, tn, fn_) in ((bd32, "bd32", lambda pb, cb: 1.0 if pb == cb else 0.0),
                                 (od64, "od64", lambda pb, cb: 1.0 if (pb != cb and pb // 2 == cb // 2) else 0.0),
                                 (od128, "od128", lambda pb, cb: 1.0 if pb // 2 != cb // 2 else 0.0)):
                for pb in range(4):
                    for cb in range(4):
                        P.op("pool", (lambda t, pb, cb, v: lambda e: e.memset(t[pb * 32:(pb + 1) * 32, cb * 32:(cb + 1) * 32], v))(t, pb, cb, fn_(pb, cb)), writes=[tn])
            P.dma("pool", lambda e: e.dma_start(out=wgb[:], in_=winv[:, :, 6144:6176]), writes=["wgb"])
            gbank = [bank(), bank()]
            for ch in range(NCH):
                b = gbank[ch // 9]; co = (ch % 9) * 32
                for k in range(16):
                    P.op("pe", (lambda b, co, ch, k: lambda e: e.matmul(psb[b][:, co:co + 32], lhsT=hT[:, k, ch * 128:(ch + 1) * 128], rhs=wgb[:, k, :],
                                                                          start=(k == 0), stop=(k == 15), skip_group_check=True))(b, co, ch, k),
                         reads=["wgb"] + hT_all, writes=[("ps", b)], inc=(k == 15))
            P.op("act", lambda e: e.activation(out=nega[:], in_=pv(s_gal), func=AF.Exp), reads=[("par", 0, INF)], writes=["nega"])
            P.op("dve", lambda e: e.tensor_scalar(out=nega[:], in0=nega[:], scalar1=-1.0, scalar2=None, op0=ALU.mult), reads=["nega"], writes=["nega"])
            for half in range(2):
                b = gbank[half]
                pvw = psb[b][:, 0:288].rearrange("p (c f) -> p c f", f=32)
                cs = slice(half * 9, half * 9 + 9)
                P.op("act", (lambda pvw, cs: lambda e: e.activation(out=beta[:, cs, :], in_=pvw[:, :, 0:16], func=AF.Sigmoid))(pvw, cs), reads=[("ps", b)], writes=["beta"])
                P.op("dve", (lambda pvw, cs: lambda e: e.tensor_tensor(out=gg[:, cs, :], in0=pvw[:, :, 16:32], in1=pv(s_gdt).unsqueeze(1).to_broadcast([128, 9, 16]), op=ALU.add))(pvw, cs),
                     reads=[("ps", b), ("par", 0, INF)], writes=["gg"])
            P.op("act", lambda e: e.activation(out=gg[:], in_=gg[:], func=AF.Exp), reads=["gg"], writes=["gg"])
            P.op("act", lambda e: e.activation(out=gg[:], in_=gg[:], func=AF.Ln, bias=1.0), reads=["gg"], writes=["gg"])
            P.op("dve", lambda e: e.tensor_tensor(out=gg[:], in0=gg[:], in1=nega[:].unsqueeze(1).to_broadcast([128, NCH, 16]), op=ALU.mult), reads=["gg", "nega"], writes=["gg"])
            P.op("dve", lambda e: e.tensor_scalar(out=nbeta[:], in0=beta[:], scalar1=-1.0, scalar2=None, op0=ALU.mult), reads=["beta"], writes=["nbeta"])
            for d in range(2):
                b = bank()
                tri = trif if d == 0 else trib
                P.op("pe", (lambda b, tri, d: lambda e: e.matmul(psb[b][:, 0:NCH * 8].rearrange("p (c f) -> p c f", f=8), lhsT=tri[:], rhs=gg[:, :, d * 8:(d + 1) * 8], start=True, stop=True))(b, tri, d),
                     reads=["trif", "trib", "gg"], writes=[("ps", b)])
                P.op("act", (lambda b, d: lambda e: e.copy(out=gam[:, :, d * 8:(d + 1) * 8], in_=psb[b][:, 0:NCH * 8].rearrange("p (c f) -> p c f", f=8)))(b, d),
                     reads=[("ps", b)], writes=["gam"])
            b = bank()
            P.op("pe", (lambda b: lambda e: e.matmul(psb[b][:, 0:NCH * 16], lhsT=ones[:], rhs=gg[:].rearrange("p c f -> p (c f)"), start=True, stop=True))(b),
                 reads=["ones", "gg"], writes=[("ps", b)])
            P.op("act", (lambda b: lambda e: e.copy(out=glast[:].rearrange("p c f -> p (c f)"), in_=psb[b][:, 0:NCH * 16]))(b), reads=[("ps", b)], writes=["glast"])
            P.op("dve", lambda e: e.tensor_scalar(out=ngam[:], in0=gam[:], scalar1=-1.0, scalar2=None, op0=ALU.mult), reads=["gam"], writes=["ngam"])
            P.op("act", lambda e: e.activation(out=cdl[:], in_=glast[:], func=AF.Exp), reads=["glast"], writes=["cdl"])
            P.op("dve", lambda e: e.tensor_tensor(out=kdec[:], in0=glast[:], in1=gam[:], op=ALU.subtract), reads=["glast", "gam"], writes=["kdec"])
            P.op("act", lambda e: e.activation(out=kdec[:], in_=kdec[:], func=AF.Exp), reads=["kdec"], writes=["kdec"])
            P.op("act", lambda e: e.activation(out=neg_eg[:], in_=gam[:], func=AF.Exp), reads=["gam"], writes=["neg_eg"])
            P.op("dve", lambda e: e.tensor_scalar(out=neg_eg[:], in0=neg_eg[:], scalar1=-1.0, scalar2=None, op0=ALU.mult), reads=["neg_eg"], writes=["neg_eg"])

            wm2 = [sbt(sb2a, "wm2_%d" % i, [128, 4096], BF16) for i in range(3)]
            mb2 = bank(); reserved.add(mb2)
            mpieces = [(k, pc) for k in range(16) for pc in (1, 2)]
            mpi = [0]

            def mod_pieces(n):
                for _ in range(n):
                    if mpi[0] >= len(mpieces):
                        return
                    k, pc = mpieces[mpi[0]]
                    i = mpi[0] % 3
                    first = (mpi[0] == 0)
                    mpi[0] += 1
                    P.dma("pool", lambda e: e.dma_start(out=wm2[i][:], in_=wmod_d[k * 128:(k + 1) * 128, pc * 4096:(pc + 1) * 4096]), writes=["wm2_%d" % i])
                    for j in range(32):
                        jj = (pc - 1) * 32 + j
                        P.op("pe", lambda e: e.matmul(psb[mb2][:, jj * 2:jj * 2 + 2], lhsT=wm2[i][:, j * 128:(j + 1) * 128], rhs=sccb[:, k * 2:k * 2 + 2],
                                                      start=(first and j == 0), stop=(k == 15), skip_group_check=True),
                             reads=["wm2_%d" % i, "sccb"], writes=[("ps", mb2)], inc=(j == 31))

            wg = [sbt(sb2a, "wg%d" % i, [128, 16, 512], BF16) for i in range(2)]
            stgs = [sbt(sb2a, "stgs%d" % i, [128, 512]) for i in range(4)]
            zst = [sbt(sb2a, "zst%d" % i, [128, 512], BF16) for i in range(2)]
            sti = [0]
            for h in range(8):
                w = wg[h % 2]; wn = "wg%d" % (h % 2)
                for t4 in range(4):
                    c0 = 2048 + t4 * 1024 + h * 128
                    P.dma("pool", lambda e: e.dma_start(out=w[:, :, t4 * 128:(t4 + 1) * 128], in_=winv[:, :, c0:c0 + 128]), writes=[(wn, t4)])
                for t3 in range(3):
                    def cons_p(b, gi):
                        c0, n = TG_ALL[gi]
                        i = sti[0] % 4; sti[0] += 1
                        P.op("act", lambda e: e.copy(out=stgs[i][:, 0:n], in_=psb[b][:, 0:n]), reads=[("ps", b)], writes=["stgs%d" % i])
                        load(p_scr[h, t3][:, c0:c0 + n], stgs[i][:, 0:n], reads=["stgs%d" % i], name=("p_scr", h * 3 + t3))
                    inproj(w, (wn, t3), t3 * 128, TG_ALL, cons_p)

                def cons_zs(b, gi):
                    i = gi % 2
                    P.op("act", lambda e: e.activation(out=zst[i][:], in_=psb[b][:], func=AF.Silu), reads=[("ps", b)], writes=["zst%d" % i])
                    load(zs_scr[h][:, gi * 512:(gi + 1) * 512], zst[i][:], reads=["zst%d" % i], name=("zs_scr", h))
                inproj(w, (wn, 3), 384, TG_LAT, cons_zs)
                mod_pieces(2)
            xst = [sbt(sb2a, "xst%d" % i, [128, NALL]) for i in range(2)]
            gyst = [sbt(sb2a, "gyst%d" % i, [128, S], BF16) for i in range(2)]
            for h in range(8):
                w = wg[h % 2]; wn = "wg%d" % (h % 2)
                P.dma("pool", lambda e: e.dma_start(out=w[:, :, 0:128], in_=winv[:, :, h * 128:(h + 1) * 128]), writes=[(wn, 0)])
                P.dma("pool", lambda e: e.dma_start(out=w[:, :, 128:256], in_=winv[:, :, 1024 + h * 128:1024 + (h + 1) * 128]), writes=[(wn, 1)])
                xs = xst[h % 2]; xsn = "xst%d" % (h % 2)
                gs_ = gyst[h % 2]; gsn = "gyst%d" % (h % 2)

                def cons_x(b, gi):
                    if gi == 0:
                        P.op("act", lambda e: e.copy(out=xs[:, 0:256], in_=psb[b][:, 0:256]), reads=[("ps", b)], writes=[(xsn, 0)])
                    else:
                        r0 = (gi - 1) * 8
                        ov = xs[:, 256:NALL].rearrange("p (c r) -> p r c", r=32)[:, r0:r0 + 8, :]
                        iv = psb[b][:].rearrange("p (r c) -> p r c", c=64)
                        P.op("act", lambda e: e.copy(out=ov, in_=iv), reads=[("ps", b)], writes=[(xsn, gi)])
                inproj(w, (wn, 0), 0, TG_ALL, cons_x)
                load(xs_scr[h], xs[:], reads=[xsn], name=("xs_scr", h))

                def cons_y(b, gi):
                    P.op("act", lambda e: e.activation(out=gs_[:, gi * 512:(gi + 1) * 512], in_=psb[b][:], func=AF.Gelu_apprx_tanh), reads=[("ps", b)], writes=[(gsn, gi)])
                inproj(w, (wn, 1), 128, TG_LAT, cons_y)
                load(gy_scr[h], gs_[:], reads=[gsn], name=("gy_scr", h))
                mod_pieces(2)
            mod_pieces(100)
            P.op("dve", lambda e: e.tensor_tensor(out=modT[:, 64:192].rearrange("p (j r) -> p j r", r=2),
                                                  in0=psb[mb2][:, 0:128].rearrange("p (j r) -> p j r", r=2),
                                                  in1=pv(s_bm)[:, 32:96].unsqueeze(2).to_broadcast([128, 64, 2]), op=ALU.add),
                 reads=[("ps", mb2), ("par", 0, INF)], writes=[("modT", 64, 192)])
            reserved.discard(mb2)
            mv = modT[:].rearrange("p (j r) -> p j r", r=2)
            nw = pv(s_nw).rearrange("p (w k) -> p w k", w=4)
            rp = [("par", 0, INF), "modT"]
            P.op("dve", lambda e: e.tensor_tensor(out=vecs[:, 4, :], in0=mv[:, 32:48, 0], in1=nw[:, 1, :], op=ALU.mult), reads=rp, writes=[("vecs", 4)])
            P.op("dve", lambda e: e.scalar_tensor_tensor(out=vecs[:, 5, :], in0=mv[:, 64:80, 0], scalar=1.0, in1=nw[:, 2, :], op0=ALU.add, op1=ALU.mult), reads=rp, writes=[("vecs", 5)])
            P.op("dve", lambda e: e.tensor_copy(out=vecs[:, 6, :], in_=mv[:, 48:64, 0]), reads=rp, writes=[("vecs", 6)])
            P.op("dve", lambda e: e.tensor_tensor(out=vecs[:, 7, :], in0=mv[:, 80:96, 0], in1=nw[:, 3, :], op=ALU.mult), reads=rp, writes=[("vecs", 7)])
          P.barrier()
        s_h.close()

        freeb = set(range(8))

        def take(n):
            while len(freeb) < n:
                yield
            return [freeb.pop() for _ in range(n)]

        def rel(*bs):
            for b_ in bs:
                assert b_ not in freeb
                freeb.add(b_)

        def run_tasks(gens):
            active = list(gens)
            while active:
                for g in list(active):
                    try:
                        next(g)
                    except StopIteration:
                        active.remove(g)

        if "B1" in phases:
          with ExitStack() as sb1:
            lgw = sbt(sb1, "lgw", [128, 4096], BF16)
            P.dma("pool", lambda e: e.dma_start(out=lgw[:], in_=lgw_d), writes=["lgw"])
            xpad2 = [sbt(sb1, "xpad0", [128, XW])] * 2
            gyb2 = [sbt(sb1, "gyb%d" % i, [128, S], BF16) for i in range(2)]
            acc2 = [sbt(sb1, "acc0", [128, TW])] * 2; xrb2 = [sbt(sb1, "xrb0", [128, TW], BF16)] * 2
            mixs2 = [sbt(sb1, "mixs%d" % i, [128, S], BF16) for i in range(2)]
            RtA = [[sbt(sb1, "Rt%d_%d" % (d, i), [128, TW]) for d in range(2)] for i in range(2)]
            ItA = [[sbt(sb1, "It%d_%d" % (d, i), [128, TW]) for d in range(2)] for i in range(2)]
            StA = [[sbt(sb1, "St%d_%d" % (d, i), [128, TW]) for d in range(2)] for i in range(2)]
            P.op("pool", lambda e: e.memset(xpad2[0][:], 0.0), writes=["xpad0"])
            lcw = pv(s_lcw).rearrange("p (h j) -> p h j", j=4); lcb = pv(s_lcb)
            lgb = pv(s_lgb).rearrange("p (d g h) -> p d g h", d=2, g=2)
            rpar = [("par", 0, INF)]
            BLK = [(0, 256)] + [(LT + i * 512, 512) for i in range(4)]
            rg_ = lambda nm, c0, n: (nm, c0, c0 + n)

            xp = xpad2[0]; xpn = "xpad0"; acc = acc2[0]; accn = "acc0"; xrb = xrb2[0]; xrn = "xrb0"

            def lpre(h):
                hb = h % 2
                load(xp[:, 2:258], xs_scr[h][:, 0:256], reads=[("xs_scr", h)], name=(xpn, 2, 258))
                load(xp[:, 261:2309], xs_scr[h][:, 256:NALL], reads=[("xs_scr", h)], name=(xpn, 261, 2309))
                load(gyb2[hb][:], gy_scr[h], reads=[("gy_scr", h)], name="gyb%d" % hb)
                P.op("act", lambda e: e.activation(out=acc[:], in_=xp[:, 0:TW], func=AF.Identity, scale=lcw[:, h, 0:1], bias=lcb[:, h:h + 1]), reads=[xpn] + rpar, writes=[accn])
                for j in (1, 2, 3):
                    P.op("dve", lambda e: e.scalar_tensor_tensor(out=acc[:], in0=xp[:, j:j + TW], scalar=lcw[:, h, j:j + 1], in1=acc[:], op0=ALU.mult, op1=ALU.add),
                         reads=[xpn, accn] + rpar, writes=[accn])
                P.op("pool", lambda e: e.tensor_copy(out=xrb[:], in_=acc[:]), reads=[accn], writes=[xrn])

            def lst12(h):
                hb = h % 2
                Rt, It, St = RtA[hb], ItA[hb], StA[hb]
                sfx = "_%d" % hb
                for (c0, n) in BLK:
                    for d in range(2):
                        for g, dst, dn in ((0, Rt[d], "Rt%d" % d + sfx), (1, It[d], "It%d" % d + sfx)):
                            woff = ((d * 2 + g) * 8 + h) * 128
                            bb = bank()
                            P.op("pe", lambda e: e.matmul(psb[bb][:, 0:n], lhsT=lgw[:, woff:woff + 128], rhs=xrb[:, c0:c0 + n], start=True, stop=True),
                                 reads=["lgw", rg_(xrn, c0, n)], writes=[("ps", bb)])
                            P.op("act", lambda e: e.activation(out=dst[:, c0:c0 + n], in_=psb[bb][:, 0:n], func=AF.Sigmoid, bias=lgb[:, d, g, h:h + 1]),
                                 reads=[("ps", bb)] + rpar, writes=[rg_(dn, c0, n)])
                for (c0, n) in BLK:
                    for d in range(2):
                        ci = d * 8 + h
                        P.op("act", lambda e: e.activation(out=Rt[d][:, c0:c0 + n], in_=Rt[d][:, c0:c0 + n], func=AF.Exp, scale=cneg[:, ci:ci + 1]),
                             reads=[rg_("Rt%d" % d + sfx, c0, n), "cneg"], writes=[rg_("Rt%d" % d + sfx, c0, n)])
                        P.op("pool", lambda e: e.tensor_tensor(out=It[d][:, c0:c0 + n], in0=It[d][:, c0:c0 + n], in1=acc[:, c0:c0 + n], op=ALU.mult),
                             reads=[rg_("It%d" % d + sfx, c0, n), rg_(accn, c0, n)], writes=[rg_("It%d" % d + sfx, c0, n)])
                        P.op("pool", lambda e: e.tensor_tensor(out=St[d][:, c0:c0 + n], in0=Rt[d][:, c0:c0 + n], in1=Rt[d][:, c0:c0 + n], op=ALU.mult),
                             reads=[rg_("Rt%d" % d + sfx, c0, n)], writes=[rg_("St%d" % d + sfx, c0, n)])

            def lst3(h):
                hb = h % 2
                Rt, It, St = RtA[hb], ItA[hb], StA[hb]
                Hx = St
                sfx = "_%d" % hb
                for d in range(2):
                    for (c0, n) in BLK:
                        P.op("act", lambda e: e.activation(out=St[d][:, c0:c0 + n], in_=St[d][:, c0:c0 + n], func=AF.Sqrt, scale=-1.0, bias=1.0),
                             reads=[rg_("St%d" % d + sfx, c0, n)], writes=[rg_("St%d" % d + sfx, c0, n)])
                for d in range(2):
                    order = BLK if d == 0 else [BLK[0]] + BLK[:0:-1]
                    hx = Hx[d]; hxn = "St%d" % d + sfx
                    prev = None
                    for (c0, n) in order:
                        P.op("dve", lambda e: e.tensor_tensor(out=St[d][:, c0:c0 + n], in0=St[d][:, c0:c0 + n], in1=It[d][:, c0:c0 + n], op=ALU.mult),
                             reads=[rg_("St%d" % d + sfx, c0, n), rg_("It%d" % d + sfx, c0, n)], writes=[rg_("St%d" % d + sfx, c0, n)])
                        if prev is None:
                            init = 0.0; rd = []
                        else:
                            pc0, pn = prev
                            init = hx[:, pc0 + pn - 1:pc0 + pn] if d == 0 else hx[:, pc0:pc0 + 1]
                            rd = [rg_(hxn, pc0, pn)]
                        a_v = Rt[d][:, c0:c0 + n]; u_v = St[d][:, c0:c0 + n]; o_v = hx[:, c0:c0 + n]
                        if d == 1:
                            a_v, u_v, o_v = a_v[:, ::-1], u_v[:, ::-1], o_v[:, ::-1]
                        P.op("dve", lambda e: e.tensor_tensor_scan(out=o_v, data0=a_v, data1=u_v, initial=init, op0=ALU.mult, op1=ALU.add),
                             reads=[rg_("Rt%d" % d + sfx, c0, n), rg_("St%d" % d + sfx, c0, n)] + rd, writes=[rg_(hxn, c0, n)])
                        prev = (c0, n)
                ms = mixs2[hb]; msn = "mixs%d" % hb
                P.op("dve", lambda e: e.tensor_tensor(out=Hx[0][:, LT:TW], in0=Hx[0][:, LT:TW], in1=Hx[1][:, LT:TW], op=ALU.add), reads=["St0" + sfx, "St1" + sfx], writes=["St0" + sfx])
                P.op("pool", lambda e: e.tensor_tensor(out=ms[:].rearrange("p (r c) -> p r c", c=64), in0=gyb2[hb][:].rearrange("p (r c) -> p r c", c=64),
                                                       in1=Hx[0][:, LT:TW].rearrange("p (c r) -> p r c", r=32), op=ALU.mult), reads=["gyb%d" % hb, "St0" + sfx], writes=[msn])
                load(mix_scr[h], ms[:], reads=[msn], name=("mix_scr", h))

            lpre(0)
            for h in range(8):
                lst12(h)
                if h + 1 < 8:
                    lpre(h + 1)
                lst3(h)
            assert len(freeb) == 8
          P.barrier()


        if "B2" in phases:
          with ExitStack() as sb2:
            NSETS = 3
            graw = sbt(sb2, "graw0", [128, XW]); grn = "graw0"
            cacc = sbt(sb2, "cacc", [128, TW]); sqb = sbt(sb2, "sqb", [128, TW], BF16)
            rng_ = [sbt(sb2, "rng%d" % i, [128, 512]) for i in range(2)]
            qn2 = [sbt(sb2, "qn%d" % i, [128, TW], F32R) for i in range(2)]; kn2 = [sbt(sb2, "kn%d" % i, [128, TW], F32R) for i in range(2)]
            vf = cacc
            zs2 = [sbt(sb2, "zs%d" % i, [128, S], BF16) for i in range(2)]
            Ktok2 = [sbt(sb2, "Ktok%d" % i, [128, NCH, 128]) for i in range(2)]; Vtok2 = [sbt(sb2, "Vtok%d" % i, [128, NCH, 128]) for i in range(2)]
            oacc = sbt(sb2, "oacc", [128, 16, 128])
            mixg = sbt(sb2, "mixg0", [128, S], BF16); mgn = "mixg0"
            Sst = [sbt(sb2, "Sst%d" % d, [128, 128], F32R) for d in range(2)]
            G4 = [128, 4, 128]
            SETS = []
            for si in range(NSETS):
                t = {}
                for nm in ("Gs", "DT", "Erow", "QdT", "QKT", "Kd", "Nm", "NmT"):
                    t[nm] = sbt(sb2, "%s_%d" % (nm, si), G4, F32)
                    t[nm + "_n"] = "%s_%d" % (nm, si)
                for nm, al in (("Wm", "Gs"), ("Qa", "DT"), ("Qb", "Erow")):
                    t[nm], t[nm + "_n"] = t[al], t[al + "_n"]
                t["tmpWT"], t["tmpWT_n"] = t["Nm"], t["Nm_n"]
                t["tmpZ"], t["tmpZ_n"] = t["Qa"], t["Qa_n"]
                SETS.append(t)
            Rp = [sbt(sb2, "Rp%d" % i, [128, 128], F32R) for i in range(2)]; Vn = [sbt(sb2, "Vn%d" % i, [128, 128], F32R) for i in range(2)]
            ot = [sbt(sb2, "ot%d" % i, [128, 128]) for i in range(2)]; junkg = [sbt(sb2, "junkg%d" % i, [128, 128]) for i in range(2)]
            stg = [sbt(sb2, "stg%d" % i, [128, 4]) for i in range(2)]
            pcs = [sbt(sb2, "pcs%d" % i, [128, 16, 128], BF16) for i in range(2)]
            P.op("pool", lambda e: e.memset(graw[:], 0.0), writes=[grn])
            gcw = pv(s_gcw).rearrange("p (t h j) -> p t h j", t=3, j=4)
            f32v = lambda ap: ap.bitcast(F32)
            bcast4 = lambda m: m[:].unsqueeze(1).to_broadcast([128, 4, 128])
            fl = lambda t: t[:].rearrange("p u c -> p (u c)")
            fwd_order = list(range(NCH)); bwd_order = [1, 0] + list(range(17, 1, -1))
            rpar = [("par", 0, INF)]
            w1v_ = w1_d.rearrange("(k p) (o c) -> o p k c", p=128, c=128)
            w2v_ = w2_d.rearrange("(k p) (o c) -> o p k c", p=128, c=128)
            pjobs = [(w1v_[o], w1b[o], ("w1b", o)) for o in range(64)]
            pjobs += [(w2v_[o][:, kh * 16:(kh + 1) * 16, :], w2b[o][:, kh * 16:(kh + 1) * 16, :], ("w2b", o * 4 + kh)) for o in range(16) for kh in range(4)]
            pji = [0]

            def precast_some(n):
                for _ in range(n):
                    if pji[0] >= len(pjobs):
                        return
                    src, dst, rn = pjobs[pji[0]]
                    i = pji[0] % 2
                    pji[0] += 1
                    P.dma("pool", lambda e: e.dma_start(out=pcs[i][:], in_=src), writes=["pcs%d" % i])
                    P.dma("sp", lambda e: e.dma_start(out=dst, in_=pcs[i][:]), reads=["pcs%d" % i], writes=[rn])

            import os as _os
            r_ = lambda ap: ap.bitcast(F32R)
            NH = int(_os.environ.get("K_NGDN", 8))
            NIT = NCH // 2

            def mm4(b, lhs, ln, rhs, rn):
                for u in range(4):
                    P.op("pe", lambda e: e.matmul(psb[b][:, u * 128:(u + 1) * 128], lhsT=r_(lhs[:, u, :]), rhs=r_(rhs[:, u, :]), start=True, stop=True, skip_group_check=True),
                         reads=[ln, rn], writes=[("ps", b)], inc=(u == 3))

            def transpose4(b, src, sname):
                for u in range(4):
                    P.op("pe", lambda e: e.transpose(psb[b][:, u * 128:(u + 1) * 128], src[:, u, :], ident[:]),
                         reads=[sname, "ident"], writes=[("ps", b)], inc=(u == 3))

            def headpre(h):
                hb = h % 2
                qn, kn, Ktok, Vtok, zs = qn2[hb], kn2[hb], Ktok2[hb], Vtok2[hb], zs2[hb]
                qnn, knn, Ktn, Vtn, zsn = "qn%d" % hb, "kn%d" % hb, "Ktok%d" % hb, "Vtok%d" % hb, "zs%d" % hb
                load(zs[:], zs_scr[h], reads=[("zs_scr", h)], name=zsn)
                for t3 in range(3):
                    load(graw[:, 2:258], p_scr[h, t3][:, 0:256], reads=[("p_scr", h * 3 + t3)], name=(grn, 2, 258))
                    load(graw[:, 261:2309], p_scr[h, t3][:, 256:NALL], reads=[("p_scr", h * 3 + t3)], name=(grn, 261, 2309))
                    P.op("act", lambda e: e.activation(out=cacc[:], in_=graw[:, 0:TW], func=AF.Copy, scale=gcw[:, t3, h, 0:1]), reads=[grn] + rpar, writes=["cacc"])
                    yield
                    for j in (1, 2, 3):
                        P.op("dve", lambda e: e.scalar_tensor_tensor(out=cacc[:], in0=graw[:, j:j + TW], scalar=gcw[:, t3, h, j:j + 1], in1=cacc[:], op0=ALU.mult, op1=ALU.add),
                             reads=[grn, "cacc"] + rpar, writes=["cacc"])
                        yield
                    P.op("act", lambda e: e.activation(out=cacc[:], in_=cacc[:], func=AF.Silu), reads=["cacc"], writes=["cacc"])
                    yield
                    if t3 < 2:
                        P.op("pool", lambda e: e.tensor_tensor(out=sqb[:], in0=cacc[:], in1=cacc[:], op=ALU.mult), reads=["cacc"], writes=["sqb"])
                        yield
                        sc_ = 128.0 if t3 == 0 else 1.0
                        dq = qn if t3 == 0 else kn; dqn = qnn if t3 == 0 else knn
                        for gi, (c0, n) in enumerate([(g * 512, 512) for g in range(4)] + [(2048, TW - 2048)]):
                            b, = yield from take(1)
                            rg = rng_[gi % 2]; rgn = "rng%d" % (gi % 2)
                            P.op("pe", lambda e: e.matmul(psb[b][:, 0:n], lhsT=onesb[:], rhs=sqb[:, c0:c0 + n], start=True, stop=True), reads=["onesb", "sqb"], writes=[("ps", b)])
                            yield
                            P.op("dve", lambda e: e.tensor_copy(out=rg[:, 0:n], in_=psb[b][:, 0:n]), reads=[("ps", b)], writes=[rgn])
                            rel(b)
                            yield
                            P.op("act", lambda e: e.activation(out=rg[:, 0:n], in_=rg[:, 0:n], func=AF.Ln, scale=sc_, bias=sc_ * EPS), reads=[rgn], writes=[rgn])
                            P.op("act", lambda e: e.activation(out=rg[:, 0:n], in_=rg[:, 0:n], func=AF.Exp, scale=-0.5), reads=[rgn], writes=[rgn])
                            yield
                            P.op("pool", lambda e: e.tensor_tensor(out=dq[:, c0:c0 + n], in0=cacc[:, c0:c0 + n], in1=rg[:, 0:n], op=ALU.mult), reads=["cacc", rgn], writes=[(dqn, c0, c0 + n)])
                            yield
                    if t3 >= 1:
                        srcT, sname, dstK, dname = (kn, knn, Ktok, Ktn) if t3 == 1 else (vf, "cacc", Vtok, Vtn)
                        for c4 in range(0, NCH, 4):
                            n4 = min(4, NCH - c4)
                            b, = yield from take(1)
                            for q in range(n4):
                                co = ch_off(c4 + q)
                                P.op("pe", lambda e: e.transpose(psb[b][:, q * 128:(q + 1) * 128], f32v(srcT[:, co:co + 128]), ident[:]),
                                     reads=[sname, "ident"], writes=[("ps", b)], inc=(q == n4 - 1))
                            yield
                            P.op("dve", lambda e: e.tensor_copy(out=dstK[:, c4:c4 + n4, :], in_=psb[b][:, 0:n4 * 128].rearrange("p (q c) -> p q c", c=128)),
                                 reads=[("ps", b)], writes=[dname])
                            rel(b)
                            yield

            NG = NH * NIT
            prep_done = [False] * NG
            rec_done = [[False] * NG for _ in range(2)]
            pre_done = [False] * NH
            head_fin = [False] * NH
            dir_fin = [[False] * NH for _ in range(2)]
            odone_h = [[False] * 16 for _ in range(NH)]

            if True:
                def units_of(it):
                    return [(0, fwd_order[2 * it]), (0, fwd_order[2 * it + 1]), (1, bwd_order[2 * it]), (1, bwd_order[2 * it + 1])]

                def prep(g):
                    h, it = divmod(g, NIT)
                    hb = h % 2
                    qn, kn, Ktok, Vtok, zs = qn2[hb], kn2[hb], Ktok2[hb], Vtok2[hb], zs2[hb]
                    qnn, knn, Ktn, Vtn, zsn = "qn%d" % hb, "kn%d" % hb, "Ktok%d" % hb, "Vtok%d" % hb, "zs%d" % hb
                    hd = lambda d: d * 8 + h
                    T_ = SETS[g % NSETS]
                    while (not pre_done[h]) or (g >= NSETS and not (rec_done[0][g - NSETS] and rec_done[1][g - NSETS])):
                        yield
                    units = units_of(it)
                    Gs, DT, Erow, QdT, QKT, Kd, Nm, NmT, Qa, Qb, Wm = [T_[k] for k in ("Gs", "DT", "Erow", "QdT", "QKT", "Kd", "Nm", "NmT", "Qa", "Qb", "Wm")]
                    n_ = lambda k: T_[k + "_n"]
                    precast_some(2)
                    for u, (d, ch) in enumerate(units):
                        P.op("act", lambda e: e.activation(out=NmT[:, u, :], in_=ident[:], func=AF.Copy, scale=gam[:, ch, hd(d):hd(d) + 1]),
                             reads=["ident", "gam"], writes=[(n_("NmT"), u)])
                        P.op("act", lambda e: e.activation(out=r_(Kd[:, u, :]), in_=Ktok[:, ch, :], func=AF.Copy, scale=kdec[:, ch, hd(d):hd(d) + 1]),
                             reads=[Ktn, "kdec"], writes=[(n_("Kd"), u)])
                    yield
                    bG, bK, bQ = yield from take(3)
                    P.op("pe", lambda e: e.matmul(psb[bG][:], lhsT=ones[:], rhs=fl(NmT), start=True, stop=True), reads=["ones", n_("NmT")], writes=[("ps", bG)])
                    for u, (d, ch) in enumerate(units):
                        co = ch_off(ch)
                        P.op("pe", lambda e: e.matmul(psb[bK][:, u * 128:(u + 1) * 128], lhsT=kn[:, co:co + 128], rhs=kn[:, co:co + 128], start=True, stop=True, skip_group_check=True),
                             reads=[knn], writes=[("ps", bK)], inc=False)
                        P.op("pe", lambda e: e.matmul(psb[bQ][:, u * 128:(u + 1) * 128], lhsT=kn[:, co:co + 128], rhs=qn[:, co:co + 128], start=True, stop=True, skip_group_check=True),
                             reads=[knn, qnn], writes=[("ps", bQ)], inc=(u == 3))
                    yield
                    P.op("dve", lambda e: e.tensor_copy(out=r_(fl(Gs)), in_=psb[bG][:]), reads=[("ps", bG)], writes=[n_("Gs")])
                    for u, (d, ch) in enumerate(units):
                        P.op("dve", lambda e: e.scalar_tensor_tensor(out=r_(DT[:, u, :]), in0=psb[bG][:, u * 128:(u + 1) * 128], scalar=gam[:, ch, hd(d):hd(d) + 1], in1=negm[d][:], op0=ALU.subtract, op1=ALU.add),
                             reads=[("ps", bG), "gam", "negm%d" % d], writes=[(n_("DT"), u)])
                    rel(bG)
                    yield
                    P.op("act", lambda e: e.activation(out=r_(fl(Erow)), in_=fl(Gs), func=AF.Exp), reads=[n_("Gs")], writes=[n_("Erow")])
                    P.op("act", lambda e: e.activation(out=r_(fl(DT)), in_=fl(DT), func=AF.Exp), reads=[n_("DT")], writes=[n_("DT")])
                    for u, (d, ch) in enumerate(units):
                        co = ch_off(ch)
                        if ch >= 2:
                            P.op("pool", lambda e: e.tensor_tensor(out=r_(QdT[:, u, :]), in0=f32v(qn[:, co:co + 128]), in1=Erow[:, u, :], op=ALU.mult),
                                 reads=[qnn, n_("Erow")], writes=[(n_("QdT"), u)])
                    yield
                    P.op("dve", lambda e: e.tensor_tensor(out=r_(fl(QKT)), in0=psb[bQ][:], in1=fl(DT), op=ALU.mult), reads=[("ps", bQ), n_("DT")], writes=[n_("QKT")])
                    for u, (d, ch) in enumerate(units):
                        P.op("dve", lambda e: e.scalar_tensor_tensor(out=r_(Nm[:, u, :]), in0=psb[bK][:, u * 128:(u + 1) * 128], scalar=nbeta[:, ch, hd(d):hd(d) + 1], in1=DT[:, u, :], op0=ALU.mult, op1=ALU.mult),
                             reads=[("ps", bK), "nbeta", n_("DT")], writes=[(n_("Nm"), u)])
                    rel(bK, bQ)
                    yield
                    P.op("pool", lambda e: e.tensor_tensor(out=r_(Nm[:]), in0=Nm[:], in1=bcast4(offd), op=ALU.mult), reads=[n_("Nm"), "offd"], writes=[n_("Nm")])
                    yield
                    bT, = yield from take(1)
                    transpose4(bT, Nm, n_("Nm"))
                    P.op("pool", lambda e: e.tensor_tensor(out=r_(Qa[:]), in0=Nm[:], in1=bcast4(bd32), op=ALU.mult), reads=[n_("Nm"), "bd32"], writes=[n_("Qa")])
                    yield
                    P.op("dve", lambda e: e.tensor_copy(out=fl(NmT), in_=psb[bT][:]), reads=[("ps", bT)], writes=[n_("NmT")])
                    rel(bT)
                    P.op("pool", lambda e: e.tensor_tensor(out=r_(Wm[:]), in0=Qa[:], in1=bcast4(ident), op=ALU.add), reads=[n_("Qa"), "ident"], writes=[n_("Wm")])
                    yield
                    P.op("pool", lambda e: e.tensor_tensor(out=r_(Qb[:]), in0=NmT[:], in1=bcast4(bd32), op=ALU.mult), reads=[n_("NmT"), "bd32"], writes=[n_("Qb")])
                    yield
                    for lvl in range(4):
                        last = (lvl == 3)
                        bTq, bNq = yield from take(2)
                        mm4(bTq, Qa, n_("Qa"), Qb, n_("Qb"))
                        if not last:
                            mm4(bNq, Qb, n_("Qb"), Qa, n_("Qa"))
                        yield
                        P.op("dve", lambda e: e.tensor_copy(out=r_(fl(Qb)), in_=psb[bTq][:]), reads=[("ps", bTq)], writes=[n_("Qb")])
                        if not last:
                            P.op("dve", lambda e: e.tensor_copy(out=r_(fl(Qa)), in_=psb[bNq][:]), reads=[("ps", bNq)], writes=[n_("Qa")])
                        rel(bTq, bNq)
                        yield
                        bW, = yield from take(1)
                        mm4(bW, Qb, n_("Qb"), Wm, n_("Wm"))
                        yield
                        P.op("dve", lambda e: e.tensor_tensor(out=r_(fl(Wm)), in0=psb[bW][:], in1=fl(Wm), op=ALU.add), reads=[("ps", bW), n_("Wm")], writes=[n_("Wm")])
                        rel(bW)
                        yield
                    for om, omn in ((od64, "od64"), (od128, "od128")):
                        bt, bZ = yield from take(2)
                        transpose4(bt, Wm, n_("Wm"))
                        P.op("pool", lambda e: e.tensor_tensor(out=r_(Qb[:]), in0=NmT[:], in1=bcast4(om), op=ALU.mult), reads=[n_("NmT"), omn], writes=[n_("Qb")])
                        yield
                        mm4(bZ, Qb, n_("Qb"), Wm, n_("Wm"))
                        P.op("dve", lambda e: e.tensor_copy(out=r_(fl(T_["tmpWT"])), in_=psb[bt][:]), reads=[("ps", bt)], writes=[T_["tmpWT_n"]])
                        yield
                        P.op("dve", lambda e: e.tensor_copy(out=r_(fl(T_["tmpZ"])), in_=psb[bZ][:]), reads=[("ps", bZ)], writes=[T_["tmpZ_n"]])
                        rel(bt, bZ)
                        yield
                        bW, = yield from take(1)
                        mm4(bW, T_["tmpWT"], T_["tmpWT_n"], T_["tmpZ"], T_["tmpZ_n"])
                        yield
                        P.op("dve", lambda e: e.tensor_tensor(out=r_(fl(Wm)), in0=psb[bW][:], in1=fl(Wm), op=ALU.add), reads=[("ps", bW), n_("Wm")], writes=[n_("Wm")])
                        rel(bW)
                        yield
                    prep_done[g] = True

                def recur(d, g):
                    h, it = divmod(g, NIT)
                    hb = h % 2
                    qn, kn, Ktok, Vtok, zs = qn2[hb], kn2[hb], Ktok2[hb], Vtok2[hb], zs2[hb]
                    qnn, knn, Ktn, Vtn, zsn = "qn%d" % hb, "kn%d" % hb, "Ktok%d" % hb, "Vtok%d" % hb, "zs%d" % hb
                    hd = lambda d_: d_ * 8 + h
                    odone = odone_h[h]
                    T_ = SETS[g % NSETS]
                    while (not prep_done[g]) or (it == 0 and h > 0 and not head_fin[h - 1]):
                        yield
                    if it == 0:
                        P.op("pool", lambda e: e.tensor_scalar(out=Sst[d][:], in0=ident[:], scalar1=0.0, scalar2=None, op0=ALU.mult), reads=["ident"], writes=["Sst%d" % d])
                    QdT, QKT, Kd, Wf = T_["QdT"], T_["QKT"], T_["Kd"], T_["Wm"]
                    n_ = lambda k: T_[k + "_n"]
                    units = units_of(it)
                    for u in (2 * d, 2 * d + 1):
                        ch = units[u][1]
                        co = ch_off(ch); Sd = Sst[d]; Sn = "Sst%d" % d; hdd = hd(d)
                        ri = d
                        b1, = yield from take(1)
                        P.op("pe", lambda e: e.matmul(psb[b1][:, 0:128], lhsT=kn[:, co:co + 128], rhs=Sd[:], start=True, stop=True), reads=[knn, Sn], writes=[("ps", b1)])
                        yield
                        P.op("dve", lambda e: e.scalar_tensor_tensor(out=Rp[ri][:], in0=psb[b1][:, 0:128], scalar=neg_eg[:, ch, hdd:hdd + 1], in1=Vtok[:, ch, :], op0=ALU.mult, op1=ALU.add),
                             reads=[("ps", b1), "neg_eg", Vtn], writes=["Rp%d" % ri])
                        rel(b1)
                        yield
                        b2, = yield from take(1)
                        P.op("pe", lambda e: e.matmul(psb[b2][:, 0:128], lhsT=r_(Wf[:, u, :]), rhs=Rp[ri][:], start=True, stop=True), reads=[n_("Wm"), "Rp%d" % ri], writes=[("ps", b2)])
                        yield
                        P.op("dve", lambda e: e.tensor_scalar(out=Vn[ri][:], in0=psb[b2][:, 0:128], scalar1=beta[:, ch, hdd:hdd + 1], scalar2=None, op0=ALU.mult),
                             reads=[("ps", b2), "beta"], writes=["Vn%d" % ri])
                        rel(b2)
                        yield
                        b5, b3 = yield from take(2)
                        P.op("pe", lambda e: e.matmul(psb[b5][:, 0:128], lhsT=r_(Kd[:, u, :]), rhs=Vn[ri][:], start=True, stop=True), reads=[(n_("Kd"), u), "Vn%d" % ri], writes=[("ps", b5)])
                        if ch >= 2:
                            lc = ch - 2
                            P.op("pe", lambda e: e.matmul(psb[b3][:, 0:128], lhsT=r_(QdT[:, u, :]), rhs=Sd[:], start=True, stop=False), reads=[(n_("QdT"), u), Sn], writes=[("ps", b3)], inc=False)
                            P.op("pe", lambda e: e.matmul(psb[b3][:, 0:128], lhsT=r_(QKT[:, u, :]), rhs=Vn[ri][:], start=False, stop=True), reads=[n_("QKT"), "Vn%d" % ri], writes=[("ps", b3)])
                        yield
                        P.op("dve", lambda e: e.scalar_tensor_tensor(out=Sd[:], in0=f32v(Sd[:]), scalar=cdl[:, ch, hdd:hdd + 1], in1=psb[b5][:, 0:128], op0=ALU.mult, op1=ALU.add),
                             reads=[("ps", b5), "cdl", Sn], writes=[Sn])
                        rel(b5)
                        if ch < 2:
                            rel(b3)
                        if ch >= 2:
                            if not odone[lc]:
                                odone[lc] = True
                                P.op("dve", lambda e: e.tensor_copy(out=oacc[:, lc, :], in_=psb[b3][:, 0:128]), reads=[("ps", b3)], writes=[("oacc", lc)])
                                rel(b3)
                            else:
                                o_ = ot[d]; on_ = "ot%d" % d; sg = stg[d]; sgn = "stg%d" % d
                                P.op("dve", lambda e: e.tensor_tensor(out=o_[:], in0=psb[b3][:, 0:128], in1=oacc[:, lc, :], op=ALU.add), reads=[("ps", b3), ("oacc", lc)], writes=[on_])
                                rel(b3)
                                yield
                                P.op("act", lambda e: e.activation(out=junkg[d][:], in_=o_[:], func=AF.Square, accum_out=sg[:, 0:1]), reads=[on_], writes=["junkg%d" % d, sgn])
                                P.op("act", lambda e: e.activation(out=sg[:, 1:2], in_=sg[:, 0:1], func=AF.Sqrt, scale=1.0 / 128, bias=EPS), reads=[sgn], writes=[sgn])
                                yield
                                P.op("dve", lambda e: e.reciprocal(out=sg[:, 2:3], in_=sg[:, 1:2]), reads=[sgn], writes=[sgn])
                                yield
                                P.op("act", lambda e: e.activation(out=o_[:], in_=o_[:], func=AF.Copy, scale=sg[:, 2:3]), reads=[on_, sgn], writes=[on_])
                                yield
                                b4, = yield from take(1)
                                P.op("pe", lambda e: e.transpose(psb[b4][:, 0:128], o_[:], ident[:]), reads=[on_, "ident"], writes=[("ps", b4)])
                                yield
                                P.op("dve", lambda e: e.scalar_tensor_tensor(out=mixg[:, lc * 128:(lc + 1) * 128], in0=psb[b4][:, 0:128], scalar=pv(s_gnw), in1=zs[:, lc * 128:(lc + 1) * 128], op0=ALU.mult, op1=ALU.mult),
                                     reads=[("ps", b4), zsn] + rpar, writes=[(mgn, lc)])
                                rel(b4)
                        yield
                    rec_done[d][g] = True
                    if it == NIT - 1:
                        dir_fin[d][h] = True
                        if dir_fin[1 - d][h]:
                            load(mix_scr[8 + h], mixg[:], reads=[mgn], name=("mix_scr", 8 + h))
                            head_fin[h] = True

                def chain(fn, *a):
                    for g in range(NG):
                        yield from fn(*a, g)

                def prep_lane(l):
                    for g in range(l, NG, NSETS):
                        yield from prep(g)

                def pre_chain():
                    for h in range(NH):
                        while h >= 2 and not head_fin[h - 2]:
                            yield
                        yield from headpre(h)
                        pre_done[h] = True

            run_tasks([pre_chain()] + [prep_lane(l) for l in range(NSETS)] + [chain(recur, 0), chain(recur, 1)])
            assert len(freeb) == 8
            precast_some(1000)
          P.barrier()

        if "C" in phases:
          with ExitStack() as sc:
            wo = sbt(sc, "wo", [128, 16, D], BF16)
            GM_row = make_row(sc, 4, "GM_row")
            wov = wout_d.rearrange("(k p) c -> p k c", p=128)
            for k4 in range(0, 16, 4):
                P.dma("pool", (lambda k4: lambda e: e.dma_start(out=wo[:, k4:k4 + 4, :], in_=wov[:, k4:k4 + 4, :]))(k4), writes=[("wo", k4, k4 + 4)])
            mt = [sbt(sc, "mt%d" % i, [128, 16, 512], BF16) for i in range(2)]
            xc = [sbt(sc, "xc%d" % i, [128, D]) for i in range(2)]
            x1t = [sbt(sc, "x1t%d" % i, [128, D]) for i in range(2)]
            h2t = [sbt(sc, "h2t%d" % i, [128, 16, 128], BF16) for i in range(2)]
            junkc = sbt(sc, "junkc", [128, D]); stc2 = [sbt(sc, "stc%d" % i, [128, 16]) for i in range(2)]
            mixv = mix_scr.rearrange("k p t -> p k t")
            h2v = h2_scr.rearrange("k p t -> p k t")
            def stageA(tt):
                g = tt // 4
                m = mt[g % 2]; mn = "mt%d" % (g % 2)
                if tt % 4 == 0:
                    load(m[:], mixv[:, :, g * 512:(g + 1) * 512], reads=["mix_scr"], name=mn)
                i = tt % 2
                stc = stc2[i]; stn = "stc%d" % i
                load(xc[i][:], x_d[tt * 128:(tt + 1) * 128, :], name="xc%d" % i)
                bs = []
                for cg in range(4):
                    b = bank(); bs.append(b)
                    for k in range(16):
                        P.op("pe", lambda e: e.matmul(psb[b][:], lhsT=m[:, k, (tt % 4) * 128:(tt % 4 + 1) * 128], rhs=wo[:, k, cg * 512:(cg + 1) * 512], start=(k == 0), stop=(k == 15)),
                             reads=[mn, ("wo", k)], writes=[("ps", b)], inc=(k == 15))
                for cg in range(4):
                    b = bs[cg]
                    P.op("act", lambda e: e.activation(out=junkc[:, cg * 512:(cg + 1) * 512], in_=psb[b][:], func=AF.Square, accum_out=stc[:, cg:cg + 1]),
                         reads=[("ps", b)], writes=[("junkc", cg), (stn, cg)])
                P.op("dve", lambda e: e.tensor_reduce(out=stc[:, 4:5], in_=stc[:, 0:4], axis=mybir.AxisListType.X, op=ALU.add), reads=[stn], writes=[stn])
                P.op("act", lambda e: e.activation(out=stc[:, 5:6], in_=stc[:, 4:5], func=AF.Sqrt, scale=1.0 / D, bias=EPS), reads=[stn], writes=[stn])
                P.op("dve", lambda e: e.reciprocal(out=stc[:, 6:7], in_=stc[:, 5:6]), reads=[stn], writes=[stn])
                x1 = x1t[i]; x1n = "x1t%d" % i
                for cg in range(4):
                    b = bs[cg]
                    cs = slice(cg * 512, (cg + 1) * 512)
                    P.op("act", lambda e: e.activation(out=x1[:, cs], in_=psb[b][:], func=AF.Copy, scale=stc[:, 6:7]), reads=[("ps", b), stn], writes=[(x1n, cg)])
                    P.op("dve", lambda e: e.tensor_tensor(out=x1[:, cs], in0=x1[:, cs], in1=GM_row[:, cs], op=ALU.mult), reads=[(x1n, cg), "GM_row"], writes=[(x1n, cg)])
                    P.op("pool", lambda e: e.tensor_tensor(out=x1[:, cs], in0=x1[:, cs], in1=xc[i][:, cs], op=ALU.add), reads=[(x1n, cg), "xc%d" % i], writes=[(x1n, cg)])
                load(x1_scr[tt * 128:(tt + 1) * 128, :], x1[:], q="pool", reads=[x1n], name=("x1_scr", tt))
                P.op("act", lambda e: e.activation(out=junkc[:], in_=x1[:], func=AF.Square, accum_out=stc[:, 8:9]), reads=[x1n], writes=["junkc", stn])
                P.op("act", lambda e: e.activation(out=stc[:, 9:10], in_=stc[:, 8:9], func=AF.Sqrt, scale=1.0 / D, bias=EPS), reads=[stn], writes=[stn])
                P.op("dve", lambda e: e.reciprocal(out=stc[:, 10:11], in_=stc[:, 9:10]), reads=[stn], writes=[stn])
                P.op("act", lambda e: e.activation(out=xc[i][:], in_=x1[:], func=AF.Copy, scale=stc[:, 10:11]), reads=[x1n, stn], writes=["xc%d" % i])

            def stageB(tt):
                i = tt % 2
                xn = xc[i]; xnn = "xc%d" % i
                ht = h2t[i]; htn = "h2t%d" % i
                for g4 in range(4):
                    b = bank()
                    for q in range(4):
                        k = g4 * 4 + q
                        P.op("pe", lambda e: e.transpose(psb[b][:, q * 128:(q + 1) * 128], xn[:, k * 128:(k + 1) * 128], ident[:]), reads=[xnn, "ident"], writes=[("ps", b)], inc=(q == 3))
                    for q in range(4):
                        k = g4 * 4 + q
                        P.op("dve", lambda e: e.tensor_scalar(out=ht[:, k, :], in0=psb[b][:, q * 128:(q + 1) * 128], scalar1=vecs[:, 5, k:k + 1], scalar2=vecs[:, 6, k:k + 1], op0=ALU.mult, op1=ALU.add),
                             reads=[("ps", b), "vecs"], writes=[(htn, k)])
                load(h2v[:, :, tt * 128:(tt + 1) * 128], ht[:], q="pool", reads=[htn], name=("h2_scr", tt))

            stageA(0)
            for tt in range(1, 16):
                stageA(tt)
                stageB(tt - 1)
            stageB(15)
          P.barrier()

        out_toks = []
        if "D" in phases:
          with ExitStack() as sd:
            h2 = sbt(sd, "h2", [128, 16, 512], BF16)
            GF_row = make_row(sd, 7, "GF_row")
            f1 = sbt(sd, "f1", [128, 64, 512], BF16)
            w1t = [sbt(sd, "w1t%d" % i, [128, 16, 128], BF16) for i in range(3)]
            w2t = [sbt(sd, "w2t%d" % i, [128, 64, 128], BF16) for i in range(2)]
            rl = [sbt(sd, "rl%d" % i, [128, 512]) for i in range(2)]
            y2b = [sbt(sd, "y2b%d" % i, [128, 512]) for i in range(2)]
            y2 = sbt(sd, "y2", [128, 4, D])
            x1d = sbt(sd, "x1d", [128, D]); junkd = sbt(sd, "junkd", [128, D], BF16); std = sbt(sd, "std", [128, 8])
            h2v = h2_scr.rearrange("k p t -> p k t")
            w1v = w1_d.rearrange("(k p) (o c) -> o p k c", p=128, c=128)
            w2v = w2_d.rearrange("(k p) (o c) -> o p k c", p=128, c=128)
            def epi(T, q):
                tt = T * 4 + q
                load(x1d[:], x1_scr[tt * 128:(tt + 1) * 128, :], q="pool", reads=[("x1_scr", tt)], name="x1d")
                P.op("act", lambda e: e.activation(out=junkd[:], in_=y2[:, q, :], func=AF.Square, accum_out=std[:, 0:1]), reads=["y2"], writes=["junkd", "std"])
                P.op("act", lambda e: e.activation(out=std[:, 1:2], in_=std[:, 0:1], func=AF.Sqrt, scale=1.0 / D, bias=EPS), reads=["std"], writes=["std"])
                P.op("dve", lambda e: e.reciprocal(out=std[:, 2:3], in_=std[:, 1:2]), reads=["std"], writes=["std"])
                P.op("dve", lambda e: e.scalar_tensor_tensor(out=y2[:, q, :], in0=y2[:, q, :], scalar=std[:, 2:3], in1=GF_row[:], op0=ALU.mult, op1=ALU.mult),
                     reads=["y2", "std", "GF_row"], writes=["y2"])
                P.op("dve", lambda e: e.tensor_tensor(out=x1d[:], in0=x1d[:], in1=y2[:, q, :], op=ALU.add), reads=["x1d", "y2"], writes=["x1d"])
                out_toks.append(load(out_d[tt * 128:(tt + 1) * 128, :], x1d[:], q="pool", reads=["x1d"], name=("out", tt)))

            def ff1(T):
                for o in range(64):
                    if T > 0 and o in (6, 12, 18, 24):
                        epi(T - 1, (o // 6) - 1)
                    wt = w1t[o % 3]; wtn = "w1t%d" % (o % 3)
                    load(wt[:], w1b[o], reads=[("w1b", o)], name=wtn)
                    b = bank()
                    for k in range(16):
                        P.op("pe", lambda e: e.matmul(psb[b][:], lhsT=wt[:, k, :], rhs=h2[:, k, :], start=(k == 0), stop=(k == 15)), reads=[wtn, "h2"], writes=[("ps", b)], inc=(k == 15))
                    r = rl[o % 2]; rn = "rl%d" % (o % 2)
                    P.op("act", lambda e: e.activation(out=r[:], in_=psb[b][:], func=AF.Relu), reads=[("ps", b)], writes=[rn])
                    P.op("dve", lambda e: e.tensor_tensor(out=f1[:, o, :], in0=r[:], in1=r[:], op=ALU.mult), reads=[rn], writes=[("f1", o)])

            def ff2(T):
                for o in range(16):
                    wt = w2t[o % 2]; wtn = "w2t%d" % (o % 2)
                    for kh in range(4):
                        load(wt[:, kh * 16:(kh + 1) * 16, :], w2b[o][:, kh * 16:(kh + 1) * 16, :], reads=[("w2b", o * 4 + kh)], name=(wtn, kh))
                    b = bank()
                    for k in range(64):
                        P.op("pe", lambda e: e.matmul(psb[b][:], lhsT=wt[:, k, :], rhs=f1[:, k, :], start=(k == 0), stop=(k == 63)), reads=[(wtn, k // 16), ("f1", k)], writes=[("ps", b)], inc=(k == 63))
                    yb = y2b[o % 2]; ybn = "y2b%d" % (o % 2)
                    P.op("act", lambda e: e.copy(out=yb[:], in_=psb[b][:]), reads=[("ps", b)], writes=[ybn])
                    b2 = bank()
                    for q in range(4):
                        P.op("pe", lambda e: e.transpose(psb[b2][:, q * 128:(q + 1) * 128], yb[:, q * 128:(q + 1) * 128], ident[:]), reads=[ybn, "ident"], writes=[("ps", b2)], inc=(q == 3))
                    P.op("dve", lambda e: e.tensor_copy(out=y2[:, :, o * 128:(o + 1) * 128], in_=psb[b2][:].rearrange("p (q c) -> p q c", c=128)), reads=[("ps", b2)], writes=[("y2", o)])

            load(h2[:], h2v[:, :, 0:512], reads=["h2_scr"], name="h2")
            for T in range(4):
                ff1(T)
                if T < 3:
                    load(h2[:], h2v[:, :, (T + 1) * 512:(T + 2) * 512], reads=["h2_scr"], name="h2")
                ff2(T)
            for q in range(4):
                epi(3, q)
        if not out_toks:
            zt = sbt(top, "zt", [128, D])
            P.op("pool", lambda e: e.memset(zt[:], 0.0), writes=["zt"])
            for tt in range(16):
                out_toks.append(load(out_d[tt * 128:(tt + 1) * 128, :], zt[:], reads=["zt"]))
        P.barrier()
        P.final_wait("sp", out_toks)
        global LAST_PROG
        LAST_PROG = P
        with nc.Block() as block:
            P.emit(block)
    return nc


def host_layout(inp, b):
    f = lambda a: np.ascontiguousarray(a, dtype=np.float32)
    pk = lambda v: f(v.reshape(-1, 128).T)
    cc = np.stack([pk(inp["c"][b]), pk(inp["c_ctx"])], axis=2).reshape(128, 32)
    nw = inp["norm_w"][0]
    nwT = np.stack([pk(nw[i]) for i in range(4)], axis=1).reshape(128, 64)
    lcw = inp["lru_conv_w"][0].reshape(4, 8, 128).transpose(2, 1, 0).reshape(128, 32)
    lcb = inp["lru_conv_b"][0].reshape(8, 128).T
    lgw = inp["lru_gate_w"][0].transpose(3, 0, 1, 2, 4).reshape(128, 4096)
    lgb = inp["lru_gate_b"][0].reshape(2, 2, 8, 128).transpose(3, 0, 1, 2).reshape(128, 32)
    llam = inp["lru_lambda"][0].reshape(2, 8, 128).transpose(2, 0, 1).reshape(128, 16)
    gcw = inp["gdn_conv_w"][0].reshape(4, 3, 8, 128).transpose(3, 1, 2, 0).reshape(128, 96)
    galog = np.broadcast_to(inp["gdn_a_log"][0].reshape(1, 16), (128, 16))
    gdtb = np.broadcast_to(inp["gdn_dt_bias"][0].reshape(1, 16), (128, 16))
    return {
        "x": f(inp["x"][b]), "ctx": f(inp["ctx"][b]), "cc": f(cc),
        "w_mod": f(inp["w_mod"][0]), "b_modT": pk(inp["b_mod"][0]), "nwT": f(nwT),
        "w_in": f(inp["w_in"][0]), "lcw": f(lcw), "lcb": f(lcb), "lgw": f(lgw), "lgb": f(lgb), "llam": f(llam),
        "gcw": f(gcw), "galog": f(galog), "gdtb": f(gdtb), "gnw": f(inp["gdn_norm_w"][0].reshape(128, 1)),
        "w_out": f(inp["w_out"][0]), "w_ff1": f(inp["w_ff1"][0]), "w_ff2": f(inp["w_ff2"][0]),
    }


def kernel(**inputs):
    inp = {k: np.asarray(v) for k, v in inputs.items()}
    nc = build()
    in_maps = [host_layout(inp, b) for b in range(8)]
    res = run_bass_kernel_spmd(nc, in_maps, core_ids=list(range(8)))
    return np.stack([np.asarray(r["out"], dtype=np.float32) for r in res.results], axis=0)
```
